# Optimizing a Trainium2 kernel written in Bass

```python
import math
import jax, jax.numpy as jnp
from jax import lax
import numpy as np

D_MODEL = 1024
BATCH = 2
SEQ = 8192
DEPTH = 2

CTX_LEN = 256
GRID_W = 64
HEAD_DIM = 64
ROPE_BASE = 10000.0
EPS = 1e-6
CHUNK = 128
D_RET = D_MODEL // 2
D_MLSTM = D_MODEL // 2
H_RET = D_RET // HEAD_DIM
H_MLSTM = D_MLSTM // HEAD_DIM
CONV_W = 3
AB_IN = 4 * D_RET + 4 * D_MLSTM + 4 * H_MLSTM
AB_SPLITS = tuple(int(s) for s in np.cumsum([D_RET] * 4 + [D_MLSTM] * 4))
H_ATTN = D_MODEL // HEAD_DIM
H_KV = 4
GQA_G = H_ATTN // H_KV
WINDOW = 128
BLK = 128
ATTN_IN = D_MODEL + 2 * H_KV * HEAD_DIM
D_FF = ((8 * D_MODEL // 3 + 255) // 256) * 256

kernel_name = 'hybrid_retention_mlstm_window_gqa_dit'


def _rmsnorm(x, w):
    xf = x.astype(jnp.float32)
    y = xf * lax.rsqrt(jnp.mean(xf * xf, axis=-1, keepdims=True) + EPS)
    return (y * w.astype(jnp.float32)).astype(x.dtype)


def _heads(x, h):
    return x.reshape(x.shape[0], x.shape[1], h, -1)


def _head_major(a):
    return jnp.transpose(a, (0, 2, 1, 3)).astype(jnp.float32)


def _axial_rope(L, dtype):
    rows = L // GRID_W
    row = jnp.repeat(jnp.arange(rows, dtype=jnp.float32), GRID_W)
    col = jnp.tile(jnp.arange(GRID_W, dtype=jnp.float32), rows)
    n = HEAD_DIM // 4
    inv = ROPE_BASE ** (-jnp.arange(n, dtype=jnp.float32) / n)
    ang = jnp.concatenate([row[:, None] * inv, col[:, None] * inv], axis=-1)
    return jnp.cos(ang).astype(dtype), jnp.sin(ang).astype(dtype)


def _apply_rope(x, cos, sin):
    half = x.shape[-1] // 2
    x1, x2 = x[..., :half], x[..., half:]
    c = cos[None, :, None, :]
    s = sin[None, :, None, :]
    return jnp.concatenate([x1 * c - x2 * s, x2 * c + x1 * s], axis=-1)


def _short_conv(x, w, b):
    L = x.shape[1]
    p = CONV_W // 2
    xp = jnp.pad(x, ((0, 0), (p, p), (0, 0)))
    y = b
    for j in range(CONV_W):
        y = y + w[j] * xp[:, j:j + L]
    return y


def _swiglu(h, w_in, w_out):
    g, u = jnp.split(h @ w_in, 2, axis=-1)
    return (jax.nn.silu(g) * u) @ w_out


def _retention_chunked(q, k, v, log_g, s0):
    B, H, L, dk = q.shape
    n = L // CHUNK
    k = k * dk ** -0.5
    qc = q.reshape(B, H, n, CHUNK, dk)
    kc = k.reshape(B, H, n, CHUNK, dk)
    vc = v.reshape(B, H, n, CHUNK, -1)
    pos = jnp.arange(CHUNK, dtype=jnp.float32)
    diff = pos[:, None] - pos[None, :]
    decay = jnp.where(diff >= 0, jnp.exp(log_g[:, None, None] * jnp.maximum(diff, 0.0)), 0.0)
    scores = jnp.einsum('bhnid,bhnjd->bhnij', qc, kc) * decay[None, :, None]
    intra = jnp.einsum('bhnij,bhnje->bhnie', scores, vc)
    k_dec = jnp.exp(log_g[:, None] * (CHUNK - 1.0 - pos))
    kv_local = jnp.einsum('bhnjd,hj,bhnje->nbhde', kc, k_dec, vc)
    chunk_decay = jnp.exp(log_g * CHUNK)[None, :, None, None]

    def step(s, kv):
        return chunk_decay * s + kv, s

    s_fin, s_prev = lax.scan(step, s0, kv_local)
    q_dec = jnp.exp(log_g[:, None] * (pos + 1.0))
    inter = jnp.einsum('bhnid,hi,nbhde->bhnie', qc, q_dec, s_prev)
    return (intra + inter).reshape(B, H, L, -1), s_fin


def _mlstm_chunked(q, k, v, i_pre, f_pre, state):
    B, H, L, d = q.shape
    n = L // CHUNK
    k = k * d ** -0.5
    qc = q.reshape(B, H, n, CHUNK, d)
    kc = k.reshape(B, H, n, CHUNK, d)
    vc = v.reshape(B, H, n, CHUNK, d)
    ic = i_pre.reshape(B, H, n, CHUNK)
    b = jnp.cumsum(jax.nn.log_sigmoid(f_pre).reshape(B, H, n, CHUNK), axis=-1)
    b_end = b[..., -1]
    causal = jnp.tril(jnp.ones((CHUNK, CHUNK), dtype=bool))
    d_log = jnp.where(causal, b[..., :, None] - b[..., None, :] + ic[..., None, :], -jnp.inf)
    g_end = b_end[..., None] - b + ic
    g_max = jnp.max(g_end, axis=-1)
    w_end = jnp.exp(g_end - g_max[..., None])
    kv_local = jnp.einsum('bhns,bhnsd,bhnse->nbhde', w_end, kc, vc)
    n_local = jnp.einsum('bhns,bhnsd->nbhd', w_end, kc)

    def step(carry, xs):
        c_mat, n_vec, m = carry
        kv, nl, gm, be = xs
        m_new = jnp.maximum(be + m, gm)
        a = jnp.exp(be + m - m_new)
        bb = jnp.exp(gm - m_new)
        c_new = a[..., None, None] * c_mat + bb[..., None, None] * kv
        n_new = a[..., None] * n_vec + bb[..., None] * nl
        return (c_new, n_new, m_new), (c_mat, n_vec, m)

    xs = (kv_local, n_local, jnp.moveaxis(g_max, 2, 0), jnp.moveaxis(b_end, 2, 0))
    final, (c_prev, n_prev, m_prev) = lax.scan(step, state, xs)
    m_prev = jnp.moveaxis(m_prev, 0, 2)
    a_log = b + m_prev[..., None]
    m_t = jnp.maximum(a_log, jnp.max(d_log, axis=-1))
    w = jnp.exp(d_log - m_t[..., None])
    a = jnp.exp(a_log - m_t)
    s = jnp.einsum('bhntd,bhnsd->bhnts', qc, kc) * w
    num = jnp.einsum('bhnts,bhnse->bhnte', s, vc) + a[..., None] * jnp.einsum('bhntd,nbhde->bhnte', qc, c_prev)
    den = jnp.sum(s, axis=-1) + a * jnp.einsum('bhntd,nbhd->bhnt', qc, n_prev)
    h = num / jnp.maximum(jnp.abs(den), jnp.exp(-m_t))[..., None]
    return h.reshape(B, H, L, d), final


def _sink_softmax(scores, sink):
    snk = sink[:, :, None]
    m = snk
    for s in scores:
        m = jnp.maximum(m, jnp.max(s, axis=-1))
    ex = [jnp.exp(s - m[..., None]) for s in scores]
    denom = jnp.exp(snk - m)
    for e in ex:
        denom = denom + jnp.sum(e, axis=-1)
    return [e / denom[..., None] for e in ex]


def _ret_mlstm_mixer(h, hc, rope, w_in, w_out, ret_log_gamma, ret_norm_w,
                     conv_w, conv_b, gate_b, mlstm_norm_w):
    f32 = jnp.float32
    B = h.shape[0]

    def prep(t, use_rope):
        rq, rk, rv, rg, mq, mk, mv, mo, mg = jnp.split(t @ w_in, AB_SPLITS, axis=-1)
        rq, rk, rv = _heads(rq, H_RET), _heads(rk, H_RET), _heads(rv, H_RET)
        if use_rope:
            rq, rk = rope(rq), rope(rk)
        mqk = jax.nn.silu(_short_conv(jnp.concatenate([mq, mk], axis=-1), conv_w, conv_b))
        mq, mk = jnp.split(mqk, 2, axis=-1)
        gates = (mg.astype(f32).reshape(t.shape[0], t.shape[1], 4, H_MLSTM)
                 + gate_b.astype(f32)).transpose(2, 0, 3, 1)
        ret = tuple(_head_major(a) for a in (rq, rk, rv))
        mls = tuple(_head_major(_heads(a, H_MLSTM)) for a in (mq, mk, mv))
        return ret, mls, gates, rg, mo

    ret_l, mls_l, gates_l, rg_l, mo_l = prep(h, True)
    ret_c, mls_c, gates_c, rg_c, mo_c = prep(hc, False)
    flip = lambda t: jnp.flip(t, axis=2)
    lg = ret_log_gamma.astype(f32)
    s0_r = jnp.zeros((B, H_RET, HEAD_DIM, HEAD_DIM), f32)
    s0_m = (jnp.zeros((B, H_MLSTM, HEAD_DIM, HEAD_DIM), f32),
            jnp.zeros((B, H_MLSTM, HEAD_DIM), f32),
            jnp.zeros((B, H_MLSTM), f32))

    def ret_dir(direction, ctx_args, lat_args):
        yc, s = _retention_chunked(*ctx_args, lg[direction], s0_r)
        yl, _ = _retention_chunked(*lat_args, lg[direction], s)
        return yc, yl

    def mls_dir(ctx_args, lat_args):
        yc, s = _mlstm_chunked(*ctx_args, s0_m)
        yl, _ = _mlstm_chunked(*lat_args, s)
        return yc, yl

    rc_f, rl_f = ret_dir(0, ret_c, ret_l)
    rc_b, rl_b = ret_dir(1, tuple(map(flip, ret_c)), tuple(map(flip, ret_l)))
    mc_f, ml_f = mls_dir((*mls_c, gates_c[0], gates_c[1]), (*mls_l, gates_l[0], gates_l[1]))
    mc_b, ml_b = mls_dir(tuple(map(flip, (*mls_c, gates_c[2], gates_c[3]))),
                         tuple(map(flip, (*mls_l, gates_l[2], gates_l[3]))))

    def merge(r, m, rg, mo, dtype):
        r = _rmsnorm(jnp.transpose(r, (0, 2, 1, 3)), ret_norm_w.reshape(H_RET, HEAD_DIM)).astype(dtype)
        r = r.reshape(r.shape[0], r.shape[1], D_RET) * jax.nn.silu(rg)
        m = _rmsnorm(jnp.transpose(m, (0, 2, 1, 3)), mlstm_norm_w.reshape(H_MLSTM, HEAD_DIM)).astype(dtype)
        m = m.reshape(m.shape[0], m.shape[1], D_MLSTM) * jax.nn.sigmoid(mo)
        return jnp.concatenate([r, m], axis=-1) @ w_out

    y_lat = merge(rl_f + flip(rl_b), ml_f + flip(ml_b), rg_l, mo_l, h.dtype)
    y_ctx = merge(rc_f + flip(rc_b), mc_f + flip(mc_b), rg_c, mo_c, hc.dtype)
    return y_lat, y_ctx


def _window_attn_mixer(h, hc, rope, w_in, w_out, q_norm_w, k_norm_w, sink, need_ctx_out):
    f32 = jnp.float32
    B, L, _ = h.shape
    nb = L // BLK
    scale = HEAD_DIM ** -0.5

    def prep(t):
        q, k, v = jnp.split(t @ w_in, [D_MODEL, D_MODEL + H_KV * HEAD_DIM], axis=-1)
        return (_rmsnorm(_heads(q, H_ATTN), q_norm_w), _rmsnorm(_heads(k, H_KV), k_norm_w), _heads(v, H_KV))

    q, k, v = prep(h)
    q, k = rope(q), rope(k)
    qc, kc, vc = prep(hc)
    snk = sink.astype(f32).reshape(H_KV, GQA_G)

    qb = q.reshape(B, nb, BLK, H_KV, GQA_G, HEAD_DIM)

    def band(t):
        tp = jnp.pad(t, ((0, 0), (BLK, BLK), (0, 0), (0, 0))).reshape(B, nb + 2, BLK, H_KV, HEAD_DIM)
        return jnp.concatenate([tp[:, :-2], tp[:, 1:-1], tp[:, 2:]], axis=2)

    kw, vw = band(k), band(v)
    s_win = jnp.einsum('bnqhgd,bnkhd->bnhgqk', qb, kw).astype(f32) * scale
    blk = jnp.arange(nb)[:, None, None]
    qpos = blk * BLK + jnp.arange(BLK)[None, :, None]
    kpos = (blk - 1) * BLK + jnp.arange(3 * BLK)[None, None, :]
    valid = (jnp.abs(qpos - kpos) <= WINDOW) & (kpos >= 0) & (kpos < L)
    s_win = jnp.where(valid[None, :, None, None], s_win, -jnp.inf)
    s_ctx = jnp.einsum('bnqhgd,bkhd->bnhgqk', qb, kc).astype(f32) * scale
    p_win, p_ctx = _sink_softmax([s_win, s_ctx], snk)
    o = (jnp.einsum('bnhgqk,bnkhd->bnqhgd', p_win.astype(v.dtype), vw)
         + jnp.einsum('bnhgqk,bkhd->bnqhgd', p_ctx.astype(vc.dtype), vc))
    y = o.reshape(B, L, D_MODEL) @ w_out
    if not need_ctx_out:
        return y, None
    Lc = hc.shape[1]
    qcg = qc.reshape(B, Lc, H_KV, GQA_G, HEAD_DIM)
    s_cc = jnp.einsum('bqhgd,bkhd->bhgqk', qcg, kc).astype(f32) * scale
    (p_cc,) = _sink_softmax([s_cc], snk)
    oc = jnp.einsum('bhgqk,bkhd->bqhgd', p_cc.astype(vc.dtype), vc).reshape(B, Lc, D_MODEL) @ w_out
    return y, oc


def setup_inputs(seed: int = 0) -> dict:
    key = jax.random.key(seed)
    ks = jax.random.split(key, 24)
    f32 = jnp.float32
    n_even = (DEPTH + 1) // 2
    n_odd = DEPTH // 2
    D = D_MODEL

    def nrm(k, shape, scale):
        return jax.random.normal(k, shape, f32) * scale

    base_lg = jnp.log1p(-jnp.exp2(-5.0 - jnp.arange(H_RET, dtype=f32)))
    fb = jnp.linspace(3.0, 6.0, H_MLSTM, dtype=f32)
    zb = jnp.zeros((H_MLSTM,), f32)
    gate_base = jnp.stack([zb, fb, zb, fb])
    return {
        'x': nrm(ks[0], (BATCH, SEQ, D), 1.0),
        'c': nrm(ks[1], (BATCH, D), 1.0),
        'ctx': nrm(ks[2], (BATCH, CTX_LEN, D), 1.0),
        'c_ctx': nrm(ks[3], (D,), 1.0),
        'ada_w': nrm(ks[4], (DEPTH, D, 6 * D), 0.5 * D ** -0.5),
        'ada_b': nrm(ks[5], (DEPTH, 6 * D), 0.02),
        'norm_w': 1.0 + nrm(ks[6], (DEPTH, 2, D), 0.02),
        'ffn_w_in': nrm(ks[7], (DEPTH, D, 2 * D_FF), D ** -0.5),
        'ffn_w_out': nrm(ks[8], (DEPTH, D_FF, D), D_FF ** -0.5),
        'ab_w_in': nrm(ks[9], (n_even, D, AB_IN), D ** -0.5),
        'ab_w_out': nrm(ks[10], (n_even, D_RET + D_MLSTM, D), (D_RET + D_MLSTM) ** -0.5),
        'ret_log_gamma': base_lg * (1.0 + nrm(ks[11], (n_even, 2, H_RET), 0.05)),
        'ret_norm_w': 1.0 + nrm(ks[12], (n_even, D_RET), 0.02),
        'mlstm_conv_w': nrm(ks[13], (n_even, CONV_W, 2 * D_MLSTM), CONV_W ** -0.5),
        'mlstm_conv_b': nrm(ks[14], (n_even, 2 * D_MLSTM), 0.02),
        'mlstm_gate_b': gate_base + nrm(ks[15], (n_even, 4, H_MLSTM), 0.1),
        'mlstm_norm_w': 1.0 + nrm(ks[16], (n_even, D_MLSTM), 0.02),
        'attn_w_in': nrm(ks[17], (n_odd, D, ATTN_IN), D ** -0.5),
        'attn_w_out': nrm(ks[18], (n_odd, D, D), D ** -0.5),
        'attn_q_norm_w': 1.0 + nrm(ks[19], (n_odd, HEAD_DIM), 0.02),
        'attn_k_norm_w': 1.0 + nrm(ks[20], (n_odd, HEAD_DIM), 0.02),
        'attn_sink': nrm(ks[21], (n_odd, H_ATTN), 0.5),
    }


def reference(x, c, ctx, c_ctx, ada_w, ada_b, norm_w, ffn_w_in, ffn_w_out, ab_w_in, ab_w_out,
              ret_log_gamma, ret_norm_w, mlstm_conv_w, mlstm_conv_b, mlstm_gate_b, mlstm_norm_w,
              attn_w_in, attn_w_out, attn_q_norm_w, attn_k_norm_w, attn_sink):
    L = x.shape[1]
    cos, sin = _axial_rope(L, x.dtype)
    rope = lambda t: _apply_rope(t, cos, sin)
    silu_c = jax.nn.silu(c)
    silu_cc = jax.nn.silu(c_ctx)
    for layer in range(DEPTH):
        last = layer == DEPTH - 1
        mod = (silu_c @ ada_w[layer] + ada_b[layer])[:, None, :]
        mod_c = (silu_cc @ ada_w[layer] + ada_b[layer])[None, None, :]
        sh1, sc1, g1, sh2, sc2, g2 = jnp.split(mod, 6, axis=-1)
        csh1, csc1, cg1, csh2, csc2, cg2 = jnp.split(mod_c, 6, axis=-1)
        h = _rmsnorm(x, norm_w[layer, 0]) * (1.0 + sc1) + sh1
        hc = _rmsnorm(ctx, norm_w[layer, 0]) * (1.0 + csc1) + csh1
        if layer % 2 == 0:
            e = layer // 2
            y, yc = _ret_mlstm_mixer(h, hc, rope, ab_w_in[e], ab_w_out[e], ret_log_gamma[e], ret_norm_w[e],
                                     mlstm_conv_w[e], mlstm_conv_b[e], mlstm_gate_b[e], mlstm_norm_w[e])
        else:
            o = layer // 2
            y, yc = _window_attn_mixer(h, hc, rope, attn_w_in[o], attn_w_out[o], attn_q_norm_w[o],
                                       attn_k_norm_w[o], attn_sink[o], not last)
        x = x + g1 * y
        x = x + g2 * _swiglu(_rmsnorm(x, norm_w[layer, 1]) * (1.0 + sc2) + sh2, ffn_w_in[layer], ffn_w_out[layer])
        if not last:
            ctx = ctx + cg1 * yc
            ctx = ctx + cg2 * _swiglu(_rmsnorm(ctx, norm_w[layer, 1]) * (1.0 + csc2) + csh2,
                                      ffn_w_in[layer], ffn_w_out[layer])
    return x
```

```python
import contextlib
import math
import numpy as np
import concourse.bass as bass
import concourse.mybir as mybir
from concourse.bass_utils import run_bass_kernel_spmd

F32 = mybir.dt.float32
BF16 = mybir.dt.bfloat16
ALU = mybir.AluOpType
AF = mybir.ActivationFunctionType
AX = mybir.AxisListType

_DTSZ = {F32: 4, BF16: 2}
SEM_LIMIT = 12000
DMA_POOL = 12


def _region(ap):
    sp = str(ap.space).upper()
    if 'SB' not in sp and 'PSUM' not in sp:
        return None
    if 'PSUM' in sp:
        return (ap.name, 0, 128, 0, 2048)
    pat = ap.ap
    esz = _DTSZ[ap.dtype]
    pstep, pcount = pat[0]
    off = ap.offset
    if pstep == 0:
        p0 = 0
        f0 = off
    else:
        p0 = off // pstep
        f0 = off - p0 * pstep
    ext = 0
    for stp, cn in pat[1:]:
        ext += abs(stp) * (cn - 1)
    return (ap.name, p0, p0 + pcount, f0 * esz, (f0 + ext + 1) * esz)


class Prog:
    ENGS = ('tensor', 'vector', 'scalar', 'gpsimd', 'sync')

    def __init__(self, nc, same_engine_sync=True):
        self.nc = nc
        self.ops = []
        self.track = {}
        self.same_engine_sync = same_engine_sync
        self.dma_hist = {e: [] for e in self.ENGS}
        self.store_ops = []

    def _add(self, eng, fn, outs, ins, is_dma=False, extra_deps=(), force=False):
        if getattr(self, 'frozen', False) and not force:
            return -1
        idx = len(self.ops)
        deps = set(extra_deps)
        self.ops.append(dict(eng=eng, fn=fn, deps=deps, is_dma=is_dma, signaled=False))
        for ap in ins:
            r = _region(ap)
            if r is not None:
                self._access(idx, eng, r, False, deps)
        for ap in outs:
            r = _region(ap)
            if r is not None:
                self._access(idx, eng, r, True, deps)
        if is_dma:
            h = self.dma_hist[eng]
            if len(h) >= DMA_POOL:
                deps.add(h[-DMA_POOL])
            h.append(idx)
        deps.discard(idx)
        return idx

    def _access(self, idx, eng, r, is_write, deps):
        name, p0, p1, b0, b1 = r
        recs = self.track.get(name, [])
        keep = []
        for rec in recs:
            (q0, q1, c0, c1, oi, ow, oe) = rec
            if q1 <= p0 or p1 <= q0 or c1 <= b0 or b1 <= c0 or oi == idx:
                keep.append(rec)
                continue
            if is_write or ow or (name.startswith('pb') and oe != eng):
                pe_pe = (eng == 'tensor' and oe == 'tensor')
                if not pe_pe:
                    deps.add(oi)
            covered = (p0 <= q0 and q1 <= p1 and b0 <= c0 and c1 <= b1)
            if is_write and covered and not (eng == 'tensor' and oe == 'tensor' and not ow):
                continue
            if (not is_write) and (not ow) and oe == eng and covered and not self.ops[oi]['is_dma']:
                continue
            keep.append(rec)
        keep.append((p0, p1, b0, b1, idx, is_write, eng))
        self.track[name] = keep

    def op(self, eng, fn, outs, ins):
        return self._add(eng, fn, outs, ins)

    def dma(self, eng, out, in_, store=False, **kw):
        i = self._add(eng, lambda e: e.dma_start(out=out, in_=in_, **kw), [out], [in_], is_dma=True)
        if store and i >= 0:
            self.store_ops.append(i)
        return i

    def mm(self, out, lhsT, rhs, start=True, stop=True, after=None):
        i = self.op('tensor', lambda e: e.matmul(out, lhsT, rhs, start=start, stop=stop), [out], [lhsT, rhs])
        if after is not None and i >= 0 and after >= 0:
            self.ops[i]['deps'].add(after)
        return i

    def transpose(self, out, in_, ident):
        return self.op('tensor', lambda e: e.transpose(out, in_, ident), [out], [in_, ident])

    def act(self, out, in_, func, bias=None, scale=None, accum_out=None):
        kw = {}
        ins = [in_]
        outs = [out]
        if bias is not None:
            kw['bias'] = bias
            if not isinstance(bias, (int, float)):
                ins.append(bias)
        if scale is not None:
            kw['scale'] = scale
            if not isinstance(scale, (int, float)):
                ins.append(scale)
        if accum_out is not None:
            kw['accum_out'] = accum_out
            outs.append(accum_out)
        return self.op('scalar', lambda e: e.activation(out, in_, func, **kw), outs, ins)

    def tt(self, eng, out, in0, in1, op):
        return self.op(eng, lambda e: e.tensor_tensor(out, in0, in1, op), [out], [in0, in1])

    def ts(self, eng, out, in0, s1, s2, op0, op1=None):
        ins = [in0] + [s for s in (s1, s2) if s is not None and not isinstance(s, (int, float))]
        kw = {}
        if op1 is not None:
            kw['op1'] = op1
        return self.op(eng, lambda e: e.tensor_scalar(out, in0, s1, s2, op0, **kw), [out], ins)

    def stt(self, eng, out, in0, scalar, in1, op0, op1):
        ins = [in0, in1] + ([scalar] if not isinstance(scalar, (int, float)) else [])
        return self.op(eng, lambda e: e.scalar_tensor_tensor(out, in0, scalar, in1, op0, op1), [out], ins)

    def copy(self, eng, out, in_):
        if eng == 'scalar':
            return self.op(eng, lambda e: e.copy(out, in_), [out], [in_])
        return self.op(eng, lambda e: e.tensor_copy(out, in_), [out], [in_])

    def memset(self, eng, out, val):
        return self.op(eng, lambda e: e.memset(out, val), [out], [])

    def recip(self, out, in_):
        return self.op('vector', lambda e: e.reciprocal(out, in_), [out], [in_])

    def reduce(self, eng, out, in_, op, axis=AX.X):
        return self.op(eng, lambda e: e.tensor_reduce(out, in_, axis, op), [out], [in_])

    def emit(self):
        nc = self.nc
        ops = self.ops
        self._add('sync', None, [], [], extra_deps=self.store_ops, force=True)
        for o in ops:
            if o['is_dma']:
                o['signaled'] = True
            for d in o['deps']:
                ops[d]['signaled'] = True
        cnt = {e: 0 for e in self.ENGS}
        dcnt = {e: 0 for e in self.ENGS}
        nsem_eng = {e: 0 for e in self.ENGS}
        for o in ops:
            e = o['eng']
            if not o['signaled']:
                continue
            if o['is_dma']:
                k = dcnt[e]
                dcnt[e] += 1
                o['sem'] = ('d', e, k % DMA_POOL)
                o['val'] = 16 * (k // DMA_POOL + 1)
                o['sidx'] = None
            else:
                k = cnt[e]
                cnt[e] += 1
                o['sem'] = ('c', e, k // SEM_LIMIT)
                o['val'] = (k % SEM_LIMIT) + 1
                o['sidx'] = k
                nsem_eng[e] = k // SEM_LIMIT + 1
        sems = {}
        st = contextlib.ExitStack()
        for e in self.ENGS:
            for j in range(nsem_eng[e]):
                sems[('c', e, j)] = st.enter_context(nc.semaphore(f"c_{e}_{j}"))
            for j in range(min(DMA_POOL, dcnt[e])):
                sems[('d', e, j)] = st.enter_context(nc.semaphore(f"d_{e}_{j}"))
        seen = {e: {f: -1 for f in self.ENGS} for e in self.ENGS}
        seen_dma = {e: set() for e in self.ENGS}
        per_eng = {e: [] for e in self.ENGS}
        nwaits = 0
        for o in ops:
            e = o['eng']
            waits = {}
            for d in sorted(o['deps']):
                p = ops[d]
                if p['is_dma']:
                    if d in seen_dma[e]:
                        continue
                    seen_dma[e].add(d)
                    waits[p['sem']] = max(waits.get(p['sem'], 0), p['val'])
                else:
                    f = p['eng']
                    if f == e and not self.same_engine_sync:
                        continue
                    if p['sidx'] <= seen[e][f]:
                        continue
                    seen[e][f] = p['sidx']
                    waits[p['sem']] = max(waits.get(p['sem'], 0), p['val'])
            nwaits += len(waits)
            per_eng[e].append((o, list(waits.items())))
        self.stats = dict(n_ops=len(ops), n_waits=nwaits, per_eng={e: len(v) for e, v in per_eng.items()})
        with st, nc.Block() as block:
            def body(engname):
                def run(eng):
                    for o, waits in per_eng[engname]:
                        for key, val in waits:
                            eng.wait_ge(sems[key], val)
                        if o['fn'] is None:
                            continue
                        ins = o['fn'](eng)
                        if o['signaled']:
                            ins.then_inc(sems[o['sem']], 16 if o['is_dma'] else 1)
                return run
            block.tensor(body('tensor'))
            block.vector(body('vector'))
            block.scalar(body('scalar'))
            block.gpsimd(body('gpsimd'))
            block.sync(body('sync'))
        return self.stats


D = 1024
SEQ = 8192
NCORE = 8
NW = 18
NCH = 20
NSLOT = 48
NF = 256
EPS = 1e-6
D_FF = 2816
NFT = 22
LNK = math.log(0.125)
HT_N = 2564
CTX0 = 1
WIN0 = 259
FL_VF = 0
FL_EDGE = 20
FL_FF = 22
FL_FB = 70
FL_SH = 118


def bc(ap, shape):
    return ap.broadcast_to(shape)


class Region:
    def __init__(self, t, base, cap, name):
        self.t, self.base, self.cap, self.name = t, base, cap, name
        self.off = 0
        self.peak = 0

    def alloc(self, shape, dt):
        n = 1
        for s in shape:
            n *= s
        nb = (n * _DTSZ[dt] + 63) // 64 * 64
        if self.off + nb > self.cap:
            raise RuntimeError(f"region {self.name} overflow: {self.off + nb} > {self.cap}")
        a = self.base + self.off
        v = self.t[:, a // 4:(a + nb) // 4]
        if dt != F32:
            v = v.bitcast(dt)
        v = v[:, 0:n]
        self.off += nb
        self.peak = max(self.peak, self.off)
        if len(shape) == 1:
            return v
        names = [chr(ord('a') + i) for i in range(len(shape))]
        pat = "p (" + " ".join(names) + ") -> p " + " ".join(names)
        return v.rearrange(pat, **{nm: s for nm, s in zip(names, shape)})

    def mark(self):
        return self.off

    def release(self, m):
        self.off = m


class _Stop(Exception):
    pass


ARENA_BYTES = 212480
P_BYTES = 35328
X_BYTES = 83968


def build(dbg=None):
    nc = bass.Bass("TRN2", target_bir_lowering=False)

    def din(name, shape):
        return nc.dram_tensor(name, list(shape), F32, kind="ExternalInput").ap()

    xw = din("xw", [NCH, 128, D])
    xe = din("xe", [128, D])
    xs = din("xs", [NSLOT, 128, D])
    xsh = din("xsh", [128, D])
    fl = din("fl", [1, NF])
    cvec = din("cvec", [16, 128])
    ada_w = din("ada_w", [2, D, 6 * D])
    ada_b = din("ada_b", [2, 6 * D])
    norm_w = din("norm_w", [2, 2, D])
    ffn_w_in = din("ffn_w_in", [2, D, 2 * D_FF])
    ffn_w_out = din("ffn_w_out", [2, D_FF, D])
    ab_w_in = din("ab_w_in", [1, D, 4128])
    ab_w_out = din("ab_w_out", [1, D, D])
    ret_lg = din("ret_log_gamma", [1, 2, 8])
    ret_nw = din("ret_norm_w", [1, 512])
    conv_w = din("mlstm_conv_w", [1, 3, D])
    conv_b = din("mlstm_conv_b", [1, D])
    gate_b = din("mlstm_gate_b", [1, 4, 8])
    ml_nw = din("mlstm_norm_w", [1, 512])
    at_w_in = din("attn_w_in", [1, D, 1536])
    at_w_out = din("attn_w_out", [1, D, D])
    at_qn = din("attn_q_norm_w", [1, 64])
    at_kn = din("attn_k_norm_w", [1, 64])
    at_sink = din("attn_sink", [1, 16])
    consts = din("consts", [7, 128, 128])
    amask = din("amask", [4, 128, 128])
    ropeF = din("ropeF", [2, 128, 2304])
    ropeT = din("ropeT", [2, 128, NW * 32])
    ropeS = din("ropeS", [2, 128, NSLOT * 32])
    y = nc.dram_tensor("y", [2048, D], F32, kind="ExternalOutput").ap()
    dbg_out = None
    if dbg is not None:
        dbg_out = nc.dram_tensor("dbg", [NCH, 128, D], F32, kind="ExternalOutput").ap()

    st = contextlib.ExitStack()
    with st:
        arena_t = st.enter_context(nc.sbuf_tensor("arena", [128, ARENA_BYTES // 4], F32))
        RP = Region(arena_t, 0, P_BYTES, "P")
        RX = Region(arena_t, P_BYTES, X_BYTES, "X")
        RM = Region(arena_t, P_BYTES + X_BYTES, ARENA_BYTES - P_BYTES - X_BYTES, "MF")
        banks = [st.enter_context(nc.psum_tensor(f"pb{i}", [128, 512], F32)) for i in range(8)]
        P = Prog(nc)
        P.marks = []

        def mark(nm):
            P.marks.append((nm, sum(1 for o in P.ops if o['eng'] == 'tensor')))

        def ck(name, aps):
            if dbg != name:
                return
            k = 0
            for ap in aps:
                n = ap.shape[1] if len(ap.shape) == 2 else None
                flat = ap
                npart = ap.shape[0]
                P.dma('sync' if flat.dtype == F32 else 'gpsimd', dbg_out[k][0:npart, 0:flat.shape[1]], flat, store=True)
                k += 1
            P.frozen = True

        def pb(i, shape, dt=F32, off=0):
            n = 1
            for s in shape:
                n *= s
            nb = n * _DTSZ[dt]
            v = banks[i][:, off // 4:(off + nb + 3) // 4]
            if dt != F32:
                v = v.bitcast(dt)
            v = v[:, 0:n]
            if len(shape) == 1:
                return v
            names = [chr(ord('a') + k) for k in range(len(shape))]
            pat = "p (" + " ".join(names) + ") -> p " + " ".join(names)
            return v.rearrange(pat, **{nm: s for nm, s in zip(names, shape)})

        cst = RP.alloc([7, 128], F32)
        P.dma('sync', cst, consts.rearrange("c p n -> p c n"))
        identF = cst[:, 0, :]
        MfF, MbF = cst[:, 1, :], cst[:, 2, :]
        onesF = cst[:, 5, :]
        cstb = RP.alloc([7, 128], BF16)
        P.copy('vector', cstb, cst)
        identB = cstb[:, 0, :]
        maskF_b, maskB_b = cstb[:, 3, :], cstb[:, 4, :]
        FL = RP.alloc([NF], F32)
        P.dma('sync', FL, bc(fl, [128, NF]))
        epsb = RP.alloc([1], F32)
        P.memset('vector', epsb, EPS)
        lnk = RP.alloc([1], F32)
        P.memset('vector', lnk, LNK)
        colA = RP.alloc([112], F32)
        colB = RP.alloc([48], F32)
        svf = RP.alloc([16], F32)
        sv = RP.alloc([8, 2], BF16)
        modT = RP.alloc([2, 2, 48], F32)
        gB = RP.alloc([4, D], F32)
        WM = RP.alloc([2, 2, 2, 8], F32)
        SHc = RP.alloc([2, 2, 2, 8], F32)
        junk = RP.alloc([D], BF16)
        xnb = [RP.alloc([D], BF16) for _ in range(2)]
        t32 = RP.alloc([8, 128], F32)
        ssq = RP.alloc([4], F32)

        m0 = RM.mark()
        stg = RM.alloc([128], F32)
        stg2 = RM.alloc([128], F32)
        P.dma('sync', stg[0:16, :], cvec)
        P.dma('sync', stg[16:64, :], ada_b[0].rearrange("(j p) -> j p", p=128))
        P.dma('sync', stg[64:112, :], ada_b[1].rearrange("(j p) -> j p", p=128))
        tp = pb(0, [112])
        P.transpose(tp, stg[0:112, :], identF[0:112, 0:112])
        P.copy('vector', colA, tp)
        P.dma('sync', stg2[0:32, :], norm_w.rearrange("l i (j p) -> (l i j) p", p=128))
        P.dma('sync', stg2[32:36, :], ret_nw[0].rearrange("(j p) -> j p", p=128))
        P.dma('sync', stg2[36:40, :], ml_nw[0].rearrange("(j p) -> j p", p=128))
        P.dma('sync', stg2[40:48, :], conv_b[0].rearrange("(j p) -> j p", p=128))
        tp2 = pb(0, [48], off=1024)
        P.transpose(tp2, stg2[0:48, :], identF[0:48, 0:48])
        P.copy('vector', colB, tp2)
        RM.release(m0)
        ck('A0', [colA, colB, FL, cst[:, 1, :]])
        P.act(svf, colA[:, 0:16], AF.Silu)
        P.copy('vector', sv[:, :, 0], svf[:, 0:8])
        P.copy('vector', sv[:, :, 1], svf[:, 8:16])

        gslot = {(0, 0, 2): 0, (0, 0, 5): 1, (0, 1, 2): 2, (0, 1, 5): 3, (1, 0, 2): 0, (1, 0, 5): 1}

        def modulation(l):
            m0 = RM.mark()
            svrep = RM.alloc([2, 8, 128], BF16)
            for v_ in range(2):
                P.copy('vector', svrep[:, v_, :, :], bc(svf[:, 8 * v_:8 * v_ + 8].unsqueeze(2), [128, 8, 128]))
            wblk = [RM.alloc([8, 1024], BF16) for _ in range(2)]
            bbc = RM.alloc([1024], F32)
            for j in range(6):
                wb = wblk[j % 2]
                P.dma('gpsimd', wb, ada_w[l].rearrange("(kc p) n -> p kc n", p=128)[:, :, j * 1024:(j + 1) * 1024])
                ps = pb(1, [8, 2])
                for n in range(8):
                    for kc in range(8):
                        P.mm(ps[:, n, :], wb[:, kc, n * 128:(n + 1) * 128], sv[:, kc, :], start=(kc == 0), stop=(kc == 7))
                for v_ in range(2):
                    P.tt('vector', modT[:, l, v_, j * 8:(j + 1) * 8], ps[:, :, v_],
                         colA[:, 16 + 48 * l + j * 8:16 + 48 * l + j * 8 + 8], ALU.add)
                if j in (2, 5):
                    P.dma('sync', bbc, bc(ada_b[l:l + 1, j * 1024:(j + 1) * 1024], [128, 1024]))
                    for v_ in range(2):
                        if (l, v_, j) not in gslot:
                            continue
                        for hf in range(2):
                            pg = pb(2 + hf, [512])
                            for kc in range(8):
                                P.mm(pg, svrep[:, v_, kc, :], wb[:, kc, hf * 512:(hf + 1) * 512], start=(kc == 0), stop=(kc == 7))
                            P.tt('vector', gB[:, gslot[(l, v_, j)], hf * 512:(hf + 1) * 512], pg, bbc[:, hf * 512:(hf + 1) * 512], ALU.add)
            RM.release(m0)

        def make_wm(l):
            for i in range(2):
                for v_ in range(2):
                    sc = modT[:, l, v_, (3 * i + 1) * 8:(3 * i + 2) * 8]
                    nw = colB[:, (2 * l + i) * 8:(2 * l + i) * 8 + 8]
                    P.stt('vector', WM[:, l, i, v_, :], sc, 1.0, nw, ALU.add, ALU.mult)
                    P.copy('vector', SHc[:, l, i, v_, :], modT[:, l, v_, (3 * i) * 8:(3 * i) * 8 + 8])

        modulation(0)
        make_wm(0)
        mark('mod0')
        ck('A', [colA, colB, modT.rearrange('p a b c -> p (a b c)'), gB[:, 0, :], gB[:, 3, :], WM.rearrange('p a b c d -> p (a b c d)')])

        cnt_h = [0]

        def make_hT(x_sb, dest, wcol, shcol, flag=None):
            k = cnt_h[0] % 2
            cnt_h[0] += 1
            ss = ssq[:, 2 * k:2 * k + 1]
            rs = ssq[:, 2 * k + 1:2 * k + 2]
            P.act(junk, x_sb, AF.Square, accum_out=ss)
            P.act(rs, ss, AF.Sqrt, bias=epsb, scale=1.0 / D)
            P.recip(rs, rs)
            xn = xnb[k]
            P.ts('vector', xn, x_sb, rs, None, ALU.mult)
            tps = pb(0, [8, 128], BF16)
            for kc in range(8):
                P.transpose(tps[:, kc, :], xn[:, kc * 128:(kc + 1) * 128], identB)
            P.tt('vector', t32, tps, bc(wcol.unsqueeze(2), [128, 8, 128]), ALU.mult)
            if flag is None:
                P.tt('gpsimd', dest, t32, bc(shcol.unsqueeze(2), [128, 8, 128]), ALU.add)
            else:
                P.tt('gpsimd', t32, t32, bc(shcol.unsqueeze(2), [128, 8, 128]), ALU.add)
                P.ts('vector', dest, t32, flag, None, ALU.mult)

        hT = RX.alloc([8, HT_N], BF16)
        AA = RX.alloc([NCH, 2, 16], F32)
        BB = RX.alloc([NCH, 2, 16], F32)
        DEC = RX.alloc([NCH, 2, 16], F32)
        Sacc = RX.alloc([16, 65], F32)
        RR = RX.alloc([2, 16], F32)
        expR = RX.alloc([2, 16], F32)
        mXf = RX.mark()
        mMF = RM.mark()
        win = ab_w_in[0].rearrange("(kc p) n -> p kc n", p=128)

        xbuf = [RX.alloc([D], F32) for _ in range(2)]
        P.memset('gpsimd', hT[:, :, 0:1], 0.0)
        P.memset('gpsimd', hT[:, :, 257:258], 0.0)

        def tokcols(c):
            return (CTX0 + c * 128) if c < 2 else (WIN0 + (c - 2) * 128)

        for c in range(NCH):
            xb = xbuf[c % 2]
            P.dma('sync', xb, xw[c])
            v_ = 1 if c < 2 else 0
            col0 = tokcols(c)
            fg = None if c not in (2, NCH - 1) else FL[:, FL_VF + c:FL_VF + c + 1]
            make_hT(xb, hT[:, :, col0:col0 + 128], WM[:, 0, 0, v_, :], SHc[:, 0, 0, v_, :], fg)
        hTh = RX.alloc([8, 128], BF16)
        xb = xbuf[0]
        P.dma('sync', xb, xe)
        make_hT(xb, hTh, WM[:, 0, 0, 0, :], SHc[:, 0, 0, 0, :])
        P.ts('vector', hT[:, :, 258:259], hTh[:, :, 0:1], FL[:, FL_EDGE:FL_EDGE + 1], None, ALU.mult)
        P.ts('vector', hT[:, :, 2563:2564], hTh[:, :, 1:2], FL[:, FL_EDGE + 1:FL_EDGE + 2], None, ALU.mult)

        ck('B', [hT[:, 0, 0:1024], hT[:, 7, 1540:2564]])
        mark('hT')
        Gpre = RX.alloc([NCH, 32], F32)
        LF = RX.alloc([NCH, 2, 16], F32)
        II = RX.alloc([NCH, 2, 16], F32)
        Wg = RM.alloc([8, 32], BF16)
        P.dma('gpsimd', Wg, win[:, :, 4096:4128])
        gbb = RM.alloc([32], F32)
        P.dma('sync', gbb, bc(gate_b[0].rearrange("a h -> (a h)").unsqueeze(0), [128, 32]))
        lgb = RM.alloc([2, 8], F32)
        P.dma('sync', lgb.rearrange("p a h -> p (a h)"), bc(ret_lg[0].rearrange("a h -> (a h)").unsqueeze(0), [128, 16]))
        ck('C0', [gbb, lgb.rearrange('p a h -> p (a h)'), Wg.rearrange('p a b -> p (a b)')])
        for c in range(NCH):
            c0 = tokcols(c)
            pg = pb(3, [128])[:, 32 * (c % 4):32 * (c % 4) + 32]
            for kc in range(8):
                P.mm(pg, hT[:, kc, c0:c0 + 128], Wg[:, kc, :], start=(kc == 0), stop=(kc == 7))
            P.tt('vector', Gpre[:, c, :], pg, gbb, ALU.add)
        ck('C1', [Gpre.rearrange('p a b -> p (a b)')])
        VFb = FL[:, FL_VF:FL_VF + NCH]
        tmpg = RX.alloc([NCH, 8], F32)
        for d_ in range(2):
            fcol = Gpre[:, :, 8 + 16 * d_:16 + 16 * d_]
            icol = Gpre[:, :, 16 * d_:8 + 16 * d_]
            P.act(tmpg, fcol, AF.Exp, scale=-1.0)
            P.act(tmpg, tmpg, AF.Ln, bias=1.0)
            P.stt('vector', LF[:, :, d_, 8:16], tmpg, -1.0, bc(VFb.unsqueeze(2), [128, NCH, 8]), ALU.mult, ALU.mult)
            P.tt('vector', LF[:, :, d_, 0:8], bc(lgb[:, d_, :].unsqueeze(1), [128, NCH, 8]), bc(VFb.unsqueeze(2), [128, NCH, 8]), ALU.mult)
            P.memset('gpsimd', II[:, :, d_, 0:8], 0.0)
            P.copy('gpsimd', II[:, :, d_, 8:16], icol)
        ck('C2', [LF.rearrange('p a b c -> p (a b c)'), II.rearrange('p a b c -> p (a b c)')])
        LFc = RX.alloc([2, NCH * 16], F32)
        for d_ in range(2):
            P.copy('gpsimd', LFc[:, d_, :].rearrange("p (c l) -> p c l", l=16), LF[:, :, d_, :])
        for d_ in range(2):
            pe = pb(4 + d_, [NCH, 16])
            pef = pe.rearrange("p c l -> p (c l)")
            for (a_, b_) in ((0, 128), (128, 256), (256, 320)):
                P.mm(pef[:, a_:b_], MfF if d_ == 0 else MbF, LFc[:, d_, a_:b_])
            if d_ == 0 and dbg == 'C2a':
                P.copy('vector', AA.rearrange('p a b c -> p (a b c)')[:, 0:320], pef)
                ck('C2a', [AA.rearrange('p a b c -> p (a b c)'), LFc.rearrange('p a b -> p (a b)')])
            P.act(AA[:, :, d_, :], pe, AF.Exp, scale=-1.0)
            if d_ == 0:
                ck('C2b', [AA.rearrange('p a b c -> p (a b c)')])
            P.tt('vector', BB[:, :, d_, :], pe, II[:, :, d_, :], ALU.add)
            if d_ == 0:
                ck('C2c', [BB.rearrange('p a b c -> p (a b c)')])
            P.act(BB[:, :, d_, :], BB[:, :, d_, :], AF.Exp, bias=lnk)
            if d_ == 0:
                ck('C2d', [BB.rearrange('p a b c -> p (a b c)')])
            P.tt('vector', BB[:, :, d_, :], BB[:, :, d_, :], bc(VFb.unsqueeze(2), [128, NCH, 16]), ALU.mult)
        ck('C3', [AA.rearrange('p a b c -> p (a b c)'), BB.rearrange('p a b c -> p (a b c)')])
        LFf = LF.rearrange("p a b c -> p (a b c)")
        for hf in range(2):
            pt_ = pb(6, [10, 2, 16])
            ptf = pt_.rearrange("p a b c -> p (a b c)")
            for (a_, b_) in ((0, 128), (128, 256), (256, 320)):
                P.mm(ptf[:, a_:b_], onesF, LFf[:, hf * 320 + a_:hf * 320 + b_])
            P.act(DEC[:, hf * 10:(hf + 1) * 10, :, :], pt_, AF.Exp)

        ck('C', [AA.rearrange('p a b c -> p (a b c)'), BB.rearrange('p a b c -> p (a b c)'), DEC.rearrange('p a b c -> p (a b c)'), Gpre.rearrange('p a b -> p (a b)')])
        mark('prepass')
        P.memset('vector', Sacc, 0.0)
        P.memset('vector', RR, 0.0)
        Wv = RM.alloc([8, 1024], BF16)
        P.dma('gpsimd', Wv[:, :, 0:512], win[:, :, 1024:1536])
        P.dma('gpsimd', Wv[:, :, 512:1024], win[:, :, 3072:3584])
        Wkr = RM.alloc([8, 512], BF16)
        P.dma('gpsimd', Wkr, win[:, :, 512:1024])
        Wtap = RM.alloc([3, 8, 512], BF16)
        m1 = RM.mark()
        cwk = RM.alloc([3, 512], F32)
        for j in range(3):
            P.dma('sync', cwk[:, j, :], bc(conv_w[0, j:j + 1, 512:1024], [128, 512]))
        Wkm = RM.alloc([8, 512], BF16)
        P.dma('gpsimd', Wkm, win[:, :, 2560:3072])
        for j in range(3):
            P.tt('vector', Wtap[:, j, :, :], Wkm, bc(cwk[:, j, :].unsqueeze(1), [128, 8, 512]), ALU.mult)
        RM.release(m1)
        cbb = RM.alloc([512], F32)
        P.dma('sync', cbb, bc(conv_b[0:1, 512:1024], [128, 512]))
        rS = RM.alloc([2, NSLOT, 32], F32)
        P.dma('sync', rS.rearrange("p a s f -> p a (s f)"), ropeS.rearrange("a p n -> p a n"))
        Kfb = RM.alloc([16, 128], BF16)
        Vext = RM.alloc([16, 65], BF16)
        ra = RM.alloc([2, 8, 32], F32)
        krf = [RM.alloc([512], F32) for _ in range(2)]
        Vsb = [RM.alloc([16, 64], BF16) for _ in range(2)]
        kmts = [RM.alloc([512], F32) for _ in range(2)]
        gps = [RM.alloc([32], F32) for _ in range(2)]
        xb = xbuf[1]
        P.dma('sync', xb, xsh)
        make_hT(xb, hTh, WM[:, 0, 0, 0, :], SHc[:, 0, 0, 0, :])
        P.tt('vector', hTh, hTh, bc(FL[:, FL_SH:FL_SH + 128].unsqueeze(1), [128, 8, 128]), ALU.mult)
        hTs = [RX.alloc([8, 130], BF16) for _ in range(2)]
        Ktok = RX.alloc([16, 64], BF16)
        sg = RX.alloc([160], F32)

        def slot_A(s):
            hs = hTs[s % 2]
            xb = xbuf[s % 2]
            P.dma('sync', xb, xs[s])
            make_hT(xb, hs[:, :, 1:129], WM[:, 0, 0, 0, :], SHc[:, 0, 0, 0, :])
            P.copy('gpsimd', hs[:, :, 0:1], hTh[:, :, 2 * s:2 * s + 1])
            P.copy('gpsimd', hs[:, :, 129:130], hTh[:, :, 2 * s + 1:2 * s + 2])
            pkr = pb(1, [512])
            for kc in range(8):
                P.mm(pkr, hs[:, kc, 1:129], Wkr[:, kc, :], start=(kc == 0), stop=(kc == 7))
            P.copy('scalar', krf[s % 2], pkr)
            pkm = pb(2, [512])
            for j in range(3):
                for kc in range(8):
                    P.mm(pkm, hs[:, kc, j:j + 128], Wtap[:, j, kc, :], start=(j == 0 and kc == 0), stop=(j == 2 and kc == 7))
            P.tt('vector', kmts[s % 2], pkm, cbb, ALU.add)
            for hf in range(2):
                pv = pb(3 + hf, [512])
                for kc in range(8):
                    P.mm(pv, hs[:, kc, 1:129], Wv[:, kc, hf * 512:(hf + 1) * 512], start=(kc == 0), stop=(kc == 7))
                P.copy('scalar', Vsb[s % 2][:, hf * 8:(hf + 1) * 8, :], pv.rearrange("p (h d) -> p h d", h=8))
            pg = pb(5, [32])
            for kc in range(8):
                P.mm(pg, hs[:, kc, 1:129], Wg[:, kc, :], start=(kc == 0), stop=(kc == 7))
            P.tt('vector', gps[s % 2], pg, gbb, ALU.add)

        def slot_B(s):
            ff = FL[:, FL_FF + s:FL_FF + s + 1]
            fb = FL[:, FL_FB + s:FL_FB + s + 1]
            k3 = krf[s % 2].rearrange("p (h d) -> p h d", h=8)
            x1, x2 = k3[:, :, 0:32], k3[:, :, 32:64]
            cs = bc(rS[:, 0, s, :].unsqueeze(1), [128, 8, 32])
            sn = bc(rS[:, 1, s, :].unsqueeze(1), [128, 8, 32])
            P.tt('gpsimd', ra[:, 0], x1, cs, ALU.mult)
            P.tt('gpsimd', ra[:, 1], x2, sn, ALU.mult)
            P.tt('gpsimd', Ktok[:, 0:8, 0:32], ra[:, 0], ra[:, 1], ALU.subtract)
            P.tt('gpsimd', ra[:, 0], x2, cs, ALU.mult)
            P.tt('gpsimd', ra[:, 1], x1, sn, ALU.mult)
            P.tt('gpsimd', Ktok[:, 0:8, 32:64], ra[:, 0], ra[:, 1], ALU.add)
            P.act(Ktok[:, 8:16, :], kmts[s % 2].rearrange("p (h d) -> p h d", h=8), AF.Silu)
            P.ts('gpsimd', Kfb[:, :, 0:64], Ktok, ff, None, ALU.mult)
            P.ts('gpsimd', Kfb[:, :, 64:128], Ktok, fb, None, ALU.mult)
            gp = gps[s % 2]
            fsel = sg[:, 32:40]
            isel = sg[:, 40:56]
            lf16 = sg[:, 56:72]
            lfm = sg[:, 72:104]
            rsel = sg[:, 104:120]
            bex = sg[:, 120:136]
            t8 = sg[:, 136:144]
            P.ts('vector', fsel, gp[:, 8:16], ff, None, ALU.mult)
            P.stt('vector', fsel, gp[:, 24:32], fb, fsel, ALU.mult, ALU.add)
            P.memset('vector', isel[:, 0:8], 0.0)
            P.ts('vector', isel[:, 8:16], gp[:, 0:8], ff, None, ALU.mult)
            P.stt('vector', isel[:, 8:16], gp[:, 16:24], fb, isel[:, 8:16], ALU.mult, ALU.add)
            P.act(t8, fsel, AF.Exp, scale=-1.0)
            P.act(t8, t8, AF.Ln, bias=1.0)
            P.ts('vector', lf16[:, 8:16], t8, -1.0, None, ALU.mult)
            P.ts('vector', lf16[:, 0:8], lgb[:, 0, :], ff, None, ALU.mult)
            P.stt('vector', lf16[:, 0:8], lgb[:, 1, :], fb, lf16[:, 0:8], ALU.mult, ALU.add)
            P.ts('vector', lfm[:, 0:16], lf16, ff, None, ALU.mult)
            P.ts('vector', lfm[:, 16:32], lf16, fb, None, ALU.mult)
            pe = pb(0, [16], off=0)
            P.mm(pe, MfF, lfm[:, 0:16], start=True, stop=False)
            P.mm(pe, MbF, lfm[:, 16:32], start=False, stop=True)
            pt_ = pb(0, [32], off=512)
            P.mm(pt_, onesF, lfm)
            P.ts('vector', rsel, RR[:, 0, :], ff, None, ALU.mult)
            P.stt('vector', rsel, RR[:, 1, :], fb, rsel, ALU.mult, ALU.add)
            P.tt('vector', rsel, rsel, isel, ALU.add)
            P.tt('vector', rsel, rsel, pe, ALU.add)
            P.act(bex, rsel, AF.Exp, bias=lnk)
            RRf = RR.rearrange("p a l -> p (a l)")
            P.tt('vector', RRf, RRf, pt_, ALU.add)
            P.tt('gpsimd', Vext[:, :, 0:64], Vsb[s % 2], bc(bex.unsqueeze(2), [128, 16, 64]), ALU.mult)
            P.copy('gpsimd', Vext[:, :, 64], bex)
            kvb = [pb(6, [6, 65]), pb(7, [6, 65]), pb(5, [4, 65])]
            for ln in range(16):
                P.mm(kvb[ln // 6][:, ln % 6, :], Kfb[:, ln, :], Vext[:, ln, :])
            for g3 in range(3):
                nl = 6 if g3 < 2 else 4
                P.tt('vector', Sacc[:, g3 * 6:g3 * 6 + nl, :], Sacc[:, g3 * 6:g3 * 6 + nl, :], kvb[g3], ALU.add)

        slot_A(0)
        for s in range(NSLOT):
            if s + 1 < NSLOT:
                slot_A(s + 1)
            slot_B(s)
        P.act(expR, RR, AF.Exp)
        ck('D', [Sacc.rearrange('p a b -> p (a b)')[:, 0:1024], RR.rearrange('p a b -> p (a b)'), expR.rearrange('p a b -> p (a b)')])
        RX.release(mXf)
        RM.release(mMF)

        mark('outside')
        MT = RM.alloc([8, NCH * 128], BF16)
        mMF2 = RM.mark()
        order = [list(range(NCH)), [1, 0] + list(range(NCH - 1, 1, -1))]
        first_dir = [0 if order[0].index(c_) <= order[1].index(c_) else 1 for c_ in range(NCH)]
        groups = [(CTX0, 256, 0, False)] + [(WIN0 + 512 * g_, 512, 256 + 512 * g_, True) for g_ in range(4)] + [(WIN0 + 2048, 256, 2304, True)]
        for ps_ in range(8):
            if ps_ in (1, 2, 5, 6):
                mark(f'p{ps_}_start')
            RX.release(mXf)
            RM.release(mMF2)
            is_ml = ps_ >= 4
            j = ps_ % 4
            l0 = (8 if is_ml else 0) + 2 * j
            qoff = (2048 if is_ml else 0) + 128 * j
            koff = (2560 if is_ml else 512) + 128 * j
            goff = (3584 if is_ml else 1536) + 128 * j
            voff = (3072 if is_ml else 1024) + 128 * j
            ntap = 3 if is_ml else 1
            qT = RX.alloc([NCH * 128], BF16)
            kT = RX.alloc([NCH * 128], BF16)
            gT = RX.alloc([NCH * 128], BF16)
            Kt = RX.alloc([NCH, 128], BF16)
            Vp = RX.alloc([NCH, 2, 64], BF16)
            Wq = RM.alloc([3, 8, 128], BF16)
            Wk = RM.alloc([3, 8, 128], BF16)
            Wgt = RM.alloc([8, 128], BF16)
            Wvp = RM.alloc([8, 128], BF16)
            Oacc = RM.alloc([NCH, 2, 64], F32)
            mScan = RM.mark()
            P.dma('gpsimd', Wq[:, 0], win[:, :, qoff:qoff + 128])
            P.dma('gpsimd', Wk[:, 0], win[:, :, koff:koff + 128])
            P.dma('gpsimd', Wgt, win[:, :, goff:goff + 128])
            P.dma('gpsimd', Wvp, win[:, :, voff:voff + 128])
            if is_ml:
                cwp = RM.alloc([2, 3, 128], F32)
                for wi, co in enumerate((128 * j, 512 + 128 * j)):
                    for tpi in range(3):
                        P.dma('sync', cwp[:, wi, tpi, :], bc(conv_w[0, tpi:tpi + 1, co:co + 128], [128, 128]))
                for wi, W_ in enumerate((Wq, Wk)):
                    for tpi in (2, 1, 0):
                        P.tt('vector', W_[:, tpi], W_[:, 0], bc(cwp[:, wi, tpi, :].unsqueeze(1), [128, 8, 128]), ALU.mult)
            rtmp = RM.alloc([2, 512], F32)
            rfb = [RM.alloc([2, 512], F32) for _ in range(2)]
            bi = 0
            for gi_, (hc0, n, lc0, isw) in enumerate(groups):
                rf = rfb[gi_ % 2]
                if isw and not is_ml:
                    w0_ = hc0 - WIN0
                    P.dma('sync', rf[:, :, 0:n], ropeF.rearrange("a p n -> p a n")[:, :, w0_:w0_ + n])
                for which, W_, dst in (('q', Wq, qT), ('k', Wk, kT), ('g', Wgt, gT)):
                    pp = pb(1 + (bi % 3), [512])[:, 0:n]
                    bi += 1
                    if which == 'g':
                        for kc in range(8):
                            P.mm(pp, W_[:, kc, :], hT[:, kc, hc0:hc0 + n], start=(kc == 0), stop=(kc == 7))
                        P.act(dst[:, lc0:lc0 + n], pp, AF.Sigmoid if is_ml else AF.Silu)
                        continue
                    for tpi in range(ntap):
                        sh_ = (tpi - 1) if is_ml else 0
                        for kc in range(8):
                            P.mm(pp, W_[:, tpi, kc, :], hT[:, kc, hc0 + sh_:hc0 + sh_ + n],
                                 start=(tpi == 0 and kc == 0), stop=(tpi == ntap - 1 and kc == 7))
                    if is_ml:
                        ci_ = 40 + (j if which == 'q' else 4 + j)
                        P.act(dst[:, lc0:lc0 + n], pp, AF.Silu, bias=colB[:, ci_:ci_ + 1])
                    elif not isw:
                        P.copy('scalar', dst[:, lc0:lc0 + n], pp)
                    else:
                        P.tt('vector', rtmp[:, 0, 0:n], pp, rf[:, 0, 0:n], ALU.mult)
                        for blk in range(4):
                            src = blk ^ 1
                            P.tt('vector', rtmp[blk * 32:(blk + 1) * 32, 1, 0:n], pp[src * 32:(src + 1) * 32, :],
                                 rf[src * 32:(src + 1) * 32, 1, 0:n], ALU.mult)
                        P.tt('gpsimd', dst[:, lc0:lc0 + n], rtmp[:, 0, 0:n], rtmp[:, 1, 0:n], ALU.add)
            if ps_ in (1, 5):
                mark(f'p{ps_}_proj')
            for c8 in range(0, NCH, 8):
                ncc = min(8, NCH - c8)
                tpk = pb(4, [8, 128], BF16)
                for ci in range(ncc):
                    P.transpose(tpk[:, ci, :], kT[:, (c8 + ci) * 128:(c8 + ci + 1) * 128], identB)
                P.copy('scalar', Kt[:, c8:c8 + ncc, :], tpk[:, 0:ncc, :])
            for c4 in range(0, NCH, 4):
                pv = pb(1 + (c4 // 4) % 3, [4, 128])
                for ci in range(4):
                    c0 = tokcols(c4 + ci)
                    for kc in range(8):
                        P.mm(pv[:, ci, :], hT[:, kc, c0:c0 + 128], Wvp[:, kc, :], start=(kc == 0), stop=(kc == 7))
                P.copy('vector', Vp[:, c4:c4 + 4].rearrange("p c h d -> p c (h d)"), pv)
            if ps_ == 0:
                ck('E1', [qT[:, 0:1024], kT[:, 0:1024], gT[:, 0:1024], Kt.rearrange('p a b -> p (a b)')[:, 0:1024], Vp.rearrange('p a b c -> p (a b c)')[:, 0:1024]])
            if ps_ == 4:
                ck('F1', [qT[:, 0:1024], kT[:, 0:1024], gT[:, 0:1024], Kt.rearrange('p a b -> p (a b)')[:, 0:1024], Vp.rearrange('p a b c -> p (a b c)')[:, 0:1024]])
            if ps_ in (1, 5):
                mark(f'p{ps_}_kv')
            RM.release(mScan)
            decp = RM.alloc([NCH, 2], F32)
            P.copy('vector', decp[0:64], DEC[0:64, :, :, l0])
            P.copy('vector', decp[64:128], DEC[64:128, :, :, l0 + 1])
            S32 = [RM.alloc([130], F32) for _ in range(2)]
            Sbf = [RM.alloc([130], BF16) for _ in range(2)]
            stmp = RM.alloc([130], F32)
            ecol = RM.alloc([2], F32)
            Vts = [RM.alloc([2, 65], BF16) for _ in range(4)]
            dn = RM.alloc([2, 8], F32)
            P.memset('gpsimd', stmp, 0.0)
            for d_ in range(2):
                P.memset('gpsimd', S32[d_], 0.0)
                P.copy('vector', ecol[0:64, d_:d_ + 1], expR[0:64, d_, l0:l0 + 1])
                P.copy('vector', ecol[64:128, d_:d_ + 1], expR[64:128, d_, l0 + 1:l0 + 2])
            PTs = [RM.alloc([2, 128], BF16) for _ in range(4)]
            pt_banks = [[0, 6], [3, 7]]

            def scan_front(step, d_):
                c = order[d_][step]
                tk = slice(c * 128, (c + 1) * 128)
                Vt = Vts[2 * (step % 2) + d_]
                P.tt('gpsimd', Vt[:, :, 0:64], Vp[:, c], bc(BB[:, c, d_, l0:l0 + 2].unsqueeze(2), [128, 2, 64]), ALU.mult)
                P.copy('gpsimd', Vt[:, :, 64], BB[:, c, d_, l0:l0 + 2])
                ptp = pb(pt_banks[d_][step % 2], [2, 128])
                prev_mm = None
                for h in range(2):
                    hb = slice(64 * h, 64 * h + 64)
                    prev_mm = P.mm(ptp[:, h, :], kT[hb, tk], qT[hb, tk], after=prev_mm)
                PT = PTs[2 * (step % 2) + d_]
                P.tt('vector', PT, ptp, bc((maskF_b if d_ == 0 else maskB_b).unsqueeze(1), [128, 2, 128]), ALU.mult)

            def scan_kv(step, d_):
                c = order[d_][step]
                Vt = Vts[2 * (step % 2) + d_]
                kvp = pb(2 if d_ == 0 else 5, [130])
                P.mm(kvp, Kt[:, c, :], Vt.rearrange("p h e -> p (h e)"))

            def scan_back(step, d_):
                c = order[d_][step]
                tk = slice(c * 128, (c + 1) * 128)
                S = S32[d_]
                Vt = Vts[2 * (step % 2) + d_]
                PT = PTs[2 * (step % 2) + d_]
                if step == 2:
                    r0 = 64 * d_
                    P.copy('scalar', stmp[0:64, 0:65], Sacc[r0:r0 + 64, l0, :])
                    P.copy('scalar', stmp[64:128, 65:130], Sacc[r0:r0 + 64, l0 + 1, :])
                    P.stt('vector', S, S, ecol[:, d_:d_ + 1], stmp, ALU.mult, ALU.add)
                if step > 0:
                    P.ts('gpsimd', Sbf[d_], S, decp[:, c, d_:d_ + 1], None, ALU.mult)
                ops_ = pb(1 if d_ == 0 else 4, [2, 65])
                for h in range(2):
                    hb = slice(64 * h, 64 * h + 64)
                    P.mm(ops_[:, h, :], PT[:, h, :], Vt[:, h, :], start=True, stop=(step == 0))
                    if step > 0:
                        P.mm(ops_[:, h, :], qT[hb, tk], Sbf[d_][hb, 65 * h:65 * h + 65], start=False, stop=True)
                for h in range(2):
                    at = AA[:, c, d_, l0 + h:l0 + h + 1]
                    if is_ml:
                        dd = dn[:, h, 4 * d_:4 * d_ + 4]
                        P.act(dd[:, 0:1], ops_[:, h, 64:65], AF.Abs, scale=at)
                        P.ts('vector', dd[:, 1:2], dd[:, 0:1], 1.0, None, ALU.max)
                        P.recip(dd[:, 2:3], dd[:, 1:2])
                        P.tt('vector', dd[:, 3:4], dd[:, 2:3], at, ALU.mult)
                        coef = dd[:, 3:4]
                    else:
                        coef = at
                    if first_dir[c] == d_:
                        P.act(Oacc[:, c, h, :], ops_[:, h, 0:64], AF.Identity, scale=coef)
                    else:
                        P.stt('vector', Oacc[:, c, h, :], ops_[:, h, 0:64], coef, Oacc[:, c, h, :], ALU.mult, ALU.add)
                kvp = pb(2 if d_ == 0 else 5, [130])
                if step == 0:
                    P.copy('vector', S, kvp)
                else:
                    P.stt('vector', S, S, decp[:, c, d_:d_ + 1], kvp, ALU.mult, ALU.add)

            for d_ in range(2):
                scan_front(0, d_)
                scan_kv(0, d_)
            for step in range(NCH):
                if step + 1 < NCH:
                    for d_ in range(2):
                        scan_front(step + 1, d_)
                for d_ in range(2):
                    scan_back(step, d_)
                if step + 1 < NCH:
                    for d_ in range(2):
                        scan_kv(step + 1, d_)
                if ps_ == 0 and step < 3:
                    ck('E2' + 'abc'[step], [Oacc.rearrange('p a b c -> p (a b c)')[:, 0:256], S32[0], S32[1]])
            if ps_ == 0:
                ck('E2', [Oacc.rearrange('p a b c -> p (a b c)')[:, 0:1024], Oacc.rearrange('p a b c -> p (a b c)')[:, 1024:2048]])
            if ps_ == 4:
                ck('F2', [Oacc.rearrange('p a b c -> p (a b c)')[:, 0:1024], Oacc.rearrange('p a b c -> p (a b c)')[:, 1024:2048]])
            if ps_ in (1, 5):
                mark(f'p{ps_}_scan')
            RM.release(mScan)
            sq = RM.alloc([NCH, 2, 64], F32)
            ms = RM.alloc([NCH, 2], F32)
            On = RM.alloc([NCH, 2, 64], BF16)
            P.tt('gpsimd', sq, Oacc, Oacc, ALU.mult)
            P.reduce('vector', ms, sq, ALU.add)
            P.act(ms, ms, AF.Sqrt, bias=epsb, scale=1.0 / 64)
            P.recip(ms, ms)
            P.tt('vector', On, Oacc, bc(ms.unsqueeze(3), [128, NCH, 2, 64]), ALU.mult)
            nwi = (36 if is_ml else 32) + j
            for c4 in range(0, NCH, 4):
                tpo = pb(4, [4, 128], BF16, off=(c4 // 4 % 2) * 1024)
                for ci in range(4):
                    P.transpose(tpo[:, ci, :], On[:, c4 + ci].rearrange("p h d -> p (h d)"), identB)
                P.stt('vector', MT[:, ps_, c4 * 128:(c4 + 4) * 128], tpo.rearrange("p c t -> p (c t)"), colB[:, nwi:nwi + 1],
                      gT[:, c4 * 128:(c4 + 4) * 128], ALU.mult, ALU.mult)
        ck('E4', [MT[:, 0, 0:1024], MT[:, 7, 0:1024]])
        RM.release(mMF2)
        RX.release(0)
        mark('passes')
        X1 = RX.alloc([NCH, D], F32)
        Wo = RM.alloc([8, D], BF16)
        P.dma('gpsimd', Wo, ab_w_out[0].rearrange("(kc p) n -> p kc n", p=128))
        otmp = [RM.alloc([512], F32) for _ in range(2)]
        for c in range(NCH):
            P.dma('sync', X1[:, c, :], xw[c])
        for c in range(NCH):
            gi = 2 if c < 2 else 0
            for hf in range(2):
                po = pb(1 + hf, [512])
                for kc in range(8):
                    P.mm(po, MT[:, kc, c * 128:(c + 1) * 128], Wo[:, kc, hf * 512:(hf + 1) * 512], start=(kc == 0), stop=(kc == 7))
                P.tt('vector', otmp[hf], po, gB[:, gi, hf * 512:(hf + 1) * 512], ALU.mult)
                P.tt('gpsimd', X1[:, c, hf * 512:(hf + 1) * 512], X1[:, c, hf * 512:(hf + 1) * 512], otmp[hf], ALU.add)
        RM.release(mMF)
        if dbg == 'l0m':
            for c in range(NCH):
                P.dma('sync', dbg_out[c], X1[:, c, :], store=True)

        mark('outproj0')
        def ffn(l, chunks, wm_sel, g_sel):
            m0 = RM.mark()
            w_in = ffn_w_in[l].rearrange("(kc p) n -> p kc n", p=128)
            w_out = ffn_w_out[l].rearrange("(f p) n -> p f n", p=128)
            Wout = RM.alloc([NFT, D], BF16)
            P.dma('gpsimd', Wout[:, 0:11, :], w_out[:, 0:11, :])
            P.dma('gpsimd', Wout[:, 11:22, :], w_out[:, 11:22, :])
            AT = RM.alloc([NFT, 512], BF16)
            wbuf = [RM.alloc([8, 2, 256], BF16) for _ in range(2)]
            m_ph = RM.mark()
            h2 = RM.alloc([8, 512], BF16)
            RM.release(m_ph)
            ot = [RM.alloc([512], F32) for _ in range(2)]
            RM.alloc([8 * 512 - 2 * 1024], BF16)
            bi = 0
            for q0 in range(0, len(chunks), 4):
                cq = chunks[q0:q0 + 4]
                for i, c in enumerate(cq):
                    v_ = wm_sel(c)
                    make_hT(X1[:, c, :], h2[:, :, i * 128:(i + 1) * 128], WM[:, l, 1, v_, :], SHc[:, l, 1, v_, :])
                for fp in range(11):
                    wb = wbuf[bi % 2]
                    bi += 1
                    P.dma('gpsimd', wb[:, :, 0, :], w_in[:, :, fp * 256:(fp + 1) * 256])
                    P.dma('gpsimd', wb[:, :, 1, :], w_in[:, :, D_FF + fp * 256:D_FF + (fp + 1) * 256])
                    for f2 in range(2):
                        f = fp * 2 + f2
                        pgm = pb(1 + 2 * (f % 2), [512])
                        pum = pb(2 + 2 * (f % 2), [512])
                        for kc in range(8):
                            P.mm(pgm, wb[:, kc, 0, f2 * 128:(f2 + 1) * 128], h2[:, kc, :], start=(kc == 0), stop=(kc == 7))
                        for kc in range(8):
                            P.mm(pum, wb[:, kc, 1, f2 * 128:(f2 + 1) * 128], h2[:, kc, :], start=(kc == 0), stop=(kc == 7))
                        P.act(AT[:, f, :], pgm, AF.Silu)
                        P.tt('vector', AT[:, f, :], AT[:, f, :], pum, ALU.mult)
                k_ = 0
                for i, c in enumerate(cq):
                    for hf in range(2):
                        po = pb(5 + (k_ % 3), [512])
                        o_ = ot[k_ % 2]
                        k_ += 1
                        for f in range(NFT):
                            P.mm(po, AT[:, f, i * 128:(i + 1) * 128], Wout[:, f, hf * 512:(hf + 1) * 512], start=(f == 0), stop=(f == NFT - 1))
                        P.tt('vector', o_, po, gB[:, g_sel(c), hf * 512:(hf + 1) * 512], ALU.mult)
                        P.tt('gpsimd', X1[:, c, hf * 512:(hf + 1) * 512], X1[:, c, hf * 512:(hf + 1) * 512], o_, ALU.add)
            RM.release(m0)

        if dbg != 'l0m':
            ffn(0, list(range(NCH)), lambda c: 1 if c < 2 else 0, lambda c: 3 if c < 2 else 1)
        if dbg == 'l0':
            for c in range(NCH):
                P.dma('sync', dbg_out[c], X1[:, c, :], store=True)

        mark('ffn0')
        if dbg in (None, 'l1m'):
            modulation(1)
            make_wm(1)
            m1 = RM.mark()
            awin = at_w_in[0].rearrange("(kc p) n -> p kc n", p=128)
            kT1 = RM.alloc([2, NCH * 128], BF16)
            Vx1 = RM.alloc([NCH, 4, 65], BF16)
            P.memset('vector', Vx1[:, :, :, 64:65], 1.0)
            qnb = RM.alloc([64], F32)
            knb = RM.alloc([64], F32)
            P.dma('sync', qnb, bc(at_qn, [128, 64]))
            P.dma('sync', knb, bc(at_kn, [128, 64]))
            snk = RM.alloc([16], F32)
            P.dma('sync', snk, bc(at_sink, [128, 16]))
            rT = RM.alloc([2, NW, 32], F32)
            P.dma('sync', rT.rearrange("p a s f -> p a (s f)"), ropeT.rearrange("a p n -> p a n"))
            amb = RM.alloc([4, 128], BF16)
            sm = RM.alloc([8], F32)
            sinkexp = RM.alloc([16], F32)
            hTc = [RM.alloc([8, 128], BF16) for _ in range(2)]
            nms = RM.alloc([8], F32)
            qn = RM.alloc([8, 64], F32)
            nsq = qn
            qr = RM.alloc([4, 8, 32], F32)
            qtk = RM.alloc([16, 64], BF16)
            qtk2 = RM.alloc([16, 64], BF16)
            m2 = RM.mark()
            amf = RM.alloc([4, 128], F32)
            absq = RM.alloc([128], F32)
            P.dma('sync', amf, amask.rearrange("c p n -> p c n"))
            P.copy('vector', amb, amf)
            P.act(absq[:, 0:64], qnb, AF.Abs)
            P.act(absq[:, 64:128], knb, AF.Abs)
            P.reduce('vector', sm[:, 0:1], absq[:, 0:64], ALU.max)
            P.reduce('vector', sm[:, 1:2], absq[:, 64:128], ALU.max)
            P.tt('vector', sm[:, 2:3], sm[:, 0:1], sm[:, 1:2], ALU.mult)
            P.ts('vector', sm[:, 3:4], sm[:, 2:3], -8.0, None, ALU.mult)
            negB = sm[:, 3:4]
            P.act(sinkexp, snk, AF.Exp, bias=negB)
            RM.release(m2)
            Wkv1 = RM.alloc([8, 512], BF16)
            P.dma('gpsimd', Wkv1, awin[:, :, 1024:1536])

            def qknorm_rope(ps3, nh, wb_, wch, dst):
                P.act(nsq[:, 0:nh, :], ps3, AF.Square)
                P.reduce('vector', nms[:, 0:nh], nsq[:, 0:nh, :], ALU.add)
                P.act(nms[:, 0:nh], nms[:, 0:nh], AF.Sqrt, bias=epsb, scale=1.0 / 64)
                P.recip(nms[:, 0:nh], nms[:, 0:nh])
                P.tt('vector', qn[:, 0:nh, :], ps3, bc(nms[:, 0:nh].unsqueeze(2), [128, nh, 64]), ALU.mult)
                if wch < 0:
                    P.tt('gpsimd', dst, qn[:, 0:nh, :], bc(wb_.unsqueeze(1), [128, nh, 64]), ALU.mult)
                    return
                P.tt('gpsimd', qn[:, 0:nh, :], qn[:, 0:nh, :], bc(wb_.unsqueeze(1), [128, nh, 64]), ALU.mult)
                x1, x2 = qn[:, 0:nh, 0:32], qn[:, 0:nh, 32:64]
                cs = bc(rT[:, 0, wch, :].unsqueeze(1), [128, nh, 32])
                sn = bc(rT[:, 1, wch, :].unsqueeze(1), [128, nh, 32])
                P.tt('vector', qr[:, 0, 0:nh], x1, cs, ALU.mult)
                P.tt('gpsimd', qr[:, 1, 0:nh], x2, sn, ALU.mult)
                P.tt('vector', qr[:, 2, 0:nh], x2, cs, ALU.mult)
                P.tt('gpsimd', qr[:, 3, 0:nh], x1, sn, ALU.mult)
                P.tt('vector', dst[:, :, 0:32], qr[:, 0, 0:nh], qr[:, 1, 0:nh], ALU.subtract)
                P.tt('gpsimd', dst[:, :, 32:64], qr[:, 2, 0:nh], qr[:, 3, 0:nh], ALU.add)

            for c in range(NCH):
                v_ = 1 if c < 2 else 0
                hc_ = hTc[c % 2]
                make_hT(X1[:, c, :], hc_, WM[:, 1, 0, v_, :], SHc[:, 1, 0, v_, :])
                pkv = pb(1 + (c % 2), [512])
                for kc in range(8):
                    P.mm(pkv, hc_[:, kc, :], Wkv1[:, kc, :], start=(kc == 0), stop=(kc == 7))
                P.copy('scalar', Vx1[:, c, :, 0:64], pkv[:, 256:512].rearrange("p (h d) -> p h d", h=4))
                qknorm_rope(pkv[:, 0:256].rearrange("p (h d) -> p h d", h=4), 4, knb, (c - 2) if c >= 2 else -1, qtk[:, 0:4, :])
                tpk = pb(4, [2, 128], BF16)
                for t_ in range(2):
                    P.transpose(tpk[:, t_, :], qtk[:, 2 * t_:2 * t_ + 2, :].rearrange("p h d -> p (h d)"), identB)
                P.copy('scalar', kT1[:, :, c * 128:(c + 1) * 128], tpk)
            mark('l1kv')
            RM.release(m2)
            Wq1 = RM.alloc([8, 1024], BF16)
            P.dma('gpsimd', Wq1, awin[:, :, 0:1024])
            Wo1 = RM.alloc([8, D], BF16)
            P.dma('gpsimd', Wo1, at_w_out[0].rearrange("(kc p) n -> p kc n", p=128))
            qTc = RM.alloc([2, 4, 128], BF16)
            Eb = [RM.alloc([5, 4, 128], BF16) for _ in range(2)]
            Otk = RM.alloc([16, 64], BF16)
            OT = RM.alloc([8, 128], BF16)
            dn1 = RM.alloc([8], F32)
            ot1 = [RX.alloc([512], F32), RM.alloc([512], F32)]
            ei = 0
            for i in range(16):
                c = i + 3
                hc_ = hTc[i % 2]
                make_hT(X1[:, c, :], hc_, WM[:, 1, 0, 0, :], SHc[:, 1, 0, 0, :])
                for hf in range(2):
                    pq = pb(2 + hf, [512])
                    for kc in range(8):
                        P.mm(pq, hc_[:, kc, :], Wq1[:, kc, hf * 512:(hf + 1) * 512], start=(kc == 0), stop=(kc == 7))
                    qknorm_rope(pq.rearrange("p (h d) -> p h d", h=8), 8, qnb, c - 2, qtk[:, 8 * hf:8 * hf + 8, :])
                tpq = pb(4, [2, 4, 128], BF16)
                for tp_ in range(2):
                    P.copy('gpsimd', qtk2[:, 8 * tp_:8 * tp_ + 8, :].rearrange("p (j g) d -> p g j d", g=2),
                           qtk[:, 8 * tp_:8 * tp_ + 8, :].rearrange("p (g j) d -> p g j d", g=2))
                    for j in range(4):
                        src = qtk2[:, 8 * tp_ + 2 * j:8 * tp_ + 2 * j + 2, :]
                        P.transpose(tpq[:, tp_, j, :], src.rearrange("p h d -> p (h d)"), identB)
                P.copy('scalar', qTc, tpq)
                kblocks = [(c - 1, 0 if i == 0 else 1), (c, None), (c + 1, 3 if i == 15 else 2), (0, None), (1, None)]
                for g in range(4):
                    tp_, hb = g // 2, slice(64 * (g % 2), 64 * (g % 2) + 64)
                    E = Eb[ei % 2]
                    ei += 1
                    for bi_, (kc_, mk) in enumerate(kblocks):
                        pst = pb(5 + (bi_ % 2), [4, 128])
                        P.mm(pst.rearrange("p j t -> p (j t)"), kT1[hb, tp_, kc_ * 128:(kc_ + 1) * 128],
                             qTc[hb, tp_].rearrange("p j t -> p (j t)"))
                        P.act(E[:, bi_], pst, AF.Exp, bias=negB, scale=0.125)
                        if mk is not None:
                            P.tt('gpsimd', E[:, bi_], E[:, bi_], bc(amb[:, mk, :].unsqueeze(1), [128, 4, 128]), ALU.mult)
                    pso = pb(7 if g % 2 == 0 else 3, [4, 65])
                    for j in range(4):
                        for bi_, (kc_, mk) in enumerate(kblocks):
                            P.mm(pso[:, j, :], E[:, bi_, j, :], Vx1[:, kc_, g, :], start=(bi_ == 0), stop=(bi_ == 4))
                    P.tt('vector', dn1[:, 0:4], pso[:, :, 64], sinkexp[:, 4 * g:4 * g + 4], ALU.add)
                    P.recip(dn1[:, 4:8], dn1[:, 0:4])
                    P.tt('vector', Otk[:, 4 * g:4 * g + 4, :], pso[:, :, 0:64], bc(dn1[:, 4:8].unsqueeze(2), [128, 4, 64]), ALU.mult)
                tpo = pb(4, [8, 128], BF16)
                for kc in range(8):
                    P.transpose(tpo[:, kc, :], Otk[:, 2 * kc:2 * kc + 2, :].rearrange("p h d -> p (h d)"), identB)
                P.copy('scalar', OT, tpo)
                for hf in range(2):
                    po = pb(1 + hf, [512])
                    for kc in range(8):
                        P.mm(po, OT[:, kc, :], Wo1[:, kc, hf * 512:(hf + 1) * 512], start=(kc == 0), stop=(kc == 7))
                    P.tt('vector', ot1[hf], po, gB[:, 0, hf * 512:(hf + 1) * 512], ALU.mult)
                    P.tt('gpsimd', X1[:, c, hf * 512:(hf + 1) * 512], X1[:, c, hf * 512:(hf + 1) * 512], ot1[hf], ALU.add)
            mark('l1attn')
            RM.release(m1)
            if dbg == 'l1m':
                for c in range(NCH):
                    P.dma('sync', dbg_out[c], X1[:, c, :], store=True)
            else:
                ffn(1, list(range(3, 19)), lambda c: 0, lambda c: 1)
        for i in range(16):
            P.dma('sync', y[i * 128:(i + 1) * 128, :], X1[:, i + 3, :], store=True)
        mark('end')
        stats = P.emit()
        stats['marks'] = P.marks
        stats['peaks'] = (RP.peak, RX.peak, RM.peak)
    return nc, stats


def _rope_angles(pos):
    pos = np.asarray(pos)
    row = (pos // 64).astype(np.float32)
    col = (pos % 64).astype(np.float32)
    inv = (10000.0 ** (-np.arange(16, dtype=np.float32) / 16)).astype(np.float32)
    return np.concatenate([row[:, None] * inv, col[:, None] * inv], axis=-1).astype(np.float32)


def _core_inputs(inp, core):
    b, q = core // 4, core % 4
    x = np.asarray(inp['x'], np.float32)
    ctx = np.asarray(inp['ctx'], np.float32)
    xb = x[b].reshape(64, 128, D)
    fl = np.zeros((1, NF), np.float32)
    xw = np.zeros((NCH, 128, D), np.float32)
    xw[0:2] = ctx[b].reshape(2, 128, D)
    fl[0, FL_VF:FL_VF + 2] = 1.0
    wpos = np.full((NW, 128), -1, np.int64)
    for w in range(NW):
        ch = 16 * q - 1 + w
        if 0 <= ch < 64:
            xw[2 + w] = xb[ch]
            fl[0, FL_VF + 2 + w] = 1.0
            wpos[w] = ch * 128 + np.arange(128)
    xe = np.zeros((128, D), np.float32)
    tl = 128 * (16 * q - 1) - 1
    tr = 128 * (16 * q + 17)
    if 0 <= tl < SEQ:
        xe[0] = x[b, tl]
        fl[0, FL_EDGE] = 1.0
    if 0 <= tr < SEQ:
        xe[1] = x[b, tr]
        fl[0, FL_EDGE + 1] = 1.0
    slots = [(ch, 0) for ch in range(16 * q - 2, -1, -1)] + [(ch, 1) for ch in range(16 * q + 17, 64)]
    assert len(slots) <= NSLOT
    xs = np.zeros((NSLOT, 128, D), np.float32)
    xsh = np.zeros((128, D), np.float32)
    spos = np.zeros((NSLOT, 128), np.int64)
    for s, (ch, d_) in enumerate(slots):
        xs[s] = xb[ch]
        fl[0, (FL_FF if d_ == 0 else FL_FB) + s] = 1.0
        spos[s] = ch * 128 + np.arange(128)
        t0, t1 = ch * 128 - 1, ch * 128 + 128
        if t0 >= 0:
            xsh[2 * s] = x[b, t0]
            fl[0, FL_SH + 2 * s] = 1.0
        if t1 < SEQ:
            xsh[2 * s + 1] = x[b, t1]
            fl[0, FL_SH + 2 * s + 1] = 1.0
    wp = np.where(wpos < 0, 0, wpos).reshape(-1)
    ang = _rope_angles(wp)
    cosw, sinw = np.cos(ang), np.sin(ang)
    ropeT = np.stack([cosw.reshape(NW, 128, 32).transpose(1, 0, 2).reshape(128, NW * 32),
                      sinw.reshape(NW, 128, 32).transpose(1, 0, 2).reshape(128, NW * 32)]).astype(np.float32)
    p = np.arange(128)
    fi = p % 32
    cosF = cosw[:, fi].T
    sgn = np.where((p % 64) < 32, 1.0, -1.0)[:, None]
    sinF = sinw[:, fi].T * sgn
    ropeF = np.stack([cosF, sinF]).astype(np.float32)
    angs = _rope_angles(spos.reshape(-1))
    ropeS = np.stack([np.cos(angs).reshape(NSLOT, 128, 32).transpose(1, 0, 2).reshape(128, NSLOT * 32),
                      np.sin(angs).reshape(NSLOT, 128, 32).transpose(1, 0, 2).reshape(128, NSLOT * 32)]).astype(np.float32)
    u = np.arange(128)[:, None]
    s_ = np.arange(128)[None, :]
    consts = np.zeros((7, 128, 128), np.float32)
    consts[0] = np.eye(128)
    consts[1] = (u > s_)
    consts[2] = (u < s_)
    consts[3] = (u <= s_)
    consts[4] = (u >= s_)
    consts[5] = 1.0
    amask = np.zeros((4, 128, 128), np.float32)
    mL = (u >= s_).astype(np.float32)
    mR = (u <= s_).astype(np.float32)
    amask[0] = mL * (1.0 if q > 0 else 0.0)
    amask[1] = mL
    amask[2] = mR
    amask[3] = mR * (1.0 if q < 3 else 0.0)
    cvec = np.concatenate([np.asarray(inp['c'], np.float32)[b].reshape(8, 128),
                           np.asarray(inp['c_ctx'], np.float32).reshape(8, 128)], 0)
    m = dict(xw=xw, xe=xe, xs=xs, xsh=xsh, fl=fl, cvec=cvec, consts=consts, amask=amask,
             ropeF=ropeF, ropeT=ropeT, ropeS=ropeS)
    for k in ('ada_w', 'ada_b', 'norm_w', 'ffn_w_in', 'ffn_w_out', 'ab_w_in', 'ab_w_out', 'ret_log_gamma',
              'ret_norm_w', 'mlstm_conv_w', 'mlstm_conv_b', 'mlstm_gate_b', 'mlstm_norm_w', 'attn_w_in',
              'attn_w_out', 'attn_q_norm_w', 'attn_k_norm_w', 'attn_sink'):
        m[k] = np.ascontiguousarray(np.asarray(inp[k], np.float32))
    return m


_NC_CACHE = {}


def kernel(**inp):
    if 'nc' not in _NC_CACHE:
        _NC_CACHE['nc'] = build()[0]
    nc = _NC_CACHE['nc']
    in_maps = [_core_inputs(inp, c) for c in range(NCORE)]
    res = run_bass_kernel_spmd(nc, in_maps, core_ids=list(range(NCORE)))
    out = np.zeros((2, SEQ, D), np.float32)
    for c in range(NCORE):
        b, q = c // 4, c % 4
        out[b, 2048 * q:2048 * (q + 1)] = res.results[c]["y"]
    return out
```

```python
import contextlib
import math
import numpy as np
import concourse.bass as bass
import concourse.mybir as mybir
from concourse.bass_utils import run_bass_kernel_spmd

F32 = mybir.dt.float32
BF16 = mybir.dt.bfloat16
ALU = mybir.AluOpType
AF = mybir.ActivationFunctionType
AX = mybir.AxisListType

_DTSZ = {F32: 4, BF16: 2}
SEM_LIMIT = 12000
DMA_POOL = 12


def _region(ap):
    sp = str(ap.space).upper()
    if 'SB' not in sp and 'PSUM' not in sp:
        return None
    if 'PSUM' in sp:
        return (ap.name, 0, 128, 0, 2048)
    pat = ap.ap
    esz = _DTSZ[ap.dtype]
    pstep, pcount = pat[0]
    off = ap.offset
    if pstep == 0:
        p0 = 0
        f0 = off
    else:
        p0 = off // pstep
        f0 = off - p0 * pstep
    ext = 0
    for stp, cn in pat[1:]:
        ext += abs(stp) * (cn - 1)
    return (ap.name, p0, p0 + pcount, f0 * esz, (f0 + ext + 1) * esz)


class Prog:
    ENGS = ('tensor', 'vector', 'scalar', 'gpsimd', 'sync')

    def __init__(self, nc, same_engine_sync=True):
        self.nc = nc
        self.ops = []
        self.track = {}
        self.same_engine_sync = same_engine_sync
        self.dma_hist = {e: [] for e in self.ENGS}
        self.store_ops = []

    def _add(self, eng, fn, outs, ins, is_dma=False, extra_deps=(), force=False):
        if getattr(self, 'frozen', False) and not force:
            return -1
        idx = len(self.ops)
        deps = set(extra_deps)
        self.ops.append(dict(eng=eng, fn=fn, deps=deps, is_dma=is_dma, signaled=False))
        for ap in ins:
            r = _region(ap)
            if r is not None:
                self._access(idx, eng, r, False, deps)
        for ap in outs:
            r = _region(ap)
            if r is not None:
                self._access(idx, eng, r, True, deps)
        if is_dma:
            h = self.dma_hist[eng]
            if len(h) >= DMA_POOL:
                deps.add(h[-DMA_POOL])
            h.append(idx)
        deps.discard(idx)
        return idx

    def _access(self, idx, eng, r, is_write, deps):
        name, p0, p1, b0, b1 = r
        recs = self.track.get(name, [])
        keep = []
        for rec in recs:
            (q0, q1, c0, c1, oi, ow, oe) = rec
            if q1 <= p0 or p1 <= q0 or c1 <= b0 or b1 <= c0 or oi == idx:
                keep.append(rec)
                continue
            if is_write or ow or (name.startswith('pb') and oe != eng):
                pe_pe = (eng == 'tensor' and oe == 'tensor')
                if not pe_pe:
                    deps.add(oi)
            covered = (p0 <= q0 and q1 <= p1 and b0 <= c0 and c1 <= b1)
            if is_write and covered and not (eng == 'tensor' and oe == 'tensor' and not ow):
                continue
            if (not is_write) and (not ow) and oe == eng and covered and not self.ops[oi]['is_dma']:
                continue
            keep.append(rec)
        keep.append((p0, p1, b0, b1, idx, is_write, eng))
        self.track[name] = keep

    def op(self, eng, fn, outs, ins):
        return self._add(eng, fn, outs, ins)

    def dma(self, eng, out, in_, store=False, **kw):
        i = self._add(eng, lambda e: e.dma_start(out=out, in_=in_, **kw), [out], [in_], is_dma=True)
        if store and i >= 0:
            self.store_ops.append(i)
        return i

    def mm(self, out, lhsT, rhs, start=True, stop=True, after=None):
        i = self.op('tensor', lambda e: e.matmul(out, lhsT, rhs, start=start, stop=stop), [out], [lhsT, rhs])
        if after is not None and i >= 0 and after >= 0:
            self.ops[i]['deps'].add(after)
        return i

    def transpose(self, out, in_, ident):
        return self.op('tensor', lambda e: e.transpose(out, in_, ident), [out], [in_, ident])

    def act(self, out, in_, func, bias=None, scale=None, accum_out=None):
        kw = {}
        ins = [in_]
        outs = [out]
        if bias is not None:
            kw['bias'] = bias
            if not isinstance(bias, (int, float)):
                ins.append(bias)
        if scale is not None:
            kw['scale'] = scale
            if not isinstance(scale, (int, float)):
                ins.append(scale)
        if accum_out is not None:
            kw['accum_out'] = accum_out
            outs.append(accum_out)
        return self.op('scalar', lambda e: e.activation(out, in_, func, **kw), outs, ins)

    def tt(self, eng, out, in0, in1, op):
        return self.op(eng, lambda e: e.tensor_tensor(out, in0, in1, op), [out], [in0, in1])

    def ts(self, eng, out, in0, s1, s2, op0, op1=None):
        ins = [in0] + [s for s in (s1, s2) if s is not None and not isinstance(s, (int, float))]
        kw = {}
        if op1 is not None:
            kw['op1'] = op1
        return self.op(eng, lambda e: e.tensor_scalar(out, in0, s1, s2, op0, **kw), [out], ins)

    def stt(self, eng, out, in0, scalar, in1, op0, op1):
        ins = [in0, in1] + ([scalar] if not isinstance(scalar, (int, float)) else [])
        return self.op(eng, lambda e: e.scalar_tensor_tensor(out, in0, scalar, in1, op0, op1), [out], ins)

    def copy(self, eng, out, in_):
        if eng == 'scalar':
            return self.op(eng, lambda e: e.copy(out, in_), [out], [in_])
        return self.op(eng, lambda e: e.tensor_copy(out, in_), [out], [in_])

    def memset(self, eng, out, val):
        return self.op(eng, lambda e: e.memset(out, val), [out], [])

    def recip(self, out, in_):
        return self.op('vector', lambda e: e.reciprocal(out, in_), [out], [in_])

    def reduce(self, eng, out, in_, op, axis=AX.X):
        return self.op(eng, lambda e: e.tensor_reduce(out, in_, axis, op), [out], [in_])

    def emit(self):
        nc = self.nc
        ops = self.ops
        self._add('sync', None, [], [], extra_deps=self.store_ops, force=True)
        for o in ops:
            if o['is_dma']:
                o['signaled'] = True
            for d in o['deps']:
                ops[d]['signaled'] = True
        cnt = {e: 0 for e in self.ENGS}
        dcnt = {e: 0 for e in self.ENGS}
        nsem_eng = {e: 0 for e in self.ENGS}
        for o in ops:
            e = o['eng']
            if not o['signaled']:
                continue
            if o['is_dma']:
                k = dcnt[e]
                dcnt[e] += 1
                o['sem'] = ('d', e, k % DMA_POOL)
                o['val'] = 16 * (k // DMA_POOL + 1)
                o['sidx'] = None
            else:
                k = cnt[e]
                cnt[e] += 1
                o['sem'] = ('c', e, k // SEM_LIMIT)
                o['val'] = (k % SEM_LIMIT) + 1
                o['sidx'] = k
                nsem_eng[e] = k // SEM_LIMIT + 1
        sems = {}
        st = contextlib.ExitStack()
        for e in self.ENGS:
            for j in range(nsem_eng[e]):
                sems[('c', e, j)] = st.enter_context(nc.semaphore(f"c_{e}_{j}"))
            for j in range(min(DMA_POOL, dcnt[e])):
                sems[('d', e, j)] = st.enter_context(nc.semaphore(f"d_{e}_{j}"))
        seen = {e: {f: -1 for f in self.ENGS} for e in self.ENGS}
        seen_dma = {e: set() for e in self.ENGS}
        per_eng = {e: [] for e in self.ENGS}
        nwaits = 0
        for o in ops:
            e = o['eng']
            waits = {}
            for d in sorted(o['deps']):
                p = ops[d]
                if p['is_dma']:
                    if d in seen_dma[e]:
                        continue
                    seen_dma[e].add(d)
                    waits[p['sem']] = max(waits.get(p['sem'], 0), p['val'])
                else:
                    f = p['eng']
                    if f == e and not self.same_engine_sync:
                        continue
                    if p['sidx'] <= seen[e][f]:
                        continue
                    seen[e][f] = p['sidx']
                    waits[p['sem']] = max(waits.get(p['sem'], 0), p['val'])
            nwaits += len(waits)
            per_eng[e].append((o, list(waits.items())))
        self.stats = dict(n_ops=len(ops), n_waits=nwaits, per_eng={e: len(v) for e, v in per_eng.items()})
        with st, nc.Block() as block:
            def body(engname):
                def run(eng):
                    for o, waits in per_eng[engname]:
                        for key, val in waits:
                            eng.wait_ge(sems[key], val)
                        if o['fn'] is None:
                            continue
                        ins = o['fn'](eng)
                        if o['signaled']:
                            ins.then_inc(sems[o['sem']], 16 if o['is_dma'] else 1)
                return run
            block.tensor(body('tensor'))
            block.vector(body('vector'))
            block.scalar(body('scalar'))
            block.gpsimd(body('gpsimd'))
            block.sync(body('sync'))
        return self.stats


D = 1024
SEQ = 8192
NCORE = 8
NW = 18
NCH = 20
NSLOT = 48
NF = 256
EPS = 1e-6
D_FF = 2816
NFT = 22
LNK = math.log(0.125)
HT_N = 2564
CTX0 = 1
WIN0 = 259
FL_VF = 0
FL_EDGE = 20
FL_FF = 22
FL_FB = 70
FL_SH = 118


def bc(ap, shape):
    return ap.broadcast_to(shape)


class Region:
    def __init__(self, t, base, cap, name):
        self.t, self.base, self.cap, self.name = t, base, cap, name
        self.off = 0
        self.peak = 0

    def alloc(self, shape, dt):
        n = 1
        for s in shape:
            n *= s
        nb = (n * _DTSZ[dt] + 63) // 64 * 64
        if self.off + nb > self.cap:
            raise RuntimeError(f"region {self.name} overflow: {self.off + nb} > {self.cap}")
        a = self.base + self.off
        v = self.t[:, a // 4:(a + nb) // 4]
        if dt != F32:
            v = v.bitcast(dt)
        v = v[:, 0:n]
        self.off += nb
        self.peak = max(self.peak, self.off)
        if len(shape) == 1:
            return v
        names = [chr(ord('a') + i) for i in range(len(shape))]
        pat = "p (" + " ".join(names) + ") -> p " + " ".join(names)
        return v.rearrange(pat, **{nm: s for nm, s in zip(names, shape)})

    def mark(self):
        return self.off

    def release(self, m):
        self.off = m


class _Stop(Exception):
    pass


ARENA_BYTES = 212480
P_BYTES = 35328
X_BYTES = 83968


def build(dbg=None):
    nc = bass.Bass("TRN2", target_bir_lowering=False)

    def din(name, shape):
        return nc.dram_tensor(name, list(shape), F32, kind="ExternalInput").ap()

    xw = din("xw", [NCH, 128, D])
    xe = din("xe", [128, D])
    xs = din("xs", [NSLOT, 128, D])
    xsh = din("xsh", [128, D])
    fl = din("fl", [1, NF])
    cvec = din("cvec", [16, 128])
    ada_w = din("ada_w", [2, D, 6 * D])
    ada_b = din("ada_b", [2, 6 * D])
    norm_w = din("norm_w", [2, 2, D])
    ffn_w_in = din("ffn_w_in", [2, D, 2 * D_FF])
    ffn_w_out = din("ffn_w_out", [2, D_FF, D])
    ab_w_in = din("ab_w_in", [1, D, 4128])
    ab_w_out = din("ab_w_out", [1, D, D])
    ret_lg = din("ret_log_gamma", [1, 2, 8])
    ret_nw = din("ret_norm_w", [1, 512])
    conv_w = din("mlstm_conv_w", [1, 3, D])
    conv_b = din("mlstm_conv_b", [1, D])
    gate_b = din("mlstm_gate_b", [1, 4, 8])
    ml_nw = din("mlstm_norm_w", [1, 512])
    at_w_in = din("attn_w_in", [1, D, 1536])
    at_w_out = din("attn_w_out", [1, D, D])
    at_qn = din("attn_q_norm_w", [1, 64])
    at_kn = din("attn_k_norm_w", [1, 64])
    at_sink = din("attn_sink", [1, 16])
    consts = din("consts", [7, 128, 128])
    amask = din("amask", [4, 128, 128])
    ropeF = din("ropeF", [2, 128, 2304])
    ropeT = din("ropeT", [2, 128, NW * 32])
    ropeS = din("ropeS", [2, 128, NSLOT * 32])
    y = nc.dram_tensor("y", [2048, D], F32, kind="ExternalOutput").ap()
    dbg_out = None
    if dbg is not None:
        dbg_out = nc.dram_tensor("dbg", [NCH, 128, D], F32, kind="ExternalOutput").ap()

    st = contextlib.ExitStack()
    with st:
        arena_t = st.enter_context(nc.sbuf_tensor("arena", [128, ARENA_BYTES // 4], F32))
        RP = Region(arena_t, 0, P_BYTES, "P")
        RX = Region(arena_t, P_BYTES, X_BYTES, "X")
        RM = Region(arena_t, P_BYTES + X_BYTES, ARENA_BYTES - P_BYTES - X_BYTES, "MF")
        banks = [st.enter_context(nc.psum_tensor(f"pb{i}", [128, 512], F32)) for i in range(8)]
        P = Prog(nc)
        P.marks = []

        def mark(nm):
            P.marks.append((nm, sum(1 for o in P.ops if o['eng'] == 'tensor')))

        def ck(name, aps):
            if dbg != name:
                return
            k = 0
            for ap in aps:
                n = ap.shape[1] if len(ap.shape) == 2 else None
                flat = ap
                npart = ap.shape[0]
                P.dma('sync' if flat.dtype == F32 else 'gpsimd', dbg_out[k][0:npart, 0:flat.shape[1]], flat, store=True)
                k += 1
            P.frozen = True

        def pb(i, shape, dt=F32, off=0):
            n = 1
            for s in shape:
                n *= s
            nb = n * _DTSZ[dt]
            v = banks[i][:, off // 4:(off + nb + 3) // 4]
            if dt != F32:
                v = v.bitcast(dt)
            v = v[:, 0:n]
            if len(shape) == 1:
                return v
            names = [chr(ord('a') + k) for k in range(len(shape))]
            pat = "p (" + " ".join(names) + ") -> p " + " ".join(names)
            return v.rearrange(pat, **{nm: s for nm, s in zip(names, shape)})

        cst = RP.alloc([7, 128], F32)
        P.dma('sync', cst, consts.rearrange("c p n -> p c n"))
        identF = cst[:, 0, :]
        MfF, MbF = cst[:, 1, :], cst[:, 2, :]
        onesF = cst[:, 5, :]
        cstb = RP.alloc([7, 128], BF16)
        P.copy('vector', cstb, cst)
        identB = cstb[:, 0, :]
        maskF_b, maskB_b = cstb[:, 3, :], cstb[:, 4, :]
        FL = RP.alloc([NF], F32)
        P.dma('sync', FL, bc(fl, [128, NF]))
        epsb = RP.alloc([1], F32)
        P.memset('vector', epsb, EPS)
        lnk = RP.alloc([1], F32)
        P.memset('vector', lnk, LNK)
        colA = RP.alloc([112], F32)
        colB = RP.alloc([48], F32)
        svf = RP.alloc([16], F32)
        sv = RP.alloc([8, 2], BF16)
        modT = RP.alloc([2, 2, 48], F32)
        gB = RP.alloc([4, D], F32)
        WM = RP.alloc([2, 2, 2, 8], F32)
        SHc = RP.alloc([2, 2, 2, 8], F32)
        junk = RP.alloc([D], BF16)
        xnb = [RP.alloc([D], BF16) for _ in range(2)]
        t32 = RP.alloc([8, 128], F32)
        ssq = RP.alloc([4], F32)

        m0 = RM.mark()
        stg = RM.alloc([128], F32)
        stg2 = RM.alloc([128], F32)
        P.dma('sync', stg[0:16, :], cvec)
        P.dma('sync', stg[16:64, :], ada_b[0].rearrange("(j p) -> j p", p=128))
        P.dma('sync', stg[64:112, :], ada_b[1].rearrange("(j p) -> j p", p=128))
        tp = pb(0, [112])
        P.transpose(tp, stg[0:112, :], identF[0:112, 0:112])
        P.copy('vector', colA, tp)
        P.dma('sync', stg2[0:32, :], norm_w.rearrange("l i (j p) -> (l i j) p", p=128))
        P.dma('sync', stg2[32:36, :], ret_nw[0].rearrange("(j p) -> j p", p=128))
        P.dma('sync', stg2[36:40, :], ml_nw[0].rearrange("(j p) -> j p", p=128))
        P.dma('sync', stg2[40:48, :], conv_b[0].rearrange("(j p) -> j p", p=128))
        tp2 = pb(0, [48], off=1024)
        P.transpose(tp2, stg2[0:48, :], identF[0:48, 0:48])
        P.copy('vector', colB, tp2)
        RM.release(m0)
        ck('A0', [colA, colB, FL, cst[:, 1, :]])
        P.act(svf, colA[:, 0:16], AF.Silu)
        P.copy('vector', sv[:, :, 0], svf[:, 0:8])
        P.copy('vector', sv[:, :, 1], svf[:, 8:16])

        gslot = {(0, 0, 2): 0, (0, 0, 5): 1, (0, 1, 2): 2, (0, 1, 5): 3, (1, 0, 2): 0, (1, 0, 5): 1}

        def modulation(l):
            m0 = RM.mark()
            svrep = RM.alloc([2, 8, 128], BF16)
            for v_ in range(2):
                P.copy('vector', svrep[:, v_, :, :], bc(svf[:, 8 * v_:8 * v_ + 8].unsqueeze(2), [128, 8, 128]))
            wblk = [RM.alloc([8, 1024], BF16) for _ in range(2)]
            bbc = RM.alloc([1024], F32)
            for j in range(6):
                wb = wblk[j % 2]
                P.dma('gpsimd', wb, ada_w[l].rearrange("(kc p) n -> p kc n", p=128)[:, :, j * 1024:(j + 1) * 1024])
                ps = pb(1, [8, 2])
                for n in range(8):
                    for kc in range(8):
                        P.mm(ps[:, n, :], wb[:, kc, n * 128:(n + 1) * 128], sv[:, kc, :], start=(kc == 0), stop=(kc == 7))
                for v_ in range(2):
                    P.tt('vector', modT[:, l, v_, j * 8:(j + 1) * 8], ps[:, :, v_],
                         colA[:, 16 + 48 * l + j * 8:16 + 48 * l + j * 8 + 8], ALU.add)
                if j in (2, 5):
                    P.dma('sync', bbc, bc(ada_b[l:l + 1, j * 1024:(j + 1) * 1024], [128, 1024]))
                    for v_ in range(2):
                        if (l, v_, j) not in gslot:
                            continue
                        for hf in range(2):
                            pg = pb(2 + hf, [512])
                            for kc in range(8):
                                P.mm(pg, svrep[:, v_, kc, :], wb[:, kc, hf * 512:(hf + 1) * 512], start=(kc == 0), stop=(kc == 7))
                            P.tt('vector', gB[:, gslot[(l, v_, j)], hf * 512:(hf + 1) * 512], pg, bbc[:, hf * 512:(hf + 1) * 512], ALU.add)
            RM.release(m0)

        def make_wm(l):
            for i in range(2):
                for v_ in range(2):
                    sc = modT[:, l, v_, (3 * i + 1) * 8:(3 * i + 2) * 8]
                    nw = colB[:, (2 * l + i) * 8:(2 * l + i) * 8 + 8]
                    P.stt('vector', WM[:, l, i, v_, :], sc, 1.0, nw, ALU.add, ALU.mult)
                    P.copy('vector', SHc[:, l, i, v_, :], modT[:, l, v_, (3 * i) * 8:(3 * i) * 8 + 8])

        modulation(0)
        make_wm(0)
        mark('mod0')
        ck('A', [colA, colB, modT.rearrange('p a b c -> p (a b c)'), gB[:, 0, :], gB[:, 3, :], WM.rearrange('p a b c d -> p (a b c d)')])

        cnt_h = [0]

        def make_hT(x_sb, dest, wcol, shcol, flag=None):
            k = cnt_h[0] % 2
            cnt_h[0] += 1
            ss = ssq[:, 2 * k:2 * k + 1]
            rs = ssq[:, 2 * k + 1:2 * k + 2]
            P.act(junk, x_sb, AF.Square, accum_out=ss)
            P.act(rs, ss, AF.Sqrt, bias=epsb, scale=1.0 / D)
            P.recip(rs, rs)
            xn = xnb[k]
            P.act(xn, x_sb, AF.Identity, scale=rs)
            tps = pb(0, [8, 128], BF16)
            for kc in range(8):
                P.transpose(tps[:, kc, :], xn[:, kc * 128:(kc + 1) * 128], identB)
            if flag is None:
                for kc in range(8):
                    P.act(dest[:, kc, :], tps[:, kc, :], AF.Identity, bias=shcol[:, kc:kc + 1], scale=wcol[:, kc:kc + 1])
            else:
                P.tt('vector', t32, tps, bc(wcol.unsqueeze(2), [128, 8, 128]), ALU.mult)
                P.tt('gpsimd', t32, t32, bc(shcol.unsqueeze(2), [128, 8, 128]), ALU.add)
                P.ts('vector', dest, t32, flag, None, ALU.mult)

        hT = RX.alloc([8, HT_N], BF16)
        AA = RX.alloc([NCH, 2, 16], F32)
        BB = RX.alloc([NCH, 2, 16], F32)
        DEC = RX.alloc([NCH, 2, 16], F32)
        Sacc = RX.alloc([16, 65], F32)
        RR = RX.alloc([2, 16], F32)
        expR = RX.alloc([2, 16], F32)
        mXf = RX.mark()
        mMF = RM.mark()
        win = ab_w_in[0].rearrange("(kc p) n -> p kc n", p=128)

        xbuf = [RX.alloc([D], F32) for _ in range(2)]
        P.memset('gpsimd', hT[:, :, 0:1], 0.0)
        P.memset('gpsimd', hT[:, :, 257:258], 0.0)

        def tokcols(c):
            return (CTX0 + c * 128) if c < 2 else (WIN0 + (c - 2) * 128)

        for c in range(NCH):
            xb = xbuf[c % 2]
            P.dma('sync', xb, xw[c])
            v_ = 1 if c < 2 else 0
            col0 = tokcols(c)
            fg = None if c not in (2, NCH - 1) else FL[:, FL_VF + c:FL_VF + c + 1]
            make_hT(xb, hT[:, :, col0:col0 + 128], WM[:, 0, 0, v_, :], SHc[:, 0, 0, v_, :], fg)
        hTh = RX.alloc([8, 128], BF16)
        xb = xbuf[0]
        P.dma('sync', xb, xe)
        make_hT(xb, hTh, WM[:, 0, 0, 0, :], SHc[:, 0, 0, 0, :])
        P.ts('vector', hT[:, :, 258:259], hTh[:, :, 0:1], FL[:, FL_EDGE:FL_EDGE + 1], None, ALU.mult)
        P.ts('vector', hT[:, :, 2563:2564], hTh[:, :, 1:2], FL[:, FL_EDGE + 1:FL_EDGE + 2], None, ALU.mult)

        ck('B', [hT[:, 0, 0:1024], hT[:, 7, 1540:2564]])
        mark('hT')
        Gpre = RX.alloc([NCH, 32], F32)
        LF = RX.alloc([NCH, 2, 16], F32)
        II = RX.alloc([NCH, 2, 16], F32)
        Wg = RM.alloc([8, 32], BF16)
        P.dma('gpsimd', Wg, win[:, :, 4096:4128])
        gbb = RM.alloc([32], F32)
        P.dma('sync', gbb, bc(gate_b[0].rearrange("a h -> (a h)").unsqueeze(0), [128, 32]))
        lgb = RM.alloc([2, 8], F32)
        P.dma('sync', lgb.rearrange("p a h -> p (a h)"), bc(ret_lg[0].rearrange("a h -> (a h)").unsqueeze(0), [128, 16]))
        ck('C0', [gbb, lgb.rearrange('p a h -> p (a h)'), Wg.rearrange('p a b -> p (a b)')])
        for c in range(NCH):
            c0 = tokcols(c)
            pg = pb(3, [128])[:, 32 * (c % 4):32 * (c % 4) + 32]
            for kc in range(8):
                P.mm(pg, hT[:, kc, c0:c0 + 128], Wg[:, kc, :], start=(kc == 0), stop=(kc == 7))
            P.tt('vector', Gpre[:, c, :], pg, gbb, ALU.add)
        ck('C1', [Gpre.rearrange('p a b -> p (a b)')])
        VFb = FL[:, FL_VF:FL_VF + NCH]
        tmpg = RX.alloc([NCH, 8], F32)
        for d_ in range(2):
            fcol = Gpre[:, :, 8 + 16 * d_:16 + 16 * d_]
            icol = Gpre[:, :, 16 * d_:8 + 16 * d_]
            P.act(tmpg, fcol, AF.Exp, scale=-1.0)
            P.act(tmpg, tmpg, AF.Ln, bias=1.0)
            P.stt('vector', LF[:, :, d_, 8:16], tmpg, -1.0, bc(VFb.unsqueeze(2), [128, NCH, 8]), ALU.mult, ALU.mult)
            P.tt('vector', LF[:, :, d_, 0:8], bc(lgb[:, d_, :].unsqueeze(1), [128, NCH, 8]), bc(VFb.unsqueeze(2), [128, NCH, 8]), ALU.mult)
            P.memset('gpsimd', II[:, :, d_, 0:8], 0.0)
            P.copy('gpsimd', II[:, :, d_, 8:16], icol)
        ck('C2', [LF.rearrange('p a b c -> p (a b c)'), II.rearrange('p a b c -> p (a b c)')])
        LFc = RX.alloc([2, NCH * 16], F32)
        for d_ in range(2):
            P.copy('gpsimd', LFc[:, d_, :].rearrange("p (c l) -> p c l", l=16), LF[:, :, d_, :])
        for d_ in range(2):
            pe = pb(4 + d_, [NCH, 16])
            pef = pe.rearrange("p c l -> p (c l)")
            for (a_, b_) in ((0, 128), (128, 256), (256, 320)):
                P.mm(pef[:, a_:b_], MfF if d_ == 0 else MbF, LFc[:, d_, a_:b_])
            if d_ == 0 and dbg == 'C2a':
                P.copy('vector', AA.rearrange('p a b c -> p (a b c)')[:, 0:320], pef)
                ck('C2a', [AA.rearrange('p a b c -> p (a b c)'), LFc.rearrange('p a b -> p (a b)')])
            P.act(AA[:, :, d_, :], pe, AF.Exp, scale=-1.0)
            if d_ == 0:
                ck('C2b', [AA.rearrange('p a b c -> p (a b c)')])
            P.tt('vector', BB[:, :, d_, :], pe, II[:, :, d_, :], ALU.add)
            if d_ == 0:
                ck('C2c', [BB.rearrange('p a b c -> p (a b c)')])
            P.act(BB[:, :, d_, :], BB[:, :, d_, :], AF.Exp, bias=lnk)
            if d_ == 0:
                ck('C2d', [BB.rearrange('p a b c -> p (a b c)')])
            P.tt('vector', BB[:, :, d_, :], BB[:, :, d_, :], bc(VFb.unsqueeze(2), [128, NCH, 16]), ALU.mult)
        ck('C3', [AA.rearrange('p a b c -> p (a b c)'), BB.rearrange('p a b c -> p (a b c)')])
        LFf = LF.rearrange("p a b c -> p (a b c)")
        for hf in range(2):
            pt_ = pb(6, [10, 2, 16])
            ptf = pt_.rearrange("p a b c -> p (a b c)")
            for (a_, b_) in ((0, 128), (128, 256), (256, 320)):
                P.mm(ptf[:, a_:b_], onesF, LFf[:, hf * 320 + a_:hf * 320 + b_])
            P.act(DEC[:, hf * 10:(hf + 1) * 10, :, :], pt_, AF.Exp)

        ck('C', [AA.rearrange('p a b c -> p (a b c)'), BB.rearrange('p a b c -> p (a b c)'), DEC.rearrange('p a b c -> p (a b c)'), Gpre.rearrange('p a b -> p (a b)')])
        mark('prepass')
        P.memset('vector', Sacc, 0.0)
        P.memset('vector', RR, 0.0)
        Wv = RM.alloc([8, 1024], BF16)
        P.dma('gpsimd', Wv[:, :, 0:512], win[:, :, 1024:1536])
        P.dma('gpsimd', Wv[:, :, 512:1024], win[:, :, 3072:3584])
        Wkr = RM.alloc([8, 512], BF16)
        P.dma('gpsimd', Wkr, win[:, :, 512:1024])
        Wtap = RM.alloc([3, 8, 512], BF16)
        m1 = RM.mark()
        cwk = RM.alloc([3, 512], F32)
        for j in range(3):
            P.dma('sync', cwk[:, j, :], bc(conv_w[0, j:j + 1, 512:1024], [128, 512]))
        Wkm = RM.alloc([8, 512], BF16)
        P.dma('gpsimd', Wkm, win[:, :, 2560:3072])
        for j in range(3):
            P.tt('vector', Wtap[:, j, :, :], Wkm, bc(cwk[:, j, :].unsqueeze(1), [128, 8, 512]), ALU.mult)
        RM.release(m1)
        cbb = RM.alloc([512], F32)
        P.dma('sync', cbb, bc(conv_b[0:1, 512:1024], [128, 512]))
        rS = RM.alloc([2, NSLOT, 32], F32)
        P.dma('sync', rS.rearrange("p a s f -> p a (s f)"), ropeS.rearrange("a p n -> p a n"))
        Kfb = RM.alloc([16, 128], BF16)
        Vext = RM.alloc([16, 65], BF16)
        ra = RM.alloc([2, 8, 32], F32)
        krf = [RM.alloc([512], F32) for _ in range(2)]
        Vsb = [RM.alloc([16, 64], BF16) for _ in range(2)]
        kmts = [RM.alloc([512], F32) for _ in range(2)]
        gps = [RM.alloc([32], F32) for _ in range(2)]
        xb = xbuf[1]
        P.dma('sync', xb, xsh)
        make_hT(xb, hTh, WM[:, 0, 0, 0, :], SHc[:, 0, 0, 0, :])
        P.tt('vector', hTh, hTh, bc(FL[:, FL_SH:FL_SH + 128].unsqueeze(1), [128, 8, 128]), ALU.mult)
        hTs = [RX.alloc([8, 130], BF16) for _ in range(2)]
        Ktok = RX.alloc([16, 64], BF16)
        sg = RX.alloc([160], F32)

        def slot_A(s):
            hs = hTs[s % 2]
            xb = xbuf[s % 2]
            P.dma('sync', xb, xs[s])
            make_hT(xb, hs[:, :, 1:129], WM[:, 0, 0, 0, :], SHc[:, 0, 0, 0, :])
            P.copy('scalar', hs[:, :, 0:1], hTh[:, :, 2 * s:2 * s + 1])
            P.copy('scalar', hs[:, :, 129:130], hTh[:, :, 2 * s + 1:2 * s + 2])
            pkr = pb(1, [512])
            for kc in range(8):
                P.mm(pkr, hs[:, kc, 1:129], Wkr[:, kc, :], start=(kc == 0), stop=(kc == 7))
            P.copy('scalar', krf[s % 2], pkr)
            pkm = pb(2, [512])
            for j in range(3):
                for kc in range(8):
                    P.mm(pkm, hs[:, kc, j:j + 128], Wtap[:, j, kc, :], start=(j == 0 and kc == 0), stop=(j == 2 and kc == 7))
            P.tt('vector', kmts[s % 2], pkm, cbb, ALU.add)
            for hf in range(2):
                pv = pb(3 + hf, [512])
                for kc in range(8):
                    P.mm(pv, hs[:, kc, 1:129], Wv[:, kc, hf * 512:(hf + 1) * 512], start=(kc == 0), stop=(kc == 7))
                P.copy('scalar', Vsb[s % 2][:, hf * 8:(hf + 1) * 8, :], pv.rearrange("p (h d) -> p h d", h=8))
            pg = pb(5, [32])
            for kc in range(8):
                P.mm(pg, hs[:, kc, 1:129], Wg[:, kc, :], start=(kc == 0), stop=(kc == 7))
            P.tt('vector', gps[s % 2], pg, gbb, ALU.add)

        def slot_B(s):
            ff = FL[:, FL_FF + s:FL_FF + s + 1]
            fb = FL[:, FL_FB + s:FL_FB + s + 1]
            k3 = krf[s % 2].rearrange("p (h d) -> p h d", h=8)
            x1, x2 = k3[:, :, 0:32], k3[:, :, 32:64]
            cs = bc(rS[:, 0, s, :].unsqueeze(1), [128, 8, 32])
            sn = bc(rS[:, 1, s, :].unsqueeze(1), [128, 8, 32])
            P.tt('vector', ra[:, 0], x1, cs, ALU.mult)
            P.tt('vector', ra[:, 1], x2, sn, ALU.mult)
            P.tt('vector', Ktok[:, 0:8, 0:32], ra[:, 0], ra[:, 1], ALU.subtract)
            P.tt('vector', ra[:, 0], x2, cs, ALU.mult)
            P.tt('vector', ra[:, 1], x1, sn, ALU.mult)
            P.tt('vector', Ktok[:, 0:8, 32:64], ra[:, 0], ra[:, 1], ALU.add)
            P.act(Ktok[:, 8:16, :], kmts[s % 2].rearrange("p (h d) -> p h d", h=8), AF.Silu)
            P.act(Kfb[:, :, 0:64], Ktok, AF.Identity, scale=ff)
            P.act(Kfb[:, :, 64:128], Ktok, AF.Identity, scale=fb)
            gp = gps[s % 2]
            fsel = sg[:, 32:40]
            isel = sg[:, 40:56]
            lf16 = sg[:, 56:72]
            lfm = sg[:, 72:104]
            rsel = sg[:, 104:120]
            bex = sg[:, 120:136]
            t8 = sg[:, 136:144]
            P.ts('vector', fsel, gp[:, 8:16], ff, None, ALU.mult)
            P.stt('vector', fsel, gp[:, 24:32], fb, fsel, ALU.mult, ALU.add)
            P.memset('vector', isel[:, 0:8], 0.0)
            P.ts('vector', isel[:, 8:16], gp[:, 0:8], ff, None, ALU.mult)
            P.stt('vector', isel[:, 8:16], gp[:, 16:24], fb, isel[:, 8:16], ALU.mult, ALU.add)
            P.act(t8, fsel, AF.Exp, scale=-1.0)
            P.act(t8, t8, AF.Ln, bias=1.0)
            P.ts('vector', lf16[:, 8:16], t8, -1.0, None, ALU.mult)
            P.ts('vector', lf16[:, 0:8], lgb[:, 0, :], ff, None, ALU.mult)
            P.stt('vector', lf16[:, 0:8], lgb[:, 1, :], fb, lf16[:, 0:8], ALU.mult, ALU.add)
            P.ts('vector', lfm[:, 0:16], lf16, ff, None, ALU.mult)
            P.ts('vector', lfm[:, 16:32], lf16, fb, None, ALU.mult)
            pe = pb(0, [16], off=0)
            P.mm(pe, MfF, lfm[:, 0:16], start=True, stop=False)
            P.mm(pe, MbF, lfm[:, 16:32], start=False, stop=True)
            pt_ = pb(0, [32], off=512)
            P.mm(pt_, onesF, lfm)
            P.ts('vector', rsel, RR[:, 0, :], ff, None, ALU.mult)
            P.stt('vector', rsel, RR[:, 1, :], fb, rsel, ALU.mult, ALU.add)
            P.tt('vector', rsel, rsel, isel, ALU.add)
            P.tt('vector', rsel, rsel, pe, ALU.add)
            P.act(bex, rsel, AF.Exp, bias=lnk)
            RRf = RR.rearrange("p a l -> p (a l)")
            P.tt('vector', RRf, RRf, pt_, ALU.add)
            P.tt('vector', Vext[:, :, 0:64], Vsb[s % 2], bc(bex.unsqueeze(2), [128, 16, 64]), ALU.mult)
            P.copy('scalar', Vext[:, :, 64], bex)
            kvb = [pb(6, [6, 65]), pb(7, [6, 65]), pb(5, [4, 65])]
            for ln in range(16):
                P.mm(kvb[ln // 6][:, ln % 6, :], Kfb[:, ln, :], Vext[:, ln, :])
            for g3 in range(3):
                nl = 6 if g3 < 2 else 4
                P.tt('vector', Sacc[:, g3 * 6:g3 * 6 + nl, :], Sacc[:, g3 * 6:g3 * 6 + nl, :], kvb[g3], ALU.add)

        slot_A(0)
        for s in range(NSLOT):
            if s + 1 < NSLOT:
                slot_A(s + 1)
            slot_B(s)
        P.act(expR, RR, AF.Exp)
        ck('D', [Sacc.rearrange('p a b -> p (a b)')[:, 0:1024], RR.rearrange('p a b -> p (a b)'), expR.rearrange('p a b -> p (a b)')])
        RX.release(mXf)
        RM.release(mMF)

        mark('outside')
        MT = RM.alloc([8, NCH * 128], BF16)
        mMF2 = RM.mark()
        order = [list(range(NCH)), [1, 0] + list(range(NCH - 1, 1, -1))]
        first_dir = [0 if order[0].index(c_) <= order[1].index(c_) else 1 for c_ in range(NCH)]
        groups = [(CTX0, 256, 0, False)] + [(WIN0 + 512 * g_, 512, 256 + 512 * g_, True) for g_ in range(4)] + [(WIN0 + 2048, 256, 2304, True)]
        for ps_ in range(8):
            if ps_ in (1, 2, 5, 6):
                mark(f'p{ps_}_start')
            RX.release(mXf)
            RM.release(mMF2)
            is_ml = ps_ >= 4
            j = ps_ % 4
            l0 = (8 if is_ml else 0) + 2 * j
            qoff = (2048 if is_ml else 0) + 128 * j
            koff = (2560 if is_ml else 512) + 128 * j
            goff = (3584 if is_ml else 1536) + 128 * j
            voff = (3072 if is_ml else 1024) + 128 * j
            ntap = 3 if is_ml else 1
            qT = RX.alloc([NCH * 128], BF16)
            kT = RX.alloc([NCH * 128], BF16)
            gT = RX.alloc([NCH * 128], BF16)
            Kt = RX.alloc([NCH, 128], BF16)
            Vp = RX.alloc([NCH, 2, 64], BF16)
            Wq = RM.alloc([3, 8, 128], BF16)
            Wk = RM.alloc([3, 8, 128], BF16)
            Wgt = RM.alloc([8, 128], BF16)
            Wvp = RM.alloc([8, 128], BF16)
            Oacc = RM.alloc([NCH, 2, 64], F32)
            mScan = RM.mark()
            P.dma('gpsimd', Wq[:, 0], win[:, :, qoff:qoff + 128])
            P.dma('gpsimd', Wk[:, 0], win[:, :, koff:koff + 128])
            P.dma('gpsimd', Wgt, win[:, :, goff:goff + 128])
            P.dma('gpsimd', Wvp, win[:, :, voff:voff + 128])
            if is_ml:
                cwp = RM.alloc([2, 3, 128], F32)
                for wi, co in enumerate((128 * j, 512 + 128 * j)):
                    for tpi in range(3):
                        P.dma('sync', cwp[:, wi, tpi, :], bc(conv_w[0, tpi:tpi + 1, co:co + 128], [128, 128]))
                for wi, W_ in enumerate((Wq, Wk)):
                    for tpi in (2, 1, 0):
                        P.tt('vector', W_[:, tpi], W_[:, 0], bc(cwp[:, wi, tpi, :].unsqueeze(1), [128, 8, 128]), ALU.mult)
            rtmp = RM.alloc([2, 512], F32)
            rfb = [RM.alloc([2, 512], F32) for _ in range(2)]
            bi = 0
            for gi_, (hc0, n, lc0, isw) in enumerate(groups):
                rf = rfb[gi_ % 2]
                if isw and not is_ml:
                    w0_ = hc0 - WIN0
                    P.dma('sync', rf[:, :, 0:n], ropeF.rearrange("a p n -> p a n")[:, :, w0_:w0_ + n])
                for which, W_, dst in (('q', Wq, qT), ('k', Wk, kT), ('g', Wgt, gT)):
                    pp = pb(1 + (bi % 3), [512])[:, 0:n]
                    bi += 1
                    if which == 'g':
                        for kc in range(8):
                            P.mm(pp, W_[:, kc, :], hT[:, kc, hc0:hc0 + n], start=(kc == 0), stop=(kc == 7))
                        P.act(dst[:, lc0:lc0 + n], pp, AF.Sigmoid if is_ml else AF.Silu)
                        continue
                    for tpi in range(ntap):
                        sh_ = (tpi - 1) if is_ml else 0
                        for kc in range(8):
                            P.mm(pp, W_[:, tpi, kc, :], hT[:, kc, hc0 + sh_:hc0 + sh_ + n],
                                 start=(tpi == 0 and kc == 0), stop=(tpi == ntap - 1 and kc == 7))
                    if is_ml:
                        ci_ = 40 + (j if which == 'q' else 4 + j)
                        P.act(dst[:, lc0:lc0 + n], pp, AF.Silu, bias=colB[:, ci_:ci_ + 1])
                    elif not isw:
                        P.copy('scalar', dst[:, lc0:lc0 + n], pp)
                    else:
                        P.tt('vector', rtmp[:, 0, 0:n], pp, rf[:, 0, 0:n], ALU.mult)
                        for blk in range(4):
                            src = blk ^ 1
                            P.tt('vector', rtmp[blk * 32:(blk + 1) * 32, 1, 0:n], pp[src * 32:(src + 1) * 32, :],
                                 rf[src * 32:(src + 1) * 32, 1, 0:n], ALU.mult)
                        P.tt('gpsimd', dst[:, lc0:lc0 + n], rtmp[:, 0, 0:n], rtmp[:, 1, 0:n], ALU.add)
            if ps_ in (1, 5):
                mark(f'p{ps_}_proj')
            for c8 in range(0, NCH, 8):
                ncc = min(8, NCH - c8)
                tpk = pb(4, [8, 128], BF16)
                for ci in range(ncc):
                    P.transpose(tpk[:, ci, :], kT[:, (c8 + ci) * 128:(c8 + ci + 1) * 128], identB)
                P.copy('scalar', Kt[:, c8:c8 + ncc, :], tpk[:, 0:ncc, :])
            for c4 in range(0, NCH, 4):
                pv = pb(1 + (c4 // 4) % 3, [4, 128])
                for ci in range(4):
                    c0 = tokcols(c4 + ci)
                    for kc in range(8):
                        P.mm(pv[:, ci, :], hT[:, kc, c0:c0 + 128], Wvp[:, kc, :], start=(kc == 0), stop=(kc == 7))
                P.copy('vector', Vp[:, c4:c4 + 4].rearrange("p c h d -> p c (h d)"), pv)
            if ps_ == 0:
                ck('E1', [qT[:, 0:1024], kT[:, 0:1024], gT[:, 0:1024], Kt.rearrange('p a b -> p (a b)')[:, 0:1024], Vp.rearrange('p a b c -> p (a b c)')[:, 0:1024]])
            if ps_ == 4:
                ck('F1', [qT[:, 0:1024], kT[:, 0:1024], gT[:, 0:1024], Kt.rearrange('p a b -> p (a b)')[:, 0:1024], Vp.rearrange('p a b c -> p (a b c)')[:, 0:1024]])
            if ps_ in (1, 5):
                mark(f'p{ps_}_kv')
            RM.release(mScan)
            decp = RM.alloc([NCH, 2], F32)
            P.copy('vector', decp[0:64], DEC[0:64, :, :, l0])
            P.copy('vector', decp[64:128], DEC[64:128, :, :, l0 + 1])
            S32 = [RM.alloc([130], F32) for _ in range(2)]
            Sbf = [RM.alloc([130], BF16) for _ in range(2)]
            stmp = RM.alloc([130], F32)
            ecol = RM.alloc([2], F32)
            Vts = [RM.alloc([2, 65], BF16) for _ in range(4)]
            dn = RM.alloc([2, 8], F32)
            P.memset('gpsimd', stmp, 0.0)
            for d_ in range(2):
                P.memset('gpsimd', S32[d_], 0.0)
                P.copy('vector', ecol[0:64, d_:d_ + 1], expR[0:64, d_, l0:l0 + 1])
                P.copy('vector', ecol[64:128, d_:d_ + 1], expR[64:128, d_, l0 + 1:l0 + 2])
            PTs = [RM.alloc([2, 128], BF16) for _ in range(4)]
            pt_banks = [[0, 6], [3, 7]]

            def scan_front(step, d_):
                c = order[d_][step]
                tk = slice(c * 128, (c + 1) * 128)
                Vt = Vts[2 * (step % 2) + d_]
                P.tt('gpsimd', Vt[:, :, 0:64], Vp[:, c], bc(BB[:, c, d_, l0:l0 + 2].unsqueeze(2), [128, 2, 64]), ALU.mult)
                P.copy('gpsimd', Vt[:, :, 64], BB[:, c, d_, l0:l0 + 2])
                ptp = pb(pt_banks[d_][step % 2], [2, 128])
                prev_mm = None
                for h in range(2):
                    hb = slice(64 * h, 64 * h + 64)
                    prev_mm = P.mm(ptp[:, h, :], kT[hb, tk], qT[hb, tk], after=prev_mm)
                PT = PTs[2 * (step % 2) + d_]
                P.tt('vector', PT, ptp, bc((maskF_b if d_ == 0 else maskB_b).unsqueeze(1), [128, 2, 128]), ALU.mult)

            def scan_kv(step, d_):
                c = order[d_][step]
                Vt = Vts[2 * (step % 2) + d_]
                kvp = pb(2 if d_ == 0 else 5, [130])
                P.mm(kvp, Kt[:, c, :], Vt.rearrange("p h e -> p (h e)"))

            def scan_back(step, d_):
                c = order[d_][step]
                tk = slice(c * 128, (c + 1) * 128)
                S = S32[d_]
                Vt = Vts[2 * (step % 2) + d_]
                PT = PTs[2 * (step % 2) + d_]
                if step == 2:
                    r0 = 64 * d_
                    P.copy('scalar', stmp[0:64, 0:65], Sacc[r0:r0 + 64, l0, :])
                    P.copy('scalar', stmp[64:128, 65:130], Sacc[r0:r0 + 64, l0 + 1, :])
                    P.stt('vector', S, S, ecol[:, d_:d_ + 1], stmp, ALU.mult, ALU.add)
                if step > 0:
                    P.ts('gpsimd', Sbf[d_], S, decp[:, c, d_:d_ + 1], None, ALU.mult)
                ops_ = pb(1 if d_ == 0 else 4, [2, 65])
                for h in range(2):
                    hb = slice(64 * h, 64 * h + 64)
                    P.mm(ops_[:, h, :], PT[:, h, :], Vt[:, h, :], start=True, stop=(step == 0))
                    if step > 0:
                        P.mm(ops_[:, h, :], qT[hb, tk], Sbf[d_][hb, 65 * h:65 * h + 65], start=False, stop=True)
                for h in range(2):
                    at = AA[:, c, d_, l0 + h:l0 + h + 1]
                    if is_ml:
                        dd = dn[:, h, 4 * d_:4 * d_ + 4]
                        P.act(dd[:, 0:1], ops_[:, h, 64:65], AF.Abs, scale=at)
                        P.ts('vector', dd[:, 1:2], dd[:, 0:1], 1.0, None, ALU.max)
                        P.recip(dd[:, 2:3], dd[:, 1:2])
                        P.tt('vector', dd[:, 3:4], dd[:, 2:3], at, ALU.mult)
                        coef = dd[:, 3:4]
                    else:
                        coef = at
                    if first_dir[c] == d_:
                        P.act(Oacc[:, c, h, :], ops_[:, h, 0:64], AF.Identity, scale=coef)
                    else:
                        P.stt('vector', Oacc[:, c, h, :], ops_[:, h, 0:64], coef, Oacc[:, c, h, :], ALU.mult, ALU.add)
                kvp = pb(2 if d_ == 0 else 5, [130])
                if step == 0:
                    P.copy('vector', S, kvp)
                else:
                    P.stt('vector', S, S, decp[:, c, d_:d_ + 1], kvp, ALU.mult, ALU.add)

            for d_ in range(2):
                scan_front(0, d_)
                scan_kv(0, d_)
            for step in range(NCH):
                if step + 1 < NCH:
                    for d_ in range(2):
                        scan_front(step + 1, d_)
                for d_ in range(2):
                    scan_back(step, d_)
                if step + 1 < NCH:
                    for d_ in range(2):
                        scan_kv(step + 1, d_)
                if ps_ == 0 and step < 3:
                    ck('E2' + 'abc'[step], [Oacc.rearrange('p a b c -> p (a b c)')[:, 0:256], S32[0], S32[1]])
            if ps_ == 0:
                ck('E2', [Oacc.rearrange('p a b c -> p (a b c)')[:, 0:1024], Oacc.rearrange('p a b c -> p (a b c)')[:, 1024:2048]])
            if ps_ == 4:
                ck('F2', [Oacc.rearrange('p a b c -> p (a b c)')[:, 0:1024], Oacc.rearrange('p a b c -> p (a b c)')[:, 1024:2048]])
            if ps_ in (1, 5):
                mark(f'p{ps_}_scan')
            RM.release(mScan)
            sq = RM.alloc([NCH, 2, 64], F32)
            ms = RM.alloc([NCH, 2], F32)
            On = RM.alloc([NCH, 2, 64], BF16)
            P.tt('gpsimd', sq, Oacc, Oacc, ALU.mult)
            P.reduce('vector', ms, sq, ALU.add)
            P.act(ms, ms, AF.Sqrt, bias=epsb, scale=1.0 / 64)
            P.recip(ms, ms)
            P.tt('vector', On, Oacc, bc(ms.unsqueeze(3), [128, NCH, 2, 64]), ALU.mult)
            nwi = (36 if is_ml else 32) + j
            for c4 in range(0, NCH, 4):
                tpo = pb(4, [4, 128], BF16, off=(c4 // 4 % 2) * 1024)
                for ci in range(4):
                    P.transpose(tpo[:, ci, :], On[:, c4 + ci].rearrange("p h d -> p (h d)"), identB)
                P.stt('vector', MT[:, ps_, c4 * 128:(c4 + 4) * 128], tpo.rearrange("p c t -> p (c t)"), colB[:, nwi:nwi + 1],
                      gT[:, c4 * 128:(c4 + 4) * 128], ALU.mult, ALU.mult)
        ck('E4', [MT[:, 0, 0:1024], MT[:, 7, 0:1024]])
        RM.release(mMF2)
        RX.release(0)
        mark('passes')
        X1 = RX.alloc([NCH, D], F32)
        Wo = RM.alloc([8, D], BF16)
        P.dma('gpsimd', Wo, ab_w_out[0].rearrange("(kc p) n -> p kc n", p=128))
        otmp = [RM.alloc([512], F32) for _ in range(2)]
        for c in range(NCH):
            P.dma('sync', X1[:, c, :], xw[c])
        for c in range(NCH):
            gi = 2 if c < 2 else 0
            for hf in range(2):
                po = pb(1 + hf, [512])
                for kc in range(8):
                    P.mm(po, MT[:, kc, c * 128:(c + 1) * 128], Wo[:, kc, hf * 512:(hf + 1) * 512], start=(kc == 0), stop=(kc == 7))
                P.tt('vector', otmp[hf], po, gB[:, gi, hf * 512:(hf + 1) * 512], ALU.mult)
                P.tt('gpsimd', X1[:, c, hf * 512:(hf + 1) * 512], X1[:, c, hf * 512:(hf + 1) * 512], otmp[hf], ALU.add)
        RM.release(mMF)
        if dbg == 'l0m':
            for c in range(NCH):
                P.dma('sync', dbg_out[c], X1[:, c, :], store=True)

        mark('outproj0')
        def ffn(l, chunks, wm_sel, g_sel):
            m0 = RM.mark()
            w_in = ffn_w_in[l].rearrange("(kc p) n -> p kc n", p=128)
            w_out = ffn_w_out[l].rearrange("(f p) n -> p f n", p=128)
            Wout = RM.alloc([NFT, D], BF16)
            P.dma('gpsimd', Wout[:, 0:11, :], w_out[:, 0:11, :])
            P.dma('gpsimd', Wout[:, 11:22, :], w_out[:, 11:22, :])
            AT = RM.alloc([NFT, 512], BF16)
            wbuf = [RM.alloc([8, 2, 256], BF16) for _ in range(2)]
            m_ph = RM.mark()
            h2 = RM.alloc([8, 512], BF16)
            RM.release(m_ph)
            ot = [RM.alloc([512], F32) for _ in range(2)]
            RM.alloc([8 * 512 - 2 * 1024], BF16)
            bi = 0
            for q0 in range(0, len(chunks), 4):
                cq = chunks[q0:q0 + 4]
                for i, c in enumerate(cq):
                    v_ = wm_sel(c)
                    make_hT(X1[:, c, :], h2[:, :, i * 128:(i + 1) * 128], WM[:, l, 1, v_, :], SHc[:, l, 1, v_, :])
                for fp in range(11):
                    wb = wbuf[bi % 2]
                    bi += 1
                    P.dma('gpsimd', wb[:, :, 0, :], w_in[:, :, fp * 256:(fp + 1) * 256])
                    P.dma('gpsimd', wb[:, :, 1, :], w_in[:, :, D_FF + fp * 256:D_FF + (fp + 1) * 256])
                    for f2 in range(2):
                        f = fp * 2 + f2
                        pgm = pb(1 + 2 * (f % 2), [512])
                        pum = pb(2 + 2 * (f % 2), [512])
                        for kc in range(8):
                            P.mm(pgm, wb[:, kc, 0, f2 * 128:(f2 + 1) * 128], h2[:, kc, :], start=(kc == 0), stop=(kc == 7))
                        for kc in range(8):
                            P.mm(pum, wb[:, kc, 1, f2 * 128:(f2 + 1) * 128], h2[:, kc, :], start=(kc == 0), stop=(kc == 7))
                        P.act(AT[:, f, :], pgm, AF.Silu)
                        P.tt('vector', AT[:, f, :], AT[:, f, :], pum, ALU.mult)
                k_ = 0
                for i, c in enumerate(cq):
                    for hf in range(2):
                        po = pb(5 + (k_ % 3), [512])
                        o_ = ot[k_ % 2]
                        k_ += 1
                        for f in range(NFT):
                            P.mm(po, AT[:, f, i * 128:(i + 1) * 128], Wout[:, f, hf * 512:(hf + 1) * 512], start=(f == 0), stop=(f == NFT - 1))
                        P.tt('vector', o_, po, gB[:, g_sel(c), hf * 512:(hf + 1) * 512], ALU.mult)
                        P.tt('gpsimd', X1[:, c, hf * 512:(hf + 1) * 512], X1[:, c, hf * 512:(hf + 1) * 512], o_, ALU.add)
            RM.release(m0)

        if dbg != 'l0m':
            ffn(0, list(range(NCH)), lambda c: 1 if c < 2 else 0, lambda c: 3 if c < 2 else 1)
        if dbg == 'l0':
            for c in range(NCH):
                P.dma('sync', dbg_out[c], X1[:, c, :], store=True)

        mark('ffn0')
        if dbg in (None, 'l1m'):
            modulation(1)
            make_wm(1)
            m1 = RM.mark()
            awin = at_w_in[0].rearrange("(kc p) n -> p kc n", p=128)
            kT1 = RM.alloc([2, NCH * 128], BF16)
            Vx1 = RM.alloc([NCH, 4, 65], BF16)
            P.memset('vector', Vx1[:, :, :, 64:65], 1.0)
            qnb = RM.alloc([64], F32)
            knb = RM.alloc([64], F32)
            P.dma('sync', qnb, bc(at_qn, [128, 64]))
            P.dma('sync', knb, bc(at_kn, [128, 64]))
            snk = RM.alloc([16], F32)
            P.dma('sync', snk, bc(at_sink, [128, 16]))
            rT = RM.alloc([2, NW, 32], F32)
            P.dma('sync', rT.rearrange("p a s f -> p a (s f)"), ropeT.rearrange("a p n -> p a n"))
            amb = RM.alloc([4, 128], BF16)
            sm = RM.alloc([8], F32)
            sinkexp = RM.alloc([16], F32)
            hTc = [RM.alloc([8, 128], BF16) for _ in range(2)]
            nms = RM.alloc([8], F32)
            qn = RM.alloc([8, 64], F32)
            nsq = qn
            qr = RM.alloc([4, 8, 32], F32)
            qtk = RM.alloc([16, 64], BF16)
            qtk2 = RM.alloc([16, 64], BF16)
            m2 = RM.mark()
            amf = RM.alloc([4, 128], F32)
            absq = RM.alloc([128], F32)
            P.dma('sync', amf, amask.rearrange("c p n -> p c n"))
            P.copy('vector', amb, amf)
            P.act(absq[:, 0:64], qnb, AF.Abs)
            P.act(absq[:, 64:128], knb, AF.Abs)
            P.reduce('vector', sm[:, 0:1], absq[:, 0:64], ALU.max)
            P.reduce('vector', sm[:, 1:2], absq[:, 64:128], ALU.max)
            P.tt('vector', sm[:, 2:3], sm[:, 0:1], sm[:, 1:2], ALU.mult)
            P.ts('vector', sm[:, 3:4], sm[:, 2:3], -8.0, None, ALU.mult)
            negB = sm[:, 3:4]
            P.act(sinkexp, snk, AF.Exp, bias=negB)
            RM.release(m2)
            Wkv1 = RM.alloc([8, 512], BF16)
            P.dma('gpsimd', Wkv1, awin[:, :, 1024:1536])

            def qknorm_rope(ps3, nh, wb_, wch, dst):
                P.act(nsq[:, 0:nh, :], ps3, AF.Square)
                P.reduce('vector', nms[:, 0:nh], nsq[:, 0:nh, :], ALU.add)
                P.act(nms[:, 0:nh], nms[:, 0:nh], AF.Sqrt, bias=epsb, scale=1.0 / 64)
                P.recip(nms[:, 0:nh], nms[:, 0:nh])
                P.tt('vector', qn[:, 0:nh, :], ps3, bc(nms[:, 0:nh].unsqueeze(2), [128, nh, 64]), ALU.mult)
                if wch < 0:
                    P.tt('gpsimd', dst, qn[:, 0:nh, :], bc(wb_.unsqueeze(1), [128, nh, 64]), ALU.mult)
                    return
                P.tt('gpsimd', qn[:, 0:nh, :], qn[:, 0:nh, :], bc(wb_.unsqueeze(1), [128, nh, 64]), ALU.mult)
                x1, x2 = qn[:, 0:nh, 0:32], qn[:, 0:nh, 32:64]
                cs = bc(rT[:, 0, wch, :].unsqueeze(1), [128, nh, 32])
                sn = bc(rT[:, 1, wch, :].unsqueeze(1), [128, nh, 32])
                P.tt('vector', qr[:, 0, 0:nh], x1, cs, ALU.mult)
                P.tt('gpsimd', qr[:, 1, 0:nh], x2, sn, ALU.mult)
                P.tt('vector', qr[:, 2, 0:nh], x2, cs, ALU.mult)
                P.tt('gpsimd', qr[:, 3, 0:nh], x1, sn, ALU.mult)
                P.tt('vector', dst[:, :, 0:32], qr[:, 0, 0:nh], qr[:, 1, 0:nh], ALU.subtract)
                P.tt('gpsimd', dst[:, :, 32:64], qr[:, 2, 0:nh], qr[:, 3, 0:nh], ALU.add)

            for c in range(NCH):
                v_ = 1 if c < 2 else 0
                hc_ = hTc[c % 2]
                make_hT(X1[:, c, :], hc_, WM[:, 1, 0, v_, :], SHc[:, 1, 0, v_, :])
                pkv = pb(1 + (c % 2), [512])
                for kc in range(8):
                    P.mm(pkv, hc_[:, kc, :], Wkv1[:, kc, :], start=(kc == 0), stop=(kc == 7))
                P.copy('scalar', Vx1[:, c, :, 0:64], pkv[:, 256:512].rearrange("p (h d) -> p h d", h=4))
                qknorm_rope(pkv[:, 0:256].rearrange("p (h d) -> p h d", h=4), 4, knb, (c - 2) if c >= 2 else -1, qtk[:, 0:4, :])
                tpk = pb(4, [2, 128], BF16)
                for t_ in range(2):
                    P.transpose(tpk[:, t_, :], qtk[:, 2 * t_:2 * t_ + 2, :].rearrange("p h d -> p (h d)"), identB)
                P.copy('scalar', kT1[:, :, c * 128:(c + 1) * 128], tpk)
            mark('l1kv')
            RM.release(m2)
            Wq1 = RM.alloc([8, 1024], BF16)
            P.dma('gpsimd', Wq1, awin[:, :, 0:1024])
            Wo1 = RM.alloc([8, D], BF16)
            P.dma('gpsimd', Wo1, at_w_out[0].rearrange("(kc p) n -> p kc n", p=128))
            qTc = RM.alloc([2, 4, 128], BF16)
            Eb = [RM.alloc([5, 4, 128], BF16) for _ in range(2)]
            Otk = RM.alloc([16, 64], BF16)
            OT = RM.alloc([8, 128], BF16)
            dn1 = RM.alloc([8], F32)
            ot1 = [RX.alloc([512], F32), RM.alloc([512], F32)]
            ei = 0
            for i in range(16):
                c = i + 3
                hc_ = hTc[i % 2]
                make_hT(X1[:, c, :], hc_, WM[:, 1, 0, 0, :], SHc[:, 1, 0, 0, :])
                for hf in range(2):
                    pq = pb(2 + hf, [512])
                    for kc in range(8):
                        P.mm(pq, hc_[:, kc, :], Wq1[:, kc, hf * 512:(hf + 1) * 512], start=(kc == 0), stop=(kc == 7))
                    qknorm_rope(pq.rearrange("p (h d) -> p h d", h=8), 8, qnb, c - 2, qtk[:, 8 * hf:8 * hf + 8, :])
                tpq = pb(4, [2, 4, 128], BF16)
                for tp_ in range(2):
                    P.copy('gpsimd', qtk2[:, 8 * tp_:8 * tp_ + 8, :].rearrange("p (j g) d -> p g j d", g=2),
                           qtk[:, 8 * tp_:8 * tp_ + 8, :].rearrange("p (g j) d -> p g j d", g=2))
                    for j in range(4):
                        src = qtk2[:, 8 * tp_ + 2 * j:8 * tp_ + 2 * j + 2, :]
                        P.transpose(tpq[:, tp_, j, :], src.rearrange("p h d -> p (h d)"), identB)
                P.copy('scalar', qTc, tpq)
                kblocks = [(c - 1, 0 if i == 0 else 1), (c, None), (c + 1, 3 if i == 15 else 2), (0, None), (1, None)]
                for g in range(4):
                    tp_, hb = g // 2, slice(64 * (g % 2), 64 * (g % 2) + 64)
                    E = Eb[ei % 2]
                    ei += 1
                    for bi_, (kc_, mk) in enumerate(kblocks):
                        pst = pb(5 + (bi_ % 2), [4, 128])
                        P.mm(pst.rearrange("p j t -> p (j t)"), kT1[hb, tp_, kc_ * 128:(kc_ + 1) * 128],
                             qTc[hb, tp_].rearrange("p j t -> p (j t)"))
                        P.act(E[:, bi_], pst, AF.Exp, bias=negB, scale=0.125)
                        if mk is not None:
                            P.tt('gpsimd', E[:, bi_], E[:, bi_], bc(amb[:, mk, :].unsqueeze(1), [128, 4, 128]), ALU.mult)
                    pso = pb(7 if g % 2 == 0 else 3, [4, 65])
                    for j in range(4):
                        for bi_, (kc_, mk) in enumerate(kblocks):
                            P.mm(pso[:, j, :], E[:, bi_, j, :], Vx1[:, kc_, g, :], start=(bi_ == 0), stop=(bi_ == 4))
                    P.tt('vector', dn1[:, 0:4], pso[:, :, 64], sinkexp[:, 4 * g:4 * g + 4], ALU.add)
                    P.recip(dn1[:, 4:8], dn1[:, 0:4])
                    P.tt('vector', Otk[:, 4 * g:4 * g + 4, :], pso[:, :, 0:64], bc(dn1[:, 4:8].unsqueeze(2), [128, 4, 64]), ALU.mult)
                tpo = pb(4, [8, 128], BF16)
                for kc in range(8):
                    P.transpose(tpo[:, kc, :], Otk[:, 2 * kc:2 * kc + 2, :].rearrange("p h d -> p (h d)"), identB)
                P.copy('scalar', OT, tpo)
                for hf in range(2):
                    po = pb(1 + hf, [512])
                    for kc in range(8):
                        P.mm(po, OT[:, kc, :], Wo1[:, kc, hf * 512:(hf + 1) * 512], start=(kc == 0), stop=(kc == 7))
                    P.tt('vector', ot1[hf], po, gB[:, 0, hf * 512:(hf + 1) * 512], ALU.mult)
                    P.tt('gpsimd', X1[:, c, hf * 512:(hf + 1) * 512], X1[:, c, hf * 512:(hf + 1) * 512], ot1[hf], ALU.add)
            mark('l1attn')
            RM.release(m1)
            if dbg == 'l1m':
                for c in range(NCH):
                    P.dma('sync', dbg_out[c], X1[:, c, :], store=True)
            else:
                ffn(1, list(range(3, 19)), lambda c: 0, lambda c: 1)
        for i in range(16):
            P.dma('sync', y[i * 128:(i + 1) * 128, :], X1[:, i + 3, :], store=True)
        mark('end')
        stats = P.emit()
        stats['marks'] = P.marks
        stats['peaks'] = (RP.peak, RX.peak, RM.peak)
    return nc, stats


def _rope_angles(pos):
    pos = np.asarray(pos)
    row = (pos // 64).astype(np.float32)
    col = (pos % 64).astype(np.float32)
    inv = (10000.0 ** (-np.arange(16, dtype=np.float32) / 16)).astype(np.float32)
    return np.concatenate([row[:, None] * inv, col[:, None] * inv], axis=-1).astype(np.float32)


def _core_inputs(inp, core):
    b, q = core // 4, core % 4
    x = np.asarray(inp['x'], np.float32)
    ctx = np.asarray(inp['ctx'], np.float32)
    xb = x[b].reshape(64, 128, D)
    fl = np.zeros((1, NF), np.float32)
    xw = np.zeros((NCH, 128, D), np.float32)
    xw[0:2] = ctx[b].reshape(2, 128, D)
    fl[0, FL_VF:FL_VF + 2] = 1.0
    wpos = np.full((NW, 128), -1, np.int64)
    for w in range(NW):
        ch = 16 * q - 1 + w
        if 0 <= ch < 64:
            xw[2 + w] = xb[ch]
            fl[0, FL_VF + 2 + w] = 1.0
            wpos[w] = ch * 128 + np.arange(128)
    xe = np.zeros((128, D), np.float32)
    tl = 128 * (16 * q - 1) - 1
    tr = 128 * (16 * q + 17)
    if 0 <= tl < SEQ:
        xe[0] = x[b, tl]
        fl[0, FL_EDGE] = 1.0
    if 0 <= tr < SEQ:
        xe[1] = x[b, tr]
        fl[0, FL_EDGE + 1] = 1.0
    slots = [(ch, 0) for ch in range(16 * q - 2, -1, -1)] + [(ch, 1) for ch in range(16 * q + 17, 64)]
    assert len(slots) <= NSLOT
    xs = np.zeros((NSLOT, 128, D), np.float32)
    xsh = np.zeros((128, D), np.float32)
    spos = np.zeros((NSLOT, 128), np.int64)
    for s, (ch, d_) in enumerate(slots):
        xs[s] = xb[ch]
        fl[0, (FL_FF if d_ == 0 else FL_FB) + s] = 1.0
        spos[s] = ch * 128 + np.arange(128)
        t0, t1 = ch * 128 - 1, ch * 128 + 128
        if t0 >= 0:
            xsh[2 * s] = x[b, t0]
            fl[0, FL_SH + 2 * s] = 1.0
        if t1 < SEQ:
            xsh[2 * s + 1] = x[b, t1]
            fl[0, FL_SH + 2 * s + 1] = 1.0
    wp = np.where(wpos < 0, 0, wpos).reshape(-1)
    ang = _rope_angles(wp)
    cosw, sinw = np.cos(ang), np.sin(ang)
    ropeT = np.stack([cosw.reshape(NW, 128, 32).transpose(1, 0, 2).reshape(128, NW * 32),
                      sinw.reshape(NW, 128, 32).transpose(1, 0, 2).reshape(128, NW * 32)]).astype(np.float32)
    p = np.arange(128)
    fi = p % 32
    cosF = cosw[:, fi].T
    sgn = np.where((p % 64) < 32, 1.0, -1.0)[:, None]
    sinF = sinw[:, fi].T * sgn
    ropeF = np.stack([cosF, sinF]).astype(np.float32)
    angs = _rope_angles(spos.reshape(-1))
    ropeS = np.stack([np.cos(angs).reshape(NSLOT, 128, 32).transpose(1, 0, 2).reshape(128, NSLOT * 32),
                      np.sin(angs).reshape(NSLOT, 128, 32).transpose(1, 0, 2).reshape(128, NSLOT * 32)]).astype(np.float32)
    u = np.arange(128)[:, None]
    s_ = np.arange(128)[None, :]
    consts = np.zeros((7, 128, 128), np.float32)
    consts[0] = np.eye(128)
    consts[1] = (u > s_)
    consts[2] = (u < s_)
    consts[3] = (u <= s_)
    consts[4] = (u >= s_)
    consts[5] = 1.0
    amask = np.zeros((4, 128, 128), np.float32)
    mL = (u >= s_).astype(np.float32)
    mR = (u <= s_).astype(np.float32)
    amask[0] = mL * (1.0 if q > 0 else 0.0)
    amask[1] = mL
    amask[2] = mR
    amask[3] = mR * (1.0 if q < 3 else 0.0)
    cvec = np.concatenate([np.asarray(inp['c'], np.float32)[b].reshape(8, 128),
                           np.asarray(inp['c_ctx'], np.float32).reshape(8, 128)], 0)
    m = dict(xw=xw, xe=xe, xs=xs, xsh=xsh, fl=fl, cvec=cvec, consts=consts, amask=amask,
             ropeF=ropeF, ropeT=ropeT, ropeS=ropeS)
    for k in ('ada_w', 'ada_b', 'norm_w', 'ffn_w_in', 'ffn_w_out', 'ab_w_in', 'ab_w_out', 'ret_log_gamma',
              'ret_norm_w', 'mlstm_conv_w', 'mlstm_conv_b', 'mlstm_gate_b', 'mlstm_norm_w', 'attn_w_in',
              'attn_w_out', 'attn_q_norm_w', 'attn_k_norm_w', 'attn_sink'):
        m[k] = np.ascontiguousarray(np.asarray(inp[k], np.float32))
    return m


_NC_CACHE = {}


def kernel(**inp):
    if 'nc' not in _NC_CACHE:
        _NC_CACHE['nc'] = build()[0]
    nc = _NC_CACHE['nc']
    in_maps = [_core_inputs(inp, c) for c in range(NCORE)]
    res = run_bass_kernel_spmd(nc, in_maps, core_ids=list(range(NCORE)))
    out = np.zeros((2, SEQ, D), np.float32)
    for c in range(NCORE):
        b, q = c // 4, c % 4
        out[b, 2048 * q:2048 * (q + 1)] = res.results[c]["y"]
    return out
```

```python
import contextlib
import math
import numpy as np
import concourse.bass as bass
import concourse.mybir as mybir
from concourse.bass_utils import run_bass_kernel_spmd

F32 = mybir.dt.float32
BF16 = mybir.dt.bfloat16
ALU = mybir.AluOpType
AF = mybir.ActivationFunctionType
AX = mybir.AxisListType

_DTSZ = {F32: 4, BF16: 2}
SEM_LIMIT = 12000
DMA_POOL = 12


def _region(ap):
    sp = str(ap.space).upper()
    if 'SB' not in sp and 'PSUM' not in sp:
        return None
    if 'PSUM' in sp:
        return (ap.name, 0, 128, 0, 2048)
    pat = ap.ap
    esz = _DTSZ[ap.dtype]
    pstep, pcount = pat[0]
    off = ap.offset
    if pstep == 0:
        p0 = 0
        f0 = off
    else:
        p0 = off // pstep
        f0 = off - p0 * pstep
    ext = 0
    for stp, cn in pat[1:]:
        ext += abs(stp) * (cn - 1)
    return (ap.name, p0, p0 + pcount, f0 * esz, (f0 + ext + 1) * esz)


class Prog:
    ENGS = ('tensor', 'vector', 'scalar', 'gpsimd', 'sync')

    def __init__(self, nc, same_engine_sync=True):
        self.nc = nc
        self.ops = []
        self.track = {}
        self.same_engine_sync = same_engine_sync
        self.dma_hist = {e: [] for e in self.ENGS}
        self.store_ops = []

    def _add(self, eng, fn, outs, ins, is_dma=False, extra_deps=(), force=False):
        if getattr(self, 'frozen', False) and not force:
            return -1
        idx = len(self.ops)
        deps = set(extra_deps)
        self.ops.append(dict(eng=eng, fn=fn, deps=deps, is_dma=is_dma, signaled=False))
        for ap in ins:
            r = _region(ap)
            if r is not None:
                self._access(idx, eng, r, False, deps)
        for ap in outs:
            r = _region(ap)
            if r is not None:
                self._access(idx, eng, r, True, deps)
        if is_dma:
            h = self.dma_hist[eng]
            if len(h) >= DMA_POOL:
                deps.add(h[-DMA_POOL])
            h.append(idx)
        deps.discard(idx)
        return idx

    def _access(self, idx, eng, r, is_write, deps):
        name, p0, p1, b0, b1 = r
        recs = self.track.get(name, [])
        keep = []
        for rec in recs:
            (q0, q1, c0, c1, oi, ow, oe) = rec
            if q1 <= p0 or p1 <= q0 or c1 <= b0 or b1 <= c0 or oi == idx:
                keep.append(rec)
                continue
            if is_write or ow or (name.startswith('pb') and oe != eng):
                pe_pe = (eng == 'tensor' and oe == 'tensor')
                if not pe_pe:
                    deps.add(oi)
            covered = (p0 <= q0 and q1 <= p1 and b0 <= c0 and c1 <= b1)
            if is_write and covered and not (eng == 'tensor' and oe == 'tensor' and not ow):
                continue
            if (not is_write) and (not ow) and oe == eng and covered and not self.ops[oi]['is_dma']:
                continue
            keep.append(rec)
        keep.append((p0, p1, b0, b1, idx, is_write, eng))
        self.track[name] = keep

    def op(self, eng, fn, outs, ins):
        return self._add(eng, fn, outs, ins)

    def dma(self, eng, out, in_, store=False, **kw):
        i = self._add(eng, lambda e: e.dma_start(out=out, in_=in_, **kw), [out], [in_], is_dma=True)
        if store and i >= 0:
            self.store_ops.append(i)
        return i

    def mm(self, out, lhsT, rhs, start=True, stop=True, after=None):
        i = self.op('tensor', lambda e: e.matmul(out, lhsT, rhs, start=start, stop=stop), [out], [lhsT, rhs])
        if after is not None and i >= 0 and after >= 0:
            self.ops[i]['deps'].add(after)
        return i

    def transpose(self, out, in_, ident):
        return self.op('tensor', lambda e: e.transpose(out, in_, ident), [out], [in_, ident])

    def act(self, out, in_, func, bias=None, scale=None, accum_out=None):
        kw = {}
        ins = [in_]
        outs = [out]
        if bias is not None:
            kw['bias'] = bias
            if not isinstance(bias, (int, float)):
                ins.append(bias)
        if scale is not None:
            kw['scale'] = scale
            if not isinstance(scale, (int, float)):
                ins.append(scale)
        if accum_out is not None:
            kw['accum_out'] = accum_out
            outs.append(accum_out)
        return self.op('scalar', lambda e: e.activation(out, in_, func, **kw), outs, ins)

    def tt(self, eng, out, in0, in1, op):
        return self.op(eng, lambda e: e.tensor_tensor(out, in0, in1, op), [out], [in0, in1])

    def ts(self, eng, out, in0, s1, s2, op0, op1=None):
        ins = [in0] + [s for s in (s1, s2) if s is not None and not isinstance(s, (int, float))]
        kw = {}
        if op1 is not None:
            kw['op1'] = op1
        return self.op(eng, lambda e: e.tensor_scalar(out, in0, s1, s2, op0, **kw), [out], ins)

    def stt(self, eng, out, in0, scalar, in1, op0, op1):
        ins = [in0, in1] + ([scalar] if not isinstance(scalar, (int, float)) else [])
        return self.op(eng, lambda e: e.scalar_tensor_tensor(out, in0, scalar, in1, op0, op1), [out], ins)

    def copy(self, eng, out, in_):
        if eng == 'scalar':
            return self.op(eng, lambda e: e.copy(out, in_), [out], [in_])
        return self.op(eng, lambda e: e.tensor_copy(out, in_), [out], [in_])

    def memset(self, eng, out, val):
        return self.op(eng, lambda e: e.memset(out, val), [out], [])

    def recip(self, out, in_):
        return self.op('vector', lambda e: e.reciprocal(out, in_), [out], [in_])

    def reduce(self, eng, out, in_, op, axis=AX.X):
        return self.op(eng, lambda e: e.tensor_reduce(out, in_, axis, op), [out], [in_])

    def emit(self):
        nc = self.nc
        ops = self.ops
        self._add('sync', None, [], [], extra_deps=self.store_ops, force=True)
        for o in ops:
            if o['is_dma']:
                o['signaled'] = True
            for d in o['deps']:
                ops[d]['signaled'] = True
        cnt = {e: 0 for e in self.ENGS}
        dcnt = {e: 0 for e in self.ENGS}
        nsem_eng = {e: 0 for e in self.ENGS}
        for o in ops:
            e = o['eng']
            if not o['signaled']:
                continue
            if o['is_dma']:
                k = dcnt[e]
                dcnt[e] += 1
                o['sem'] = ('d', e, k % DMA_POOL)
                o['val'] = 16 * (k // DMA_POOL + 1)
                o['sidx'] = None
            else:
                k = cnt[e]
                cnt[e] += 1
                o['sem'] = ('c', e, k // SEM_LIMIT)
                o['val'] = (k % SEM_LIMIT) + 1
                o['sidx'] = k
                nsem_eng[e] = k // SEM_LIMIT + 1
        sems = {}
        st = contextlib.ExitStack()
        for e in self.ENGS:
            for j in range(nsem_eng[e]):
                sems[('c', e, j)] = st.enter_context(nc.semaphore(f"c_{e}_{j}"))
            for j in range(min(DMA_POOL, dcnt[e])):
                sems[('d', e, j)] = st.enter_context(nc.semaphore(f"d_{e}_{j}"))
        seen = {e: {f: -1 for f in self.ENGS} for e in self.ENGS}
        seen_dma = {e: set() for e in self.ENGS}
        per_eng = {e: [] for e in self.ENGS}
        nwaits = 0
        for o in ops:
            e = o['eng']
            waits = {}
            for d in sorted(o['deps']):
                p = ops[d]
                if p['is_dma']:
                    if d in seen_dma[e]:
                        continue
                    seen_dma[e].add(d)
                    waits[p['sem']] = max(waits.get(p['sem'], 0), p['val'])
                else:
                    f = p['eng']
                    if f == e and not self.same_engine_sync:
                        continue
                    if p['sidx'] <= seen[e][f]:
                        continue
                    seen[e][f] = p['sidx']
                    waits[p['sem']] = max(waits.get(p['sem'], 0), p['val'])
            nwaits += len(waits)
            per_eng[e].append((o, list(waits.items())))
        self.stats = dict(n_ops=len(ops), n_waits=nwaits, per_eng={e: len(v) for e, v in per_eng.items()})
        with st, nc.Block() as block:
            def body(engname):
                def run(eng):
                    for o, waits in per_eng[engname]:
                        for key, val in waits:
                            eng.wait_ge(sems[key], val)
                        if o['fn'] is None:
                            continue
                        ins = o['fn'](eng)
                        if o['signaled']:
                            ins.then_inc(sems[o['sem']], 16 if o['is_dma'] else 1)
                return run
            block.tensor(body('tensor'))
            block.vector(body('vector'))
            block.scalar(body('scalar'))
            block.gpsimd(body('gpsimd'))
            block.sync(body('sync'))
        return self.stats


D = 1024
SEQ = 8192
NCORE = 8
NW = 18
NCH = 20
NSLOT = 48
NF = 256
EPS = 1e-6
D_FF = 2816
NFT = 22
LNK = math.log(0.125)
HT_N = 2564
CTX0 = 1
WIN0 = 259
FL_VF = 0
FL_EDGE = 20
FL_FF = 22
FL_FB = 70
FL_SH = 118


def bc(ap, shape):
    return ap.broadcast_to(shape)


class Region:
    def __init__(self, t, base, cap, name):
        self.t, self.base, self.cap, self.name = t, base, cap, name
        self.off = 0
        self.peak = 0

    def alloc(self, shape, dt):
        n = 1
        for s in shape:
            n *= s
        nb = (n * _DTSZ[dt] + 63) // 64 * 64
        if self.off + nb > self.cap:
            raise RuntimeError(f"region {self.name} overflow: {self.off + nb} > {self.cap}")
        a = self.base + self.off
        v = self.t[:, a // 4:(a + nb) // 4]
        if dt != F32:
            v = v.bitcast(dt)
        v = v[:, 0:n]
        self.off += nb
        self.peak = max(self.peak, self.off)
        if len(shape) == 1:
            return v
        names = [chr(ord('a') + i) for i in range(len(shape))]
        pat = "p (" + " ".join(names) + ") -> p " + " ".join(names)
        return v.rearrange(pat, **{nm: s for nm, s in zip(names, shape)})

    def mark(self):
        return self.off

    def release(self, m):
        self.off = m


class _Stop(Exception):
    pass


ARENA_BYTES = 212480
P_BYTES = 35328
X_BYTES = 83968


def build(dbg=None):
    nc = bass.Bass("TRN2", target_bir_lowering=False)

    def din(name, shape):
        return nc.dram_tensor(name, list(shape), F32, kind="ExternalInput").ap()

    xw = din("xw", [NCH, 128, D])
    xe = din("xe", [128, D])
    xs = din("xs", [NSLOT, 128, D])
    xsh = din("xsh", [128, D])
    fl = din("fl", [1, NF])
    cvec = din("cvec", [16, 128])
    ada_w = din("ada_w", [2, D, 6 * D])
    ada_b = din("ada_b", [2, 6 * D])
    norm_w = din("norm_w", [2, 2, D])
    ffn_w_in = din("ffn_w_in", [2, D, 2 * D_FF])
    ffn_w_out = din("ffn_w_out", [2, D_FF, D])
    ab_w_in = din("ab_w_in", [1, D, 4128])
    ab_w_out = din("ab_w_out", [1, D, D])
    ret_lg = din("ret_log_gamma", [1, 2, 8])
    ret_nw = din("ret_norm_w", [1, 512])
    conv_w = din("mlstm_conv_w", [1, 3, D])
    conv_b = din("mlstm_conv_b", [1, D])
    gate_b = din("mlstm_gate_b", [1, 4, 8])
    ml_nw = din("mlstm_norm_w", [1, 512])
    at_w_in = din("attn_w_in", [1, D, 1536])
    at_w_out = din("attn_w_out", [1, D, D])
    at_qn = din("attn_q_norm_w", [1, 64])
    at_kn = din("attn_k_norm_w", [1, 64])
    at_sink = din("attn_sink", [1, 16])
    consts = din("consts", [7, 128, 128])
    amask = din("amask", [4, 128, 128])
    ropeF = din("ropeF", [2, 128, 2304])
    ropeT = din("ropeT", [2, 128, NW * 32])
    ropeS = din("ropeS", [2, 128, NSLOT * 32])
    y = nc.dram_tensor("y", [2048, D], F32, kind="ExternalOutput").ap()
    dbg_out = None
    if dbg is not None:
        dbg_out = nc.dram_tensor("dbg", [NCH, 128, D], F32, kind="ExternalOutput").ap()

    st = contextlib.ExitStack()
    with st:
        arena_t = st.enter_context(nc.sbuf_tensor("arena", [128, ARENA_BYTES // 4], F32))
        RP = Region(arena_t, 0, P_BYTES, "P")
        RX = Region(arena_t, P_BYTES, X_BYTES, "X")
        RM = Region(arena_t, P_BYTES + X_BYTES, ARENA_BYTES - P_BYTES - X_BYTES, "MF")
        banks = [st.enter_context(nc.psum_tensor(f"pb{i}", [128, 512], F32)) for i in range(8)]
        P = Prog(nc)
        P.marks = []

        def mark(nm):
            P.marks.append((nm, sum(1 for o in P.ops if o['eng'] == 'tensor')))

        def ck(name, aps):
            if dbg != name:
                return
            k = 0
            for ap in aps:
                n = ap.shape[1] if len(ap.shape) == 2 else None
                flat = ap
                npart = ap.shape[0]
                P.dma('sync' if flat.dtype == F32 else 'gpsimd', dbg_out[k][0:npart, 0:flat.shape[1]], flat, store=True)
                k += 1
            P.frozen = True

        def pb(i, shape, dt=F32, off=0):
            n = 1
            for s in shape:
                n *= s
            nb = n * _DTSZ[dt]
            v = banks[i][:, off // 4:(off + nb + 3) // 4]
            if dt != F32:
                v = v.bitcast(dt)
            v = v[:, 0:n]
            if len(shape) == 1:
                return v
            names = [chr(ord('a') + k) for k in range(len(shape))]
            pat = "p (" + " ".join(names) + ") -> p " + " ".join(names)
            return v.rearrange(pat, **{nm: s for nm, s in zip(names, shape)})

        cst = RP.alloc([7, 128], F32)
        P.dma('sync', cst, consts.rearrange("c p n -> p c n"))
        identF = cst[:, 0, :]
        MfF, MbF = cst[:, 1, :], cst[:, 2, :]
        onesF = cst[:, 5, :]
        cstb = RP.alloc([7, 128], BF16)
        P.copy('vector', cstb, cst)
        identB = cstb[:, 0, :]
        maskF_b, maskB_b = cstb[:, 3, :], cstb[:, 4, :]
        FL = RP.alloc([NF], F32)
        P.dma('sync', FL, bc(fl, [128, NF]))
        epsb = RP.alloc([1], F32)
        P.memset('vector', epsb, EPS)
        lnk = RP.alloc([1], F32)
        P.memset('vector', lnk, LNK)
        colA = RP.alloc([112], F32)
        colB = RP.alloc([48], F32)
        svf = RP.alloc([16], F32)
        sv = RP.alloc([8, 2], BF16)
        modT = RP.alloc([2, 2, 48], F32)
        gB = RP.alloc([4, D], F32)
        WM = RP.alloc([2, 2, 2, 8], F32)
        SHc = RP.alloc([2, 2, 2, 8], F32)
        junk = RP.alloc([D], BF16)
        xnb = [RP.alloc([D], BF16) for _ in range(2)]
        t32 = RP.alloc([8, 128], F32)
        ssq = RP.alloc([4], F32)

        m0 = RM.mark()
        stg = RM.alloc([128], F32)
        stg2 = RM.alloc([128], F32)
        P.dma('sync', stg[0:16, :], cvec)
        P.dma('sync', stg[16:64, :], ada_b[0].rearrange("(j p) -> j p", p=128))
        P.dma('sync', stg[64:112, :], ada_b[1].rearrange("(j p) -> j p", p=128))
        tp = pb(0, [112])
        P.transpose(tp, stg[0:112, :], identF[0:112, 0:112])
        P.copy('vector', colA, tp)
        P.dma('sync', stg2[0:32, :], norm_w.rearrange("l i (j p) -> (l i j) p", p=128))
        P.dma('sync', stg2[32:36, :], ret_nw[0].rearrange("(j p) -> j p", p=128))
        P.dma('sync', stg2[36:40, :], ml_nw[0].rearrange("(j p) -> j p", p=128))
        P.dma('sync', stg2[40:48, :], conv_b[0].rearrange("(j p) -> j p", p=128))
        tp2 = pb(0, [48], off=1024)
        P.transpose(tp2, stg2[0:48, :], identF[0:48, 0:48])
        P.copy('vector', colB, tp2)
        RM.release(m0)
        ck('A0', [colA, colB, FL, cst[:, 1, :]])
        P.act(svf, colA[:, 0:16], AF.Silu)
        P.copy('vector', sv[:, :, 0], svf[:, 0:8])
        P.copy('vector', sv[:, :, 1], svf[:, 8:16])

        gslot = {(0, 0, 2): 0, (0, 0, 5): 1, (0, 1, 2): 2, (0, 1, 5): 3, (1, 0, 2): 0, (1, 0, 5): 1}

        def modulation(l):
            m0 = RM.mark()
            svrep = RM.alloc([2, 8, 128], BF16)
            for v_ in range(2):
                P.copy('vector', svrep[:, v_, :, :], bc(svf[:, 8 * v_:8 * v_ + 8].unsqueeze(2), [128, 8, 128]))
            wblk = [RM.alloc([8, 1024], BF16) for _ in range(2)]
            bbc = RM.alloc([1024], F32)
            for j in range(6):
                wb = wblk[j % 2]
                P.dma('gpsimd', wb, ada_w[l].rearrange("(kc p) n -> p kc n", p=128)[:, :, j * 1024:(j + 1) * 1024])
                ps = pb(1, [8, 2])
                for n in range(8):
                    for kc in range(8):
                        P.mm(ps[:, n, :], wb[:, kc, n * 128:(n + 1) * 128], sv[:, kc, :], start=(kc == 0), stop=(kc == 7))
                for v_ in range(2):
                    P.tt('vector', modT[:, l, v_, j * 8:(j + 1) * 8], ps[:, :, v_],
                         colA[:, 16 + 48 * l + j * 8:16 + 48 * l + j * 8 + 8], ALU.add)
                if j in (2, 5):
                    P.dma('sync', bbc, bc(ada_b[l:l + 1, j * 1024:(j + 1) * 1024], [128, 1024]))
                    for v_ in range(2):
                        if (l, v_, j) not in gslot:
                            continue
                        for hf in range(2):
                            pg = pb(2 + hf, [512])
                            for kc in range(8):
                                P.mm(pg, svrep[:, v_, kc, :], wb[:, kc, hf * 512:(hf + 1) * 512], start=(kc == 0), stop=(kc == 7))
                            P.tt('vector', gB[:, gslot[(l, v_, j)], hf * 512:(hf + 1) * 512], pg, bbc[:, hf * 512:(hf + 1) * 512], ALU.add)
            RM.release(m0)

        def make_wm(l):
            for i in range(2):
                for v_ in range(2):
                    sc = modT[:, l, v_, (3 * i + 1) * 8:(3 * i + 2) * 8]
                    nw = colB[:, (2 * l + i) * 8:(2 * l + i) * 8 + 8]
                    P.stt('vector', WM[:, l, i, v_, :], sc, 1.0, nw, ALU.add, ALU.mult)
                    P.copy('vector', SHc[:, l, i, v_, :], modT[:, l, v_, (3 * i) * 8:(3 * i) * 8 + 8])

        modulation(0)
        make_wm(0)
        mark('mod0')
        ck('A', [colA, colB, modT.rearrange('p a b c -> p (a b c)'), gB[:, 0, :], gB[:, 3, :], WM.rearrange('p a b c d -> p (a b c d)')])

        cnt_h = [0]

        def make_hT(x_sb, dest, wcol, shcol, flag=None):
            k = cnt_h[0] % 2
            cnt_h[0] += 1
            ss = ssq[:, 2 * k:2 * k + 1]
            rs = ssq[:, 2 * k + 1:2 * k + 2]
            P.act(junk, x_sb, AF.Square, accum_out=ss)
            P.act(rs, ss, AF.Sqrt, bias=epsb, scale=1.0 / D)
            P.recip(rs, rs)
            xn = xnb[k]
            P.act(xn, x_sb, AF.Identity, scale=rs)
            tps = pb(0, [8, 128], BF16)
            for kc in range(8):
                P.transpose(tps[:, kc, :], xn[:, kc * 128:(kc + 1) * 128], identB)
            if flag is None:
                for kc in range(8):
                    P.act(dest[:, kc, :], tps[:, kc, :], AF.Identity, bias=shcol[:, kc:kc + 1], scale=wcol[:, kc:kc + 1])
            else:
                P.tt('vector', t32, tps, bc(wcol.unsqueeze(2), [128, 8, 128]), ALU.mult)
                P.tt('gpsimd', t32, t32, bc(shcol.unsqueeze(2), [128, 8, 128]), ALU.add)
                P.ts('vector', dest, t32, flag, None, ALU.mult)

        hT = RX.alloc([8, HT_N], BF16)
        AA = RX.alloc([NCH, 2, 16], F32)
        BB = RX.alloc([NCH, 2, 16], F32)
        DEC = RX.alloc([NCH, 2, 16], F32)
        Sacc = RX.alloc([16, 65], F32)
        RR = RX.alloc([2, 16], F32)
        expR = RX.alloc([2, 16], F32)
        mXf = RX.mark()
        mMF = RM.mark()
        win = ab_w_in[0].rearrange("(kc p) n -> p kc n", p=128)

        xbuf = [RX.alloc([D], F32) for _ in range(2)]
        P.memset('gpsimd', hT[:, :, 0:1], 0.0)
        P.memset('gpsimd', hT[:, :, 257:258], 0.0)

        def tokcols(c):
            return (CTX0 + c * 128) if c < 2 else (WIN0 + (c - 2) * 128)

        for c in range(NCH):
            xb = xbuf[c % 2]
            P.dma('sync', xb, xw[c])
            v_ = 1 if c < 2 else 0
            col0 = tokcols(c)
            fg = None if c not in (2, NCH - 1) else FL[:, FL_VF + c:FL_VF + c + 1]
            make_hT(xb, hT[:, :, col0:col0 + 128], WM[:, 0, 0, v_, :], SHc[:, 0, 0, v_, :], fg)
        hTh = RX.alloc([8, 128], BF16)
        xb = xbuf[0]
        P.dma('sync', xb, xe)
        make_hT(xb, hTh, WM[:, 0, 0, 0, :], SHc[:, 0, 0, 0, :])
        P.ts('vector', hT[:, :, 258:259], hTh[:, :, 0:1], FL[:, FL_EDGE:FL_EDGE + 1], None, ALU.mult)
        P.ts('vector', hT[:, :, 2563:2564], hTh[:, :, 1:2], FL[:, FL_EDGE + 1:FL_EDGE + 2], None, ALU.mult)

        ck('B', [hT[:, 0, 0:1024], hT[:, 7, 1540:2564]])
        mark('hT')
        Gpre = RX.alloc([NCH, 32], F32)
        LF = RX.alloc([NCH, 2, 16], F32)
        II = RX.alloc([NCH, 2, 16], F32)
        Wg = RM.alloc([8, 32], BF16)
        P.dma('gpsimd', Wg, win[:, :, 4096:4128])
        gbb = RM.alloc([32], F32)
        P.dma('sync', gbb, bc(gate_b[0].rearrange("a h -> (a h)").unsqueeze(0), [128, 32]))
        lgb = RM.alloc([2, 8], F32)
        P.dma('sync', lgb.rearrange("p a h -> p (a h)"), bc(ret_lg[0].rearrange("a h -> (a h)").unsqueeze(0), [128, 16]))
        ck('C0', [gbb, lgb.rearrange('p a h -> p (a h)'), Wg.rearrange('p a b -> p (a b)')])
        for c in range(NCH):
            c0 = tokcols(c)
            pg = pb(3, [128])[:, 32 * (c % 4):32 * (c % 4) + 32]
            for kc in range(8):
                P.mm(pg, hT[:, kc, c0:c0 + 128], Wg[:, kc, :], start=(kc == 0), stop=(kc == 7))
            P.tt('vector', Gpre[:, c, :], pg, gbb, ALU.add)
        ck('C1', [Gpre.rearrange('p a b -> p (a b)')])
        VFb = FL[:, FL_VF:FL_VF + NCH]
        tmpg = RX.alloc([NCH, 8], F32)
        for d_ in range(2):
            fcol = Gpre[:, :, 8 + 16 * d_:16 + 16 * d_]
            icol = Gpre[:, :, 16 * d_:8 + 16 * d_]
            P.act(tmpg, fcol, AF.Exp, scale=-1.0)
            P.act(tmpg, tmpg, AF.Ln, bias=1.0)
            P.stt('vector', LF[:, :, d_, 8:16], tmpg, -1.0, bc(VFb.unsqueeze(2), [128, NCH, 8]), ALU.mult, ALU.mult)
            P.tt('vector', LF[:, :, d_, 0:8], bc(lgb[:, d_, :].unsqueeze(1), [128, NCH, 8]), bc(VFb.unsqueeze(2), [128, NCH, 8]), ALU.mult)
            P.memset('gpsimd', II[:, :, d_, 0:8], 0.0)
            P.copy('gpsimd', II[:, :, d_, 8:16], icol)
        ck('C2', [LF.rearrange('p a b c -> p (a b c)'), II.rearrange('p a b c -> p (a b c)')])
        LFc = RX.alloc([2, NCH * 16], F32)
        for d_ in range(2):
            P.copy('gpsimd', LFc[:, d_, :].rearrange("p (c l) -> p c l", l=16), LF[:, :, d_, :])
        for d_ in range(2):
            pe = pb(4 + d_, [NCH, 16])
            pef = pe.rearrange("p c l -> p (c l)")
            for (a_, b_) in ((0, 128), (128, 256), (256, 320)):
                P.mm(pef[:, a_:b_], MfF if d_ == 0 else MbF, LFc[:, d_, a_:b_])
            if d_ == 0 and dbg == 'C2a':
                P.copy('vector', AA.rearrange('p a b c -> p (a b c)')[:, 0:320], pef)
                ck('C2a', [AA.rearrange('p a b c -> p (a b c)'), LFc.rearrange('p a b -> p (a b)')])
            P.act(AA[:, :, d_, :], pe, AF.Exp, scale=-1.0)
            if d_ == 0:
                ck('C2b', [AA.rearrange('p a b c -> p (a b c)')])
            P.tt('vector', BB[:, :, d_, :], pe, II[:, :, d_, :], ALU.add)
            if d_ == 0:
                ck('C2c', [BB.rearrange('p a b c -> p (a b c)')])
            P.act(BB[:, :, d_, :], BB[:, :, d_, :], AF.Exp, bias=lnk)
            if d_ == 0:
                ck('C2d', [BB.rearrange('p a b c -> p (a b c)')])
            P.tt('vector', BB[:, :, d_, :], BB[:, :, d_, :], bc(VFb.unsqueeze(2), [128, NCH, 16]), ALU.mult)
        ck('C3', [AA.rearrange('p a b c -> p (a b c)'), BB.rearrange('p a b c -> p (a b c)')])
        LFf = LF.rearrange("p a b c -> p (a b c)")
        for hf in range(2):
            pt_ = pb(6, [10, 2, 16])
            ptf = pt_.rearrange("p a b c -> p (a b c)")
            for (a_, b_) in ((0, 128), (128, 256), (256, 320)):
                P.mm(ptf[:, a_:b_], onesF, LFf[:, hf * 320 + a_:hf * 320 + b_])
            P.act(DEC[:, hf * 10:(hf + 1) * 10, :, :], pt_, AF.Exp)

        ck('C', [AA.rearrange('p a b c -> p (a b c)'), BB.rearrange('p a b c -> p (a b c)'), DEC.rearrange('p a b c -> p (a b c)'), Gpre.rearrange('p a b -> p (a b)')])
        mark('prepass')
        P.memset('vector', Sacc, 0.0)
        P.memset('vector', RR, 0.0)
        Wv = RM.alloc([8, 1024], BF16)
        P.dma('gpsimd', Wv[:, :, 0:512], win[:, :, 1024:1536])
        P.dma('gpsimd', Wv[:, :, 512:1024], win[:, :, 3072:3584])
        Wkr = RM.alloc([8, 512], BF16)
        P.dma('gpsimd', Wkr, win[:, :, 512:1024])
        Wtap = RM.alloc([3, 8, 512], BF16)
        m1 = RM.mark()
        cwk = RM.alloc([3, 512], F32)
        for j in range(3):
            P.dma('sync', cwk[:, j, :], bc(conv_w[0, j:j + 1, 512:1024], [128, 512]))
        Wkm = RM.alloc([8, 512], BF16)
        P.dma('gpsimd', Wkm, win[:, :, 2560:3072])
        for j in range(3):
            P.tt('vector', Wtap[:, j, :, :], Wkm, bc(cwk[:, j, :].unsqueeze(1), [128, 8, 512]), ALU.mult)
        RM.release(m1)
        cbb = RM.alloc([512], F32)
        P.dma('sync', cbb, bc(conv_b[0:1, 512:1024], [128, 512]))
        rS = RM.alloc([2, NSLOT, 32], F32)
        P.dma('sync', rS.rearrange("p a s f -> p a (s f)"), ropeS.rearrange("a p n -> p a n"))
        Kfb = RM.alloc([16, 128], BF16)
        Vext = RM.alloc([16, 65], BF16)
        ra = RM.alloc([2, 8, 32], F32)
        krf = [RM.alloc([512], F32) for _ in range(2)]
        Vsb = [RM.alloc([16, 64], BF16) for _ in range(2)]
        kmts = [RM.alloc([512], F32) for _ in range(2)]
        gps = [RM.alloc([32], F32) for _ in range(2)]
        xb = xbuf[1]
        P.dma('sync', xb, xsh)
        make_hT(xb, hTh, WM[:, 0, 0, 0, :], SHc[:, 0, 0, 0, :])
        P.tt('vector', hTh, hTh, bc(FL[:, FL_SH:FL_SH + 128].unsqueeze(1), [128, 8, 128]), ALU.mult)
        hTs = [RX.alloc([8, 130], BF16) for _ in range(2)]
        Ktok = RX.alloc([16, 64], BF16)
        sg = RX.alloc([160], F32)

        def slot_A(s):
            hs = hTs[s % 2]
            xb = xbuf[s % 2]
            P.dma('sync', xb, xs[s])
            make_hT(xb, hs[:, :, 1:129], WM[:, 0, 0, 0, :], SHc[:, 0, 0, 0, :])
            P.copy('scalar', hs[:, :, 0:1], hTh[:, :, 2 * s:2 * s + 1])
            P.copy('scalar', hs[:, :, 129:130], hTh[:, :, 2 * s + 1:2 * s + 2])
            pkr = pb(1, [512])
            for kc in range(8):
                P.mm(pkr, hs[:, kc, 1:129], Wkr[:, kc, :], start=(kc == 0), stop=(kc == 7))
            P.copy('scalar', krf[s % 2], pkr)
            pkm = pb(2, [512])
            for j in range(3):
                for kc in range(8):
                    P.mm(pkm, hs[:, kc, j:j + 128], Wtap[:, j, kc, :], start=(j == 0 and kc == 0), stop=(j == 2 and kc == 7))
            P.tt('vector', kmts[s % 2], pkm, cbb, ALU.add)
            for hf in range(2):
                pv = pb(3 + hf, [512])
                for kc in range(8):
                    P.mm(pv, hs[:, kc, 1:129], Wv[:, kc, hf * 512:(hf + 1) * 512], start=(kc == 0), stop=(kc == 7))
                P.copy('scalar', Vsb[s % 2][:, hf * 8:(hf + 1) * 8, :], pv.rearrange("p (h d) -> p h d", h=8))
            pg = pb(5, [32])
            for kc in range(8):
                P.mm(pg, hs[:, kc, 1:129], Wg[:, kc, :], start=(kc == 0), stop=(kc == 7))
            P.tt('vector', gps[s % 2], pg, gbb, ALU.add)

        def slot_B(s):
            ff = FL[:, FL_FF + s:FL_FF + s + 1]
            fb = FL[:, FL_FB + s:FL_FB + s + 1]
            k3 = krf[s % 2].rearrange("p (h d) -> p h d", h=8)
            x1, x2 = k3[:, :, 0:32], k3[:, :, 32:64]
            cs = bc(rS[:, 0, s, :].unsqueeze(1), [128, 8, 32])
            sn = bc(rS[:, 1, s, :].unsqueeze(1), [128, 8, 32])
            P.tt('vector', ra[:, 0], x1, cs, ALU.mult)
            P.tt('vector', ra[:, 1], x2, sn, ALU.mult)
            P.tt('vector', Ktok[:, 0:8, 0:32], ra[:, 0], ra[:, 1], ALU.subtract)
            P.tt('vector', ra[:, 0], x2, cs, ALU.mult)
            P.tt('vector', ra[:, 1], x1, sn, ALU.mult)
            P.tt('vector', Ktok[:, 0:8, 32:64], ra[:, 0], ra[:, 1], ALU.add)
            P.act(Ktok[:, 8:16, :], kmts[s % 2].rearrange("p (h d) -> p h d", h=8), AF.Silu)
            P.act(Kfb[:, :, 0:64], Ktok, AF.Identity, scale=ff)
            P.act(Kfb[:, :, 64:128], Ktok, AF.Identity, scale=fb)
            gp = gps[s % 2]
            fsel = sg[:, 32:40]
            isel = sg[:, 40:56]
            lf16 = sg[:, 56:72]
            lfm = sg[:, 72:104]
            rsel = sg[:, 104:120]
            bex = sg[:, 120:136]
            t8 = sg[:, 136:144]
            P.ts('vector', fsel, gp[:, 8:16], ff, None, ALU.mult)
            P.stt('vector', fsel, gp[:, 24:32], fb, fsel, ALU.mult, ALU.add)
            P.memset('vector', isel[:, 0:8], 0.0)
            P.ts('vector', isel[:, 8:16], gp[:, 0:8], ff, None, ALU.mult)
            P.stt('vector', isel[:, 8:16], gp[:, 16:24], fb, isel[:, 8:16], ALU.mult, ALU.add)
            P.act(t8, fsel, AF.Exp, scale=-1.0)
            P.act(t8, t8, AF.Ln, bias=1.0)
            P.ts('vector', lf16[:, 8:16], t8, -1.0, None, ALU.mult)
            P.ts('vector', lf16[:, 0:8], lgb[:, 0, :], ff, None, ALU.mult)
            P.stt('vector', lf16[:, 0:8], lgb[:, 1, :], fb, lf16[:, 0:8], ALU.mult, ALU.add)
            P.ts('vector', lfm[:, 0:16], lf16, ff, None, ALU.mult)
            P.ts('vector', lfm[:, 16:32], lf16, fb, None, ALU.mult)
            pe = pb(0, [16], off=0)
            P.mm(pe, MfF, lfm[:, 0:16], start=True, stop=False)
            P.mm(pe, MbF, lfm[:, 16:32], start=False, stop=True)
            pt_ = pb(0, [32], off=512)
            P.mm(pt_, onesF, lfm)
            P.ts('vector', rsel, RR[:, 0, :], ff, None, ALU.mult)
            P.stt('vector', rsel, RR[:, 1, :], fb, rsel, ALU.mult, ALU.add)
            P.tt('vector', rsel, rsel, isel, ALU.add)
            P.tt('vector', rsel, rsel, pe, ALU.add)
            P.act(bex, rsel, AF.Exp, bias=lnk)
            RRf = RR.rearrange("p a l -> p (a l)")
            P.tt('vector', RRf, RRf, pt_, ALU.add)
            P.tt('vector', Vext[:, :, 0:64], Vsb[s % 2], bc(bex.unsqueeze(2), [128, 16, 64]), ALU.mult)
            P.copy('scalar', Vext[:, :, 64], bex)
            kvb = [pb(6, [6, 65]), pb(7, [6, 65]), pb(5, [4, 65])]
            for ln in range(16):
                P.mm(kvb[ln // 6][:, ln % 6, :], Kfb[:, ln, :], Vext[:, ln, :])
            for g3 in range(3):
                nl = 6 if g3 < 2 else 4
                P.tt('vector', Sacc[:, g3 * 6:g3 * 6 + nl, :], Sacc[:, g3 * 6:g3 * 6 + nl, :], kvb[g3], ALU.add)

        slot_A(0)
        for s in range(NSLOT):
            if s + 1 < NSLOT:
                slot_A(s + 1)
            slot_B(s)
        P.act(expR, RR, AF.Exp)
        ck('D', [Sacc.rearrange('p a b -> p (a b)')[:, 0:1024], RR.rearrange('p a b -> p (a b)'), expR.rearrange('p a b -> p (a b)')])
        RX.release(mXf)
        RM.release(mMF)

        mark('outside')
        MT = RM.alloc([8, NCH * 128], BF16)
        mMF2 = RM.mark()
        order = [list(range(NCH)), [1, 0] + list(range(NCH - 1, 1, -1))]
        first_dir = [0 if order[0].index(c_) <= order[1].index(c_) else 1 for c_ in range(NCH)]
        groups = [(CTX0, 256, 0, False)] + [(WIN0 + 512 * g_, 512, 256 + 512 * g_, True) for g_ in range(4)] + [(WIN0 + 2048, 256, 2304, True)]
        for ps_ in range(8):
            if ps_ in (1, 2, 5, 6):
                mark(f'p{ps_}_start')
            RX.release(mXf)
            RM.release(mMF2)
            is_ml = ps_ >= 4
            j = ps_ % 4
            l0 = (8 if is_ml else 0) + 2 * j
            qoff = (2048 if is_ml else 0) + 128 * j
            koff = (2560 if is_ml else 512) + 128 * j
            goff = (3584 if is_ml else 1536) + 128 * j
            voff = (3072 if is_ml else 1024) + 128 * j
            ntap = 3 if is_ml else 1
            qT = RX.alloc([NCH * 128], BF16)
            kT = RX.alloc([NCH * 128], BF16)
            gT = RX.alloc([NCH * 128], BF16)
            Kt = RX.alloc([NCH, 128], BF16)
            Vp = RX.alloc([NCH, 2, 64], BF16)
            Wq = RM.alloc([3, 8, 128], BF16)
            Wk = RM.alloc([3, 8, 128], BF16)
            Wgt = RM.alloc([8, 128], BF16)
            Wvp = RM.alloc([8, 128], BF16)
            Oacc = RM.alloc([NCH, 2, 64], F32)
            mScan = RM.mark()
            P.dma('gpsimd', Wq[:, 0], win[:, :, qoff:qoff + 128])
            P.dma('gpsimd', Wk[:, 0], win[:, :, koff:koff + 128])
            P.dma('gpsimd', Wgt, win[:, :, goff:goff + 128])
            P.dma('gpsimd', Wvp, win[:, :, voff:voff + 128])
            if is_ml:
                cwp = RM.alloc([2, 3, 128], F32)
                for wi, co in enumerate((128 * j, 512 + 128 * j)):
                    for tpi in range(3):
                        P.dma('sync', cwp[:, wi, tpi, :], bc(conv_w[0, tpi:tpi + 1, co:co + 128], [128, 128]))
                for wi, W_ in enumerate((Wq, Wk)):
                    for tpi in (2, 1, 0):
                        P.tt('vector', W_[:, tpi], W_[:, 0], bc(cwp[:, wi, tpi, :].unsqueeze(1), [128, 8, 128]), ALU.mult)
            rtmp = RM.alloc([2, 512], F32)
            rfb = [RM.alloc([2, 512], F32) for _ in range(2)]
            bi = 0
            for gi_, (hc0, n, lc0, isw) in enumerate(groups):
                rf = rfb[gi_ % 2]
                if isw and not is_ml:
                    w0_ = hc0 - WIN0
                    P.dma('sync', rf[:, :, 0:n], ropeF.rearrange("a p n -> p a n")[:, :, w0_:w0_ + n])
                for which, W_, dst in (('q', Wq, qT), ('k', Wk, kT), ('g', Wgt, gT)):
                    pp = pb(1 + (bi % 3), [512])[:, 0:n]
                    bi += 1
                    if which == 'g':
                        for kc in range(8):
                            P.mm(pp, W_[:, kc, :], hT[:, kc, hc0:hc0 + n], start=(kc == 0), stop=(kc == 7))
                        P.act(dst[:, lc0:lc0 + n], pp, AF.Sigmoid if is_ml else AF.Silu)
                        continue
                    for tpi in range(ntap):
                        sh_ = (tpi - 1) if is_ml else 0
                        for kc in range(8):
                            P.mm(pp, W_[:, tpi, kc, :], hT[:, kc, hc0 + sh_:hc0 + sh_ + n],
                                 start=(tpi == 0 and kc == 0), stop=(tpi == ntap - 1 and kc == 7))
                    if is_ml:
                        ci_ = 40 + (j if which == 'q' else 4 + j)
                        P.act(dst[:, lc0:lc0 + n], pp, AF.Silu, bias=colB[:, ci_:ci_ + 1])
                    elif not isw:
                        P.copy('scalar', dst[:, lc0:lc0 + n], pp)
                    else:
                        P.tt('vector', rtmp[:, 0, 0:n], pp, rf[:, 0, 0:n], ALU.mult)
                        for blk in range(4):
                            src = blk ^ 1
                            P.tt('vector', rtmp[blk * 32:(blk + 1) * 32, 1, 0:n], pp[src * 32:(src + 1) * 32, :],
                                 rf[src * 32:(src + 1) * 32, 1, 0:n], ALU.mult)
                        P.tt('vector', dst[:, lc0:lc0 + n], rtmp[:, 0, 0:n], rtmp[:, 1, 0:n], ALU.add)
            if ps_ in (1, 5):
                mark(f'p{ps_}_proj')
            for c8 in range(0, NCH, 8):
                ncc = min(8, NCH - c8)
                tpk = pb(4, [8, 128], BF16)
                for ci in range(ncc):
                    P.transpose(tpk[:, ci, :], kT[:, (c8 + ci) * 128:(c8 + ci + 1) * 128], identB)
                P.copy('scalar', Kt[:, c8:c8 + ncc, :], tpk[:, 0:ncc, :])
            for c4 in range(0, NCH, 4):
                pv = pb(1 + (c4 // 4) % 3, [4, 128])
                for ci in range(4):
                    c0 = tokcols(c4 + ci)
                    for kc in range(8):
                        P.mm(pv[:, ci, :], hT[:, kc, c0:c0 + 128], Wvp[:, kc, :], start=(kc == 0), stop=(kc == 7))
                P.copy('vector', Vp[:, c4:c4 + 4].rearrange("p c h d -> p c (h d)"), pv)
            if ps_ == 0:
                ck('E1', [qT[:, 0:1024], kT[:, 0:1024], gT[:, 0:1024], Kt.rearrange('p a b -> p (a b)')[:, 0:1024], Vp.rearrange('p a b c -> p (a b c)')[:, 0:1024]])
            if ps_ == 4:
                ck('F1', [qT[:, 0:1024], kT[:, 0:1024], gT[:, 0:1024], Kt.rearrange('p a b -> p (a b)')[:, 0:1024], Vp.rearrange('p a b c -> p (a b c)')[:, 0:1024]])
            if ps_ in (1, 5):
                mark(f'p{ps_}_kv')
            RM.release(mScan)
            decp = RM.alloc([NCH, 2], F32)
            P.copy('vector', decp[0:64], DEC[0:64, :, :, l0])
            P.copy('vector', decp[64:128], DEC[64:128, :, :, l0 + 1])
            S32 = [RM.alloc([130], F32) for _ in range(2)]
            Sbf = [RM.alloc([130], BF16) for _ in range(2)]
            stmp = RM.alloc([130], F32)
            ecol = RM.alloc([2], F32)
            Vts = [RM.alloc([2, 65], BF16) for _ in range(4)]
            dn = RM.alloc([2, 8], F32)
            P.memset('gpsimd', stmp, 0.0)
            for d_ in range(2):
                P.memset('gpsimd', S32[d_], 0.0)
                P.copy('vector', ecol[0:64, d_:d_ + 1], expR[0:64, d_, l0:l0 + 1])
                P.copy('vector', ecol[64:128, d_:d_ + 1], expR[64:128, d_, l0 + 1:l0 + 2])
            PTs = [RM.alloc([2, 128], BF16) for _ in range(4)]
            pt_banks = [[0, 6], [3, 7]]

            def scan_front(step, d_):
                c = order[d_][step]
                tk = slice(c * 128, (c + 1) * 128)
                Vt = Vts[2 * (step % 2) + d_]
                P.tt('vector', Vt[:, :, 0:64], Vp[:, c], bc(BB[:, c, d_, l0:l0 + 2].unsqueeze(2), [128, 2, 64]), ALU.mult)
                P.copy('scalar', Vt[:, :, 64], BB[:, c, d_, l0:l0 + 2])
                ptp = pb(pt_banks[d_][step % 2], [2, 128])
                prev_mm = None
                for h in range(2):
                    hb = slice(64 * h, 64 * h + 64)
                    prev_mm = P.mm(ptp[:, h, :], kT[hb, tk], qT[hb, tk], after=prev_mm)
                PT = PTs[2 * (step % 2) + d_]
                P.tt('vector', PT, ptp, bc((maskF_b if d_ == 0 else maskB_b).unsqueeze(1), [128, 2, 128]), ALU.mult)

            def scan_kv(step, d_):
                c = order[d_][step]
                Vt = Vts[2 * (step % 2) + d_]
                kvp = pb(2 if d_ == 0 else 5, [130])
                P.mm(kvp, Kt[:, c, :], Vt.rearrange("p h e -> p (h e)"))

            def scan_back(step, d_):
                c = order[d_][step]
                tk = slice(c * 128, (c + 1) * 128)
                S = S32[d_]
                Vt = Vts[2 * (step % 2) + d_]
                PT = PTs[2 * (step % 2) + d_]
                if step == 2:
                    r0 = 64 * d_
                    P.copy('scalar', stmp[0:64, 0:65], Sacc[r0:r0 + 64, l0, :])
                    P.copy('scalar', stmp[64:128, 65:130], Sacc[r0:r0 + 64, l0 + 1, :])
                    P.stt('vector', S, S, ecol[:, d_:d_ + 1], stmp, ALU.mult, ALU.add)
                if step > 0:
                    P.act(Sbf[d_], S, AF.Identity, scale=decp[:, c, d_:d_ + 1])
                ops_ = pb(1 if d_ == 0 else 4, [2, 65])
                for h in range(2):
                    hb = slice(64 * h, 64 * h + 64)
                    P.mm(ops_[:, h, :], PT[:, h, :], Vt[:, h, :], start=True, stop=(step == 0))
                    if step > 0:
                        P.mm(ops_[:, h, :], qT[hb, tk], Sbf[d_][hb, 65 * h:65 * h + 65], start=False, stop=True)
                for h in range(2):
                    at = AA[:, c, d_, l0 + h:l0 + h + 1]
                    if is_ml:
                        dd = dn[:, h, 4 * d_:4 * d_ + 4]
                        P.act(dd[:, 0:1], ops_[:, h, 64:65], AF.Abs, scale=at)
                        P.ts('vector', dd[:, 1:2], dd[:, 0:1], 1.0, None, ALU.max)
                        P.recip(dd[:, 2:3], dd[:, 1:2])
                        P.tt('vector', dd[:, 3:4], dd[:, 2:3], at, ALU.mult)
                        coef = dd[:, 3:4]
                    else:
                        coef = at
                    if first_dir[c] == d_:
                        P.act(Oacc[:, c, h, :], ops_[:, h, 0:64], AF.Identity, scale=coef)
                    else:
                        P.stt('vector', Oacc[:, c, h, :], ops_[:, h, 0:64], coef, Oacc[:, c, h, :], ALU.mult, ALU.add)
                kvp = pb(2 if d_ == 0 else 5, [130])
                if step == 0:
                    P.copy('vector', S, kvp)
                else:
                    P.stt('vector', S, S, decp[:, c, d_:d_ + 1], kvp, ALU.mult, ALU.add)

            for d_ in range(2):
                scan_front(0, d_)
                scan_kv(0, d_)
            for step in range(NCH):
                if step + 1 < NCH:
                    for d_ in range(2):
                        scan_front(step + 1, d_)
                for d_ in range(2):
                    scan_back(step, d_)
                if step + 1 < NCH:
                    for d_ in range(2):
                        scan_kv(step + 1, d_)
                if ps_ == 0 and step < 3:
                    ck('E2' + 'abc'[step], [Oacc.rearrange('p a b c -> p (a b c)')[:, 0:256], S32[0], S32[1]])
            if ps_ == 0:
                ck('E2', [Oacc.rearrange('p a b c -> p (a b c)')[:, 0:1024], Oacc.rearrange('p a b c -> p (a b c)')[:, 1024:2048]])
            if ps_ == 4:
                ck('F2', [Oacc.rearrange('p a b c -> p (a b c)')[:, 0:1024], Oacc.rearrange('p a b c -> p (a b c)')[:, 1024:2048]])
            if ps_ in (1, 5):
                mark(f'p{ps_}_scan')
            RM.release(mScan)
            sq = RM.alloc([NCH, 2, 64], F32)
            ms = RM.alloc([NCH, 2], F32)
            On = RM.alloc([NCH, 2, 64], BF16)
            P.act(sq, Oacc, AF.Square)
            P.reduce('vector', ms, sq, ALU.add)
            P.act(ms, ms, AF.Sqrt, bias=epsb, scale=1.0 / 64)
            P.recip(ms, ms)
            P.tt('vector', On, Oacc, bc(ms.unsqueeze(3), [128, NCH, 2, 64]), ALU.mult)
            nwi = (36 if is_ml else 32) + j
            for c4 in range(0, NCH, 4):
                tpo = pb(4, [4, 128], BF16, off=(c4 // 4 % 2) * 1024)
                for ci in range(4):
                    P.transpose(tpo[:, ci, :], On[:, c4 + ci].rearrange("p h d -> p (h d)"), identB)
                P.stt('vector', MT[:, ps_, c4 * 128:(c4 + 4) * 128], tpo.rearrange("p c t -> p (c t)"), colB[:, nwi:nwi + 1],
                      gT[:, c4 * 128:(c4 + 4) * 128], ALU.mult, ALU.mult)
        ck('E4', [MT[:, 0, 0:1024], MT[:, 7, 0:1024]])
        RM.release(mMF2)
        RX.release(0)
        mark('passes')
        X1 = RX.alloc([NCH, D], F32)
        Wo = RM.alloc([8, D], BF16)
        P.dma('gpsimd', Wo, ab_w_out[0].rearrange("(kc p) n -> p kc n", p=128))
        otmp = [RM.alloc([512], F32) for _ in range(2)]
        for c in range(NCH):
            P.dma('sync', X1[:, c, :], xw[c])
        for c in range(NCH):
            gi = 2 if c < 2 else 0
            for hf in range(2):
                po = pb(1 + hf, [512])
                for kc in range(8):
                    P.mm(po, MT[:, kc, c * 128:(c + 1) * 128], Wo[:, kc, hf * 512:(hf + 1) * 512], start=(kc == 0), stop=(kc == 7))
                P.tt('vector', otmp[hf], po, gB[:, gi, hf * 512:(hf + 1) * 512], ALU.mult)
                P.tt('vector', X1[:, c, hf * 512:(hf + 1) * 512], X1[:, c, hf * 512:(hf + 1) * 512], otmp[hf], ALU.add)
        RM.release(mMF)
        if dbg == 'l0m':
            for c in range(NCH):
                P.dma('sync', dbg_out[c], X1[:, c, :], store=True)

        mark('outproj0')
        def ffn(l, chunks, wm_sel, g_sel):
            m0 = RM.mark()
            w_in = ffn_w_in[l].rearrange("(kc p) n -> p kc n", p=128)
            w_out = ffn_w_out[l].rearrange("(f p) n -> p f n", p=128)
            Wout = RM.alloc([NFT, D], BF16)
            P.dma('gpsimd', Wout[:, 0:11, :], w_out[:, 0:11, :])
            P.dma('gpsimd', Wout[:, 11:22, :], w_out[:, 11:22, :])
            AT = RM.alloc([NFT, 512], BF16)
            wbuf = [RM.alloc([8, 2, 256], BF16) for _ in range(2)]
            m_ph = RM.mark()
            h2 = RM.alloc([8, 512], BF16)
            RM.release(m_ph)
            ot = [RM.alloc([512], F32) for _ in range(2)]
            RM.alloc([8 * 512 - 2 * 1024], BF16)
            bi = 0
            for q0 in range(0, len(chunks), 4):
                cq = chunks[q0:q0 + 4]
                for i, c in enumerate(cq):
                    v_ = wm_sel(c)
                    make_hT(X1[:, c, :], h2[:, :, i * 128:(i + 1) * 128], WM[:, l, 1, v_, :], SHc[:, l, 1, v_, :])
                for fp in range(11):
                    wb = wbuf[bi % 2]
                    bi += 1
                    P.dma('gpsimd', wb[:, :, 0, :], w_in[:, :, fp * 256:(fp + 1) * 256])
                    P.dma('gpsimd', wb[:, :, 1, :], w_in[:, :, D_FF + fp * 256:D_FF + (fp + 1) * 256])
                    for f2 in range(2):
                        f = fp * 2 + f2
                        pgm = pb(1 + 2 * (f % 2), [512])
                        pum = pb(2 + 2 * (f % 2), [512])
                        for kc in range(8):
                            P.mm(pgm, wb[:, kc, 0, f2 * 128:(f2 + 1) * 128], h2[:, kc, :], start=(kc == 0), stop=(kc == 7))
                        for kc in range(8):
                            P.mm(pum, wb[:, kc, 1, f2 * 128:(f2 + 1) * 128], h2[:, kc, :], start=(kc == 0), stop=(kc == 7))
                        P.act(AT[:, f, :], pgm, AF.Silu)
                        P.tt('vector', AT[:, f, :], AT[:, f, :], pum, ALU.mult)
                k_ = 0
                for i, c in enumerate(cq):
                    for hf in range(2):
                        po = pb(5 + (k_ % 3), [512])
                        o_ = ot[k_ % 2]
                        k_ += 1
                        for f in range(NFT):
                            P.mm(po, AT[:, f, i * 128:(i + 1) * 128], Wout[:, f, hf * 512:(hf + 1) * 512], start=(f == 0), stop=(f == NFT - 1))
                        P.tt('vector', o_, po, gB[:, g_sel(c), hf * 512:(hf + 1) * 512], ALU.mult)
                        P.tt('vector', X1[:, c, hf * 512:(hf + 1) * 512], X1[:, c, hf * 512:(hf + 1) * 512], o_, ALU.add)
            RM.release(m0)

        if dbg != 'l0m':
            ffn(0, list(range(NCH)), lambda c: 1 if c < 2 else 0, lambda c: 3 if c < 2 else 1)
        if dbg == 'l0':
            for c in range(NCH):
                P.dma('sync', dbg_out[c], X1[:, c, :], store=True)

        mark('ffn0')
        if dbg in (None, 'l1m'):
            modulation(1)
            make_wm(1)
            m1 = RM.mark()
            awin = at_w_in[0].rearrange("(kc p) n -> p kc n", p=128)
            kT1 = RM.alloc([2, NCH * 128], BF16)
            Vx1 = RM.alloc([NCH, 4, 65], BF16)
            P.memset('vector', Vx1[:, :, :, 64:65], 1.0)
            qnb = RM.alloc([64], F32)
            knb = RM.alloc([64], F32)
            P.dma('sync', qnb, bc(at_qn, [128, 64]))
            P.dma('sync', knb, bc(at_kn, [128, 64]))
            snk = RM.alloc([16], F32)
            P.dma('sync', snk, bc(at_sink, [128, 16]))
            rT = RM.alloc([2, NW, 32], F32)
            P.dma('sync', rT.rearrange("p a s f -> p a (s f)"), ropeT.rearrange("a p n -> p a n"))
            amb = RM.alloc([4, 4, 128], BF16)
            sm = RM.alloc([8], F32)
            sinkexp = RM.alloc([16], F32)
            hTc = [RM.alloc([8, 128], BF16)] * 2
            nms = RM.alloc([8], F32)
            qn = RM.alloc([8, 64], F32)
            nsq = qn
            qr = RM.alloc([2, 8, 32], F32)
            qtk = RM.alloc([16, 64], BF16)
            qtk2 = RM.alloc([16, 64], BF16)
            m2 = RM.mark()
            amf = RM.alloc([4, 128], F32)
            absq = RM.alloc([128], F32)
            P.dma('sync', amf, amask.rearrange("c p n -> p c n"))
            P.ts('vector', amf, amf, 1.0, 30000.0, ALU.subtract, ALU.mult)
            for v4 in range(4):
                P.copy('vector', amb[:, v4], bc(amf[:, v4, :].unsqueeze(1), [128, 4, 128]))
            P.act(absq[:, 0:64], qnb, AF.Abs)
            P.act(absq[:, 64:128], knb, AF.Abs)
            P.reduce('vector', sm[:, 0:1], absq[:, 0:64], ALU.max)
            P.reduce('vector', sm[:, 1:2], absq[:, 64:128], ALU.max)
            P.tt('vector', sm[:, 2:3], sm[:, 0:1], sm[:, 1:2], ALU.mult)
            P.ts('vector', sm[:, 3:4], sm[:, 2:3], -8.0, None, ALU.mult)
            negB = sm[:, 3:4]
            P.act(sinkexp, snk, AF.Exp, bias=negB)
            RM.release(m2)
            Wkv1 = RM.alloc([8, 512], BF16)
            P.dma('gpsimd', Wkv1, awin[:, :, 1024:1536])

            def qknorm_rope(ps3, nh, wb_, wch, dst):
                P.act(nsq[:, 0:nh, :], ps3, AF.Square)
                P.reduce('vector', nms[:, 0:nh], nsq[:, 0:nh, :], ALU.add)
                P.act(nms[:, 0:nh], nms[:, 0:nh], AF.Sqrt, bias=epsb, scale=1.0 / 64)
                P.recip(nms[:, 0:nh], nms[:, 0:nh])
                P.tt('vector', qn[:, 0:nh, :], ps3, bc(nms[:, 0:nh].unsqueeze(2), [128, nh, 64]), ALU.mult)
                if wch < 0:
                    P.tt('vector', dst, qn[:, 0:nh, :], bc(wb_.unsqueeze(1), [128, nh, 64]), ALU.mult)
                    return
                P.tt('vector', qn[:, 0:nh, :], qn[:, 0:nh, :], bc(wb_.unsqueeze(1), [128, nh, 64]), ALU.mult)
                x1, x2 = qn[:, 0:nh, 0:32], qn[:, 0:nh, 32:64]
                cs = bc(rT[:, 0, wch, :].unsqueeze(1), [128, nh, 32])
                sn = bc(rT[:, 1, wch, :].unsqueeze(1), [128, nh, 32])
                P.tt('vector', qr[:, 0, 0:nh], x1, cs, ALU.mult)
                P.tt('vector', qr[:, 1, 0:nh], x2, sn, ALU.mult)
                P.tt('vector', dst[:, :, 0:32], qr[:, 0, 0:nh], qr[:, 1, 0:nh], ALU.subtract)
                P.tt('vector', qr[:, 0, 0:nh], x2, cs, ALU.mult)
                P.tt('vector', qr[:, 1, 0:nh], x1, sn, ALU.mult)
                P.tt('vector', dst[:, :, 32:64], qr[:, 0, 0:nh], qr[:, 1, 0:nh], ALU.add)

            for c in range(NCH):
                v_ = 1 if c < 2 else 0
                hc_ = hTc[c % 2]
                make_hT(X1[:, c, :], hc_, WM[:, 1, 0, v_, :], SHc[:, 1, 0, v_, :])
                pkv = pb(1 + (c % 2), [512])
                for kc in range(8):
                    P.mm(pkv, hc_[:, kc, :], Wkv1[:, kc, :], start=(kc == 0), stop=(kc == 7))
                P.copy('scalar', Vx1[:, c, :, 0:64], pkv[:, 256:512].rearrange("p (h d) -> p h d", h=4))
                qknorm_rope(pkv[:, 0:256].rearrange("p (h d) -> p h d", h=4), 4, knb, (c - 2) if c >= 2 else -1, qtk[:, 0:4, :])
                tpk = pb(4, [2, 128], BF16)
                for t_ in range(2):
                    P.transpose(tpk[:, t_, :], qtk[:, 2 * t_:2 * t_ + 2, :].rearrange("p h d -> p (h d)"), identB)
                P.copy('scalar', kT1[:, :, c * 128:(c + 1) * 128], tpk)
            mark('l1kv')
            RM.release(m2)
            Wq1 = RM.alloc([8, 1024], BF16)
            P.dma('gpsimd', Wq1, awin[:, :, 0:1024])
            Wo1 = RM.alloc([8, D], BF16)
            P.dma('gpsimd', Wo1, at_w_out[0].rearrange("(kc p) n -> p kc n", p=128))
            qTc = RM.alloc([2, 4, 128], BF16)
            Eb = [RM.alloc([5, 4, 128], BF16) for _ in range(2)]
            Otk = RM.alloc([16, 64], BF16)
            OT = RM.alloc([8, 128], BF16)
            dn1 = RM.alloc([8], F32)
            ot1 = [RX.alloc([512], F32), RM.alloc([512], F32)]
            ei = 0
            for i in range(16):
                c = i + 3
                hc_ = hTc[i % 2]
                make_hT(X1[:, c, :], hc_, WM[:, 1, 0, 0, :], SHc[:, 1, 0, 0, :])
                for hf in range(2):
                    pq = pb(2 + hf, [512])
                    for kc in range(8):
                        P.mm(pq, hc_[:, kc, :], Wq1[:, kc, hf * 512:(hf + 1) * 512], start=(kc == 0), stop=(kc == 7))
                    qknorm_rope(pq.rearrange("p (h d) -> p h d", h=8), 8, qnb, c - 2, qtk[:, 8 * hf:8 * hf + 8, :])
                tpq = pb(4, [2, 4, 128], BF16)
                for tp_ in range(2):
                    P.copy('scalar', qtk2[:, 8 * tp_:8 * tp_ + 8, :].rearrange("p (j g) d -> p g j d", g=2),
                           qtk[:, 8 * tp_:8 * tp_ + 8, :].rearrange("p (g j) d -> p g j d", g=2))
                    for j in range(4):
                        src = qtk2[:, 8 * tp_ + 2 * j:8 * tp_ + 2 * j + 2, :]
                        P.transpose(tpq[:, tp_, j, :], src.rearrange("p h d -> p (h d)"), identB)
                P.copy('scalar', qTc, tpq)
                kblocks = [(c - 1, 0 if i == 0 else 1), (c, None), (c + 1, 3 if i == 15 else 2), (0, None), (1, None)]
                for g in range(4):
                    tp_, hb = g // 2, slice(64 * (g % 2), 64 * (g % 2) + 64)
                    E = Eb[ei % 2]
                    ei += 1
                    for bi_, (kc_, mk) in enumerate(kblocks):
                        pst = pb(5 + (bi_ % 2), [4, 128])
                        P.mm(pst.rearrange("p j t -> p (j t)"), kT1[hb, tp_, kc_ * 128:(kc_ + 1) * 128],
                             qTc[hb, tp_].rearrange("p j t -> p (j t)"), start=True, stop=(mk is None))
                        if mk is not None:
                            P.mm(pst.rearrange("p j t -> p (j t)"), identB, amb[:, mk].rearrange("p j t -> p (j t)"), start=False, stop=True)
                        P.act(E[:, bi_], pst, AF.Exp, bias=negB, scale=0.125)
                    pso = pb(7 if g % 2 == 0 else 3, [4, 65])
                    for j in range(4):
                        for bi_, (kc_, mk) in enumerate(kblocks):
                            P.mm(pso[:, j, :], E[:, bi_, j, :], Vx1[:, kc_, g, :], start=(bi_ == 0), stop=(bi_ == 4))
                    P.tt('vector', dn1[:, 0:4], pso[:, :, 64], sinkexp[:, 4 * g:4 * g + 4], ALU.add)
                    P.recip(dn1[:, 4:8], dn1[:, 0:4])
                    P.tt('vector', Otk[:, 4 * g:4 * g + 4, :], pso[:, :, 0:64], bc(dn1[:, 4:8].unsqueeze(2), [128, 4, 64]), ALU.mult)
                tpo = pb(4, [8, 128], BF16)
                for kc in range(8):
                    P.transpose(tpo[:, kc, :], Otk[:, 2 * kc:2 * kc + 2, :].rearrange("p h d -> p (h d)"), identB)
                P.copy('scalar', OT, tpo)
                for hf in range(2):
                    po = pb(1 + hf, [512])
                    for kc in range(8):
                        P.mm(po, OT[:, kc, :], Wo1[:, kc, hf * 512:(hf + 1) * 512], start=(kc == 0), stop=(kc == 7))
                    P.tt('vector', ot1[hf], po, gB[:, 0, hf * 512:(hf + 1) * 512], ALU.mult)
                    P.tt('vector', X1[:, c, hf * 512:(hf + 1) * 512], X1[:, c, hf * 512:(hf + 1) * 512], ot1[hf], ALU.add)
            mark('l1attn')
            RM.release(m1)
            if dbg == 'l1m':
                for c in range(NCH):
                    P.dma('sync', dbg_out[c], X1[:, c, :], store=True)
            else:
                ffn(1, list(range(3, 19)), lambda c: 0, lambda c: 1)
        for i in range(16):
            P.dma('sync', y[i * 128:(i + 1) * 128, :], X1[:, i + 3, :], store=True)
        mark('end')
        stats = P.emit()
        stats['marks'] = P.marks
        stats['peaks'] = (RP.peak, RX.peak, RM.peak)
    return nc, stats


def _rope_angles(pos):
    pos = np.asarray(pos)
    row = (pos // 64).astype(np.float32)
    col = (pos % 64).astype(np.float32)
    inv = (10000.0 ** (-np.arange(16, dtype=np.float32) / 16)).astype(np.float32)
    return np.concatenate([row[:, None] * inv, col[:, None] * inv], axis=-1).astype(np.float32)


def _core_inputs(inp, core):
    b, q = core // 4, core % 4
    x = np.asarray(inp['x'], np.float32)
    ctx = np.asarray(inp['ctx'], np.float32)
    xb = x[b].reshape(64, 128, D)
    fl = np.zeros((1, NF), np.float32)
    xw = np.zeros((NCH, 128, D), np.float32)
    xw[0:2] = ctx[b].reshape(2, 128, D)
    fl[0, FL_VF:FL_VF + 2] = 1.0
    wpos = np.full((NW, 128), -1, np.int64)
    for w in range(NW):
        ch = 16 * q - 1 + w
        if 0 <= ch < 64:
            xw[2 + w] = xb[ch]
            fl[0, FL_VF + 2 + w] = 1.0
            wpos[w] = ch * 128 + np.arange(128)
    xe = np.zeros((128, D), np.float32)
    tl = 128 * (16 * q - 1) - 1
    tr = 128 * (16 * q + 17)
    if 0 <= tl < SEQ:
        xe[0] = x[b, tl]
        fl[0, FL_EDGE] = 1.0
    if 0 <= tr < SEQ:
        xe[1] = x[b, tr]
        fl[0, FL_EDGE + 1] = 1.0
    slots = [(ch, 0) for ch in range(16 * q - 2, -1, -1)] + [(ch, 1) for ch in range(16 * q + 17, 64)]
    assert len(slots) <= NSLOT
    xs = np.zeros((NSLOT, 128, D), np.float32)
    xsh = np.zeros((128, D), np.float32)
    spos = np.zeros((NSLOT, 128), np.int64)
    for s, (ch, d_) in enumerate(slots):
        xs[s] = xb[ch]
        fl[0, (FL_FF if d_ == 0 else FL_FB) + s] = 1.0
        spos[s] = ch * 128 + np.arange(128)
        t0, t1 = ch * 128 - 1, ch * 128 + 128
        if t0 >= 0:
            xsh[2 * s] = x[b, t0]
            fl[0, FL_SH + 2 * s] = 1.0
        if t1 < SEQ:
            xsh[2 * s + 1] = x[b, t1]
            fl[0, FL_SH + 2 * s + 1] = 1.0
    wp = np.where(wpos < 0, 0, wpos).reshape(-1)
    ang = _rope_angles(wp)
    cosw, sinw = np.cos(ang), np.sin(ang)
    ropeT = np.stack([cosw.reshape(NW, 128, 32).transpose(1, 0, 2).reshape(128, NW * 32),
                      sinw.reshape(NW, 128, 32).transpose(1, 0, 2).reshape(128, NW * 32)]).astype(np.float32)
    p = np.arange(128)
    fi = p % 32
    cosF = cosw[:, fi].T
    sgn = np.where((p % 64) < 32, 1.0, -1.0)[:, None]
    sinF = sinw[:, fi].T * sgn
    ropeF = np.stack([cosF, sinF]).astype(np.float32)
    angs = _rope_angles(spos.reshape(-1))
    ropeS = np.stack([np.cos(angs).reshape(NSLOT, 128, 32).transpose(1, 0, 2).reshape(128, NSLOT * 32),
                      np.sin(angs).reshape(NSLOT, 128, 32).transpose(1, 0, 2).reshape(128, NSLOT * 32)]).astype(np.float32)
    u = np.arange(128)[:, None]
    s_ = np.arange(128)[None, :]
    consts = np.zeros((7, 128, 128), np.float32)
    consts[0] = np.eye(128)
    consts[1] = (u > s_)
    consts[2] = (u < s_)
    consts[3] = (u <= s_)
    consts[4] = (u >= s_)
    consts[5] = 1.0
    amask = np.zeros((4, 128, 128), np.float32)
    mL = (u >= s_).astype(np.float32)
    mR = (u <= s_).astype(np.float32)
    amask[0] = mL * (1.0 if q > 0 else 0.0)
    amask[1] = mL
    amask[2] = mR
    amask[3] = mR * (1.0 if q < 3 else 0.0)
    cvec = np.concatenate([np.asarray(inp['c'], np.float32)[b].reshape(8, 128),
                           np.asarray(inp['c_ctx'], np.float32).reshape(8, 128)], 0)
    m = dict(xw=xw, xe=xe, xs=xs, xsh=xsh, fl=fl, cvec=cvec, consts=consts, amask=amask,
             ropeF=ropeF, ropeT=ropeT, ropeS=ropeS)
    for k in ('ada_w', 'ada_b', 'norm_w', 'ffn_w_in', 'ffn_w_out', 'ab_w_in', 'ab_w_out', 'ret_log_gamma',
              'ret_norm_w', 'mlstm_conv_w', 'mlstm_conv_b', 'mlstm_gate_b', 'mlstm_norm_w', 'attn_w_in',
              'attn_w_out', 'attn_q_norm_w', 'attn_k_norm_w', 'attn_sink'):
        m[k] = np.ascontiguousarray(np.asarray(inp[k], np.float32))
    return m


_NC_CACHE = {}


def kernel(**inp):
    if 'nc' not in _NC_CACHE:
        _NC_CACHE['nc'] = build()[0]
    nc = _NC_CACHE['nc']
    in_maps = [_core_inputs(inp, c) for c in range(NCORE)]
    res = run_bass_kernel_spmd(nc, in_maps, core_ids=list(range(NCORE)))
    out = np.zeros((2, SEQ, D), np.float32)
    for c in range(NCORE):
        b, q = c // 4, c % 4
        out[b, 2048 * q:2048 * (q + 1)] = res.results[c]["y"]
    return out
```

```python
import contextlib
import math
import numpy as np
import concourse.bass as bass
import concourse.mybir as mybir
from concourse.bass_utils import run_bass_kernel_spmd

F32 = mybir.dt.float32
BF16 = mybir.dt.bfloat16
ALU = mybir.AluOpType
AF = mybir.ActivationFunctionType
AX = mybir.AxisListType

_DTSZ = {F32: 4, BF16: 2}
SEM_LIMIT = 12000
DMA_POOL = 12


def _region(ap):
    sp = str(ap.space).upper()
    if 'SB' not in sp and 'PSUM' not in sp:
        return None
    if 'PSUM' in sp:
        return (ap.name, 0, 128, 0, 2048)
    pat = ap.ap
    esz = _DTSZ[ap.dtype]
    pstep, pcount = pat[0]
    off = ap.offset
    if pstep == 0:
        p0 = 0
        f0 = off
    else:
        p0 = off // pstep
        f0 = off - p0 * pstep
    ext = 0
    for stp, cn in pat[1:]:
        ext += abs(stp) * (cn - 1)
    return (ap.name, p0, p0 + pcount, f0 * esz, (f0 + ext + 1) * esz)


class Prog:
    ENGS = ('tensor', 'vector', 'scalar', 'gpsimd', 'sync')

    def __init__(self, nc, same_engine_sync=True):
        self.nc = nc
        self.ops = []
        self.track = {}
        self.same_engine_sync = same_engine_sync
        self.dma_hist = {e: [] for e in self.ENGS}
        self.store_ops = []

    def _add(self, eng, fn, outs, ins, is_dma=False, extra_deps=(), force=False):
        if getattr(self, 'frozen', False) and not force:
            return -1
        idx = len(self.ops)
        deps = set(extra_deps)
        self.ops.append(dict(eng=eng, fn=fn, deps=deps, is_dma=is_dma, signaled=False))
        for ap in ins:
            r = _region(ap)
            if r is not None:
                self._access(idx, eng, r, False, deps)
        for ap in outs:
            r = _region(ap)
            if r is not None:
                self._access(idx, eng, r, True, deps)
        if is_dma:
            h = self.dma_hist[eng]
            if len(h) >= DMA_POOL:
                deps.add(h[-DMA_POOL])
            h.append(idx)
        deps.discard(idx)
        return idx

    def _access(self, idx, eng, r, is_write, deps):
        name, p0, p1, b0, b1 = r
        recs = self.track.get(name, [])
        keep = []
        for rec in recs:
            (q0, q1, c0, c1, oi, ow, oe) = rec
            if q1 <= p0 or p1 <= q0 or c1 <= b0 or b1 <= c0 or oi == idx:
                keep.append(rec)
                continue
            if is_write or ow or (name.startswith('pb') and oe != eng):
                pe_pe = (eng == 'tensor' and oe == 'tensor')
                if not pe_pe:
                    deps.add(oi)
            covered = (p0 <= q0 and q1 <= p1 and b0 <= c0 and c1 <= b1)
            if is_write and covered and not (eng == 'tensor' and oe == 'tensor' and not ow):
                continue
            if (not is_write) and (not ow) and oe == eng and covered and not self.ops[oi]['is_dma']:
                continue
            keep.append(rec)
        keep.append((p0, p1, b0, b1, idx, is_write, eng))
        self.track[name] = keep

    def op(self, eng, fn, outs, ins):
        return self._add(eng, fn, outs, ins)

    def dma(self, eng, out, in_, store=False, **kw):
        i = self._add(eng, lambda e: e.dma_start(out=out, in_=in_, **kw), [out], [in_], is_dma=True)
        if store and i >= 0:
            self.store_ops.append(i)
        return i

    def mm(self, out, lhsT, rhs, start=True, stop=True, after=None):
        i = self.op('tensor', lambda e: e.matmul(out, lhsT, rhs, start=start, stop=stop), [out], [lhsT, rhs])
        if after is not None and i >= 0 and after >= 0:
            self.ops[i]['deps'].add(after)
        return i

    def transpose(self, out, in_, ident):
        return self.op('tensor', lambda e: e.transpose(out, in_, ident), [out], [in_, ident])

    def act(self, out, in_, func, bias=None, scale=None, accum_out=None):
        kw = {}
        ins = [in_]
        outs = [out]
        if bias is not None:
            kw['bias'] = bias
            if not isinstance(bias, (int, float)):
                ins.append(bias)
        if scale is not None:
            kw['scale'] = scale
            if not isinstance(scale, (int, float)):
                ins.append(scale)
        if accum_out is not None:
            kw['accum_out'] = accum_out
            outs.append(accum_out)
        return self.op('scalar', lambda e: e.activation(out, in_, func, **kw), outs, ins)

    def tt(self, eng, out, in0, in1, op):
        return self.op(eng, lambda e: e.tensor_tensor(out, in0, in1, op), [out], [in0, in1])

    def ts(self, eng, out, in0, s1, s2, op0, op1=None):
        ins = [in0] + [s for s in (s1, s2) if s is not None and not isinstance(s, (int, float))]
        kw = {}
        if op1 is not None:
            kw['op1'] = op1
        return self.op(eng, lambda e: e.tensor_scalar(out, in0, s1, s2, op0, **kw), [out], ins)

    def stt(self, eng, out, in0, scalar, in1, op0, op1):
        ins = [in0, in1] + ([scalar] if not isinstance(scalar, (int, float)) else [])
        return self.op(eng, lambda e: e.scalar_tensor_tensor(out, in0, scalar, in1, op0, op1), [out], ins)

    def copy(self, eng, out, in_):
        if eng == 'scalar':
            return self.op(eng, lambda e: e.copy(out, in_), [out], [in_])
        return self.op(eng, lambda e: e.tensor_copy(out, in_), [out], [in_])

    def memset(self, eng, out, val):
        return self.op(eng, lambda e: e.memset(out, val), [out], [])

    def recip(self, out, in_):
        return self.op('vector', lambda e: e.reciprocal(out, in_), [out], [in_])

    def reduce(self, eng, out, in_, op, axis=AX.X):
        return self.op(eng, lambda e: e.tensor_reduce(out, in_, axis, op), [out], [in_])

    def emit(self):
        nc = self.nc
        ops = self.ops
        self._add('sync', None, [], [], extra_deps=self.store_ops, force=True)
        for o in ops:
            if o['is_dma']:
                o['signaled'] = True
            for d in o['deps']:
                ops[d]['signaled'] = True
        cnt = {e: 0 for e in self.ENGS}
        dcnt = {e: 0 for e in self.ENGS}
        nsem_eng = {e: 0 for e in self.ENGS}
        for o in ops:
            e = o['eng']
            if not o['signaled']:
                continue
            if o['is_dma']:
                k = dcnt[e]
                dcnt[e] += 1
                o['sem'] = ('d', e, k % DMA_POOL)
                o['val'] = 16 * (k // DMA_POOL + 1)
                o['sidx'] = None
            else:
                k = cnt[e]
                cnt[e] += 1
                o['sem'] = ('c', e, k // SEM_LIMIT)
                o['val'] = (k % SEM_LIMIT) + 1
                o['sidx'] = k
                nsem_eng[e] = k // SEM_LIMIT + 1
        sems = {}
        st = contextlib.ExitStack()
        for e in self.ENGS:
            for j in range(nsem_eng[e]):
                sems[('c', e, j)] = st.enter_context(nc.semaphore(f"c_{e}_{j}"))
            for j in range(min(DMA_POOL, dcnt[e])):
                sems[('d', e, j)] = st.enter_context(nc.semaphore(f"d_{e}_{j}"))
        seen = {e: {f: -1 for f in self.ENGS} for e in self.ENGS}
        seen_dma = {e: set() for e in self.ENGS}
        per_eng = {e: [] for e in self.ENGS}
        nwaits = 0
        for o in ops:
            e = o['eng']
            waits = {}
            for d in sorted(o['deps']):
                p = ops[d]
                if p['is_dma']:
                    if d in seen_dma[e]:
                        continue
                    seen_dma[e].add(d)
                    waits[p['sem']] = max(waits.get(p['sem'], 0), p['val'])
                else:
                    f = p['eng']
                    if f == e and not self.same_engine_sync:
                        continue
                    if p['sidx'] <= seen[e][f]:
                        continue
                    seen[e][f] = p['sidx']
                    waits[p['sem']] = max(waits.get(p['sem'], 0), p['val'])
            nwaits += len(waits)
            per_eng[e].append((o, list(waits.items())))
        self.stats = dict(n_ops=len(ops), n_waits=nwaits, per_eng={e: len(v) for e, v in per_eng.items()})
        with st, nc.Block() as block:
            def body(engname):
                def run(eng):
                    for o, waits in per_eng[engname]:
                        for key, val in waits:
                            eng.wait_ge(sems[key], val)
                        if o['fn'] is None:
                            continue
                        ins = o['fn'](eng)
                        if o['signaled']:
                            ins.then_inc(sems[o['sem']], 16 if o['is_dma'] else 1)
                return run
            block.tensor(body('tensor'))
            block.vector(body('vector'))
            block.scalar(body('scalar'))
            block.gpsimd(body('gpsimd'))
            block.sync(body('sync'))
        return self.stats


D = 1024
SEQ = 8192
NCORE = 8
NW = 18
NCH = 20
NSLOT = 48
NF = 256
EPS = 1e-6
D_FF = 2816
NFT = 22
LNK = math.log(0.125)
HT_N = 2564
CTX0 = 1
WIN0 = 259
FL_VF = 0
FL_EDGE = 20
FL_FF = 22
FL_FB = 70
FL_SH = 118


def bc(ap, shape):
    return ap.broadcast_to(shape)


class Region:
    def __init__(self, t, base, cap, name):
        self.t, self.base, self.cap, self.name = t, base, cap, name
        self.off = 0
        self.peak = 0

    def alloc(self, shape, dt):
        n = 1
        for s in shape:
            n *= s
        nb = (n * _DTSZ[dt] + 63) // 64 * 64
        if self.off + nb > self.cap:
            raise RuntimeError(f"region {self.name} overflow: {self.off + nb} > {self.cap}")
        a = self.base + self.off
        v = self.t[:, a // 4:(a + nb) // 4]
        if dt != F32:
            v = v.bitcast(dt)
        v = v[:, 0:n]
        self.off += nb
        self.peak = max(self.peak, self.off)
        if len(shape) == 1:
            return v
        names = [chr(ord('a') + i) for i in range(len(shape))]
        pat = "p (" + " ".join(names) + ") -> p " + " ".join(names)
        return v.rearrange(pat, **{nm: s for nm, s in zip(names, shape)})

    def mark(self):
        return self.off

    def release(self, m):
        self.off = m


class _Stop(Exception):
    pass


ARENA_BYTES = 212480
P_BYTES = 35328
X_BYTES = 83968


def build(dbg=None):
    nc = bass.Bass("TRN2", target_bir_lowering=False)

    def din(name, shape):
        return nc.dram_tensor(name, list(shape), F32, kind="ExternalInput").ap()

    xw = din("xw", [NCH, 128, D])
    xe = din("xe", [128, D])
    xs = din("xs", [NSLOT, 128, D])
    xsh = din("xsh", [128, D])
    fl = din("fl", [1, NF])
    cvec = din("cvec", [16, 128])
    ada_w = din("ada_w", [2, D, 6 * D])
    ada_b = din("ada_b", [2, 6 * D])
    norm_w = din("norm_w", [2, 2, D])
    ffn_w_in = din("ffn_w_in", [2, D, 2 * D_FF])
    ffn_w_out = din("ffn_w_out", [2, D_FF, D])
    ab_w_in = din("ab_w_in", [1, D, 4128])
    ab_w_out = din("ab_w_out", [1, D, D])
    ret_lg = din("ret_log_gamma", [1, 2, 8])
    ret_nw = din("ret_norm_w", [1, 512])
    conv_w = din("mlstm_conv_w", [1, 3, D])
    conv_b = din("mlstm_conv_b", [1, D])
    gate_b = din("mlstm_gate_b", [1, 4, 8])
    ml_nw = din("mlstm_norm_w", [1, 512])
    at_w_in = din("attn_w_in", [1, D, 1536])
    at_w_out = din("attn_w_out", [1, D, D])
    at_qn = din("attn_q_norm_w", [1, 64])
    at_kn = din("attn_k_norm_w", [1, 64])
    at_sink = din("attn_sink", [1, 16])
    consts = din("consts", [7, 128, 128])
    amask = din("amask", [4, 128, 128])
    ropeF = din("ropeF", [2, 128, 2304])
    ropeT = din("ropeT", [2, 128, NW * 32])
    ropeS = din("ropeS", [2, 128, NSLOT * 32])
    y = nc.dram_tensor("y", [2048, D], F32, kind="ExternalOutput").ap()
    dbg_out = None
    if dbg is not None:
        dbg_out = nc.dram_tensor("dbg", [NCH, 128, D], F32, kind="ExternalOutput").ap()

    st = contextlib.ExitStack()
    with st:
        arena_t = st.enter_context(nc.sbuf_tensor("arena", [128, ARENA_BYTES // 4], F32))
        RP = Region(arena_t, 0, P_BYTES, "P")
        RX = Region(arena_t, P_BYTES, X_BYTES, "X")
        RM = Region(arena_t, P_BYTES + X_BYTES, ARENA_BYTES - P_BYTES - X_BYTES, "MF")
        banks = [st.enter_context(nc.psum_tensor(f"pb{i}", [128, 512], F32)) for i in range(8)]
        P = Prog(nc)
        P.marks = []

        def mark(nm):
            P.marks.append((nm, sum(1 for o in P.ops if o['eng'] == 'tensor')))

        def ck(name, aps):
            if dbg != name:
                return
            k = 0
            for ap in aps:
                n = ap.shape[1] if len(ap.shape) == 2 else None
                flat = ap
                npart = ap.shape[0]
                P.dma('sync' if flat.dtype == F32 else 'gpsimd', dbg_out[k][0:npart, 0:flat.shape[1]], flat, store=True)
                k += 1
            P.frozen = True

        def pb(i, shape, dt=F32, off=0):
            n = 1
            for s in shape:
                n *= s
            nb = n * _DTSZ[dt]
            v = banks[i][:, off // 4:(off + nb + 3) // 4]
            if dt != F32:
                v = v.bitcast(dt)
            v = v[:, 0:n]
            if len(shape) == 1:
                return v
            names = [chr(ord('a') + k) for k in range(len(shape))]
            pat = "p (" + " ".join(names) + ") -> p " + " ".join(names)
            return v.rearrange(pat, **{nm: s for nm, s in zip(names, shape)})

        cst = RP.alloc([7, 128], F32)
        P.dma('sync', cst, consts.rearrange("c p n -> p c n"))
        identF = cst[:, 0, :]
        MfF, MbF = cst[:, 1, :], cst[:, 2, :]
        onesF = cst[:, 5, :]
        cstb = RP.alloc([7, 128], BF16)
        P.copy('vector', cstb, cst)
        identB = cstb[:, 0, :]
        maskF_b, maskB_b = cstb[:, 3, :], cstb[:, 4, :]
        FL = RP.alloc([NF], F32)
        P.dma('sync', FL, bc(fl, [128, NF]))
        epsb = RP.alloc([1], F32)
        P.memset('vector', epsb, EPS)
        lnk = RP.alloc([1], F32)
        P.memset('vector', lnk, LNK)
        colA = RP.alloc([112], F32)
        colB = RP.alloc([48], F32)
        svf = RP.alloc([16], F32)
        sv = RP.alloc([8, 2], BF16)
        modT = RP.alloc([2, 2, 48], F32)
        gB = RP.alloc([4, D], F32)
        WM = RP.alloc([2, 2, 2, 8], F32)
        SHc = RP.alloc([2, 2, 2, 8], F32)
        junk = RP.alloc([D], BF16)
        xnb = [RP.alloc([D], BF16) for _ in range(2)]
        t32 = RP.alloc([8, 128], F32)
        ssq = RP.alloc([4], F32)

        m0 = RM.mark()
        stg = RM.alloc([128], F32)
        stg2 = RM.alloc([128], F32)
        P.dma('sync', stg[0:16, :], cvec)
        P.dma('sync', stg[16:64, :], ada_b[0].rearrange("(j p) -> j p", p=128))
        P.dma('sync', stg[64:112, :], ada_b[1].rearrange("(j p) -> j p", p=128))
        tp = pb(0, [112])
        P.transpose(tp, stg[0:112, :], identF[0:112, 0:112])
        P.copy('vector', colA, tp)
        P.dma('sync', stg2[0:32, :], norm_w.rearrange("l i (j p) -> (l i j) p", p=128))
        P.dma('sync', stg2[32:36, :], ret_nw[0].rearrange("(j p) -> j p", p=128))
        P.dma('sync', stg2[36:40, :], ml_nw[0].rearrange("(j p) -> j p", p=128))
        P.dma('sync', stg2[40:48, :], conv_b[0].rearrange("(j p) -> j p", p=128))
        tp2 = pb(0, [48], off=1024)
        P.transpose(tp2, stg2[0:48, :], identF[0:48, 0:48])
        P.copy('vector', colB, tp2)
        RM.release(m0)
        ck('A0', [colA, colB, FL, cst[:, 1, :]])
        P.act(svf, colA[:, 0:16], AF.Silu)
        P.copy('vector', sv[:, :, 0], svf[:, 0:8])
        P.copy('vector', sv[:, :, 1], svf[:, 8:16])

        gslot = {(0, 0, 2): 0, (0, 0, 5): 1, (0, 1, 2): 2, (0, 1, 5): 3, (1, 0, 2): 0, (1, 0, 5): 1}

        def modulation(l):
            m0 = RM.mark()
            svrep = RM.alloc([2, 8, 128], BF16)
            for v_ in range(2):
                P.copy('vector', svrep[:, v_, :, :], bc(svf[:, 8 * v_:8 * v_ + 8].unsqueeze(2), [128, 8, 128]))
            wblk = [RM.alloc([8, 1024], BF16) for _ in range(2)]
            bbc = RM.alloc([1024], F32)
            for j in range(6):
                wb = wblk[j % 2]
                P.dma('gpsimd', wb, ada_w[l].rearrange("(kc p) n -> p kc n", p=128)[:, :, j * 1024:(j + 1) * 1024])
                ps = pb(1, [8, 2])
                for n in range(8):
                    for kc in range(8):
                        P.mm(ps[:, n, :], wb[:, kc, n * 128:(n + 1) * 128], sv[:, kc, :], start=(kc == 0), stop=(kc == 7))
                for v_ in range(2):
                    P.tt('vector', modT[:, l, v_, j * 8:(j + 1) * 8], ps[:, :, v_],
                         colA[:, 16 + 48 * l + j * 8:16 + 48 * l + j * 8 + 8], ALU.add)
                if j in (2, 5):
                    P.dma('sync', bbc, bc(ada_b[l:l + 1, j * 1024:(j + 1) * 1024], [128, 1024]))
                    for v_ in range(2):
                        if (l, v_, j) not in gslot:
                            continue
                        for hf in range(2):
                            pg = pb(2 + hf, [512])
                            for kc in range(8):
                                P.mm(pg, svrep[:, v_, kc, :], wb[:, kc, hf * 512:(hf + 1) * 512], start=(kc == 0), stop=(kc == 7))
                            P.tt('vector', gB[:, gslot[(l, v_, j)], hf * 512:(hf + 1) * 512], pg, bbc[:, hf * 512:(hf + 1) * 512], ALU.add)
            RM.release(m0)

        def make_wm(l):
            for i in range(2):
                for v_ in range(2):
                    sc = modT[:, l, v_, (3 * i + 1) * 8:(3 * i + 2) * 8]
                    nw = colB[:, (2 * l + i) * 8:(2 * l + i) * 8 + 8]
                    P.stt('vector', WM[:, l, i, v_, :], sc, 1.0, nw, ALU.add, ALU.mult)
                    P.copy('vector', SHc[:, l, i, v_, :], modT[:, l, v_, (3 * i) * 8:(3 * i) * 8 + 8])

        modulation(0)
        make_wm(0)
        mark('mod0')
        ck('A', [colA, colB, modT.rearrange('p a b c -> p (a b c)'), gB[:, 0, :], gB[:, 3, :], WM.rearrange('p a b c d -> p (a b c d)')])

        cnt_h = [0]

        def make_hT(x_sb, dest, wcol, shcol, flag=None):
            k = cnt_h[0] % 2
            cnt_h[0] += 1
            ss = ssq[:, 2 * k:2 * k + 1]
            rs = ssq[:, 2 * k + 1:2 * k + 2]
            P.act(junk, x_sb, AF.Square, accum_out=ss)
            P.act(rs, ss, AF.Sqrt, bias=epsb, scale=1.0 / D)
            P.recip(rs, rs)
            xn = xnb[k]
            P.act(xn, x_sb, AF.Identity, scale=rs)
            tps = pb(0, [8, 128], BF16)
            for kc in range(8):
                P.transpose(tps[:, kc, :], xn[:, kc * 128:(kc + 1) * 128], identB)
            if flag is None:
                for kc in range(8):
                    P.act(dest[:, kc, :], tps[:, kc, :], AF.Identity, bias=shcol[:, kc:kc + 1], scale=wcol[:, kc:kc + 1])
            else:
                P.tt('vector', t32, tps, bc(wcol.unsqueeze(2), [128, 8, 128]), ALU.mult)
                P.tt('gpsimd', t32, t32, bc(shcol.unsqueeze(2), [128, 8, 128]), ALU.add)
                P.ts('vector', dest, t32, flag, None, ALU.mult)

        hT = RX.alloc([8, HT_N], BF16)
        AA = RX.alloc([NCH, 2, 16], F32)
        BB = RX.alloc([NCH, 2, 16], F32)
        DEC = RX.alloc([NCH, 2, 16], F32)
        Sacc = RX.alloc([16, 65], F32)
        RR = RX.alloc([2, 16], F32)
        expR = RX.alloc([2, 16], F32)
        mXf = RX.mark()
        mMF = RM.mark()
        win = ab_w_in[0].rearrange("(kc p) n -> p kc n", p=128)

        xbuf = [RX.alloc([D], F32) for _ in range(2)]
        P.memset('gpsimd', hT[:, :, 0:1], 0.0)
        P.memset('gpsimd', hT[:, :, 257:258], 0.0)

        def tokcols(c):
            return (CTX0 + c * 128) if c < 2 else (WIN0 + (c - 2) * 128)

        for c in range(NCH):
            xb = xbuf[c % 2]
            P.dma('sync', xb, xw[c])
            v_ = 1 if c < 2 else 0
            col0 = tokcols(c)
            fg = None if c not in (2, NCH - 1) else FL[:, FL_VF + c:FL_VF + c + 1]
            make_hT(xb, hT[:, :, col0:col0 + 128], WM[:, 0, 0, v_, :], SHc[:, 0, 0, v_, :], fg)
        hTh = RX.alloc([8, 128], BF16)
        xb = xbuf[0]
        P.dma('sync', xb, xe)
        make_hT(xb, hTh, WM[:, 0, 0, 0, :], SHc[:, 0, 0, 0, :])
        P.ts('vector', hT[:, :, 258:259], hTh[:, :, 0:1], FL[:, FL_EDGE:FL_EDGE + 1], None, ALU.mult)
        P.ts('vector', hT[:, :, 2563:2564], hTh[:, :, 1:2], FL[:, FL_EDGE + 1:FL_EDGE + 2], None, ALU.mult)

        ck('B', [hT[:, 0, 0:1024], hT[:, 7, 1540:2564]])
        mark('hT')
        Gpre = RX.alloc([NCH, 32], F32)
        LF = RX.alloc([NCH, 2, 16], F32)
        II = RX.alloc([NCH, 2, 16], F32)
        Wg = RM.alloc([8, 32], BF16)
        P.dma('gpsimd', Wg, win[:, :, 4096:4128])
        gbb = RM.alloc([32], F32)
        P.dma('sync', gbb, bc(gate_b[0].rearrange("a h -> (a h)").unsqueeze(0), [128, 32]))
        lgb = RM.alloc([2, 8], F32)
        P.dma('sync', lgb.rearrange("p a h -> p (a h)"), bc(ret_lg[0].rearrange("a h -> (a h)").unsqueeze(0), [128, 16]))
        ck('C0', [gbb, lgb.rearrange('p a h -> p (a h)'), Wg.rearrange('p a b -> p (a b)')])
        for c in range(NCH):
            c0 = tokcols(c)
            pg = pb(3, [128])[:, 32 * (c % 4):32 * (c % 4) + 32]
            for kc in range(8):
                P.mm(pg, hT[:, kc, c0:c0 + 128], Wg[:, kc, :], start=(kc == 0), stop=(kc == 7))
            P.tt('vector', Gpre[:, c, :], pg, gbb, ALU.add)
        ck('C1', [Gpre.rearrange('p a b -> p (a b)')])
        VFb = FL[:, FL_VF:FL_VF + NCH]
        tmpg = RX.alloc([NCH, 8], F32)
        for d_ in range(2):
            fcol = Gpre[:, :, 8 + 16 * d_:16 + 16 * d_]
            icol = Gpre[:, :, 16 * d_:8 + 16 * d_]
            P.act(tmpg, fcol, AF.Exp, scale=-1.0)
            P.act(tmpg, tmpg, AF.Ln, bias=1.0)
            P.stt('vector', LF[:, :, d_, 8:16], tmpg, -1.0, bc(VFb.unsqueeze(2), [128, NCH, 8]), ALU.mult, ALU.mult)
            P.tt('vector', LF[:, :, d_, 0:8], bc(lgb[:, d_, :].unsqueeze(1), [128, NCH, 8]), bc(VFb.unsqueeze(2), [128, NCH, 8]), ALU.mult)
            P.memset('gpsimd', II[:, :, d_, 0:8], 0.0)
            P.copy('gpsimd', II[:, :, d_, 8:16], icol)
        ck('C2', [LF.rearrange('p a b c -> p (a b c)'), II.rearrange('p a b c -> p (a b c)')])
        LFc = RX.alloc([2, NCH * 16], F32)
        for d_ in range(2):
            P.copy('gpsimd', LFc[:, d_, :].rearrange("p (c l) -> p c l", l=16), LF[:, :, d_, :])
        for d_ in range(2):
            pe = pb(4 + d_, [NCH, 16])
            pef = pe.rearrange("p c l -> p (c l)")
            for (a_, b_) in ((0, 128), (128, 256), (256, 320)):
                P.mm(pef[:, a_:b_], MfF if d_ == 0 else MbF, LFc[:, d_, a_:b_])
            if d_ == 0 and dbg == 'C2a':
                P.copy('vector', AA.rearrange('p a b c -> p (a b c)')[:, 0:320], pef)
                ck('C2a', [AA.rearrange('p a b c -> p (a b c)'), LFc.rearrange('p a b -> p (a b)')])
            P.act(AA[:, :, d_, :], pe, AF.Exp, scale=-1.0)
            if d_ == 0:
                ck('C2b', [AA.rearrange('p a b c -> p (a b c)')])
            P.tt('vector', BB[:, :, d_, :], pe, II[:, :, d_, :], ALU.add)
            if d_ == 0:
                ck('C2c', [BB.rearrange('p a b c -> p (a b c)')])
            P.act(BB[:, :, d_, :], BB[:, :, d_, :], AF.Exp, bias=lnk)
            if d_ == 0:
                ck('C2d', [BB.rearrange('p a b c -> p (a b c)')])
            P.tt('vector', BB[:, :, d_, :], BB[:, :, d_, :], bc(VFb.unsqueeze(2), [128, NCH, 16]), ALU.mult)
        ck('C3', [AA.rearrange('p a b c -> p (a b c)'), BB.rearrange('p a b c -> p (a b c)')])
        LFf = LF.rearrange("p a b c -> p (a b c)")
        for hf in range(2):
            pt_ = pb(6, [10, 2, 16])
            ptf = pt_.rearrange("p a b c -> p (a b c)")
            for (a_, b_) in ((0, 128), (128, 256), (256, 320)):
                P.mm(ptf[:, a_:b_], onesF, LFf[:, hf * 320 + a_:hf * 320 + b_])
            P.act(DEC[:, hf * 10:(hf + 1) * 10, :, :], pt_, AF.Exp)

        ck('C', [AA.rearrange('p a b c -> p (a b c)'), BB.rearrange('p a b c -> p (a b c)'), DEC.rearrange('p a b c -> p (a b c)'), Gpre.rearrange('p a b -> p (a b)')])
        mark('prepass')
        P.memset('vector', Sacc, 0.0)
        P.memset('vector', RR, 0.0)
        Wv = RM.alloc([8, 1024], BF16)
        P.dma('gpsimd', Wv[:, :, 0:512], win[:, :, 1024:1536])
        P.dma('gpsimd', Wv[:, :, 512:1024], win[:, :, 3072:3584])
        Wkr = RM.alloc([8, 512], BF16)
        P.dma('gpsimd', Wkr, win[:, :, 512:1024])
        Wtap = RM.alloc([3, 8, 512], BF16)
        m1 = RM.mark()
        cwk = RM.alloc([3, 512], F32)
        for j in range(3):
            P.dma('sync', cwk[:, j, :], bc(conv_w[0, j:j + 1, 512:1024], [128, 512]))
        Wkm = RM.alloc([8, 512], BF16)
        P.dma('gpsimd', Wkm, win[:, :, 2560:3072])
        for j in range(3):
            P.tt('vector', Wtap[:, j, :, :], Wkm, bc(cwk[:, j, :].unsqueeze(1), [128, 8, 512]), ALU.mult)
        RM.release(m1)
        cbb = RM.alloc([512], F32)
        P.dma('sync', cbb, bc(conv_b[0:1, 512:1024], [128, 512]))
        rS = RM.alloc([2, NSLOT, 32], F32)
        P.dma('sync', rS.rearrange("p a s f -> p a (s f)"), ropeS.rearrange("a p n -> p a n"))
        Kfb = RM.alloc([16, 128], BF16)
        Vext = RM.alloc([16, 65], BF16)
        ra = RM.alloc([2, 8, 32], F32)
        krf = [RM.alloc([512], F32) for _ in range(2)]
        Vsb = [RM.alloc([16, 64], BF16) for _ in range(2)]
        kmts = [RM.alloc([512], F32) for _ in range(2)]
        gps = [RM.alloc([32], F32) for _ in range(2)]
        xb = xbuf[1]
        P.dma('sync', xb, xsh)
        make_hT(xb, hTh, WM[:, 0, 0, 0, :], SHc[:, 0, 0, 0, :])
        P.tt('vector', hTh, hTh, bc(FL[:, FL_SH:FL_SH + 128].unsqueeze(1), [128, 8, 128]), ALU.mult)
        hTs = [RX.alloc([8, 130], BF16) for _ in range(2)]
        Ktok = RX.alloc([16, 64], BF16)
        sg = RX.alloc([160], F32)

        def slot_A(s):
            hs = hTs[s % 2]
            xb = xbuf[s % 2]
            P.dma('sync', xb, xs[s])
            make_hT(xb, hs[:, :, 1:129], WM[:, 0, 0, 0, :], SHc[:, 0, 0, 0, :])
            P.copy('scalar', hs[:, :, 0:1], hTh[:, :, 2 * s:2 * s + 1])
            P.copy('scalar', hs[:, :, 129:130], hTh[:, :, 2 * s + 1:2 * s + 2])
            pkr = pb(1, [512])
            for kc in range(8):
                P.mm(pkr, hs[:, kc, 1:129], Wkr[:, kc, :], start=(kc == 0), stop=(kc == 7))
            P.copy('scalar', krf[s % 2], pkr)
            pkm = pb(2, [512])
            for j in range(3):
                for kc in range(8):
                    P.mm(pkm, hs[:, kc, j:j + 128], Wtap[:, j, kc, :], start=(j == 0 and kc == 0), stop=(j == 2 and kc == 7))
            P.tt('vector', kmts[s % 2], pkm, cbb, ALU.add)
            for hf in range(2):
                pv = pb(3 + hf, [512])
                for kc in range(8):
                    P.mm(pv, hs[:, kc, 1:129], Wv[:, kc, hf * 512:(hf + 1) * 512], start=(kc == 0), stop=(kc == 7))
                P.copy('scalar', Vsb[s % 2][:, hf * 8:(hf + 1) * 8, :], pv.rearrange("p (h d) -> p h d", h=8))
            pg = pb(5, [32])
            for kc in range(8):
                P.mm(pg, hs[:, kc, 1:129], Wg[:, kc, :], start=(kc == 0), stop=(kc == 7))
            P.tt('vector', gps[s % 2], pg, gbb, ALU.add)

        def slot_B(s):
            ff = FL[:, FL_FF + s:FL_FF + s + 1]
            fb = FL[:, FL_FB + s:FL_FB + s + 1]
            k3 = krf[s % 2].rearrange("p (h d) -> p h d", h=8)
            x1, x2 = k3[:, :, 0:32], k3[:, :, 32:64]
            cs = bc(rS[:, 0, s, :].unsqueeze(1), [128, 8, 32])
            sn = bc(rS[:, 1, s, :].unsqueeze(1), [128, 8, 32])
            P.tt('vector', ra[:, 0], x1, cs, ALU.mult)
            P.tt('vector', ra[:, 1], x2, sn, ALU.mult)
            P.tt('vector', Ktok[:, 0:8, 0:32], ra[:, 0], ra[:, 1], ALU.subtract)
            P.tt('vector', ra[:, 0], x2, cs, ALU.mult)
            P.tt('vector', ra[:, 1], x1, sn, ALU.mult)
            P.tt('vector', Ktok[:, 0:8, 32:64], ra[:, 0], ra[:, 1], ALU.add)
            P.act(Ktok[:, 8:16, :], kmts[s % 2].rearrange("p (h d) -> p h d", h=8), AF.Silu)
            P.act(Kfb[:, :, 0:64], Ktok, AF.Identity, scale=ff)
            P.act(Kfb[:, :, 64:128], Ktok, AF.Identity, scale=fb)
            gp = gps[s % 2]
            fsel = sg[:, 32:40]
            isel = sg[:, 40:56]
            lf16 = sg[:, 56:72]
            lfm = sg[:, 72:104]
            rsel = sg[:, 104:120]
            bex = sg[:, 120:136]
            t8 = sg[:, 136:144]
            P.ts('vector', fsel, gp[:, 8:16], ff, None, ALU.mult)
            P.stt('vector', fsel, gp[:, 24:32], fb, fsel, ALU.mult, ALU.add)
            P.memset('vector', isel[:, 0:8], 0.0)
            P.ts('vector', isel[:, 8:16], gp[:, 0:8], ff, None, ALU.mult)
            P.stt('vector', isel[:, 8:16], gp[:, 16:24], fb, isel[:, 8:16], ALU.mult, ALU.add)
            P.act(t8, fsel, AF.Exp, scale=-1.0)
            P.act(t8, t8, AF.Ln, bias=1.0)
            P.ts('vector', lf16[:, 8:16], t8, -1.0, None, ALU.mult)
            P.ts('vector', lf16[:, 0:8], lgb[:, 0, :], ff, None, ALU.mult)
            P.stt('vector', lf16[:, 0:8], lgb[:, 1, :], fb, lf16[:, 0:8], ALU.mult, ALU.add)
            P.ts('vector', lfm[:, 0:16], lf16, ff, None, ALU.mult)
            P.ts('vector', lfm[:, 16:32], lf16, fb, None, ALU.mult)
            pe = pb(0, [16], off=0)
            P.mm(pe, MfF, lfm[:, 0:16], start=True, stop=False)
            P.mm(pe, MbF, lfm[:, 16:32], start=False, stop=True)
            pt_ = pb(0, [32], off=512)
            P.mm(pt_, onesF, lfm)
            P.ts('vector', rsel, RR[:, 0, :], ff, None, ALU.mult)
            P.stt('vector', rsel, RR[:, 1, :], fb, rsel, ALU.mult, ALU.add)
            P.tt('vector', rsel, rsel, isel, ALU.add)
            P.tt('vector', rsel, rsel, pe, ALU.add)
            P.act(bex, rsel, AF.Exp, bias=lnk)
            RRf = RR.rearrange("p a l -> p (a l)")
            P.tt('vector', RRf, RRf, pt_, ALU.add)
            P.tt('vector', Vext[:, :, 0:64], Vsb[s % 2], bc(bex.unsqueeze(2), [128, 16, 64]), ALU.mult)
            P.copy('scalar', Vext[:, :, 64], bex)
            kvb = [pb(6, [6, 65]), pb(7, [6, 65]), pb(5, [4, 65])]
            for ln in range(16):
                P.mm(kvb[ln // 6][:, ln % 6, :], Kfb[:, ln, :], Vext[:, ln, :])
            for g3 in range(3):
                nl = 6 if g3 < 2 else 4
                P.tt('vector', Sacc[:, g3 * 6:g3 * 6 + nl, :], Sacc[:, g3 * 6:g3 * 6 + nl, :], kvb[g3], ALU.add)

        slot_A(0)
        for s in range(NSLOT):
            if s + 1 < NSLOT:
                slot_A(s + 1)
            slot_B(s)
        P.act(expR, RR, AF.Exp)
        ck('D', [Sacc.rearrange('p a b -> p (a b)')[:, 0:1024], RR.rearrange('p a b -> p (a b)'), expR.rearrange('p a b -> p (a b)')])
        RX.release(mXf)
        RM.release(mMF)

        mark('outside')
        MT = RM.alloc([8, NCH * 128], BF16)
        mMF2 = RM.mark()
        order = [list(range(NCH)), [1, 0] + list(range(NCH - 1, 1, -1))]
        first_dir = [0 if order[0].index(c_) <= order[1].index(c_) else 1 for c_ in range(NCH)]
        groups = [(CTX0, 256, 0, False)] + [(WIN0 + 512 * g_, 512, 256 + 512 * g_, True) for g_ in range(4)] + [(WIN0 + 2048, 256, 2304, True)]
        for ps_ in range(8):
            if ps_ in (1, 2, 5, 6):
                mark(f'p{ps_}_start')
            RX.release(mXf)
            RM.release(mMF2)
            is_ml = ps_ >= 4
            j = ps_ % 4
            l0 = (8 if is_ml else 0) + 2 * j
            qoff = (2048 if is_ml else 0) + 128 * j
            koff = (2560 if is_ml else 512) + 128 * j
            goff = (3584 if is_ml else 1536) + 128 * j
            voff = (3072 if is_ml else 1024) + 128 * j
            ntap = 3 if is_ml else 1
            qT = RX.alloc([NCH * 128], BF16)
            kT = RX.alloc([NCH * 128], BF16)
            gT = RX.alloc([NCH * 128], BF16)
            Kt = RX.alloc([NCH, 128], BF16)
            Vp = RX.alloc([NCH, 2, 64], BF16)
            Wq = RM.alloc([3, 8, 128], BF16)
            Wk = RM.alloc([3, 8, 128], BF16)
            Wgt = RM.alloc([8, 128], BF16)
            Wvp = RM.alloc([8, 128], BF16)
            Oacc = RM.alloc([NCH, 2, 64], F32)
            mScan = RM.mark()
            P.dma('gpsimd', Wq[:, 0], win[:, :, qoff:qoff + 128])
            P.dma('gpsimd', Wk[:, 0], win[:, :, koff:koff + 128])
            P.dma('gpsimd', Wgt, win[:, :, goff:goff + 128])
            P.dma('gpsimd', Wvp, win[:, :, voff:voff + 128])
            if is_ml:
                cwp = RM.alloc([2, 3, 128], F32)
                for wi, co in enumerate((128 * j, 512 + 128 * j)):
                    for tpi in range(3):
                        P.dma('sync', cwp[:, wi, tpi, :], bc(conv_w[0, tpi:tpi + 1, co:co + 128], [128, 128]))
                for wi, W_ in enumerate((Wq, Wk)):
                    for tpi in (2, 1, 0):
                        P.tt('vector', W_[:, tpi], W_[:, 0], bc(cwp[:, wi, tpi, :].unsqueeze(1), [128, 8, 128]), ALU.mult)
            rtmp = RM.alloc([2, 512], F32)
            rfb = [RM.alloc([2, 512], F32) for _ in range(2)]
            bi = 0
            for gi_, (hc0, n, lc0, isw) in enumerate(groups):
                rf = rfb[gi_ % 2]
                if isw and not is_ml:
                    w0_ = hc0 - WIN0
                    P.dma('sync', rf[:, :, 0:n], ropeF.rearrange("a p n -> p a n")[:, :, w0_:w0_ + n])
                for which, W_, dst in (('q', Wq, qT), ('k', Wk, kT), ('g', Wgt, gT)):
                    pp = pb(1 + (bi % 3), [512])[:, 0:n]
                    bi += 1
                    if which == 'g':
                        for kc in range(8):
                            P.mm(pp, W_[:, kc, :], hT[:, kc, hc0:hc0 + n], start=(kc == 0), stop=(kc == 7))
                        P.act(dst[:, lc0:lc0 + n], pp, AF.Sigmoid if is_ml else AF.Silu)
                        continue
                    for tpi in range(ntap):
                        sh_ = (tpi - 1) if is_ml else 0
                        for kc in range(8):
                            P.mm(pp, W_[:, tpi, kc, :], hT[:, kc, hc0 + sh_:hc0 + sh_ + n],
                                 start=(tpi == 0 and kc == 0), stop=(tpi == ntap - 1 and kc == 7))
                    if is_ml:
                        ci_ = 40 + (j if which == 'q' else 4 + j)
                        P.act(dst[:, lc0:lc0 + n], pp, AF.Silu, bias=colB[:, ci_:ci_ + 1])
                    elif not isw:
                        P.copy('scalar', dst[:, lc0:lc0 + n], pp)
                    else:
                        P.tt('vector', rtmp[:, 0, 0:n], pp, rf[:, 0, 0:n], ALU.mult)
                        for blk in range(4):
                            src = blk ^ 1
                            P.tt('vector', rtmp[blk * 32:(blk + 1) * 32, 1, 0:n], pp[src * 32:(src + 1) * 32, :],
                                 rf[src * 32:(src + 1) * 32, 1, 0:n], ALU.mult)
                        P.tt('vector', dst[:, lc0:lc0 + n], rtmp[:, 0, 0:n], rtmp[:, 1, 0:n], ALU.add)
            if ps_ in (1, 5):
                mark(f'p{ps_}_proj')
            for c8 in range(0, NCH, 8):
                ncc = min(8, NCH - c8)
                tpk = pb(4, [8, 128], BF16)
                for ci in range(ncc):
                    P.transpose(tpk[:, ci, :], kT[:, (c8 + ci) * 128:(c8 + ci + 1) * 128], identB)
                P.copy('scalar', Kt[:, c8:c8 + ncc, :], tpk[:, 0:ncc, :])
            for c4 in range(0, NCH, 4):
                pv = pb(1 + (c4 // 4) % 3, [4, 128])
                for ci in range(4):
                    c0 = tokcols(c4 + ci)
                    for kc in range(8):
                        P.mm(pv[:, ci, :], hT[:, kc, c0:c0 + 128], Wvp[:, kc, :], start=(kc == 0), stop=(kc == 7))
                P.copy('vector', Vp[:, c4:c4 + 4].rearrange("p c h d -> p c (h d)"), pv)
            if ps_ == 0:
                ck('E1', [qT[:, 0:1024], kT[:, 0:1024], gT[:, 0:1024], Kt.rearrange('p a b -> p (a b)')[:, 0:1024], Vp.rearrange('p a b c -> p (a b c)')[:, 0:1024]])
            if ps_ == 4:
                ck('F1', [qT[:, 0:1024], kT[:, 0:1024], gT[:, 0:1024], Kt.rearrange('p a b -> p (a b)')[:, 0:1024], Vp.rearrange('p a b c -> p (a b c)')[:, 0:1024]])
            if ps_ in (1, 5):
                mark(f'p{ps_}_kv')
            RM.release(mScan)
            decp = RM.alloc([NCH, 2], F32)
            P.copy('vector', decp[0:64], DEC[0:64, :, :, l0])
            P.copy('vector', decp[64:128], DEC[64:128, :, :, l0 + 1])
            S32 = [RM.alloc([130], F32) for _ in range(2)]
            Sbf = [RM.alloc([130], BF16) for _ in range(2)]
            stmp = RM.alloc([130], F32)
            ecol = RM.alloc([2], F32)
            Vts = [RM.alloc([2, 65], BF16) for _ in range(4)]
            dn = RM.alloc([2, 8], F32)
            P.memset('gpsimd', stmp, 0.0)
            for d_ in range(2):
                P.memset('gpsimd', S32[d_], 0.0)
                P.copy('vector', ecol[0:64, d_:d_ + 1], expR[0:64, d_, l0:l0 + 1])
                P.copy('vector', ecol[64:128, d_:d_ + 1], expR[64:128, d_, l0 + 1:l0 + 2])
            PTs = [RM.alloc([2, 128], BF16) for _ in range(4)]
            pt_banks = [[0, 6], [3, 7]]

            def scan_front(step, d_):
                c = order[d_][step]
                tk = slice(c * 128, (c + 1) * 128)
                Vt = Vts[2 * (step % 2) + d_]
                P.tt('vector', Vt[:, :, 0:64], Vp[:, c], bc(BB[:, c, d_, l0:l0 + 2].unsqueeze(2), [128, 2, 64]), ALU.mult)
                P.copy('scalar', Vt[:, :, 64], BB[:, c, d_, l0:l0 + 2])
                ptp = pb(pt_banks[d_][step % 2], [2, 128])
                prev_mm = None
                for h in range(2):
                    hb = slice(64 * h, 64 * h + 64)
                    prev_mm = P.mm(ptp[:, h, :], kT[hb, tk], qT[hb, tk], after=prev_mm)
                PT = PTs[2 * (step % 2) + d_]
                P.tt('vector', PT, ptp, bc((maskF_b if d_ == 0 else maskB_b).unsqueeze(1), [128, 2, 128]), ALU.mult)

            def scan_kv(step, d_):
                c = order[d_][step]
                Vt = Vts[2 * (step % 2) + d_]
                kvp = pb(2 if d_ == 0 else 5, [130])
                P.mm(kvp, Kt[:, c, :], Vt.rearrange("p h e -> p (h e)"))

            def scan_back(step, d_):
                c = order[d_][step]
                tk = slice(c * 128, (c + 1) * 128)
                S = S32[d_]
                Vt = Vts[2 * (step % 2) + d_]
                PT = PTs[2 * (step % 2) + d_]
                if step == 2:
                    r0 = 64 * d_
                    P.copy('scalar', stmp[0:64, 0:65], Sacc[r0:r0 + 64, l0, :])
                    P.copy('scalar', stmp[64:128, 65:130], Sacc[r0:r0 + 64, l0 + 1, :])
                    P.stt('vector', S, S, ecol[:, d_:d_ + 1], stmp, ALU.mult, ALU.add)
                if step > 0:
                    P.act(Sbf[d_], S, AF.Identity, scale=decp[:, c, d_:d_ + 1])
                ops_ = pb(1 if d_ == 0 else 4, [2, 65])
                for h in range(2):
                    hb = slice(64 * h, 64 * h + 64)
                    P.mm(ops_[:, h, :], PT[:, h, :], Vt[:, h, :], start=True, stop=(step == 0))
                    if step > 0:
                        P.mm(ops_[:, h, :], qT[hb, tk], Sbf[d_][hb, 65 * h:65 * h + 65], start=False, stop=True)
                for h in range(2):
                    at = AA[:, c, d_, l0 + h:l0 + h + 1]
                    if is_ml:
                        dd = dn[:, h, 4 * d_:4 * d_ + 4]
                        P.act(dd[:, 0:1], ops_[:, h, 64:65], AF.Abs, scale=at)
                        P.ts('vector', dd[:, 1:2], dd[:, 0:1], 1.0, None, ALU.max)
                        P.recip(dd[:, 2:3], dd[:, 1:2])
                        P.tt('vector', dd[:, 3:4], dd[:, 2:3], at, ALU.mult)
                        coef = dd[:, 3:4]
                    else:
                        coef = at
                    if first_dir[c] == d_:
                        P.act(Oacc[:, c, h, :], ops_[:, h, 0:64], AF.Identity, scale=coef)
                    else:
                        P.stt('vector', Oacc[:, c, h, :], ops_[:, h, 0:64], coef, Oacc[:, c, h, :], ALU.mult, ALU.add)
                kvp = pb(2 if d_ == 0 else 5, [130])
                if step == 0:
                    P.copy('vector', S, kvp)
                else:
                    P.stt('vector', S, S, decp[:, c, d_:d_ + 1], kvp, ALU.mult, ALU.add)

            for d_ in range(2):
                scan_front(0, d_)
                scan_kv(0, d_)
            for step in range(NCH):
                if step + 1 < NCH:
                    for d_ in range(2):
                        scan_front(step + 1, d_)
                for d_ in range(2):
                    scan_back(step, d_)
                if step + 1 < NCH:
                    for d_ in range(2):
                        scan_kv(step + 1, d_)
                if ps_ == 0 and step < 3:
                    ck('E2' + 'abc'[step], [Oacc.rearrange('p a b c -> p (a b c)')[:, 0:256], S32[0], S32[1]])
            if ps_ == 0:
                ck('E2', [Oacc.rearrange('p a b c -> p (a b c)')[:, 0:1024], Oacc.rearrange('p a b c -> p (a b c)')[:, 1024:2048]])
            if ps_ == 4:
                ck('F2', [Oacc.rearrange('p a b c -> p (a b c)')[:, 0:1024], Oacc.rearrange('p a b c -> p (a b c)')[:, 1024:2048]])
            if ps_ in (1, 5):
                mark(f'p{ps_}_scan')
            RM.release(mScan)
            sq = RM.alloc([NCH, 2, 64], F32)
            ms = RM.alloc([NCH, 2], F32)
            On = RM.alloc([NCH, 2, 64], BF16)
            P.act(sq, Oacc, AF.Square)
            P.reduce('vector', ms, sq, ALU.add)
            P.act(ms, ms, AF.Sqrt, bias=epsb, scale=1.0 / 64)
            P.recip(ms, ms)
            P.tt('vector', On, Oacc, bc(ms.unsqueeze(3), [128, NCH, 2, 64]), ALU.mult)
            nwi = (36 if is_ml else 32) + j
            for c4 in range(0, NCH, 4):
                tpo = pb(4, [4, 128], BF16, off=(c4 // 4 % 2) * 1024)
                for ci in range(4):
                    P.transpose(tpo[:, ci, :], On[:, c4 + ci].rearrange("p h d -> p (h d)"), identB)
                P.stt('vector', MT[:, ps_, c4 * 128:(c4 + 4) * 128], tpo.rearrange("p c t -> p (c t)"), colB[:, nwi:nwi + 1],
                      gT[:, c4 * 128:(c4 + 4) * 128], ALU.mult, ALU.mult)
        ck('E4', [MT[:, 0, 0:1024], MT[:, 7, 0:1024]])
        RM.release(mMF2)
        RX.release(0)
        mark('passes')
        X1 = RX.alloc([NCH, D], F32)
        xsp = RX.alloc([512], F32)
        Wo = RM.alloc([8, D], BF16)
        P.dma('gpsimd', Wo, ab_w_out[0].rearrange("(kc p) n -> p kc n", p=128))
        otmp = [RM.alloc([512], F32) for _ in range(2)]
        for c in range(NCH):
            P.dma('sync', X1[:, c, :], xw[c])
        for c in range(NCH):
            gi = 2 if c < 2 else 0
            for hf in range(2):
                po = pb(1 + hf, [512])
                for kc in range(8):
                    P.mm(po, MT[:, kc, c * 128:(c + 1) * 128], Wo[:, kc, hf * 512:(hf + 1) * 512], start=(kc == 0), stop=(kc == 7))
                P.tt('vector', otmp[hf], po, gB[:, gi, hf * 512:(hf + 1) * 512], ALU.mult)
                P.tt('vector', X1[:, c, hf * 512:(hf + 1) * 512], X1[:, c, hf * 512:(hf + 1) * 512], otmp[hf], ALU.add)
        RM.release(mMF)
        if dbg == 'l0m':
            for c in range(NCH):
                P.dma('sync', dbg_out[c], X1[:, c, :], store=True)

        mark('outproj0')
        def ffn(l, chunks, wm_sel, g_sel):
            m0 = RM.mark()
            w_in = ffn_w_in[l].rearrange("(kc p) n -> p kc n", p=128)
            w_out = ffn_w_out[l].rearrange("(f p) n -> p f n", p=128)
            Wout = RM.alloc([NFT, D], BF16)
            P.dma('gpsimd', Wout[:, 0:11, :], w_out[:, 0:11, :])
            P.dma('gpsimd', Wout[:, 11:22, :], w_out[:, 11:22, :])
            AT = RM.alloc([NFT, 512], BF16)
            wbuf = [RM.alloc([8, 2, 256], BF16) for _ in range(2)]
            h2 = RM.alloc([8, 512], BF16)
            o_ = xsp
            nq = len(chunks) // 4

            def prep(qi, i):
                c = chunks[4 * qi + i]
                v_ = wm_sel(c)
                make_hT(X1[:, c, :], h2[:, :, i * 128:(i + 1) * 128], WM[:, l, 1, v_, :], SHc[:, l, 1, v_, :])

            for i in range(4):
                prep(0, i)
            bi = 0
            for qi in range(nq):
                cq = chunks[4 * qi:4 * qi + 4]
                for fp in range(11):
                    wb = wbuf[bi % 2]
                    bi += 1
                    P.dma('gpsimd', wb[:, :, 0, :], w_in[:, :, fp * 256:(fp + 1) * 256])
                    P.dma('gpsimd', wb[:, :, 1, :], w_in[:, :, D_FF + fp * 256:D_FF + (fp + 1) * 256])
                    for f2 in range(2):
                        f = fp * 2 + f2
                        pgm = pb(1 + 2 * (f % 2), [512])
                        pum = pb(2 + 2 * (f % 2), [512])
                        for kc in range(8):
                            P.mm(pgm, wb[:, kc, 0, f2 * 128:(f2 + 1) * 128], h2[:, kc, :], start=(kc == 0), stop=(kc == 7))
                        for kc in range(8):
                            P.mm(pum, wb[:, kc, 1, f2 * 128:(f2 + 1) * 128], h2[:, kc, :], start=(kc == 0), stop=(kc == 7))
                        P.act(AT[:, f, :], pgm, AF.Silu)
                        P.tt('vector', AT[:, f, :], AT[:, f, :], pum, ALU.mult)
                k_ = 0
                for i, c in enumerate(cq):
                    for hf in range(2):
                        po = pb(5 + (k_ % 3), [512])
                        k_ += 1
                        for f in range(NFT):
                            P.mm(po, AT[:, f, i * 128:(i + 1) * 128], Wout[:, f, hf * 512:(hf + 1) * 512], start=(f == 0), stop=(f == NFT - 1))
                        P.tt('vector', o_, po, gB[:, g_sel(c), hf * 512:(hf + 1) * 512], ALU.mult)
                        P.tt('vector', X1[:, c, hf * 512:(hf + 1) * 512], X1[:, c, hf * 512:(hf + 1) * 512], o_, ALU.add)
                    if qi + 1 < nq:
                        prep(qi + 1, i)
            RM.release(m0)

        if dbg != 'l0m':
            ffn(0, list(range(NCH)), lambda c: 1 if c < 2 else 0, lambda c: 3 if c < 2 else 1)
        if dbg == 'l0':
            for c in range(NCH):
                P.dma('sync', dbg_out[c], X1[:, c, :], store=True)

        mark('ffn0')
        if dbg in (None, 'l1m'):
            modulation(1)
            make_wm(1)
            m1 = RM.mark()
            awin = at_w_in[0].rearrange("(kc p) n -> p kc n", p=128)
            kT1 = RM.alloc([2, NCH * 128], BF16)
            Vx1 = RM.alloc([NCH, 4, 65], BF16)
            P.memset('vector', Vx1[:, :, :, 64:65], 1.0)
            qnb = RM.alloc([64], F32)
            knb = RM.alloc([64], F32)
            P.dma('sync', qnb, bc(at_qn, [128, 64]))
            P.dma('sync', knb, bc(at_kn, [128, 64]))
            snk = RM.alloc([16], F32)
            P.dma('sync', snk, bc(at_sink, [128, 16]))
            rT = RM.alloc([2, NW, 32], F32)
            P.dma('sync', rT.rearrange("p a s f -> p a (s f)"), ropeT.rearrange("a p n -> p a n"))
            amb = RM.alloc([4, 4, 128], BF16)
            sm = RM.alloc([8], F32)
            sinkexp = RM.alloc([16], F32)
            hTc = [RM.alloc([8, 128], BF16)] * 2
            nms = RM.alloc([8], F32)
            qn = RM.alloc([8, 64], F32)
            nsq = qn
            qr = RM.alloc([2, 8, 32], F32)
            qtk = RM.alloc([16, 64], BF16)
            qtk2 = RM.alloc([16, 64], BF16)
            m2 = RM.mark()
            amf = RM.alloc([4, 128], F32)
            absq = RM.alloc([128], F32)
            P.dma('sync', amf, amask.rearrange("c p n -> p c n"))
            P.ts('vector', amf, amf, 1.0, 30000.0, ALU.subtract, ALU.mult)
            for v4 in range(4):
                P.copy('vector', amb[:, v4], bc(amf[:, v4, :].unsqueeze(1), [128, 4, 128]))
            P.act(absq[:, 0:64], qnb, AF.Abs)
            P.act(absq[:, 64:128], knb, AF.Abs)
            P.reduce('vector', sm[:, 0:1], absq[:, 0:64], ALU.max)
            P.reduce('vector', sm[:, 1:2], absq[:, 64:128], ALU.max)
            P.tt('vector', sm[:, 2:3], sm[:, 0:1], sm[:, 1:2], ALU.mult)
            P.ts('vector', sm[:, 3:4], sm[:, 2:3], -8.0, None, ALU.mult)
            negB = sm[:, 3:4]
            P.act(sinkexp, snk, AF.Exp, bias=negB)
            RM.release(m2)
            Wkv1 = RM.alloc([8, 512], BF16)
            P.dma('gpsimd', Wkv1, awin[:, :, 1024:1536])

            def qknorm_rope(ps3, nh, wb_, wch, dst):
                P.act(nsq[:, 0:nh, :], ps3, AF.Square)
                P.reduce('vector', nms[:, 0:nh], nsq[:, 0:nh, :], ALU.add)
                P.act(nms[:, 0:nh], nms[:, 0:nh], AF.Sqrt, bias=epsb, scale=1.0 / 64)
                P.recip(nms[:, 0:nh], nms[:, 0:nh])
                P.tt('vector', qn[:, 0:nh, :], ps3, bc(nms[:, 0:nh].unsqueeze(2), [128, nh, 64]), ALU.mult)
                if wch < 0:
                    P.tt('vector', dst, qn[:, 0:nh, :], bc(wb_.unsqueeze(1), [128, nh, 64]), ALU.mult)
                    return
                P.tt('vector', qn[:, 0:nh, :], qn[:, 0:nh, :], bc(wb_.unsqueeze(1), [128, nh, 64]), ALU.mult)
                x1, x2 = qn[:, 0:nh, 0:32], qn[:, 0:nh, 32:64]
                cs = bc(rT[:, 0, wch, :].unsqueeze(1), [128, nh, 32])
                sn = bc(rT[:, 1, wch, :].unsqueeze(1), [128, nh, 32])
                P.tt('vector', qr[:, 0, 0:nh], x1, cs, ALU.mult)
                P.tt('vector', qr[:, 1, 0:nh], x2, sn, ALU.mult)
                P.tt('vector', dst[:, :, 0:32], qr[:, 0, 0:nh], qr[:, 1, 0:nh], ALU.subtract)
                P.tt('vector', qr[:, 0, 0:nh], x2, cs, ALU.mult)
                P.tt('vector', qr[:, 1, 0:nh], x1, sn, ALU.mult)
                P.tt('vector', dst[:, :, 32:64], qr[:, 0, 0:nh], qr[:, 1, 0:nh], ALU.add)

            for c in range(NCH):
                v_ = 1 if c < 2 else 0
                hc_ = hTc[c % 2]
                make_hT(X1[:, c, :], hc_, WM[:, 1, 0, v_, :], SHc[:, 1, 0, v_, :])
                pkv = pb(1 + (c % 2), [512])
                for kc in range(8):
                    P.mm(pkv, hc_[:, kc, :], Wkv1[:, kc, :], start=(kc == 0), stop=(kc == 7))
                P.copy('scalar', Vx1[:, c, :, 0:64], pkv[:, 256:512].rearrange("p (h d) -> p h d", h=4))
                qknorm_rope(pkv[:, 0:256].rearrange("p (h d) -> p h d", h=4), 4, knb, (c - 2) if c >= 2 else -1, qtk[:, 0:4, :])
                tpk = pb(4, [2, 128], BF16)
                for t_ in range(2):
                    P.transpose(tpk[:, t_, :], qtk[:, 2 * t_:2 * t_ + 2, :].rearrange("p h d -> p (h d)"), identB)
                P.copy('scalar', kT1[:, :, c * 128:(c + 1) * 128], tpk)
            mark('l1kv')
            RM.release(m2)
            Wq1 = RM.alloc([8, 1024], BF16)
            P.dma('gpsimd', Wq1, awin[:, :, 0:1024])
            Wo1 = RM.alloc([8, D], BF16)
            P.dma('gpsimd', Wo1, at_w_out[0].rearrange("(kc p) n -> p kc n", p=128))
            qTc = RM.alloc([2, 4, 128], BF16)
            Eb = [RM.alloc([5, 4, 128], BF16) for _ in range(2)]
            Otk = RM.alloc([16, 64], BF16)
            OT = RM.alloc([8, 128], BF16)
            dn1 = RM.alloc([8], F32)
            ot1 = [xsp, RM.alloc([512], F32)]
            ei = 0
            for i in range(16):
                c = i + 3
                hc_ = hTc[i % 2]
                make_hT(X1[:, c, :], hc_, WM[:, 1, 0, 0, :], SHc[:, 1, 0, 0, :])
                for hf in range(2):
                    pq = pb(2 + hf, [512])
                    for kc in range(8):
                        P.mm(pq, hc_[:, kc, :], Wq1[:, kc, hf * 512:(hf + 1) * 512], start=(kc == 0), stop=(kc == 7))
                    qknorm_rope(pq.rearrange("p (h d) -> p h d", h=8), 8, qnb, c - 2, qtk[:, 8 * hf:8 * hf + 8, :])
                tpq = pb(4, [2, 4, 128], BF16)
                for tp_ in range(2):
                    P.copy('scalar', qtk2[:, 8 * tp_:8 * tp_ + 8, :].rearrange("p (j g) d -> p g j d", g=2),
                           qtk[:, 8 * tp_:8 * tp_ + 8, :].rearrange("p (g j) d -> p g j d", g=2))
                    for j in range(4):
                        src = qtk2[:, 8 * tp_ + 2 * j:8 * tp_ + 2 * j + 2, :]
                        P.transpose(tpq[:, tp_, j, :], src.rearrange("p h d -> p (h d)"), identB)
                P.copy('scalar', qTc, tpq)
                kblocks = [(c - 1, 0 if i == 0 else 1), (c, None), (c + 1, 3 if i == 15 else 2), (0, None), (1, None)]
                st_banks = [[5, 6], [0, 2]]

                def att_front(g):
                    tp_, hb = g // 2, slice(64 * (g % 2), 64 * (g % 2) + 64)
                    E = Eb[g % 2]
                    for bi_, (kc_, mk) in enumerate(kblocks):
                        pst = pb(st_banks[g % 2][bi_ % 2], [4, 128])
                        P.mm(pst.rearrange("p j t -> p (j t)"), kT1[hb, tp_, kc_ * 128:(kc_ + 1) * 128],
                             qTc[hb, tp_].rearrange("p j t -> p (j t)"), start=True, stop=(mk is None))
                        if mk is not None:
                            P.mm(pst.rearrange("p j t -> p (j t)"), identB, amb[:, mk].rearrange("p j t -> p (j t)"), start=False, stop=True)
                        P.act(E[:, bi_], pst, AF.Exp, bias=negB, scale=0.125)

                def att_back(g):
                    E = Eb[g % 2]
                    pso = pb(7 if g % 2 == 0 else 3, [4, 65])
                    for j in range(4):
                        for bi_, (kc_, mk) in enumerate(kblocks):
                            P.mm(pso[:, j, :], E[:, bi_, j, :], Vx1[:, kc_, g, :], start=(bi_ == 0), stop=(bi_ == 4))
                    P.tt('vector', dn1[:, 0:4], pso[:, :, 64], sinkexp[:, 4 * g:4 * g + 4], ALU.add)
                    P.recip(dn1[:, 4:8], dn1[:, 0:4])
                    P.tt('vector', Otk[:, 4 * g:4 * g + 4, :], pso[:, :, 0:64], bc(dn1[:, 4:8].unsqueeze(2), [128, 4, 64]), ALU.mult)

                att_front(0)
                for g in range(4):
                    if g + 1 < 4:
                        att_front(g + 1)
                    att_back(g)
                tpo = pb(4, [8, 128], BF16)
                for kc in range(8):
                    P.transpose(tpo[:, kc, :], Otk[:, 2 * kc:2 * kc + 2, :].rearrange("p h d -> p (h d)"), identB)
                P.copy('scalar', OT, tpo)
                for hf in range(2):
                    po = pb(1 + hf, [512])
                    for kc in range(8):
                        P.mm(po, OT[:, kc, :], Wo1[:, kc, hf * 512:(hf + 1) * 512], start=(kc == 0), stop=(kc == 7))
                    P.tt('vector', ot1[hf], po, gB[:, 0, hf * 512:(hf + 1) * 512], ALU.mult)
                    P.tt('vector', X1[:, c, hf * 512:(hf + 1) * 512], X1[:, c, hf * 512:(hf + 1) * 512], ot1[hf], ALU.add)
            mark('l1attn')
            RM.release(m1)
            if dbg == 'l1m':
                for c in range(NCH):
                    P.dma('sync', dbg_out[c], X1[:, c, :], store=True)
            else:
                ffn(1, list(range(3, 19)), lambda c: 0, lambda c: 1)
        for i in range(16):
            P.dma('sync', y[i * 128:(i + 1) * 128, :], X1[:, i + 3, :], store=True)
        mark('end')
        stats = P.emit()
        stats['marks'] = P.marks
        stats['peaks'] = (RP.peak, RX.peak, RM.peak)
    return nc, stats


def _rope_angles(pos):
    pos = np.asarray(pos)
    row = (pos // 64).astype(np.float32)
    col = (pos % 64).astype(np.float32)
    inv = (10000.0 ** (-np.arange(16, dtype=np.float32) / 16)).astype(np.float32)
    return np.concatenate([row[:, None] * inv, col[:, None] * inv], axis=-1).astype(np.float32)


def _core_inputs(inp, core):
    b, q = core // 4, core % 4
    x = np.asarray(inp['x'], np.float32)
    ctx = np.asarray(inp['ctx'], np.float32)
    xb = x[b].reshape(64, 128, D)
    fl = np.zeros((1, NF), np.float32)
    xw = np.zeros((NCH, 128, D), np.float32)
    xw[0:2] = ctx[b].reshape(2, 128, D)
    fl[0, FL_VF:FL_VF + 2] = 1.0
    wpos = np.full((NW, 128), -1, np.int64)
    for w in range(NW):
        ch = 16 * q - 1 + w
        if 0 <= ch < 64:
            xw[2 + w] = xb[ch]
            fl[0, FL_VF + 2 + w] = 1.0
            wpos[w] = ch * 128 + np.arange(128)
    xe = np.zeros((128, D), np.float32)
    tl = 128 * (16 * q - 1) - 1
    tr = 128 * (16 * q + 17)
    if 0 <= tl < SEQ:
        xe[0] = x[b, tl]
        fl[0, FL_EDGE] = 1.0
    if 0 <= tr < SEQ:
        xe[1] = x[b, tr]
        fl[0, FL_EDGE + 1] = 1.0
    slots = [(ch, 0) for ch in range(16 * q - 2, -1, -1)] + [(ch, 1) for ch in range(16 * q + 17, 64)]
    assert len(slots) <= NSLOT
    xs = np.zeros((NSLOT, 128, D), np.float32)
    xsh = np.zeros((128, D), np.float32)
    spos = np.zeros((NSLOT, 128), np.int64)
    for s, (ch, d_) in enumerate(slots):
        xs[s] = xb[ch]
        fl[0, (FL_FF if d_ == 0 else FL_FB) + s] = 1.0
        spos[s] = ch * 128 + np.arange(128)
        t0, t1 = ch * 128 - 1, ch * 128 + 128
        if t0 >= 0:
            xsh[2 * s] = x[b, t0]
            fl[0, FL_SH + 2 * s] = 1.0
        if t1 < SEQ:
            xsh[2 * s + 1] = x[b, t1]
            fl[0, FL_SH + 2 * s + 1] = 1.0
    wp = np.where(wpos < 0, 0, wpos).reshape(-1)
    ang = _rope_angles(wp)
    cosw, sinw = np.cos(ang), np.sin(ang)
    ropeT = np.stack([cosw.reshape(NW, 128, 32).transpose(1, 0, 2).reshape(128, NW * 32),
                      sinw.reshape(NW, 128, 32).transpose(1, 0, 2).reshape(128, NW * 32)]).astype(np.float32)
    p = np.arange(128)
    fi = p % 32
    cosF = cosw[:, fi].T
    sgn = np.where((p % 64) < 32, 1.0, -1.0)[:, None]
    sinF = sinw[:, fi].T * sgn
    ropeF = np.stack([cosF, sinF]).astype(np.float32)
    angs = _rope_angles(spos.reshape(-1))
    ropeS = np.stack([np.cos(angs).reshape(NSLOT, 128, 32).transpose(1, 0, 2).reshape(128, NSLOT * 32),
                      np.sin(angs).reshape(NSLOT, 128, 32).transpose(1, 0, 2).reshape(128, NSLOT * 32)]).astype(np.float32)
    u = np.arange(128)[:, None]
    s_ = np.arange(128)[None, :]
    consts = np.zeros((7, 128, 128), np.float32)
    consts[0] = np.eye(128)
    consts[1] = (u > s_)
    consts[2] = (u < s_)
    consts[3] = (u <= s_)
    consts[4] = (u >= s_)
    consts[5] = 1.0
    amask = np.zeros((4, 128, 128), np.float32)
    mL = (u >= s_).astype(np.float32)
    mR = (u <= s_).astype(np.float32)
    amask[0] = mL * (1.0 if q > 0 else 0.0)
    amask[1] = mL
    amask[2] = mR
    amask[3] = mR * (1.0 if q < 3 else 0.0)
    cvec = np.concatenate([np.asarray(inp['c'], np.float32)[b].reshape(8, 128),
                           np.asarray(inp['c_ctx'], np.float32).reshape(8, 128)], 0)
    m = dict(xw=xw, xe=xe, xs=xs, xsh=xsh, fl=fl, cvec=cvec, consts=consts, amask=amask,
             ropeF=ropeF, ropeT=ropeT, ropeS=ropeS)
    for k in ('ada_w', 'ada_b', 'norm_w', 'ffn_w_in', 'ffn_w_out', 'ab_w_in', 'ab_w_out', 'ret_log_gamma',
              'ret_norm_w', 'mlstm_conv_w', 'mlstm_conv_b', 'mlstm_gate_b', 'mlstm_norm_w', 'attn_w_in',
              'attn_w_out', 'attn_q_norm_w', 'attn_k_norm_w', 'attn_sink'):
        m[k] = np.ascontiguousarray(np.asarray(inp[k], np.float32))
    return m


_NC_CACHE = {}


def kernel(**inp):
    if 'nc' not in _NC_CACHE:
        _NC_CACHE['nc'] = build()[0]
    nc = _NC_CACHE['nc']
    in_maps = [_core_inputs(inp, c) for c in range(NCORE)]
    res = run_bass_kernel_spmd(nc, in_maps, core_ids=list(range(NCORE)))
    out = np.zeros((2, SEQ, D), np.float32)
    for c in range(NCORE):
        b, q = c // 4, c % 4
        out[b, 2048 * q:2048 * (q + 1)] = res.results[c]["y"]
    return out
```

```python
import contextlib
import math
import numpy as np
import concourse.bass as bass
import concourse.mybir as mybir
from concourse.bass_utils import run_bass_kernel_spmd

F32 = mybir.dt.float32
BF16 = mybir.dt.bfloat16
ALU = mybir.AluOpType
AF = mybir.ActivationFunctionType
AX = mybir.AxisListType

_DTSZ = {F32: 4, BF16: 2}
SEM_LIMIT = 12000
DMA_POOL = 12


def _region(ap):
    sp = str(ap.space).upper()
    if 'SB' not in sp and 'PSUM' not in sp:
        return None
    if 'PSUM' in sp:
        return (ap.name, 0, 128, 0, 2048)
    pat = ap.ap
    esz = _DTSZ[ap.dtype]
    pstep, pcount = pat[0]
    off = ap.offset
    if pstep == 0:
        p0 = 0
        f0 = off
    else:
        p0 = off // pstep
        f0 = off - p0 * pstep
    ext = 0
    for stp, cn in pat[1:]:
        ext += abs(stp) * (cn - 1)
    return (ap.name, p0, p0 + pcount, f0 * esz, (f0 + ext + 1) * esz)


class Prog:
    ENGS = ('tensor', 'vector', 'scalar', 'gpsimd', 'sync')

    def __init__(self, nc, same_engine_sync=True):
        self.nc = nc
        self.ops = []
        self.track = {}
        self.same_engine_sync = same_engine_sync
        self.dma_hist = {e: [] for e in self.ENGS}
        self.store_ops = []

    def _add(self, eng, fn, outs, ins, is_dma=False, extra_deps=(), force=False):
        if getattr(self, 'frozen', False) and not force:
            return -1
        idx = len(self.ops)
        deps = set(extra_deps)
        self.ops.append(dict(eng=eng, fn=fn, deps=deps, is_dma=is_dma, signaled=False))
        for ap in ins:
            r = _region(ap)
            if r is not None:
                self._access(idx, eng, r, False, deps)
        for ap in outs:
            r = _region(ap)
            if r is not None:
                self._access(idx, eng, r, True, deps)
        if is_dma:
            h = self.dma_hist[eng]
            if len(h) >= DMA_POOL:
                deps.add(h[-DMA_POOL])
            h.append(idx)
        deps.discard(idx)
        return idx

    def _access(self, idx, eng, r, is_write, deps):
        name, p0, p1, b0, b1 = r
        recs = self.track.get(name, [])
        keep = []
        for rec in recs:
            (q0, q1, c0, c1, oi, ow, oe) = rec
            if q1 <= p0 or p1 <= q0 or c1 <= b0 or b1 <= c0 or oi == idx:
                keep.append(rec)
                continue
            if is_write or ow or (name.startswith('pb') and oe != eng):
                pe_pe = (eng == 'tensor' and oe == 'tensor')
                if not pe_pe:
                    deps.add(oi)
            covered = (p0 <= q0 and q1 <= p1 and b0 <= c0 and c1 <= b1)
            if is_write and covered and not (eng == 'tensor' and oe == 'tensor' and not ow):
                continue
            if (not is_write) and (not ow) and oe == eng and covered and not self.ops[oi]['is_dma']:
                continue
            keep.append(rec)
        keep.append((p0, p1, b0, b1, idx, is_write, eng))
        self.track[name] = keep

    def op(self, eng, fn, outs, ins):
        return self._add(eng, fn, outs, ins)

    def dma(self, eng, out, in_, store=False, **kw):
        i = self._add(eng, lambda e: e.dma_start(out=out, in_=in_, **kw), [out], [in_], is_dma=True)
        if store and i >= 0:
            self.store_ops.append(i)
        return i

    def mm(self, out, lhsT, rhs, start=True, stop=True, after=None):
        i = self.op('tensor', lambda e: e.matmul(out, lhsT, rhs, start=start, stop=stop), [out], [lhsT, rhs])
        if after is not None and i >= 0 and after >= 0:
            self.ops[i]['deps'].add(after)
        return i

    def transpose(self, out, in_, ident):
        return self.op('tensor', lambda e: e.transpose(out, in_, ident), [out], [in_, ident])

    def act(self, out, in_, func, bias=None, scale=None, accum_out=None):
        kw = {}
        ins = [in_]
        outs = [out]
        if bias is not None:
            kw['bias'] = bias
            if not isinstance(bias, (int, float)):
                ins.append(bias)
        if scale is not None:
            kw['scale'] = scale
            if not isinstance(scale, (int, float)):
                ins.append(scale)
        if accum_out is not None:
            kw['accum_out'] = accum_out
            outs.append(accum_out)
        return self.op('scalar', lambda e: e.activation(out, in_, func, **kw), outs, ins)

    def tt(self, eng, out, in0, in1, op):
        return self.op(eng, lambda e: e.tensor_tensor(out, in0, in1, op), [out], [in0, in1])

    def ts(self, eng, out, in0, s1, s2, op0, op1=None):
        ins = [in0] + [s for s in (s1, s2) if s is not None and not isinstance(s, (int, float))]
        kw = {}
        if op1 is not None:
            kw['op1'] = op1
        return self.op(eng, lambda e: e.tensor_scalar(out, in0, s1, s2, op0, **kw), [out], ins)

    def stt(self, eng, out, in0, scalar, in1, op0, op1):
        ins = [in0, in1] + ([scalar] if not isinstance(scalar, (int, float)) else [])
        return self.op(eng, lambda e: e.scalar_tensor_tensor(out, in0, scalar, in1, op0, op1), [out], ins)

    def copy(self, eng, out, in_):
        if eng == 'scalar':
            return self.op(eng, lambda e: e.copy(out, in_), [out], [in_])
        return self.op(eng, lambda e: e.tensor_copy(out, in_), [out], [in_])

    def memset(self, eng, out, val):
        return self.op(eng, lambda e: e.memset(out, val), [out], [])

    def recip(self, out, in_):
        return self.op('vector', lambda e: e.reciprocal(out, in_), [out], [in_])

    def reduce(self, eng, out, in_, op, axis=AX.X):
        return self.op(eng, lambda e: e.tensor_reduce(out, in_, axis, op), [out], [in_])

    def emit(self):
        nc = self.nc
        ops = self.ops
        self._add('sync', None, [], [], extra_deps=self.store_ops, force=True)
        for o in ops:
            if o['is_dma']:
                o['signaled'] = True
            for d in o['deps']:
                ops[d]['signaled'] = True
        cnt = {e: 0 for e in self.ENGS}
        dcnt = {e: 0 for e in self.ENGS}
        nsem_eng = {e: 0 for e in self.ENGS}
        for o in ops:
            e = o['eng']
            if not o['signaled']:
                continue
            if o['is_dma']:
                k = dcnt[e]
                dcnt[e] += 1
                o['sem'] = ('d', e, k % DMA_POOL)
                o['val'] = 16 * (k // DMA_POOL + 1)
                o['sidx'] = None
            else:
                k = cnt[e]
                cnt[e] += 1
                o['sem'] = ('c', e, k // SEM_LIMIT)
                o['val'] = (k % SEM_LIMIT) + 1
                o['sidx'] = k
                nsem_eng[e] = k // SEM_LIMIT + 1
        sems = {}
        st = contextlib.ExitStack()
        for e in self.ENGS:
            for j in range(nsem_eng[e]):
                sems[('c', e, j)] = st.enter_context(nc.semaphore(f"c_{e}_{j}"))
            for j in range(min(DMA_POOL, dcnt[e])):
                sems[('d', e, j)] = st.enter_context(nc.semaphore(f"d_{e}_{j}"))
        seen = {e: {f: -1 for f in self.ENGS} for e in self.ENGS}
        seen_dma = {e: set() for e in self.ENGS}
        per_eng = {e: [] for e in self.ENGS}
        nwaits = 0
        for o in ops:
            e = o['eng']
            waits = {}
            for d in sorted(o['deps']):
                p = ops[d]
                if p['is_dma']:
                    if d in seen_dma[e]:
                        continue
                    seen_dma[e].add(d)
                    waits[p['sem']] = max(waits.get(p['sem'], 0), p['val'])
                else:
                    f = p['eng']
                    if f == e and not self.same_engine_sync:
                        continue
                    if p['sidx'] <= seen[e][f]:
                        continue
                    seen[e][f] = p['sidx']
                    waits[p['sem']] = max(waits.get(p['sem'], 0), p['val'])
            nwaits += len(waits)
            per_eng[e].append((o, list(waits.items())))
        self.stats = dict(n_ops=len(ops), n_waits=nwaits, per_eng={e: len(v) for e, v in per_eng.items()})
        with st, nc.Block() as block:
            def body(engname):
                def run(eng):
                    for o, waits in per_eng[engname]:
                        for key, val in waits:
                            eng.wait_ge(sems[key], val)
                        if o['fn'] is None:
                            continue
                        ins = o['fn'](eng)
                        if o['signaled']:
                            ins.then_inc(sems[o['sem']], 16 if o['is_dma'] else 1)
                return run
            block.tensor(body('tensor'))
            block.vector(body('vector'))
            block.scalar(body('scalar'))
            block.gpsimd(body('gpsimd'))
            block.sync(body('sync'))
        return self.stats


D = 1024
SEQ = 8192
NCORE = 8
NW = 18
NCH = 20
NSLOT = 48
NF = 256
EPS = 1e-6
D_FF = 2816
NFT = 22
LNK = math.log(0.125)
HT_N = 2564
CTX0 = 1
WIN0 = 259
FL_VF = 0
FL_EDGE = 20
FL_FF = 22
FL_FB = 70
FL_SH = 118


def bc(ap, shape):
    return ap.broadcast_to(shape)


class Region:
    def __init__(self, t, base, cap, name):
        self.t, self.base, self.cap, self.name = t, base, cap, name
        self.off = 0
        self.peak = 0

    def alloc(self, shape, dt):
        n = 1
        for s in shape:
            n *= s
        nb = (n * _DTSZ[dt] + 63) // 64 * 64
        if self.off + nb > self.cap:
            raise RuntimeError(f"region {self.name} overflow: {self.off + nb} > {self.cap}")
        a = self.base + self.off
        v = self.t[:, a // 4:(a + nb) // 4]
        if dt != F32:
            v = v.bitcast(dt)
        v = v[:, 0:n]
        self.off += nb
        self.peak = max(self.peak, self.off)
        if len(shape) == 1:
            return v
        names = [chr(ord('a') + i) for i in range(len(shape))]
        pat = "p (" + " ".join(names) + ") -> p " + " ".join(names)
        return v.rearrange(pat, **{nm: s for nm, s in zip(names, shape)})

    def mark(self):
        return self.off

    def release(self, m):
        self.off = m


class _Stop(Exception):
    pass


ARENA_BYTES = 212480
P_BYTES = 35328
X_BYTES = 83968


def build(dbg=None):
    nc = bass.Bass("TRN2", target_bir_lowering=False)

    def din(name, shape):
        return nc.dram_tensor(name, list(shape), F32, kind="ExternalInput").ap()

    xw = din("xw", [NCH, 128, D])
    xe = din("xe", [128, D])
    xs = din("xs", [NSLOT, 128, D])
    xsh = din("xsh", [128, D])
    fl = din("fl", [1, NF])
    cvec = din("cvec", [16, 128])
    ada_w = din("ada_w", [2, D, 6 * D])
    ada_b = din("ada_b", [2, 6 * D])
    norm_w = din("norm_w", [2, 2, D])
    ffn_w_in = din("ffn_w_in", [2, D, 2 * D_FF])
    ffn_w_out = din("ffn_w_out", [2, D_FF, D])
    ab_w_in = din("ab_w_in", [1, D, 4128])
    ab_w_out = din("ab_w_out", [1, D, D])
    ret_lg = din("ret_log_gamma", [1, 2, 8])
    ret_nw = din("ret_norm_w", [1, 512])
    conv_w = din("mlstm_conv_w", [1, 3, D])
    conv_b = din("mlstm_conv_b", [1, D])
    gate_b = din("mlstm_gate_b", [1, 4, 8])
    ml_nw = din("mlstm_norm_w", [1, 512])
    at_w_in = din("attn_w_in", [1, D, 1536])
    at_w_out = din("attn_w_out", [1, D, D])
    at_qn = din("attn_q_norm_w", [1, 64])
    at_kn = din("attn_k_norm_w", [1, 64])
    at_sink = din("attn_sink", [1, 16])
    consts = din("consts", [7, 128, 128])
    amask = din("amask", [4, 128, 128])
    ropeF = din("ropeF", [2, 128, 2304])
    ropeT = din("ropeT", [2, 128, NW * 32])
    ropeS = din("ropeS", [2, 128, NSLOT * 32])
    y = nc.dram_tensor("y", [2048, D], F32, kind="ExternalOutput").ap()
    dbg_out = None
    if dbg is not None:
        dbg_out = nc.dram_tensor("dbg", [NCH, 128, D], F32, kind="ExternalOutput").ap()

    st = contextlib.ExitStack()
    with st:
        arena_t = st.enter_context(nc.sbuf_tensor("arena", [128, ARENA_BYTES // 4], F32))
        RP = Region(arena_t, 0, P_BYTES, "P")
        RX = Region(arena_t, P_BYTES, X_BYTES, "X")
        RM = Region(arena_t, P_BYTES + X_BYTES, ARENA_BYTES - P_BYTES - X_BYTES, "MF")
        banks = [st.enter_context(nc.psum_tensor(f"pb{i}", [128, 512], F32)) for i in range(8)]
        P = Prog(nc)
        P.marks = []

        def mark(nm):
            P.marks.append((nm, sum(1 for o in P.ops if o['eng'] == 'tensor')))

        def ck(name, aps):
            if dbg != name:
                return
            k = 0
            for ap in aps:
                n = ap.shape[1] if len(ap.shape) == 2 else None
                flat = ap
                npart = ap.shape[0]
                P.dma('sync' if flat.dtype == F32 else 'gpsimd', dbg_out[k][0:npart, 0:flat.shape[1]], flat, store=True)
                k += 1
            P.frozen = True

        def pb(i, shape, dt=F32, off=0):
            n = 1
            for s in shape:
                n *= s
            nb = n * _DTSZ[dt]
            v = banks[i][:, off // 4:(off + nb + 3) // 4]
            if dt != F32:
                v = v.bitcast(dt)
            v = v[:, 0:n]
            if len(shape) == 1:
                return v
            names = [chr(ord('a') + k) for k in range(len(shape))]
            pat = "p (" + " ".join(names) + ") -> p " + " ".join(names)
            return v.rearrange(pat, **{nm: s for nm, s in zip(names, shape)})

        cst = RP.alloc([7, 128], F32)
        P.dma('sync', cst, consts.rearrange("c p n -> p c n"))
        identF = cst[:, 0, :]
        MfF, MbF = cst[:, 1, :], cst[:, 2, :]
        onesF = cst[:, 5, :]
        cstb = RP.alloc([7, 128], BF16)
        P.copy('vector', cstb, cst)
        identB = cstb[:, 0, :]
        maskF_b, maskB_b = cstb[:, 3, :], cstb[:, 4, :]
        FL = RP.alloc([NF], F32)
        P.dma('sync', FL, bc(fl, [128, NF]))
        epsb = RP.alloc([1], F32)
        P.memset('vector', epsb, EPS)
        lnk = RP.alloc([1], F32)
        P.memset('vector', lnk, LNK)
        colA = RP.alloc([112], F32)
        colB = RP.alloc([48], F32)
        svf = RP.alloc([16], F32)
        sv = RP.alloc([8, 2], BF16)
        modT = RP.alloc([2, 2, 48], F32)
        gB = RP.alloc([4, D], F32)
        WM = RP.alloc([2, 2, 2, 8], F32)
        SHc = RP.alloc([2, 2, 2, 8], F32)
        junk = RP.alloc([D], BF16)
        xnb = [RP.alloc([D], BF16) for _ in range(2)]
        t32 = RP.alloc([8, 128], F32)
        ssq = RP.alloc([4], F32)

        m0 = RM.mark()
        stg = RM.alloc([128], F32)
        stg2 = RM.alloc([128], F32)
        P.dma('sync', stg[0:16, :], cvec)
        P.dma('sync', stg[16:64, :], ada_b[0].rearrange("(j p) -> j p", p=128))
        P.dma('sync', stg[64:112, :], ada_b[1].rearrange("(j p) -> j p", p=128))
        tp = pb(0, [112])
        P.transpose(tp, stg[0:112, :], identF[0:112, 0:112])
        P.copy('vector', colA, tp)
        P.dma('sync', stg2[0:32, :], norm_w.rearrange("l i (j p) -> (l i j) p", p=128))
        P.dma('sync', stg2[32:36, :], ret_nw[0].rearrange("(j p) -> j p", p=128))
        P.dma('sync', stg2[36:40, :], ml_nw[0].rearrange("(j p) -> j p", p=128))
        P.dma('sync', stg2[40:48, :], conv_b[0].rearrange("(j p) -> j p", p=128))
        tp2 = pb(0, [48], off=1024)
        P.transpose(tp2, stg2[0:48, :], identF[0:48, 0:48])
        P.copy('vector', colB, tp2)
        RM.release(m0)
        ck('A0', [colA, colB, FL, cst[:, 1, :]])
        P.act(svf, colA[:, 0:16], AF.Silu)
        P.copy('vector', sv[:, :, 0], svf[:, 0:8])
        P.copy('vector', sv[:, :, 1], svf[:, 8:16])

        gslot = {(0, 0, 2): 0, (0, 0, 5): 1, (0, 1, 2): 2, (0, 1, 5): 3, (1, 0, 2): 0, (1, 0, 5): 1}

        def modulation(l):
            m0 = RM.mark()
            svrep = RM.alloc([2, 8, 128], BF16)
            for v_ in range(2):
                P.copy('vector', svrep[:, v_, :, :], bc(svf[:, 8 * v_:8 * v_ + 8].unsqueeze(2), [128, 8, 128]))
            wblk = [RM.alloc([8, 1024], BF16) for _ in range(2)]
            bbc = RM.alloc([1024], F32)
            for j in range(6):
                wb = wblk[j % 2]
                P.dma('gpsimd', wb, ada_w[l].rearrange("(kc p) n -> p kc n", p=128)[:, :, j * 1024:(j + 1) * 1024])
                ps = pb(1, [8, 2])
                for n in range(8):
                    for kc in range(8):
                        P.mm(ps[:, n, :], wb[:, kc, n * 128:(n + 1) * 128], sv[:, kc, :], start=(kc == 0), stop=(kc == 7))
                for v_ in range(2):
                    P.tt('vector', modT[:, l, v_, j * 8:(j + 1) * 8], ps[:, :, v_],
                         colA[:, 16 + 48 * l + j * 8:16 + 48 * l + j * 8 + 8], ALU.add)
                if j in (2, 5):
                    P.dma('sync', bbc, bc(ada_b[l:l + 1, j * 1024:(j + 1) * 1024], [128, 1024]))
                    for v_ in range(2):
                        if (l, v_, j) not in gslot:
                            continue
                        for hf in range(2):
                            pg = pb(2 + hf, [512])
                            for kc in range(8):
                                P.mm(pg, svrep[:, v_, kc, :], wb[:, kc, hf * 512:(hf + 1) * 512], start=(kc == 0), stop=(kc == 7))
                            P.tt('vector', gB[:, gslot[(l, v_, j)], hf * 512:(hf + 1) * 512], pg, bbc[:, hf * 512:(hf + 1) * 512], ALU.add)
            RM.release(m0)

        def make_wm(l):
            for i in range(2):
                for v_ in range(2):
                    sc = modT[:, l, v_, (3 * i + 1) * 8:(3 * i + 2) * 8]
                    nw = colB[:, (2 * l + i) * 8:(2 * l + i) * 8 + 8]
                    P.stt('vector', WM[:, l, i, v_, :], sc, 1.0, nw, ALU.add, ALU.mult)
                    P.copy('vector', SHc[:, l, i, v_, :], modT[:, l, v_, (3 * i) * 8:(3 * i) * 8 + 8])

        modulation(0)
        make_wm(0)
        mark('mod0')
        ck('A', [colA, colB, modT.rearrange('p a b c -> p (a b c)'), gB[:, 0, :], gB[:, 3, :], WM.rearrange('p a b c d -> p (a b c d)')])

        cnt_h = [0]

        def make_hT(x_sb, dest, wcol, shcol, flag=None):
            k = cnt_h[0] % 2
            cnt_h[0] += 1
            ss = ssq[:, 2 * k:2 * k + 1]
            rs = ssq[:, 2 * k + 1:2 * k + 2]
            P.act(junk, x_sb, AF.Square, accum_out=ss)
            P.act(rs, ss, AF.Sqrt, bias=epsb, scale=1.0 / D)
            P.recip(rs, rs)
            xn = xnb[k]
            P.act(xn, x_sb, AF.Identity, scale=rs)
            tps = pb(0, [8, 128], BF16)
            for kc in range(8):
                P.transpose(tps[:, kc, :], xn[:, kc * 128:(kc + 1) * 128], identB)
            if flag is None:
                for kc in range(8):
                    P.act(dest[:, kc, :], tps[:, kc, :], AF.Identity, bias=shcol[:, kc:kc + 1], scale=wcol[:, kc:kc + 1])
            else:
                P.tt('vector', t32, tps, bc(wcol.unsqueeze(2), [128, 8, 128]), ALU.mult)
                P.tt('gpsimd', t32, t32, bc(shcol.unsqueeze(2), [128, 8, 128]), ALU.add)
                P.ts('vector', dest, t32, flag, None, ALU.mult)

        hT = RX.alloc([8, HT_N], BF16)
        AA = RX.alloc([NCH, 2, 16], F32)
        BB = RX.alloc([NCH, 2, 16], F32)
        DEC = RX.alloc([NCH, 2, 16], F32)
        Sacc = RX.alloc([16, 65], F32)
        RR = RX.alloc([2, 16], F32)
        expR = RX.alloc([2, 16], F32)
        mXf = RX.mark()
        mMF = RM.mark()
        win = ab_w_in[0].rearrange("(kc p) n -> p kc n", p=128)

        xbuf = [RX.alloc([D], F32) for _ in range(2)]
        P.memset('gpsimd', hT[:, :, 0:1], 0.0)
        P.memset('gpsimd', hT[:, :, 257:258], 0.0)

        def tokcols(c):
            return (CTX0 + c * 128) if c < 2 else (WIN0 + (c - 2) * 128)

        for c in range(NCH):
            xb = xbuf[c % 2]
            P.dma('sync', xb, xw[c])
            v_ = 1 if c < 2 else 0
            col0 = tokcols(c)
            fg = None if c not in (2, NCH - 1) else FL[:, FL_VF + c:FL_VF + c + 1]
            make_hT(xb, hT[:, :, col0:col0 + 128], WM[:, 0, 0, v_, :], SHc[:, 0, 0, v_, :], fg)
        hTh = RX.alloc([8, 128], BF16)
        xb = xbuf[0]
        P.dma('sync', xb, xe)
        make_hT(xb, hTh, WM[:, 0, 0, 0, :], SHc[:, 0, 0, 0, :])
        P.ts('vector', hT[:, :, 258:259], hTh[:, :, 0:1], FL[:, FL_EDGE:FL_EDGE + 1], None, ALU.mult)
        P.ts('vector', hT[:, :, 2563:2564], hTh[:, :, 1:2], FL[:, FL_EDGE + 1:FL_EDGE + 2], None, ALU.mult)

        ck('B', [hT[:, 0, 0:1024], hT[:, 7, 1540:2564]])
        mark('hT')
        Gpre = RX.alloc([NCH, 32], F32)
        LF = RX.alloc([NCH, 2, 16], F32)
        II = RX.alloc([NCH, 2, 16], F32)
        Wg = RM.alloc([8, 32], BF16)
        P.dma('gpsimd', Wg, win[:, :, 4096:4128])
        gbb = RM.alloc([32], F32)
        P.dma('sync', gbb, bc(gate_b[0].rearrange("a h -> (a h)").unsqueeze(0), [128, 32]))
        lgb = RM.alloc([2, 8], F32)
        P.dma('sync', lgb.rearrange("p a h -> p (a h)"), bc(ret_lg[0].rearrange("a h -> (a h)").unsqueeze(0), [128, 16]))
        ck('C0', [gbb, lgb.rearrange('p a h -> p (a h)'), Wg.rearrange('p a b -> p (a b)')])
        for c in range(NCH):
            c0 = tokcols(c)
            pg = pb(3, [128])[:, 32 * (c % 4):32 * (c % 4) + 32]
            for kc in range(8):
                P.mm(pg, hT[:, kc, c0:c0 + 128], Wg[:, kc, :], start=(kc == 0), stop=(kc == 7))
            P.tt('vector', Gpre[:, c, :], pg, gbb, ALU.add)
        ck('C1', [Gpre.rearrange('p a b -> p (a b)')])
        VFb = FL[:, FL_VF:FL_VF + NCH]
        tmpg = RX.alloc([NCH, 8], F32)
        for d_ in range(2):
            fcol = Gpre[:, :, 8 + 16 * d_:16 + 16 * d_]
            icol = Gpre[:, :, 16 * d_:8 + 16 * d_]
            P.act(tmpg, fcol, AF.Exp, scale=-1.0)
            P.act(tmpg, tmpg, AF.Ln, bias=1.0)
            P.stt('vector', LF[:, :, d_, 8:16], tmpg, -1.0, bc(VFb.unsqueeze(2), [128, NCH, 8]), ALU.mult, ALU.mult)
            P.tt('vector', LF[:, :, d_, 0:8], bc(lgb[:, d_, :].unsqueeze(1), [128, NCH, 8]), bc(VFb.unsqueeze(2), [128, NCH, 8]), ALU.mult)
            P.memset('gpsimd', II[:, :, d_, 0:8], 0.0)
            P.copy('gpsimd', II[:, :, d_, 8:16], icol)
        ck('C2', [LF.rearrange('p a b c -> p (a b c)'), II.rearrange('p a b c -> p (a b c)')])
        LFc = RX.alloc([2, NCH * 16], F32)
        for d_ in range(2):
            P.copy('gpsimd', LFc[:, d_, :].rearrange("p (c l) -> p c l", l=16), LF[:, :, d_, :])
        for d_ in range(2):
            pe = pb(4 + d_, [NCH, 16])
            pef = pe.rearrange("p c l -> p (c l)")
            for (a_, b_) in ((0, 128), (128, 256), (256, 320)):
                P.mm(pef[:, a_:b_], MfF if d_ == 0 else MbF, LFc[:, d_, a_:b_])
            if d_ == 0 and dbg == 'C2a':
                P.copy('vector', AA.rearrange('p a b c -> p (a b c)')[:, 0:320], pef)
                ck('C2a', [AA.rearrange('p a b c -> p (a b c)'), LFc.rearrange('p a b -> p (a b)')])
            P.act(AA[:, :, d_, :], pe, AF.Exp, scale=-1.0)
            if d_ == 0:
                ck('C2b', [AA.rearrange('p a b c -> p (a b c)')])
            P.tt('vector', BB[:, :, d_, :], pe, II[:, :, d_, :], ALU.add)
            if d_ == 0:
                ck('C2c', [BB.rearrange('p a b c -> p (a b c)')])
            P.act(BB[:, :, d_, :], BB[:, :, d_, :], AF.Exp, bias=lnk)
            if d_ == 0:
                ck('C2d', [BB.rearrange('p a b c -> p (a b c)')])
            P.tt('vector', BB[:, :, d_, :], BB[:, :, d_, :], bc(VFb.unsqueeze(2), [128, NCH, 16]), ALU.mult)
        ck('C3', [AA.rearrange('p a b c -> p (a b c)'), BB.rearrange('p a b c -> p (a b c)')])
        LFf = LF.rearrange("p a b c -> p (a b c)")
        for hf in range(2):
            pt_ = pb(6, [10, 2, 16])
            ptf = pt_.rearrange("p a b c -> p (a b c)")
            for (a_, b_) in ((0, 128), (128, 256), (256, 320)):
                P.mm(ptf[:, a_:b_], onesF, LFf[:, hf * 320 + a_:hf * 320 + b_])
            P.act(DEC[:, hf * 10:(hf + 1) * 10, :, :], pt_, AF.Exp)

        ck('C', [AA.rearrange('p a b c -> p (a b c)'), BB.rearrange('p a b c -> p (a b c)'), DEC.rearrange('p a b c -> p (a b c)'), Gpre.rearrange('p a b -> p (a b)')])
        mark('prepass')
        P.memset('vector', Sacc, 0.0)
        P.memset('vector', RR, 0.0)
        Wv = RM.alloc([8, 1024], BF16)
        P.dma('gpsimd', Wv[:, :, 0:512], win[:, :, 1024:1536])
        P.dma('gpsimd', Wv[:, :, 512:1024], win[:, :, 3072:3584])
        Wkr = RM.alloc([8, 512], BF16)
        P.dma('gpsimd', Wkr, win[:, :, 512:1024])
        Wtap = RM.alloc([3, 8, 512], BF16)
        m1 = RM.mark()
        cwk = RM.alloc([3, 512], F32)
        for j in range(3):
            P.dma('sync', cwk[:, j, :], bc(conv_w[0, j:j + 1, 512:1024], [128, 512]))
        Wkm = RM.alloc([8, 512], BF16)
        P.dma('gpsimd', Wkm, win[:, :, 2560:3072])
        for j in range(3):
            P.tt('vector', Wtap[:, j, :, :], Wkm, bc(cwk[:, j, :].unsqueeze(1), [128, 8, 512]), ALU.mult)
        RM.release(m1)
        cbb = RM.alloc([512], F32)
        P.dma('sync', cbb, bc(conv_b[0:1, 512:1024], [128, 512]))
        rS = RM.alloc([2, NSLOT, 32], F32)
        P.dma('sync', rS.rearrange("p a s f -> p a (s f)"), ropeS.rearrange("a p n -> p a n"))
        Kfb = RM.alloc([16, 128], BF16)
        Vext = RM.alloc([16, 65], BF16)
        ra = RM.alloc([2, 8, 32], F32)
        krf = [RM.alloc([512], F32) for _ in range(2)]
        Vsb = [RM.alloc([16, 64], BF16) for _ in range(2)]
        kmts = [RM.alloc([512], F32) for _ in range(2)]
        gps = [RM.alloc([32], F32) for _ in range(2)]
        xb = xbuf[1]
        P.dma('sync', xb, xsh)
        make_hT(xb, hTh, WM[:, 0, 0, 0, :], SHc[:, 0, 0, 0, :])
        P.tt('vector', hTh, hTh, bc(FL[:, FL_SH:FL_SH + 128].unsqueeze(1), [128, 8, 128]), ALU.mult)
        hTs = [RX.alloc([8, 130], BF16) for _ in range(2)]
        Ktok = RX.alloc([16, 64], BF16)
        sg = RX.alloc([160], F32)

        def slot_A1(s):
            k = s % 2
            xb = xbuf[k]
            P.dma('sync', xb, xs[s])
            ss = ssq[:, 2 * k:2 * k + 1]
            rs = ssq[:, 2 * k + 1:2 * k + 2]
            P.act(junk, xb, AF.Square, accum_out=ss)
            P.act(rs, ss, AF.Sqrt, bias=epsb, scale=1.0 / D)
            P.recip(rs, rs)
            P.act(xnb[k], xb, AF.Identity, scale=rs)

        def slot_A2(s):
            k = s % 2
            hs = hTs[k]
            xn = xnb[k]
            wcol, shcol = WM[:, 0, 0, 0, :], SHc[:, 0, 0, 0, :]
            P.copy('scalar', hs[:, :, 0:1], hTh[:, :, 2 * s:2 * s + 1])
            P.copy('scalar', hs[:, :, 129:130], hTh[:, :, 2 * s + 1:2 * s + 2])
            tps = pb(0, [8, 128], BF16)
            for kc in range(8):
                P.transpose(tps[:, kc, :], xn[:, kc * 128:(kc + 1) * 128], identB)
            for kc in range(8):
                P.act(hs[:, kc, 1:129], tps[:, kc, :], AF.Identity, bias=shcol[:, kc:kc + 1], scale=wcol[:, kc:kc + 1])
            pkr = pb(1, [512])
            for kc in range(8):
                P.mm(pkr, hs[:, kc, 1:129], Wkr[:, kc, :], start=(kc == 0), stop=(kc == 7))
            P.copy('scalar', krf[k], pkr)
            pkm = pb(2, [512])
            for j in range(3):
                for kc in range(8):
                    P.mm(pkm, hs[:, kc, j:j + 128], Wtap[:, j, kc, :], start=(j == 0 and kc == 0), stop=(j == 2 and kc == 7))
            P.tt('vector', kmts[k], pkm, cbb, ALU.add)
            for hf in range(2):
                pv = pb(3 + hf, [512])
                for kc in range(8):
                    P.mm(pv, hs[:, kc, 1:129], Wv[:, kc, hf * 512:(hf + 1) * 512], start=(kc == 0), stop=(kc == 7))
                P.copy('scalar', Vsb[k][:, hf * 8:(hf + 1) * 8, :], pv.rearrange("p (h d) -> p h d", h=8))
            pg = pb(5, [32])
            for kc in range(8):
                P.mm(pg, hs[:, kc, 1:129], Wg[:, kc, :], start=(kc == 0), stop=(kc == 7))
            P.tt('vector', gps[k], pg, gbb, ALU.add)

        fsel = sg[:, 32:40]
        isel = sg[:, 40:56]
        lf16 = sg[:, 56:72]
        lfm = sg[:, 72:104]
        rsel = sg[:, 104:120]
        bex = sg[:, 120:136]
        t8 = sg[:, 136:144]
        RRf = RR.rearrange("p a l -> p (a l)")

        def slot_B_early(s):
            ff = FL[:, FL_FF + s:FL_FF + s + 1]
            fb = FL[:, FL_FB + s:FL_FB + s + 1]
            gp = gps[s % 2]
            P.ts('vector', fsel, gp[:, 8:16], ff, None, ALU.mult)
            P.stt('vector', fsel, gp[:, 24:32], fb, fsel, ALU.mult, ALU.add)
            P.act(t8, fsel, AF.Exp, scale=-1.0)
            P.act(t8, t8, AF.Ln, bias=1.0)
            P.ts('vector', lf16[:, 0:8], lgb[:, 0, :], ff, None, ALU.mult)
            P.stt('vector', lf16[:, 0:8], lgb[:, 1, :], fb, lf16[:, 0:8], ALU.mult, ALU.add)
            P.ts('vector', lf16[:, 8:16], t8, -1.0, None, ALU.mult)
            P.ts('vector', lfm[:, 0:16], lf16, ff, None, ALU.mult)
            P.ts('vector', lfm[:, 16:32], lf16, fb, None, ALU.mult)
            pe = pb(5, [16], off=1024)
            P.mm(pe, MfF, lfm[:, 0:16], start=True, stop=False)
            P.mm(pe, MbF, lfm[:, 16:32], start=False, stop=True)
            pt_ = pb(5, [32], off=1536)
            P.mm(pt_, onesF, lfm)
            P.memset('vector', isel[:, 0:8], 0.0)
            P.ts('vector', isel[:, 8:16], gp[:, 0:8], ff, None, ALU.mult)
            P.stt('vector', isel[:, 8:16], gp[:, 16:24], fb, isel[:, 8:16], ALU.mult, ALU.add)
            P.ts('vector', rsel, RR[:, 0, :], ff, None, ALU.mult)
            P.stt('vector', rsel, RR[:, 1, :], fb, rsel, ALU.mult, ALU.add)
            P.tt('vector', rsel, rsel, isel, ALU.add)
            k3 = krf[s % 2].rearrange("p (h d) -> p h d", h=8)
            x1, x2 = k3[:, :, 0:32], k3[:, :, 32:64]
            cs = bc(rS[:, 0, s, :].unsqueeze(1), [128, 8, 32])
            sn = bc(rS[:, 1, s, :].unsqueeze(1), [128, 8, 32])
            P.tt('vector', ra[:, 0], x1, cs, ALU.mult)
            P.tt('vector', ra[:, 1], x2, sn, ALU.mult)
            P.tt('vector', Ktok[:, 0:8, 0:32], ra[:, 0], ra[:, 1], ALU.subtract)
            P.tt('vector', ra[:, 0], x2, cs, ALU.mult)
            P.tt('vector', ra[:, 1], x1, sn, ALU.mult)
            P.tt('vector', Ktok[:, 0:8, 32:64], ra[:, 0], ra[:, 1], ALU.add)
            P.act(Ktok[:, 8:16, :], kmts[s % 2].rearrange("p (h d) -> p h d", h=8), AF.Silu)
            P.act(Kfb[:, :, 0:64], Ktok, AF.Identity, scale=ff)
            P.act(Kfb[:, :, 64:128], Ktok, AF.Identity, scale=fb)
            P.tt('vector', rsel, rsel, pe, ALU.add)
            P.act(bex, rsel, AF.Exp, bias=lnk)
            P.tt('vector', RRf, RRf, pt_, ALU.add)
            P.tt('vector', Vext[:, :, 0:64], Vsb[s % 2], bc(bex.unsqueeze(2), [128, 16, 64]), ALU.mult)
            P.copy('scalar', Vext[:, :, 64], bex)

        def slot_B_late(s):
            kvb = [pb(6, [6, 65]), pb(7, [6, 65]), pb(5, [4, 65])]
            for ln in range(16):
                P.mm(kvb[ln // 6][:, ln % 6, :], Kfb[:, ln, :], Vext[:, ln, :])
            for g3 in range(3):
                nl = 6 if g3 < 2 else 4
                P.tt('vector', Sacc[:, g3 * 6:g3 * 6 + nl, :], Sacc[:, g3 * 6:g3 * 6 + nl, :], kvb[g3], ALU.add)

        slot_A1(0)
        slot_A1(1)
        slot_A2(0)
        for s in range(NSLOT):
            if s + 2 < NSLOT:
                slot_A1(s + 2)
            slot_B_early(s)
            if s + 1 < NSLOT:
                slot_A2(s + 1)
            slot_B_late(s)
        P.act(expR, RR, AF.Exp)
        ck('D', [Sacc.rearrange('p a b -> p (a b)')[:, 0:1024], RR.rearrange('p a b -> p (a b)'), expR.rearrange('p a b -> p (a b)')])
        RX.release(mXf)
        RM.release(mMF)

        mark('outside')
        MT = RM.alloc([8, NCH * 128], BF16)
        mMF2 = RM.mark()
        order = [list(range(NCH)), [1, 0] + list(range(NCH - 1, 1, -1))]
        first_dir = [0 if order[0].index(c_) <= order[1].index(c_) else 1 for c_ in range(NCH)]
        groups = [(CTX0, 256, 0, False)] + [(WIN0 + 512 * g_, 512, 256 + 512 * g_, True) for g_ in range(4)] + [(WIN0 + 2048, 256, 2304, True)]
        for ps_ in range(8):
            if ps_ in (1, 2, 5, 6):
                mark(f'p{ps_}_start')
            RX.release(mXf)
            RM.release(mMF2)
            is_ml = ps_ >= 4
            j = ps_ % 4
            l0 = (8 if is_ml else 0) + 2 * j
            qoff = (2048 if is_ml else 0) + 128 * j
            koff = (2560 if is_ml else 512) + 128 * j
            goff = (3584 if is_ml else 1536) + 128 * j
            voff = (3072 if is_ml else 1024) + 128 * j
            ntap = 3 if is_ml else 1
            qT = RX.alloc([NCH * 128], BF16)
            kT = RX.alloc([NCH * 128], BF16)
            gT = RX.alloc([NCH * 128], BF16)
            Kt = RX.alloc([NCH, 128], BF16)
            Vp = RX.alloc([NCH, 2, 64], BF16)
            Wq = RM.alloc([3, 8, 128], BF16)
            Wk = RM.alloc([3, 8, 128], BF16)
            Wgt = RM.alloc([8, 128], BF16)
            Wvp = RM.alloc([8, 128], BF16)
            Oacc = RM.alloc([NCH, 2, 64], F32)
            mScan = RM.mark()
            P.dma('gpsimd', Wq[:, 0], win[:, :, qoff:qoff + 128])
            P.dma('gpsimd', Wk[:, 0], win[:, :, koff:koff + 128])
            P.dma('gpsimd', Wgt, win[:, :, goff:goff + 128])
            P.dma('gpsimd', Wvp, win[:, :, voff:voff + 128])
            if is_ml:
                cwp = RM.alloc([2, 3, 128], F32)
                for wi, co in enumerate((128 * j, 512 + 128 * j)):
                    for tpi in range(3):
                        P.dma('sync', cwp[:, wi, tpi, :], bc(conv_w[0, tpi:tpi + 1, co:co + 128], [128, 128]))
                for wi, W_ in enumerate((Wq, Wk)):
                    for tpi in (2, 1, 0):
                        P.tt('vector', W_[:, tpi], W_[:, 0], bc(cwp[:, wi, tpi, :].unsqueeze(1), [128, 8, 128]), ALU.mult)
            rtmp = RM.alloc([2, 512], F32)
            rfb = [RM.alloc([2, 512], F32) for _ in range(2)]
            bi = 0
            for gi_, (hc0, n, lc0, isw) in enumerate(groups):
                rf = rfb[gi_ % 2]
                if isw and not is_ml:
                    w0_ = hc0 - WIN0
                    P.dma('sync', rf[:, :, 0:n], ropeF.rearrange("a p n -> p a n")[:, :, w0_:w0_ + n])
                for which, W_, dst in (('q', Wq, qT), ('k', Wk, kT), ('g', Wgt, gT)):
                    pp = pb(1 + (bi % 3), [512])[:, 0:n]
                    bi += 1
                    if which == 'g':
                        for kc in range(8):
                            P.mm(pp, W_[:, kc, :], hT[:, kc, hc0:hc0 + n], start=(kc == 0), stop=(kc == 7))
                        P.act(dst[:, lc0:lc0 + n], pp, AF.Sigmoid if is_ml else AF.Silu)
                        continue
                    for tpi in range(ntap):
                        sh_ = (tpi - 1) if is_ml else 0
                        for kc in range(8):
                            P.mm(pp, W_[:, tpi, kc, :], hT[:, kc, hc0 + sh_:hc0 + sh_ + n],
                                 start=(tpi == 0 and kc == 0), stop=(tpi == ntap - 1 and kc == 7))
                    if is_ml:
                        ci_ = 40 + (j if which == 'q' else 4 + j)
                        P.act(dst[:, lc0:lc0 + n], pp, AF.Silu, bias=colB[:, ci_:ci_ + 1])
                    elif not isw:
                        P.copy('scalar', dst[:, lc0:lc0 + n], pp)
                    else:
                        P.tt('vector', rtmp[:, 0, 0:n], pp, rf[:, 0, 0:n], ALU.mult)
                        for blk in range(4):
                            src = blk ^ 1
                            P.tt('vector', rtmp[blk * 32:(blk + 1) * 32, 1, 0:n], pp[src * 32:(src + 1) * 32, :],
                                 rf[src * 32:(src + 1) * 32, 1, 0:n], ALU.mult)
                        P.tt('vector', dst[:, lc0:lc0 + n], rtmp[:, 0, 0:n], rtmp[:, 1, 0:n], ALU.add)
            if ps_ in (1, 5):
                mark(f'p{ps_}_proj')
            for c8 in range(0, NCH, 8):
                ncc = min(8, NCH - c8)
                tpk = pb(4, [8, 128], BF16)
                for ci in range(ncc):
                    P.transpose(tpk[:, ci, :], kT[:, (c8 + ci) * 128:(c8 + ci + 1) * 128], identB)
                P.copy('scalar', Kt[:, c8:c8 + ncc, :], tpk[:, 0:ncc, :])
            for c4 in range(0, NCH, 4):
                pv = pb(1 + (c4 // 4) % 3, [4, 128])
                for ci in range(4):
                    c0 = tokcols(c4 + ci)
                    for kc in range(8):
                        P.mm(pv[:, ci, :], hT[:, kc, c0:c0 + 128], Wvp[:, kc, :], start=(kc == 0), stop=(kc == 7))
                P.copy('vector', Vp[:, c4:c4 + 4].rearrange("p c h d -> p c (h d)"), pv)
            if ps_ == 0:
                ck('E1', [qT[:, 0:1024], kT[:, 0:1024], gT[:, 0:1024], Kt.rearrange('p a b -> p (a b)')[:, 0:1024], Vp.rearrange('p a b c -> p (a b c)')[:, 0:1024]])
            if ps_ == 4:
                ck('F1', [qT[:, 0:1024], kT[:, 0:1024], gT[:, 0:1024], Kt.rearrange('p a b -> p (a b)')[:, 0:1024], Vp.rearrange('p a b c -> p (a b c)')[:, 0:1024]])
            if ps_ in (1, 5):
                mark(f'p{ps_}_kv')
            RM.release(mScan)
            decp = RM.alloc([NCH, 2], F32)
            P.copy('vector', decp[0:64], DEC[0:64, :, :, l0])
            P.copy('vector', decp[64:128], DEC[64:128, :, :, l0 + 1])
            S32 = [RM.alloc([130], F32) for _ in range(2)]
            Sbf = [RM.alloc([130], BF16) for _ in range(2)]
            stmp = RM.alloc([130], F32)
            ecol = RM.alloc([2], F32)
            Vts = [RM.alloc([2, 65], BF16) for _ in range(4)]
            dn = RM.alloc([2, 8], F32)
            P.memset('gpsimd', stmp, 0.0)
            for d_ in range(2):
                P.memset('gpsimd', S32[d_], 0.0)
                P.copy('vector', ecol[0:64, d_:d_ + 1], expR[0:64, d_, l0:l0 + 1])
                P.copy('vector', ecol[64:128, d_:d_ + 1], expR[64:128, d_, l0 + 1:l0 + 2])
            PTs = [RM.alloc([2, 128], BF16) for _ in range(4)]
            pt_banks = [[0, 6], [3, 7]]

            def scan_front(step, d_):
                c = order[d_][step]
                tk = slice(c * 128, (c + 1) * 128)
                Vt = Vts[2 * (step % 2) + d_]
                P.tt('vector', Vt[:, :, 0:64], Vp[:, c], bc(BB[:, c, d_, l0:l0 + 2].unsqueeze(2), [128, 2, 64]), ALU.mult)
                P.copy('scalar', Vt[:, :, 64], BB[:, c, d_, l0:l0 + 2])
                ptp = pb(pt_banks[d_][step % 2], [2, 128])
                prev_mm = None
                for h in range(2):
                    hb = slice(64 * h, 64 * h + 64)
                    prev_mm = P.mm(ptp[:, h, :], kT[hb, tk], qT[hb, tk], after=prev_mm)
                PT = PTs[2 * (step % 2) + d_]
                P.tt('vector', PT, ptp, bc((maskF_b if d_ == 0 else maskB_b).unsqueeze(1), [128, 2, 128]), ALU.mult)

            def scan_kv(step, d_):
                c = order[d_][step]
                Vt = Vts[2 * (step % 2) + d_]
                kvp = pb(2 if d_ == 0 else 5, [130])
                P.mm(kvp, Kt[:, c, :], Vt.rearrange("p h e -> p (h e)"))

            def scan_back(step, d_):
                c = order[d_][step]
                tk = slice(c * 128, (c + 1) * 128)
                S = S32[d_]
                Vt = Vts[2 * (step % 2) + d_]
                PT = PTs[2 * (step % 2) + d_]
                if step == 2:
                    r0 = 64 * d_
                    P.copy('scalar', stmp[0:64, 0:65], Sacc[r0:r0 + 64, l0, :])
                    P.copy('scalar', stmp[64:128, 65:130], Sacc[r0:r0 + 64, l0 + 1, :])
                    P.stt('vector', S, S, ecol[:, d_:d_ + 1], stmp, ALU.mult, ALU.add)
                if step > 0:
                    P.act(Sbf[d_], S, AF.Identity, scale=decp[:, c, d_:d_ + 1])
                ops_ = pb(1 if d_ == 0 else 4, [2, 65])
                for h in range(2):
                    hb = slice(64 * h, 64 * h + 64)
                    P.mm(ops_[:, h, :], PT[:, h, :], Vt[:, h, :], start=True, stop=(step == 0))
                    if step > 0:
                        P.mm(ops_[:, h, :], qT[hb, tk], Sbf[d_][hb, 65 * h:65 * h + 65], start=False, stop=True)
                for h in range(2):
                    at = AA[:, c, d_, l0 + h:l0 + h + 1]
                    if is_ml:
                        dd = dn[:, h, 4 * d_:4 * d_ + 4]
                        P.act(dd[:, 0:1], ops_[:, h, 64:65], AF.Abs, scale=at)
                        P.ts('vector', dd[:, 1:2], dd[:, 0:1], 1.0, None, ALU.max)
                        P.recip(dd[:, 2:3], dd[:, 1:2])
                        P.tt('vector', dd[:, 3:4], dd[:, 2:3], at, ALU.mult)
                        coef = dd[:, 3:4]
                    else:
                        coef = at
                    if first_dir[c] == d_:
                        P.act(Oacc[:, c, h, :], ops_[:, h, 0:64], AF.Identity, scale=coef)
                    else:
                        P.stt('vector', Oacc[:, c, h, :], ops_[:, h, 0:64], coef, Oacc[:, c, h, :], ALU.mult, ALU.add)
                kvp = pb(2 if d_ == 0 else 5, [130])
                if step == 0:
                    P.copy('vector', S, kvp)
                else:
                    P.stt('vector', S, S, decp[:, c, d_:d_ + 1], kvp, ALU.mult, ALU.add)

            for d_ in range(2):
                scan_front(0, d_)
                scan_kv(0, d_)
            for step in range(NCH):
                if step + 1 < NCH:
                    for d_ in range(2):
                        scan_front(step + 1, d_)
                for d_ in range(2):
                    scan_back(step, d_)
                if step + 1 < NCH:
                    for d_ in range(2):
                        scan_kv(step + 1, d_)
                if ps_ == 0 and step < 3:
                    ck('E2' + 'abc'[step], [Oacc.rearrange('p a b c -> p (a b c)')[:, 0:256], S32[0], S32[1]])
            if ps_ == 0:
                ck('E2', [Oacc.rearrange('p a b c -> p (a b c)')[:, 0:1024], Oacc.rearrange('p a b c -> p (a b c)')[:, 1024:2048]])
            if ps_ == 4:
                ck('F2', [Oacc.rearrange('p a b c -> p (a b c)')[:, 0:1024], Oacc.rearrange('p a b c -> p (a b c)')[:, 1024:2048]])
            if ps_ in (1, 5):
                mark(f'p{ps_}_scan')
            RM.release(mScan)
            sq = RM.alloc([NCH, 2, 64], F32)
            ms = RM.alloc([NCH, 2], F32)
            On = RM.alloc([NCH, 2, 64], BF16)
            P.act(sq, Oacc, AF.Square)
            P.reduce('vector', ms, sq, ALU.add)
            P.act(ms, ms, AF.Sqrt, bias=epsb, scale=1.0 / 64)
            P.recip(ms, ms)
            P.tt('vector', On, Oacc, bc(ms.unsqueeze(3), [128, NCH, 2, 64]), ALU.mult)
            nwi = (36 if is_ml else 32) + j
            for c4 in range(0, NCH, 4):
                tpo = pb(4, [4, 128], BF16, off=(c4 // 4 % 2) * 1024)
                for ci in range(4):
                    P.transpose(tpo[:, ci, :], On[:, c4 + ci].rearrange("p h d -> p (h d)"), identB)
                P.stt('vector', MT[:, ps_, c4 * 128:(c4 + 4) * 128], tpo.rearrange("p c t -> p (c t)"), colB[:, nwi:nwi + 1],
                      gT[:, c4 * 128:(c4 + 4) * 128], ALU.mult, ALU.mult)
        ck('E4', [MT[:, 0, 0:1024], MT[:, 7, 0:1024]])
        RM.release(mMF2)
        RX.release(0)
        mark('passes')
        X1 = RX.alloc([NCH, D], F32)
        xsp = RX.alloc([512], F32)
        Wo = RM.alloc([8, D], BF16)
        P.dma('gpsimd', Wo, ab_w_out[0].rearrange("(kc p) n -> p kc n", p=128))
        otmp = [RM.alloc([512], F32) for _ in range(2)]
        for c in range(NCH):
            P.dma('sync', X1[:, c, :], xw[c])
        for c in range(NCH):
            gi = 2 if c < 2 else 0
            for hf in range(2):
                po = pb(1 + hf, [512])
                for kc in range(8):
                    P.mm(po, MT[:, kc, c * 128:(c + 1) * 128], Wo[:, kc, hf * 512:(hf + 1) * 512], start=(kc == 0), stop=(kc == 7))
                P.tt('vector', otmp[hf], po, gB[:, gi, hf * 512:(hf + 1) * 512], ALU.mult)
                P.tt('vector', X1[:, c, hf * 512:(hf + 1) * 512], X1[:, c, hf * 512:(hf + 1) * 512], otmp[hf], ALU.add)
        RM.release(mMF)
        if dbg == 'l0m':
            for c in range(NCH):
                P.dma('sync', dbg_out[c], X1[:, c, :], store=True)

        mark('outproj0')
        def ffn(l, chunks, wm_sel, g_sel):
            m0 = RM.mark()
            w_in = ffn_w_in[l].rearrange("(kc p) n -> p kc n", p=128)
            w_out = ffn_w_out[l].rearrange("(f p) n -> p f n", p=128)
            Wout = RM.alloc([NFT, D], BF16)
            P.dma('gpsimd', Wout[:, 0:11, :], w_out[:, 0:11, :])
            P.dma('gpsimd', Wout[:, 11:22, :], w_out[:, 11:22, :])
            AT = RM.alloc([NFT, 512], BF16)
            wbuf = [RM.alloc([8, 2, 256], BF16) for _ in range(2)]
            h2 = RM.alloc([8, 512], BF16)
            o_ = xsp
            nq = len(chunks) // 4

            def prep(qi, i):
                c = chunks[4 * qi + i]
                v_ = wm_sel(c)
                make_hT(X1[:, c, :], h2[:, :, i * 128:(i + 1) * 128], WM[:, l, 1, v_, :], SHc[:, l, 1, v_, :])

            for i in range(4):
                prep(0, i)
            bi = 0
            for qi in range(nq):
                cq = chunks[4 * qi:4 * qi + 4]
                for fp in range(11):
                    wb = wbuf[bi % 2]
                    bi += 1
                    P.dma('gpsimd', wb[:, :, 0, :], w_in[:, :, fp * 256:(fp + 1) * 256])
                    P.dma('gpsimd', wb[:, :, 1, :], w_in[:, :, D_FF + fp * 256:D_FF + (fp + 1) * 256])
                    for f2 in range(2):
                        f = fp * 2 + f2
                        pgm = pb(1 + 2 * (f % 2), [512])
                        pum = pb(2 + 2 * (f % 2), [512])
                        for kc in range(8):
                            P.mm(pgm, wb[:, kc, 0, f2 * 128:(f2 + 1) * 128], h2[:, kc, :], start=(kc == 0), stop=(kc == 7))
                        for kc in range(8):
                            P.mm(pum, wb[:, kc, 1, f2 * 128:(f2 + 1) * 128], h2[:, kc, :], start=(kc == 0), stop=(kc == 7))
                        P.act(AT[:, f, :], pgm, AF.Silu)
                        P.tt('vector', AT[:, f, :], AT[:, f, :], pum, ALU.mult)
                k_ = 0
                for i, c in enumerate(cq):
                    for hf in range(2):
                        po = pb(5 + (k_ % 3), [512])
                        k_ += 1
                        for f in range(NFT):
                            P.mm(po, AT[:, f, i * 128:(i + 1) * 128], Wout[:, f, hf * 512:(hf + 1) * 512], start=(f == 0), stop=(f == NFT - 1))
                        P.tt('vector', o_, po, gB[:, g_sel(c), hf * 512:(hf + 1) * 512], ALU.mult)
                        P.tt('vector', X1[:, c, hf * 512:(hf + 1) * 512], X1[:, c, hf * 512:(hf + 1) * 512], o_, ALU.add)
                    if qi + 1 < nq:
                        prep(qi + 1, i)
            RM.release(m0)

        if dbg != 'l0m':
            ffn(0, list(range(NCH)), lambda c: 1 if c < 2 else 0, lambda c: 3 if c < 2 else 1)
        if dbg == 'l0':
            for c in range(NCH):
                P.dma('sync', dbg_out[c], X1[:, c, :], store=True)

        mark('ffn0')
        if dbg in (None, 'l1m'):
            modulation(1)
            make_wm(1)
            m1 = RM.mark()
            awin = at_w_in[0].rearrange("(kc p) n -> p kc n", p=128)
            kT1 = RM.alloc([2, NCH * 128], BF16)
            Vx1 = RM.alloc([NCH, 4, 65], BF16)
            P.memset('vector', Vx1[:, :, :, 64:65], 1.0)
            qnb = RM.alloc([64], F32)
            knb = RM.alloc([64], F32)
            P.dma('sync', qnb, bc(at_qn, [128, 64]))
            P.dma('sync', knb, bc(at_kn, [128, 64]))
            snk = RM.alloc([16], F32)
            P.dma('sync', snk, bc(at_sink, [128, 16]))
            rT = RM.alloc([2, NW, 32], F32)
            P.dma('sync', rT.rearrange("p a s f -> p a (s f)"), ropeT.rearrange("a p n -> p a n"))
            amb = RM.alloc([4, 4, 128], BF16)
            sm = RM.alloc([8], F32)
            sinkexp = RM.alloc([16], F32)
            hTc = [RM.alloc([8, 128], BF16)] * 2
            nms = RM.alloc([8], F32)
            qn = RM.alloc([8, 64], F32)
            nsq = qn
            qr = RM.alloc([2, 8, 32], F32)
            qtk = RM.alloc([16, 64], BF16)
            qtk2 = RM.alloc([16, 64], BF16)
            m2 = RM.mark()
            amf = RM.alloc([4, 128], F32)
            absq = RM.alloc([128], F32)
            P.dma('sync', amf, amask.rearrange("c p n -> p c n"))
            P.ts('vector', amf, amf, 1.0, 30000.0, ALU.subtract, ALU.mult)
            for v4 in range(4):
                P.copy('vector', amb[:, v4], bc(amf[:, v4, :].unsqueeze(1), [128, 4, 128]))
            P.act(absq[:, 0:64], qnb, AF.Abs)
            P.act(absq[:, 64:128], knb, AF.Abs)
            P.reduce('vector', sm[:, 0:1], absq[:, 0:64], ALU.max)
            P.reduce('vector', sm[:, 1:2], absq[:, 64:128], ALU.max)
            P.tt('vector', sm[:, 2:3], sm[:, 0:1], sm[:, 1:2], ALU.mult)
            P.ts('vector', sm[:, 3:4], sm[:, 2:3], -8.0, None, ALU.mult)
            negB = sm[:, 3:4]
            P.act(sinkexp, snk, AF.Exp, bias=negB)
            RM.release(m2)
            Wkv1 = RM.alloc([8, 512], BF16)
            P.dma('gpsimd', Wkv1, awin[:, :, 1024:1536])

            def qknorm_rope(ps3, nh, wb_, wch, dst):
                P.act(nsq[:, 0:nh, :], ps3, AF.Square)
                P.reduce('vector', nms[:, 0:nh], nsq[:, 0:nh, :], ALU.add)
                P.act(nms[:, 0:nh], nms[:, 0:nh], AF.Sqrt, bias=epsb, scale=1.0 / 64)
                P.recip(nms[:, 0:nh], nms[:, 0:nh])
                P.tt('vector', qn[:, 0:nh, :], ps3, bc(nms[:, 0:nh].unsqueeze(2), [128, nh, 64]), ALU.mult)
                if wch < 0:
                    P.tt('vector', dst, qn[:, 0:nh, :], bc(wb_.unsqueeze(1), [128, nh, 64]), ALU.mult)
                    return
                P.tt('vector', qn[:, 0:nh, :], qn[:, 0:nh, :], bc(wb_.unsqueeze(1), [128, nh, 64]), ALU.mult)
                x1, x2 = qn[:, 0:nh, 0:32], qn[:, 0:nh, 32:64]
                cs = bc(rT[:, 0, wch, :].unsqueeze(1), [128, nh, 32])
                sn = bc(rT[:, 1, wch, :].unsqueeze(1), [128, nh, 32])
                P.tt('vector', qr[:, 0, 0:nh], x1, cs, ALU.mult)
                P.tt('vector', qr[:, 1, 0:nh], x2, sn, ALU.mult)
                P.tt('vector', dst[:, :, 0:32], qr[:, 0, 0:nh], qr[:, 1, 0:nh], ALU.subtract)
                P.tt('vector', qr[:, 0, 0:nh], x2, cs, ALU.mult)
                P.tt('vector', qr[:, 1, 0:nh], x1, sn, ALU.mult)
                P.tt('vector', dst[:, :, 32:64], qr[:, 0, 0:nh], qr[:, 1, 0:nh], ALU.add)

            for c in range(NCH):
                v_ = 1 if c < 2 else 0
                hc_ = hTc[c % 2]
                make_hT(X1[:, c, :], hc_, WM[:, 1, 0, v_, :], SHc[:, 1, 0, v_, :])
                pkv = pb(1 + (c % 2), [512])
                for kc in range(8):
                    P.mm(pkv, hc_[:, kc, :], Wkv1[:, kc, :], start=(kc == 0), stop=(kc == 7))
                P.copy('scalar', Vx1[:, c, :, 0:64], pkv[:, 256:512].rearrange("p (h d) -> p h d", h=4))
                qknorm_rope(pkv[:, 0:256].rearrange("p (h d) -> p h d", h=4), 4, knb, (c - 2) if c >= 2 else -1, qtk[:, 0:4, :])
                tpk = pb(4, [2, 128], BF16)
                for t_ in range(2):
                    P.transpose(tpk[:, t_, :], qtk[:, 2 * t_:2 * t_ + 2, :].rearrange("p h d -> p (h d)"), identB)
                P.copy('scalar', kT1[:, :, c * 128:(c + 1) * 128], tpk)
            mark('l1kv')
            RM.release(m2)
            Wq1 = RM.alloc([8, 1024], BF16)
            P.dma('gpsimd', Wq1, awin[:, :, 0:1024])
            Wo1 = RM.alloc([8, D], BF16)
            P.dma('gpsimd', Wo1, at_w_out[0].rearrange("(kc p) n -> p kc n", p=128))
            qTc = RM.alloc([2, 4, 128], BF16)
            Eb = [RM.alloc([5, 4, 128], BF16) for _ in range(2)]
            Otk = RM.alloc([16, 64], BF16)
            OT = RM.alloc([8, 128], BF16)
            dn1 = RM.alloc([8], F32)
            ot1 = [xsp, RM.alloc([512], F32)]
            ei = 0
            for i in range(16):
                c = i + 3
                hc_ = hTc[i % 2]
                make_hT(X1[:, c, :], hc_, WM[:, 1, 0, 0, :], SHc[:, 1, 0, 0, :])
                for hf in range(2):
                    pq = pb(2 + hf, [512])
                    for kc in range(8):
                        P.mm(pq, hc_[:, kc, :], Wq1[:, kc, hf * 512:(hf + 1) * 512], start=(kc == 0), stop=(kc == 7))
                    qknorm_rope(pq.rearrange("p (h d) -> p h d", h=8), 8, qnb, c - 2, qtk[:, 8 * hf:8 * hf + 8, :])
                tpq = pb(4, [2, 4, 128], BF16)
                for tp_ in range(2):
                    P.copy('scalar', qtk2[:, 8 * tp_:8 * tp_ + 8, :].rearrange("p (j g) d -> p g j d", g=2),
                           qtk[:, 8 * tp_:8 * tp_ + 8, :].rearrange("p (g j) d -> p g j d", g=2))
                    for j in range(4):
                        src = qtk2[:, 8 * tp_ + 2 * j:8 * tp_ + 2 * j + 2, :]
                        P.transpose(tpq[:, tp_, j, :], src.rearrange("p h d -> p (h d)"), identB)
                P.copy('scalar', qTc, tpq)
                kblocks = [(c - 1, 0 if i == 0 else 1), (c, None), (c + 1, 3 if i == 15 else 2), (0, None), (1, None)]
                st_banks = [[5, 6], [0, 2]]

                def att_front(g):
                    tp_, hb = g // 2, slice(64 * (g % 2), 64 * (g % 2) + 64)
                    E = Eb[g % 2]
                    for bi_, (kc_, mk) in enumerate(kblocks):
                        pst = pb(st_banks[g % 2][bi_ % 2], [4, 128])
                        P.mm(pst.rearrange("p j t -> p (j t)"), kT1[hb, tp_, kc_ * 128:(kc_ + 1) * 128],
                             qTc[hb, tp_].rearrange("p j t -> p (j t)"), start=True, stop=(mk is None))
                        if mk is not None:
                            P.mm(pst.rearrange("p j t -> p (j t)"), identB, amb[:, mk].rearrange("p j t -> p (j t)"), start=False, stop=True)
                        P.act(E[:, bi_], pst, AF.Exp, bias=negB, scale=0.125)

                def att_back(g):
                    E = Eb[g % 2]
                    pso = pb(7 if g % 2 == 0 else 3, [4, 65])
                    for j in range(4):
                        for bi_, (kc_, mk) in enumerate(kblocks):
                            P.mm(pso[:, j, :], E[:, bi_, j, :], Vx1[:, kc_, g, :], start=(bi_ == 0), stop=(bi_ == 4))
                    P.tt('vector', dn1[:, 0:4], pso[:, :, 64], sinkexp[:, 4 * g:4 * g + 4], ALU.add)
                    P.recip(dn1[:, 4:8], dn1[:, 0:4])
                    P.tt('vector', Otk[:, 4 * g:4 * g + 4, :], pso[:, :, 0:64], bc(dn1[:, 4:8].unsqueeze(2), [128, 4, 64]), ALU.mult)

                att_front(0)
                for g in range(4):
                    if g + 1 < 4:
                        att_front(g + 1)
                    att_back(g)
                tpo = pb(4, [8, 128], BF16)
                for kc in range(8):
                    P.transpose(tpo[:, kc, :], Otk[:, 2 * kc:2 * kc + 2, :].rearrange("p h d -> p (h d)"), identB)
                P.copy('scalar', OT, tpo)
                for hf in range(2):
                    po = pb(1 + hf, [512])
                    for kc in range(8):
                        P.mm(po, OT[:, kc, :], Wo1[:, kc, hf * 512:(hf + 1) * 512], start=(kc == 0), stop=(kc == 7))
                    P.tt('vector', ot1[hf], po, gB[:, 0, hf * 512:(hf + 1) * 512], ALU.mult)
                    P.tt('vector', X1[:, c, hf * 512:(hf + 1) * 512], X1[:, c, hf * 512:(hf + 1) * 512], ot1[hf], ALU.add)
            mark('l1attn')
            RM.release(m1)
            if dbg == 'l1m':
                for c in range(NCH):
                    P.dma('sync', dbg_out[c], X1[:, c, :], store=True)
            else:
                ffn(1, list(range(3, 19)), lambda c: 0, lambda c: 1)
        for i in range(16):
            P.dma('sync', y[i * 128:(i + 1) * 128, :], X1[:, i + 3, :], store=True)
        mark('end')
        stats = P.emit()
        stats['marks'] = P.marks
        stats['peaks'] = (RP.peak, RX.peak, RM.peak)
    return nc, stats


def _rope_angles(pos):
    pos = np.asarray(pos)
    row = (pos // 64).astype(np.float32)
    col = (pos % 64).astype(np.float32)
    inv = (10000.0 ** (-np.arange(16, dtype=np.float32) / 16)).astype(np.float32)
    return np.concatenate([row[:, None] * inv, col[:, None] * inv], axis=-1).astype(np.float32)


def _core_inputs(inp, core):
    b, q = core // 4, core % 4
    x = np.asarray(inp['x'], np.float32)
    ctx = np.asarray(inp['ctx'], np.float32)
    xb = x[b].reshape(64, 128, D)
    fl = np.zeros((1, NF), np.float32)
    xw = np.zeros((NCH, 128, D), np.float32)
    xw[0:2] = ctx[b].reshape(2, 128, D)
    fl[0, FL_VF:FL_VF + 2] = 1.0
    wpos = np.full((NW, 128), -1, np.int64)
    for w in range(NW):
        ch = 16 * q - 1 + w
        if 0 <= ch < 64:
            xw[2 + w] = xb[ch]
            fl[0, FL_VF + 2 + w] = 1.0
            wpos[w] = ch * 128 + np.arange(128)
    xe = np.zeros((128, D), np.float32)
    tl = 128 * (16 * q - 1) - 1
    tr = 128 * (16 * q + 17)
    if 0 <= tl < SEQ:
        xe[0] = x[b, tl]
        fl[0, FL_EDGE] = 1.0
    if 0 <= tr < SEQ:
        xe[1] = x[b, tr]
        fl[0, FL_EDGE + 1] = 1.0
    slots = [(ch, 0) for ch in range(16 * q - 2, -1, -1)] + [(ch, 1) for ch in range(16 * q + 17, 64)]
    assert len(slots) <= NSLOT
    xs = np.zeros((NSLOT, 128, D), np.float32)
    xsh = np.zeros((128, D), np.float32)
    spos = np.zeros((NSLOT, 128), np.int64)
    for s, (ch, d_) in enumerate(slots):
        xs[s] = xb[ch]
        fl[0, (FL_FF if d_ == 0 else FL_FB) + s] = 1.0
        spos[s] = ch * 128 + np.arange(128)
        t0, t1 = ch * 128 - 1, ch * 128 + 128
        if t0 >= 0:
            xsh[2 * s] = x[b, t0]
            fl[0, FL_SH + 2 * s] = 1.0
        if t1 < SEQ:
            xsh[2 * s + 1] = x[b, t1]
            fl[0, FL_SH + 2 * s + 1] = 1.0
    wp = np.where(wpos < 0, 0, wpos).reshape(-1)
    ang = _rope_angles(wp)
    cosw, sinw = np.cos(ang), np.sin(ang)
    ropeT = np.stack([cosw.reshape(NW, 128, 32).transpose(1, 0, 2).reshape(128, NW * 32),
                      sinw.reshape(NW, 128, 32).transpose(1, 0, 2).reshape(128, NW * 32)]).astype(np.float32)
    p = np.arange(128)
    fi = p % 32
    cosF = cosw[:, fi].T
    sgn = np.where((p % 64) < 32, 1.0, -1.0)[:, None]
    sinF = sinw[:, fi].T * sgn
    ropeF = np.stack([cosF, sinF]).astype(np.float32)
    angs = _rope_angles(spos.reshape(-1))
    ropeS = np.stack([np.cos(angs).reshape(NSLOT, 128, 32).transpose(1, 0, 2).reshape(128, NSLOT * 32),
                      np.sin(angs).reshape(NSLOT, 128, 32).transpose(1, 0, 2).reshape(128, NSLOT * 32)]).astype(np.float32)
    u = np.arange(128)[:, None]
    s_ = np.arange(128)[None, :]
    consts = np.zeros((7, 128, 128), np.float32)
    consts[0] = np.eye(128)
    consts[1] = (u > s_)
    consts[2] = (u < s_)
    consts[3] = (u <= s_)
    consts[4] = (u >= s_)
    consts[5] = 1.0
    amask = np.zeros((4, 128, 128), np.float32)
    mL = (u >= s_).astype(np.float32)
    mR = (u <= s_).astype(np.float32)
    amask[0] = mL * (1.0 if q > 0 else 0.0)
    amask[1] = mL
    amask[2] = mR
    amask[3] = mR * (1.0 if q < 3 else 0.0)
    cvec = np.concatenate([np.asarray(inp['c'], np.float32)[b].reshape(8, 128),
                           np.asarray(inp['c_ctx'], np.float32).reshape(8, 128)], 0)
    m = dict(xw=xw, xe=xe, xs=xs, xsh=xsh, fl=fl, cvec=cvec, consts=consts, amask=amask,
             ropeF=ropeF, ropeT=ropeT, ropeS=ropeS)
    for k in ('ada_w', 'ada_b', 'norm_w', 'ffn_w_in', 'ffn_w_out', 'ab_w_in', 'ab_w_out', 'ret_log_gamma',
              'ret_norm_w', 'mlstm_conv_w', 'mlstm_conv_b', 'mlstm_gate_b', 'mlstm_norm_w', 'attn_w_in',
              'attn_w_out', 'attn_q_norm_w', 'attn_k_norm_w', 'attn_sink'):
        m[k] = np.ascontiguousarray(np.asarray(inp[k], np.float32))
    return m


_NC_CACHE = {}


def kernel(**inp):
    if 'nc' not in _NC_CACHE:
        _NC_CACHE['nc'] = build()[0]
    nc = _NC_CACHE['nc']
    in_maps = [_core_inputs(inp, c) for c in range(NCORE)]
    res = run_bass_kernel_spmd(nc, in_maps, core_ids=list(range(NCORE)))
    out = np.zeros((2, SEQ, D), np.float32)
    for c in range(NCORE):
        b, q = c // 4, c % 4
        out[b, 2048 * q:2048 * (q + 1)] = res.results[c]["y"]
    return out
```

```python
import contextlib
import math
import numpy as np
import concourse.bass as bass
import concourse.mybir as mybir
from concourse.bass_utils import run_bass_kernel_spmd

F32 = mybir.dt.float32
BF16 = mybir.dt.bfloat16
ALU = mybir.AluOpType
AF = mybir.ActivationFunctionType
AX = mybir.AxisListType

_DTSZ = {F32: 4, BF16: 2}
SEM_LIMIT = 12000
DMA_POOL = 12


def _region(ap):
    sp = str(ap.space).upper()
    if 'SB' not in sp and 'PSUM' not in sp:
        return None
    if 'PSUM' in sp:
        return (ap.name, 0, 128, 0, 2048)
    pat = ap.ap
    esz = _DTSZ[ap.dtype]
    pstep, pcount = pat[0]
    off = ap.offset
    if pstep == 0:
        p0 = 0
        f0 = off
    else:
        p0 = off // pstep
        f0 = off - p0 * pstep
    ext = 0
    for stp, cn in pat[1:]:
        ext += abs(stp) * (cn - 1)
    return (ap.name, p0, p0 + pcount, f0 * esz, (f0 + ext + 1) * esz)


class Prog:
    ENGS = ('tensor', 'vector', 'scalar', 'gpsimd', 'sync')

    def __init__(self, nc, same_engine_sync=True):
        self.nc = nc
        self.ops = []
        self.track = {}
        self.same_engine_sync = same_engine_sync
        self.dma_hist = {e: [] for e in self.ENGS}
        self.store_ops = []

    def _add(self, eng, fn, outs, ins, is_dma=False, extra_deps=(), force=False):
        if getattr(self, 'frozen', False) and not force:
            return -1
        idx = len(self.ops)
        deps = set(extra_deps)
        self.ops.append(dict(eng=eng, fn=fn, deps=deps, is_dma=is_dma, signaled=False))
        for ap in ins:
            r = _region(ap)
            if r is not None:
                self._access(idx, eng, r, False, deps)
        for ap in outs:
            r = _region(ap)
            if r is not None:
                self._access(idx, eng, r, True, deps)
        if is_dma:
            h = self.dma_hist[eng]
            if len(h) >= DMA_POOL:
                deps.add(h[-DMA_POOL])
            h.append(idx)
        deps.discard(idx)
        return idx

    def _access(self, idx, eng, r, is_write, deps):
        name, p0, p1, b0, b1 = r
        recs = self.track.get(name, [])
        keep = []
        for rec in recs:
            (q0, q1, c0, c1, oi, ow, oe) = rec
            if q1 <= p0 or p1 <= q0 or c1 <= b0 or b1 <= c0 or oi == idx:
                keep.append(rec)
                continue
            if is_write or ow or (name.startswith('pb') and oe != eng):
                pe_pe = (eng == 'tensor' and oe == 'tensor')
                if not pe_pe:
                    deps.add(oi)
            covered = (p0 <= q0 and q1 <= p1 and b0 <= c0 and c1 <= b1)
            if is_write and covered and not (eng == 'tensor' and oe == 'tensor' and not ow):
                continue
            if (not is_write) and (not ow) and oe == eng and covered and not self.ops[oi]['is_dma']:
                continue
            keep.append(rec)
        keep.append((p0, p1, b0, b1, idx, is_write, eng))
        self.track[name] = keep

    def op(self, eng, fn, outs, ins):
        return self._add(eng, fn, outs, ins)

    def dma(self, eng, out, in_, store=False, **kw):
        i = self._add(eng, lambda e: e.dma_start(out=out, in_=in_, **kw), [out], [in_], is_dma=True)
        if store and i >= 0:
            self.store_ops.append(i)
        return i

    def mm(self, out, lhsT, rhs, start=True, stop=True, after=None):
        i = self.op('tensor', lambda e: e.matmul(out, lhsT, rhs, start=start, stop=stop), [out], [lhsT, rhs])
        if after is not None and i >= 0 and after >= 0:
            self.ops[i]['deps'].add(after)
        return i

    def transpose(self, out, in_, ident):
        return self.op('tensor', lambda e: e.transpose(out, in_, ident), [out], [in_, ident])

    def act(self, out, in_, func, bias=None, scale=None, accum_out=None):
        kw = {}
        ins = [in_]
        outs = [out]
        if bias is not None:
            kw['bias'] = bias
            if not isinstance(bias, (int, float)):
                ins.append(bias)
        if scale is not None:
            kw['scale'] = scale
            if not isinstance(scale, (int, float)):
                ins.append(scale)
        if accum_out is not None:
            kw['accum_out'] = accum_out
            outs.append(accum_out)
        return self.op('scalar', lambda e: e.activation(out, in_, func, **kw), outs, ins)

    def tt(self, eng, out, in0, in1, op):
        return self.op(eng, lambda e: e.tensor_tensor(out, in0, in1, op), [out], [in0, in1])

    def ts(self, eng, out, in0, s1, s2, op0, op1=None):
        ins = [in0] + [s for s in (s1, s2) if s is not None and not isinstance(s, (int, float))]
        kw = {}
        if op1 is not None:
            kw['op1'] = op1
        return self.op(eng, lambda e: e.tensor_scalar(out, in0, s1, s2, op0, **kw), [out], ins)

    def stt(self, eng, out, in0, scalar, in1, op0, op1):
        ins = [in0, in1] + ([scalar] if not isinstance(scalar, (int, float)) else [])
        return self.op(eng, lambda e: e.scalar_tensor_tensor(out, in0, scalar, in1, op0, op1), [out], ins)

    def copy(self, eng, out, in_):
        if eng == 'scalar':
            return self.op(eng, lambda e: e.copy(out, in_), [out], [in_])
        return self.op(eng, lambda e: e.tensor_copy(out, in_), [out], [in_])

    def memset(self, eng, out, val):
        return self.op(eng, lambda e: e.memset(out, val), [out], [])

    def recip(self, out, in_):
        return self.op('vector', lambda e: e.reciprocal(out, in_), [out], [in_])

    def reduce(self, eng, out, in_, op, axis=AX.X):
        return self.op(eng, lambda e: e.tensor_reduce(out, in_, axis, op), [out], [in_])

    def emit(self):
        nc = self.nc
        ops = self.ops
        self._add('sync', None, [], [], extra_deps=self.store_ops, force=True)
        for o in ops:
            if o['is_dma']:
                o['signaled'] = True
            for d in o['deps']:
                ops[d]['signaled'] = True
        cnt = {e: 0 for e in self.ENGS}
        dcnt = {e: 0 for e in self.ENGS}
        nsem_eng = {e: 0 for e in self.ENGS}
        for o in ops:
            e = o['eng']
            if not o['signaled']:
                continue
            if o['is_dma']:
                k = dcnt[e]
                dcnt[e] += 1
                o['sem'] = ('d', e, k % DMA_POOL)
                o['val'] = 16 * (k // DMA_POOL + 1)
                o['sidx'] = None
            else:
                k = cnt[e]
                cnt[e] += 1
                o['sem'] = ('c', e, k // SEM_LIMIT)
                o['val'] = (k % SEM_LIMIT) + 1
                o['sidx'] = k
                nsem_eng[e] = k // SEM_LIMIT + 1
        sems = {}
        st = contextlib.ExitStack()
        for e in self.ENGS:
            for j in range(nsem_eng[e]):
                sems[('c', e, j)] = st.enter_context(nc.semaphore(f"c_{e}_{j}"))
            for j in range(min(DMA_POOL, dcnt[e])):
                sems[('d', e, j)] = st.enter_context(nc.semaphore(f"d_{e}_{j}"))
        seen = {e: {f: -1 for f in self.ENGS} for e in self.ENGS}
        seen_dma = {e: set() for e in self.ENGS}
        per_eng = {e: [] for e in self.ENGS}
        nwaits = 0
        for o in ops:
            e = o['eng']
            waits = {}
            for d in sorted(o['deps']):
                p = ops[d]
                if p['is_dma']:
                    if d in seen_dma[e]:
                        continue
                    seen_dma[e].add(d)
                    waits[p['sem']] = max(waits.get(p['sem'], 0), p['val'])
                else:
                    f = p['eng']
                    if f == e and not self.same_engine_sync:
                        continue
                    if p['sidx'] <= seen[e][f]:
                        continue
                    seen[e][f] = p['sidx']
                    waits[p['sem']] = max(waits.get(p['sem'], 0), p['val'])
            nwaits += len(waits)
            per_eng[e].append((o, list(waits.items())))
        self.stats = dict(n_ops=len(ops), n_waits=nwaits, per_eng={e: len(v) for e, v in per_eng.items()})
        with st, nc.Block() as block:
            def body(engname):
                def run(eng):
                    for o, waits in per_eng[engname]:
                        for key, val in waits:
                            eng.wait_ge(sems[key], val)
                        if o['fn'] is None:
                            continue
                        ins = o['fn'](eng)
                        if o['signaled']:
                            ins.then_inc(sems[o['sem']], 16 if o['is_dma'] else 1)
                return run
            block.tensor(body('tensor'))
            block.vector(body('vector'))
            block.scalar(body('scalar'))
            block.gpsimd(body('gpsimd'))
            block.sync(body('sync'))
        return self.stats


D = 1024
SEQ = 8192
NCORE = 8
NW = 18
NCH = 20
NSLOT = 48
NF = 256
EPS = 1e-6
D_FF = 2816
NFT = 22
LNK = math.log(0.125)
HT_N = 2564
CTX0 = 1
WIN0 = 259
FL_VF = 0
FL_EDGE = 20
FL_FF = 22
FL_FB = 70
FL_SH = 118


def bc(ap, shape):
    return ap.broadcast_to(shape)


class Region:
    def __init__(self, t, base, cap, name):
        self.t, self.base, self.cap, self.name = t, base, cap, name
        self.off = 0
        self.peak = 0

    def alloc(self, shape, dt):
        n = 1
        for s in shape:
            n *= s
        nb = (n * _DTSZ[dt] + 63) // 64 * 64
        if self.off + nb > self.cap:
            raise RuntimeError(f"region {self.name} overflow: {self.off + nb} > {self.cap}")
        a = self.base + self.off
        v = self.t[:, a // 4:(a + nb) // 4]
        if dt != F32:
            v = v.bitcast(dt)
        v = v[:, 0:n]
        self.off += nb
        self.peak = max(self.peak, self.off)
        if len(shape) == 1:
            return v
        names = [chr(ord('a') + i) for i in range(len(shape))]
        pat = "p (" + " ".join(names) + ") -> p " + " ".join(names)
        return v.rearrange(pat, **{nm: s for nm, s in zip(names, shape)})

    def mark(self):
        return self.off

    def release(self, m):
        self.off = m


class _Stop(Exception):
    pass


ARENA_BYTES = 212480
P_BYTES = 35328
X_BYTES = 83968


def build(dbg=None):
    nc = bass.Bass("TRN2", target_bir_lowering=False)

    def din(name, shape):
        return nc.dram_tensor(name, list(shape), F32, kind="ExternalInput").ap()

    xw = din("xw", [NCH, 128, D])
    xe = din("xe", [128, D])
    xs = din("xs", [NSLOT, 128, D])
    xsh = din("xsh", [128, D])
    fl = din("fl", [1, NF])
    cvec = din("cvec", [16, 128])
    ada_w = din("ada_w", [2, D, 6 * D])
    ada_b = din("ada_b", [2, 6 * D])
    norm_w = din("norm_w", [2, 2, D])
    ffn_w_in = din("ffn_w_in", [2, D, 2 * D_FF])
    ffn_w_out = din("ffn_w_out", [2, D_FF, D])
    ab_w_in = din("ab_w_in", [1, D, 4128])
    ab_w_out = din("ab_w_out", [1, D, D])
    ret_lg = din("ret_log_gamma", [1, 2, 8])
    ret_nw = din("ret_norm_w", [1, 512])
    conv_w = din("mlstm_conv_w", [1, 3, D])
    conv_b = din("mlstm_conv_b", [1, D])
    gate_b = din("mlstm_gate_b", [1, 4, 8])
    ml_nw = din("mlstm_norm_w", [1, 512])
    at_w_in = din("attn_w_in", [1, D, 1536])
    at_w_out = din("attn_w_out", [1, D, D])
    at_qn = din("attn_q_norm_w", [1, 64])
    at_kn = din("attn_k_norm_w", [1, 64])
    at_sink = din("attn_sink", [1, 16])
    consts = din("consts", [7, 128, 128])
    amask = din("amask", [4, 128, 128])
    ropeF = din("ropeF", [2, 128, 2304])
    ropeT = din("ropeT", [2, 128, NW * 32])
    ropeS = din("ropeS", [2, 128, NSLOT * 32])
    y = nc.dram_tensor("y", [2048, D], F32, kind="ExternalOutput").ap()
    dbg_out = None
    if dbg is not None:
        dbg_out = nc.dram_tensor("dbg", [NCH, 128, D], F32, kind="ExternalOutput").ap()

    st = contextlib.ExitStack()
    with st:
        arena_t = st.enter_context(nc.sbuf_tensor("arena", [128, ARENA_BYTES // 4], F32))
        RP = Region(arena_t, 0, P_BYTES, "P")
        RX = Region(arena_t, P_BYTES, X_BYTES, "X")
        RM = Region(arena_t, P_BYTES + X_BYTES, ARENA_BYTES - P_BYTES - X_BYTES, "MF")
        banks = [st.enter_context(nc.psum_tensor(f"pb{i}", [128, 512], F32)) for i in range(8)]
        P = Prog(nc)
        P.marks = []

        def mark(nm):
            P.marks.append((nm, sum(1 for o in P.ops if o['eng'] == 'tensor')))

        def ck(name, aps):
            if dbg != name:
                return
            k = 0
            for ap in aps:
                n = ap.shape[1] if len(ap.shape) == 2 else None
                flat = ap
                npart = ap.shape[0]
                P.dma('sync' if flat.dtype == F32 else 'gpsimd', dbg_out[k][0:npart, 0:flat.shape[1]], flat, store=True)
                k += 1
            P.frozen = True

        def pb(i, shape, dt=F32, off=0):
            n = 1
            for s in shape:
                n *= s
            nb = n * _DTSZ[dt]
            v = banks[i][:, off // 4:(off + nb + 3) // 4]
            if dt != F32:
                v = v.bitcast(dt)
            v = v[:, 0:n]
            if len(shape) == 1:
                return v
            names = [chr(ord('a') + k) for k in range(len(shape))]
            pat = "p (" + " ".join(names) + ") -> p " + " ".join(names)
            return v.rearrange(pat, **{nm: s for nm, s in zip(names, shape)})

        cst = RP.alloc([7, 128], F32)
        P.dma('sync', cst, consts.rearrange("c p n -> p c n"))
        identF = cst[:, 0, :]
        MfF, MbF = cst[:, 1, :], cst[:, 2, :]
        onesF = cst[:, 5, :]
        cstb = RP.alloc([7, 128], BF16)
        P.copy('vector', cstb, cst)
        identB = cstb[:, 0, :]
        maskF_b, maskB_b = cstb[:, 3, :], cstb[:, 4, :]
        FL = RP.alloc([NF], F32)
        P.dma('sync', FL, bc(fl, [128, NF]))
        epsb = RP.alloc([1], F32)
        P.memset('vector', epsb, EPS)
        lnk = RP.alloc([1], F32)
        P.memset('vector', lnk, LNK)
        colA = RP.alloc([112], F32)
        colB = RP.alloc([48], F32)
        svf = RP.alloc([16], F32)
        sv = RP.alloc([8, 2], BF16)
        modT = RP.alloc([2, 2, 48], F32)
        gB = RP.alloc([4, D], F32)
        WM = RP.alloc([2, 2, 2, 8], F32)
        SHc = RP.alloc([2, 2, 2, 8], F32)
        junk = RP.alloc([D], BF16)
        xnb = [RP.alloc([D], BF16) for _ in range(2)]
        t32 = RP.alloc([8, 128], F32)
        ssq = RP.alloc([4], F32)

        m0 = RM.mark()
        stg = RM.alloc([128], F32)
        stg2 = RM.alloc([128], F32)
        P.dma('sync', stg[0:16, :], cvec)
        P.dma('sync', stg[16:64, :], ada_b[0].rearrange("(j p) -> j p", p=128))
        P.dma('sync', stg[64:112, :], ada_b[1].rearrange("(j p) -> j p", p=128))
        tp = pb(0, [112])
        P.transpose(tp, stg[0:112, :], identF[0:112, 0:112])
        P.copy('vector', colA, tp)
        P.dma('sync', stg2[0:32, :], norm_w.rearrange("l i (j p) -> (l i j) p", p=128))
        P.dma('sync', stg2[32:36, :], ret_nw[0].rearrange("(j p) -> j p", p=128))
        P.dma('sync', stg2[36:40, :], ml_nw[0].rearrange("(j p) -> j p", p=128))
        P.dma('sync', stg2[40:48, :], conv_b[0].rearrange("(j p) -> j p", p=128))
        tp2 = pb(0, [48], off=1024)
        P.transpose(tp2, stg2[0:48, :], identF[0:48, 0:48])
        P.copy('vector', colB, tp2)
        RM.release(m0)
        ck('A0', [colA, colB, FL, cst[:, 1, :]])
        P.act(svf, colA[:, 0:16], AF.Silu)
        P.copy('vector', sv[:, :, 0], svf[:, 0:8])
        P.copy('vector', sv[:, :, 1], svf[:, 8:16])

        gslot = {(0, 0, 2): 0, (0, 0, 5): 1, (0, 1, 2): 2, (0, 1, 5): 3, (1, 0, 2): 0, (1, 0, 5): 1}

        def modulation(l):
            m0 = RM.mark()
            svrep = RM.alloc([2, 8, 128], BF16)
            for v_ in range(2):
                P.copy('vector', svrep[:, v_, :, :], bc(svf[:, 8 * v_:8 * v_ + 8].unsqueeze(2), [128, 8, 128]))
            wblk = [RM.alloc([8, 1024], BF16) for _ in range(2)]
            bbc = RM.alloc([1024], F32)
            for j in range(6):
                wb = wblk[j % 2]
                P.dma('gpsimd', wb, ada_w[l].rearrange("(kc p) n -> p kc n", p=128)[:, :, j * 1024:(j + 1) * 1024])
                ps = pb(1, [8, 2])
                for n in range(8):
                    for kc in range(8):
                        P.mm(ps[:, n, :], wb[:, kc, n * 128:(n + 1) * 128], sv[:, kc, :], start=(kc == 0), stop=(kc == 7))
                for v_ in range(2):
                    P.tt('vector', modT[:, l, v_, j * 8:(j + 1) * 8], ps[:, :, v_],
                         colA[:, 16 + 48 * l + j * 8:16 + 48 * l + j * 8 + 8], ALU.add)
                if j in (2, 5):
                    P.dma('sync', bbc, bc(ada_b[l:l + 1, j * 1024:(j + 1) * 1024], [128, 1024]))
                    for v_ in range(2):
                        if (l, v_, j) not in gslot:
                            continue
                        for hf in range(2):
                            pg = pb(2 + hf, [512])
                            for kc in range(8):
                                P.mm(pg, svrep[:, v_, kc, :], wb[:, kc, hf * 512:(hf + 1) * 512], start=(kc == 0), stop=(kc == 7))
                            P.tt('vector', gB[:, gslot[(l, v_, j)], hf * 512:(hf + 1) * 512], pg, bbc[:, hf * 512:(hf + 1) * 512], ALU.add)
            RM.release(m0)

        def make_wm(l):
            for i in range(2):
                for v_ in range(2):
                    sc = modT[:, l, v_, (3 * i + 1) * 8:(3 * i + 2) * 8]
                    nw = colB[:, (2 * l + i) * 8:(2 * l + i) * 8 + 8]
                    P.stt('vector', WM[:, l, i, v_, :], sc, 1.0, nw, ALU.add, ALU.mult)
                    P.copy('vector', SHc[:, l, i, v_, :], modT[:, l, v_, (3 * i) * 8:(3 * i) * 8 + 8])

        modulation(0)
        make_wm(0)
        mark('mod0')
        ck('A', [colA, colB, modT.rearrange('p a b c -> p (a b c)'), gB[:, 0, :], gB[:, 3, :], WM.rearrange('p a b c d -> p (a b c d)')])

        cnt_h = [0]

        def make_hT(x_sb, dest, wcol, shcol, flag=None, alt_bank=None):
            k = cnt_h[0] % 2
            cnt_h[0] += 1
            ss = ssq[:, 2 * k:2 * k + 1]
            rs = ssq[:, 2 * k + 1:2 * k + 2]
            P.act(junk, x_sb, AF.Square, accum_out=ss)
            P.act(rs, ss, AF.Sqrt, bias=epsb, scale=1.0 / D)
            P.recip(rs, rs)
            xn = xnb[k]
            P.act(xn, x_sb, AF.Identity, scale=rs)
            tps = pb(0 if (alt_bank is None or k == 0) else alt_bank, [8, 128], BF16)
            for kc in range(8):
                P.transpose(tps[:, kc, :], xn[:, kc * 128:(kc + 1) * 128], identB)
            if flag is None:
                for kc in range(8):
                    P.act(dest[:, kc, :], tps[:, kc, :], AF.Identity, bias=shcol[:, kc:kc + 1], scale=wcol[:, kc:kc + 1])
            else:
                P.tt('vector', t32, tps, bc(wcol.unsqueeze(2), [128, 8, 128]), ALU.mult)
                P.tt('gpsimd', t32, t32, bc(shcol.unsqueeze(2), [128, 8, 128]), ALU.add)
                P.ts('vector', dest, t32, flag, None, ALU.mult)

        hT = RX.alloc([8, HT_N], BF16)
        AA = RX.alloc([NCH, 2, 16], F32)
        BB = RX.alloc([NCH, 2, 16], F32)
        DEC = RX.alloc([NCH, 2, 16], F32)
        Sacc = RX.alloc([16, 65], F32)
        RR = RX.alloc([2, 16], F32)
        expR = RX.alloc([2, 16], F32)
        mXf = RX.mark()
        mMF = RM.mark()
        win = ab_w_in[0].rearrange("(kc p) n -> p kc n", p=128)

        xbuf = [RX.alloc([D], F32) for _ in range(2)]
        P.memset('gpsimd', hT[:, :, 0:1], 0.0)
        P.memset('gpsimd', hT[:, :, 257:258], 0.0)

        def tokcols(c):
            return (CTX0 + c * 128) if c < 2 else (WIN0 + (c - 2) * 128)

        for c in range(NCH):
            xb = xbuf[c % 2]
            P.dma('sync', xb, xw[c])
            v_ = 1 if c < 2 else 0
            col0 = tokcols(c)
            fg = None if c not in (2, NCH - 1) else FL[:, FL_VF + c:FL_VF + c + 1]
            make_hT(xb, hT[:, :, col0:col0 + 128], WM[:, 0, 0, v_, :], SHc[:, 0, 0, v_, :], fg, alt_bank=7)
        hTh = RX.alloc([8, 128], BF16)
        xb = xbuf[0]
        P.dma('sync', xb, xe)
        make_hT(xb, hTh, WM[:, 0, 0, 0, :], SHc[:, 0, 0, 0, :])
        P.ts('vector', hT[:, :, 258:259], hTh[:, :, 0:1], FL[:, FL_EDGE:FL_EDGE + 1], None, ALU.mult)
        P.ts('vector', hT[:, :, 2563:2564], hTh[:, :, 1:2], FL[:, FL_EDGE + 1:FL_EDGE + 2], None, ALU.mult)

        ck('B', [hT[:, 0, 0:1024], hT[:, 7, 1540:2564]])
        mark('hT')
        Gpre = RX.alloc([NCH, 32], F32)
        LF = RX.alloc([NCH, 2, 16], F32)
        II = RX.alloc([NCH, 2, 16], F32)
        Wg = RM.alloc([8, 32], BF16)
        P.dma('gpsimd', Wg, win[:, :, 4096:4128])
        gbb = RM.alloc([32], F32)
        P.dma('sync', gbb, bc(gate_b[0].rearrange("a h -> (a h)").unsqueeze(0), [128, 32]))
        lgb = RM.alloc([2, 8], F32)
        P.dma('sync', lgb.rearrange("p a h -> p (a h)"), bc(ret_lg[0].rearrange("a h -> (a h)").unsqueeze(0), [128, 16]))
        ck('C0', [gbb, lgb.rearrange('p a h -> p (a h)'), Wg.rearrange('p a b -> p (a b)')])
        for c in range(NCH):
            c0 = tokcols(c)
            pg = pb(3, [128])[:, 32 * (c % 4):32 * (c % 4) + 32]
            for kc in range(8):
                P.mm(pg, hT[:, kc, c0:c0 + 128], Wg[:, kc, :], start=(kc == 0), stop=(kc == 7))
            P.tt('vector', Gpre[:, c, :], pg, gbb, ALU.add)
        ck('C1', [Gpre.rearrange('p a b -> p (a b)')])
        VFb = FL[:, FL_VF:FL_VF + NCH]
        tmpg = RX.alloc([NCH, 8], F32)
        for d_ in range(2):
            fcol = Gpre[:, :, 8 + 16 * d_:16 + 16 * d_]
            icol = Gpre[:, :, 16 * d_:8 + 16 * d_]
            P.act(tmpg, fcol, AF.Exp, scale=-1.0)
            P.act(tmpg, tmpg, AF.Ln, bias=1.0)
            P.stt('vector', LF[:, :, d_, 8:16], tmpg, -1.0, bc(VFb.unsqueeze(2), [128, NCH, 8]), ALU.mult, ALU.mult)
            P.tt('vector', LF[:, :, d_, 0:8], bc(lgb[:, d_, :].unsqueeze(1), [128, NCH, 8]), bc(VFb.unsqueeze(2), [128, NCH, 8]), ALU.mult)
            P.memset('gpsimd', II[:, :, d_, 0:8], 0.0)
            P.copy('gpsimd', II[:, :, d_, 8:16], icol)
        ck('C2', [LF.rearrange('p a b c -> p (a b c)'), II.rearrange('p a b c -> p (a b c)')])
        LFc = RX.alloc([2, NCH * 16], F32)
        for d_ in range(2):
            P.copy('gpsimd', LFc[:, d_, :].rearrange("p (c l) -> p c l", l=16), LF[:, :, d_, :])
        for d_ in range(2):
            pe = pb(4 + d_, [NCH, 16])
            pef = pe.rearrange("p c l -> p (c l)")
            for (a_, b_) in ((0, 128), (128, 256), (256, 320)):
                P.mm(pef[:, a_:b_], MfF if d_ == 0 else MbF, LFc[:, d_, a_:b_])
            if d_ == 0 and dbg == 'C2a':
                P.copy('vector', AA.rearrange('p a b c -> p (a b c)')[:, 0:320], pef)
                ck('C2a', [AA.rearrange('p a b c -> p (a b c)'), LFc.rearrange('p a b -> p (a b)')])
            P.act(AA[:, :, d_, :], pe, AF.Exp, scale=-1.0)
            if d_ == 0:
                ck('C2b', [AA.rearrange('p a b c -> p (a b c)')])
            P.tt('vector', BB[:, :, d_, :], pe, II[:, :, d_, :], ALU.add)
            if d_ == 0:
                ck('C2c', [BB.rearrange('p a b c -> p (a b c)')])
            P.act(BB[:, :, d_, :], BB[:, :, d_, :], AF.Exp, bias=lnk)
            if d_ == 0:
                ck('C2d', [BB.rearrange('p a b c -> p (a b c)')])
            P.tt('vector', BB[:, :, d_, :], BB[:, :, d_, :], bc(VFb.unsqueeze(2), [128, NCH, 16]), ALU.mult)
        ck('C3', [AA.rearrange('p a b c -> p (a b c)'), BB.rearrange('p a b c -> p (a b c)')])
        LFf = LF.rearrange("p a b c -> p (a b c)")
        for hf in range(2):
            pt_ = pb(6, [10, 2, 16])
            ptf = pt_.rearrange("p a b c -> p (a b c)")
            for (a_, b_) in ((0, 128), (128, 256), (256, 320)):
                P.mm(ptf[:, a_:b_], onesF, LFf[:, hf * 320 + a_:hf * 320 + b_])
            P.act(DEC[:, hf * 10:(hf + 1) * 10, :, :], pt_, AF.Exp)

        ck('C', [AA.rearrange('p a b c -> p (a b c)'), BB.rearrange('p a b c -> p (a b c)'), DEC.rearrange('p a b c -> p (a b c)'), Gpre.rearrange('p a b -> p (a b)')])
        mark('prepass')
        P.memset('vector', Sacc, 0.0)
        P.memset('vector', RR, 0.0)
        Wv = RM.alloc([8, 1024], BF16)
        P.dma('gpsimd', Wv[:, :, 0:512], win[:, :, 1024:1536])
        P.dma('gpsimd', Wv[:, :, 512:1024], win[:, :, 3072:3584])
        Wkr = RM.alloc([8, 512], BF16)
        P.dma('gpsimd', Wkr, win[:, :, 512:1024])
        Wtap = RM.alloc([3, 8, 512], BF16)
        m1 = RM.mark()
        cwk = RM.alloc([3, 512], F32)
        for j in range(3):
            P.dma('sync', cwk[:, j, :], bc(conv_w[0, j:j + 1, 512:1024], [128, 512]))
        Wkm = RM.alloc([8, 512], BF16)
        P.dma('gpsimd', Wkm, win[:, :, 2560:3072])
        for j in range(3):
            P.tt('vector', Wtap[:, j, :, :], Wkm, bc(cwk[:, j, :].unsqueeze(1), [128, 8, 512]), ALU.mult)
        RM.release(m1)
        cbb = RM.alloc([512], F32)
        P.dma('sync', cbb, bc(conv_b[0:1, 512:1024], [128, 512]))
        rS = RM.alloc([2, NSLOT, 32], F32)
        P.dma('sync', rS.rearrange("p a s f -> p a (s f)"), ropeS.rearrange("a p n -> p a n"))
        Kfb = RM.alloc([16, 128], BF16)
        Vext = RM.alloc([16, 65], BF16)
        ra = RM.alloc([2, 8, 32], F32)
        krf = [RM.alloc([512], F32) for _ in range(2)]
        Vsb = [RM.alloc([16, 64], BF16) for _ in range(2)]
        kmts = [RM.alloc([512], F32) for _ in range(2)]
        gps = [RM.alloc([32], F32) for _ in range(2)]
        xb = xbuf[1]
        P.dma('sync', xb, xsh)
        make_hT(xb, hTh, WM[:, 0, 0, 0, :], SHc[:, 0, 0, 0, :])
        P.tt('vector', hTh, hTh, bc(FL[:, FL_SH:FL_SH + 128].unsqueeze(1), [128, 8, 128]), ALU.mult)
        hTs = [RX.alloc([8, 130], BF16) for _ in range(2)]
        Ktok = RX.alloc([16, 64], BF16)
        sg = RX.alloc([160], F32)

        def slot_A1(s):
            k = s % 2
            xb = xbuf[k]
            P.dma('sync', xb, xs[s])
            ss = ssq[:, 2 * k:2 * k + 1]
            rs = ssq[:, 2 * k + 1:2 * k + 2]
            P.act(junk, xb, AF.Square, accum_out=ss)
            P.act(rs, ss, AF.Sqrt, bias=epsb, scale=1.0 / D)
            P.recip(rs, rs)
            P.act(xnb[k], xb, AF.Identity, scale=rs)

        def slot_A2(s):
            k = s % 2
            hs = hTs[k]
            xn = xnb[k]
            wcol, shcol = WM[:, 0, 0, 0, :], SHc[:, 0, 0, 0, :]
            P.copy('scalar', hs[:, :, 0:1], hTh[:, :, 2 * s:2 * s + 1])
            P.copy('scalar', hs[:, :, 129:130], hTh[:, :, 2 * s + 1:2 * s + 2])
            tps = pb(0, [8, 128], BF16)
            for kc in range(8):
                P.transpose(tps[:, kc, :], xn[:, kc * 128:(kc + 1) * 128], identB)
            for kc in range(8):
                P.act(hs[:, kc, 1:129], tps[:, kc, :], AF.Identity, bias=shcol[:, kc:kc + 1], scale=wcol[:, kc:kc + 1])
            pkr = pb(1, [512])
            for kc in range(8):
                P.mm(pkr, hs[:, kc, 1:129], Wkr[:, kc, :], start=(kc == 0), stop=(kc == 7))
            P.copy('scalar', krf[k], pkr)
            pkm = pb(2, [512])
            for j in range(3):
                for kc in range(8):
                    P.mm(pkm, hs[:, kc, j:j + 128], Wtap[:, j, kc, :], start=(j == 0 and kc == 0), stop=(j == 2 and kc == 7))
            P.tt('vector', kmts[k], pkm, cbb, ALU.add)
            for hf in range(2):
                pv = pb(3 + hf, [512])
                for kc in range(8):
                    P.mm(pv, hs[:, kc, 1:129], Wv[:, kc, hf * 512:(hf + 1) * 512], start=(kc == 0), stop=(kc == 7))
                P.copy('scalar', Vsb[k][:, hf * 8:(hf + 1) * 8, :], pv.rearrange("p (h d) -> p h d", h=8))
            pg = pb(5, [32])
            for kc in range(8):
                P.mm(pg, hs[:, kc, 1:129], Wg[:, kc, :], start=(kc == 0), stop=(kc == 7))
            P.tt('vector', gps[k], pg, gbb, ALU.add)

        fsel = sg[:, 32:40]
        isel = sg[:, 40:56]
        lf16 = sg[:, 56:72]
        lfm = sg[:, 72:104]
        rsel = sg[:, 104:120]
        bex = sg[:, 120:136]
        t8 = sg[:, 136:144]
        RRf = RR.rearrange("p a l -> p (a l)")

        def slot_B_early(s):
            ff = FL[:, FL_FF + s:FL_FF + s + 1]
            fb = FL[:, FL_FB + s:FL_FB + s + 1]
            gp = gps[s % 2]
            P.ts('vector', fsel, gp[:, 8:16], ff, None, ALU.mult)
            P.stt('vector', fsel, gp[:, 24:32], fb, fsel, ALU.mult, ALU.add)
            P.act(t8, fsel, AF.Exp, scale=-1.0)
            P.act(t8, t8, AF.Ln, bias=1.0)
            P.ts('vector', lf16[:, 0:8], lgb[:, 0, :], ff, None, ALU.mult)
            P.stt('vector', lf16[:, 0:8], lgb[:, 1, :], fb, lf16[:, 0:8], ALU.mult, ALU.add)
            P.ts('vector', lf16[:, 8:16], t8, -1.0, None, ALU.mult)
            P.ts('vector', lfm[:, 0:16], lf16, ff, None, ALU.mult)
            P.ts('vector', lfm[:, 16:32], lf16, fb, None, ALU.mult)
            pe = pb(5, [16], off=1024)
            P.mm(pe, MfF, lfm[:, 0:16], start=True, stop=False)
            P.mm(pe, MbF, lfm[:, 16:32], start=False, stop=True)
            pt_ = pb(5, [32], off=1536)
            P.mm(pt_, onesF, lfm)
            P.memset('vector', isel[:, 0:8], 0.0)
            P.ts('vector', isel[:, 8:16], gp[:, 0:8], ff, None, ALU.mult)
            P.stt('vector', isel[:, 8:16], gp[:, 16:24], fb, isel[:, 8:16], ALU.mult, ALU.add)
            P.ts('vector', rsel, RR[:, 0, :], ff, None, ALU.mult)
            P.stt('vector', rsel, RR[:, 1, :], fb, rsel, ALU.mult, ALU.add)
            P.tt('vector', rsel, rsel, isel, ALU.add)
            k3 = krf[s % 2].rearrange("p (h d) -> p h d", h=8)
            x1, x2 = k3[:, :, 0:32], k3[:, :, 32:64]
            cs = bc(rS[:, 0, s, :].unsqueeze(1), [128, 8, 32])
            sn = bc(rS[:, 1, s, :].unsqueeze(1), [128, 8, 32])
            P.tt('vector', ra[:, 0], x1, cs, ALU.mult)
            P.tt('vector', ra[:, 1], x2, sn, ALU.mult)
            P.tt('vector', Ktok[:, 0:8, 0:32], ra[:, 0], ra[:, 1], ALU.subtract)
            P.tt('vector', ra[:, 0], x2, cs, ALU.mult)
            P.tt('vector', ra[:, 1], x1, sn, ALU.mult)
            P.tt('vector', Ktok[:, 0:8, 32:64], ra[:, 0], ra[:, 1], ALU.add)
            P.act(Ktok[:, 8:16, :], kmts[s % 2].rearrange("p (h d) -> p h d", h=8), AF.Silu)
            P.act(Kfb[:, :, 0:64], Ktok, AF.Identity, scale=ff)
            P.act(Kfb[:, :, 64:128], Ktok, AF.Identity, scale=fb)
            P.tt('vector', rsel, rsel, pe, ALU.add)
            P.act(bex, rsel, AF.Exp, bias=lnk)
            P.tt('vector', RRf, RRf, pt_, ALU.add)
            P.tt('vector', Vext[:, :, 0:64], Vsb[s % 2], bc(bex.unsqueeze(2), [128, 16, 64]), ALU.mult)
            P.copy('scalar', Vext[:, :, 64], bex)

        def slot_B_late(s):
            kvb = [pb(6, [6, 65]), pb(7, [6, 65]), pb(5, [4, 65])]
            for ln in range(16):
                P.mm(kvb[ln // 6][:, ln % 6, :], Kfb[:, ln, :], Vext[:, ln, :])
            for g3 in range(3):
                nl = 6 if g3 < 2 else 4
                P.tt('vector', Sacc[:, g3 * 6:g3 * 6 + nl, :], Sacc[:, g3 * 6:g3 * 6 + nl, :], kvb[g3], ALU.add)

        slot_A1(0)
        slot_A1(1)
        slot_A2(0)
        for s in range(NSLOT):
            if s + 2 < NSLOT:
                slot_A1(s + 2)
            slot_B_early(s)
            if s + 1 < NSLOT:
                slot_A2(s + 1)
            slot_B_late(s)
        P.act(expR, RR, AF.Exp)
        ck('D', [Sacc.rearrange('p a b -> p (a b)')[:, 0:1024], RR.rearrange('p a b -> p (a b)'), expR.rearrange('p a b -> p (a b)')])
        RX.release(mXf)
        RM.release(mMF)

        mark('outside')
        MT = RM.alloc([8, NCH * 128], BF16)
        mMF2 = RM.mark()
        order = [list(range(NCH)), [1, 0] + list(range(NCH - 1, 1, -1))]
        first_dir = [0 if order[0].index(c_) <= order[1].index(c_) else 1 for c_ in range(NCH)]
        groups = [(CTX0, 256, 0, False)] + [(WIN0 + 512 * g_, 512, 256 + 512 * g_, True) for g_ in range(4)] + [(WIN0 + 2048, 256, 2304, True)]
        for ps_ in range(8):
            if ps_ in (1, 2, 5, 6):
                mark(f'p{ps_}_start')
            RX.release(mXf)
            RM.release(mMF2)
            is_ml = ps_ >= 4
            j = ps_ % 4
            l0 = (8 if is_ml else 0) + 2 * j
            qoff = (2048 if is_ml else 0) + 128 * j
            koff = (2560 if is_ml else 512) + 128 * j
            goff = (3584 if is_ml else 1536) + 128 * j
            voff = (3072 if is_ml else 1024) + 128 * j
            ntap = 3 if is_ml else 1
            qT = RX.alloc([NCH * 128], BF16)
            kT = RX.alloc([NCH * 128], BF16)
            gT = RX.alloc([NCH * 128], BF16)
            Kt = RX.alloc([NCH, 128], BF16)
            Vp = RX.alloc([NCH, 2, 64], BF16)
            Wq = RM.alloc([3, 8, 128], BF16)
            Wk = RM.alloc([3, 8, 128], BF16)
            Wgt = RM.alloc([8, 128], BF16)
            Wvp = RM.alloc([8, 128], BF16)
            Oacc = RM.alloc([NCH, 2, 64], F32)
            mScan = RM.mark()
            P.dma('gpsimd', Wq[:, 0], win[:, :, qoff:qoff + 128])
            P.dma('gpsimd', Wk[:, 0], win[:, :, koff:koff + 128])
            P.dma('gpsimd', Wgt, win[:, :, goff:goff + 128])
            P.dma('gpsimd', Wvp, win[:, :, voff:voff + 128])
            if is_ml:
                cwp = RM.alloc([2, 3, 128], F32)
                for wi, co in enumerate((128 * j, 512 + 128 * j)):
                    for tpi in range(3):
                        P.dma('sync', cwp[:, wi, tpi, :], bc(conv_w[0, tpi:tpi + 1, co:co + 128], [128, 128]))
                for wi, W_ in enumerate((Wq, Wk)):
                    for tpi in (2, 1, 0):
                        P.tt('vector', W_[:, tpi], W_[:, 0], bc(cwp[:, wi, tpi, :].unsqueeze(1), [128, 8, 128]), ALU.mult)
            rtmp = RM.alloc([2, 512], F32)
            rfb = [RM.alloc([2, 512], F32) for _ in range(2)]
            bi = 0
            for gi_, (hc0, n, lc0, isw) in enumerate(groups):
                rf = rfb[gi_ % 2]
                if isw and not is_ml:
                    w0_ = hc0 - WIN0
                    P.dma('sync', rf[:, :, 0:n], ropeF.rearrange("a p n -> p a n")[:, :, w0_:w0_ + n])
                for which, W_, dst in (('q', Wq, qT), ('k', Wk, kT), ('g', Wgt, gT)):
                    pp = pb(1 + (bi % 3), [512])[:, 0:n]
                    bi += 1
                    if which == 'g':
                        for kc in range(8):
                            P.mm(pp, W_[:, kc, :], hT[:, kc, hc0:hc0 + n], start=(kc == 0), stop=(kc == 7))
                        P.act(dst[:, lc0:lc0 + n], pp, AF.Sigmoid if is_ml else AF.Silu)
                        continue
                    for tpi in range(ntap):
                        sh_ = (tpi - 1) if is_ml else 0
                        for kc in range(8):
                            P.mm(pp, W_[:, tpi, kc, :], hT[:, kc, hc0 + sh_:hc0 + sh_ + n],
                                 start=(tpi == 0 and kc == 0), stop=(tpi == ntap - 1 and kc == 7))
                    if is_ml:
                        ci_ = 40 + (j if which == 'q' else 4 + j)
                        P.act(dst[:, lc0:lc0 + n], pp, AF.Silu, bias=colB[:, ci_:ci_ + 1])
                    elif not isw:
                        P.copy('scalar', dst[:, lc0:lc0 + n], pp)
                    else:
                        P.tt('vector', rtmp[:, 0, 0:n], pp, rf[:, 0, 0:n], ALU.mult)
                        for blk in range(4):
                            src = blk ^ 1
                            P.tt('vector', rtmp[blk * 32:(blk + 1) * 32, 1, 0:n], pp[src * 32:(src + 1) * 32, :],
                                 rf[src * 32:(src + 1) * 32, 1, 0:n], ALU.mult)
                        P.tt('vector', dst[:, lc0:lc0 + n], rtmp[:, 0, 0:n], rtmp[:, 1, 0:n], ALU.add)
            if ps_ in (1, 5):
                mark(f'p{ps_}_proj')
            for c8 in range(0, NCH, 8):
                ncc = min(8, NCH - c8)
                tpk = pb(4, [8, 128], BF16)
                for ci in range(ncc):
                    P.transpose(tpk[:, ci, :], kT[:, (c8 + ci) * 128:(c8 + ci + 1) * 128], identB)
                P.copy('scalar', Kt[:, c8:c8 + ncc, :], tpk[:, 0:ncc, :])
            for c4 in range(0, NCH, 4):
                pv = pb(1 + (c4 // 4) % 3, [4, 128])
                for ci in range(4):
                    c0 = tokcols(c4 + ci)
                    for kc in range(8):
                        P.mm(pv[:, ci, :], hT[:, kc, c0:c0 + 128], Wvp[:, kc, :], start=(kc == 0), stop=(kc == 7))
                P.copy('vector', Vp[:, c4:c4 + 4].rearrange("p c h d -> p c (h d)"), pv)
            if ps_ == 0:
                ck('E1', [qT[:, 0:1024], kT[:, 0:1024], gT[:, 0:1024], Kt.rearrange('p a b -> p (a b)')[:, 0:1024], Vp.rearrange('p a b c -> p (a b c)')[:, 0:1024]])
            if ps_ == 4:
                ck('F1', [qT[:, 0:1024], kT[:, 0:1024], gT[:, 0:1024], Kt.rearrange('p a b -> p (a b)')[:, 0:1024], Vp.rearrange('p a b c -> p (a b c)')[:, 0:1024]])
            if ps_ in (1, 5):
                mark(f'p{ps_}_kv')
            RM.release(mScan)
            decp = RM.alloc([NCH, 2], F32)
            P.copy('vector', decp[0:64], DEC[0:64, :, :, l0])
            P.copy('vector', decp[64:128], DEC[64:128, :, :, l0 + 1])
            S32 = [RM.alloc([130], F32) for _ in range(2)]
            Sbf = [RM.alloc([130], BF16) for _ in range(2)]
            stmp = RM.alloc([130], F32)
            ecol = RM.alloc([2], F32)
            Vts = [RM.alloc([2, 65], BF16) for _ in range(4)]
            dn = RM.alloc([2, 8], F32)
            P.memset('gpsimd', stmp, 0.0)
            for d_ in range(2):
                P.memset('gpsimd', S32[d_], 0.0)
                P.copy('vector', ecol[0:64, d_:d_ + 1], expR[0:64, d_, l0:l0 + 1])
                P.copy('vector', ecol[64:128, d_:d_ + 1], expR[64:128, d_, l0 + 1:l0 + 2])
            PTs = [RM.alloc([2, 128], BF16) for _ in range(4)]
            pt_banks = [[0, 6], [3, 7]]

            def scan_front(step, d_):
                c = order[d_][step]
                tk = slice(c * 128, (c + 1) * 128)
                Vt = Vts[2 * (step % 2) + d_]
                P.tt('vector', Vt[:, :, 0:64], Vp[:, c], bc(BB[:, c, d_, l0:l0 + 2].unsqueeze(2), [128, 2, 64]), ALU.mult)
                P.copy('scalar', Vt[:, :, 64], BB[:, c, d_, l0:l0 + 2])
                ptp = pb(pt_banks[d_][step % 2], [2, 128])
                prev_mm = None
                for h in range(2):
                    hb = slice(64 * h, 64 * h + 64)
                    prev_mm = P.mm(ptp[:, h, :], kT[hb, tk], qT[hb, tk], after=prev_mm)
                PT = PTs[2 * (step % 2) + d_]
                P.tt('vector', PT, ptp, bc((maskF_b if d_ == 0 else maskB_b).unsqueeze(1), [128, 2, 128]), ALU.mult)

            def scan_kv(step, d_):
                c = order[d_][step]
                Vt = Vts[2 * (step % 2) + d_]
                kvp = pb(2 if d_ == 0 else 5, [130])
                P.mm(kvp, Kt[:, c, :], Vt.rearrange("p h e -> p (h e)"))

            def scan_back(step, d_):
                c = order[d_][step]
                tk = slice(c * 128, (c + 1) * 128)
                S = S32[d_]
                Vt = Vts[2 * (step % 2) + d_]
                PT = PTs[2 * (step % 2) + d_]
                if step == 2:
                    r0 = 64 * d_
                    P.copy('scalar', stmp[0:64, 0:65], Sacc[r0:r0 + 64, l0, :])
                    P.copy('scalar', stmp[64:128, 65:130], Sacc[r0:r0 + 64, l0 + 1, :])
                    P.stt('vector', S, S, ecol[:, d_:d_ + 1], stmp, ALU.mult, ALU.add)
                if step > 0:
                    P.act(Sbf[d_], S, AF.Identity, scale=decp[:, c, d_:d_ + 1])
                ops_ = pb(1 if d_ == 0 else 4, [2, 65])
                for h in range(2):
                    hb = slice(64 * h, 64 * h + 64)
                    P.mm(ops_[:, h, :], PT[:, h, :], Vt[:, h, :], start=True, stop=(step == 0))
                    if step > 0:
                        P.mm(ops_[:, h, :], qT[hb, tk], Sbf[d_][hb, 65 * h:65 * h + 65], start=False, stop=True)
                at2 = AA[:, c, d_, l0:l0 + 2]
                if is_ml:
                    dd = dn[:, d_, :]
                    P.tt('vector', dd[:, 0:2], ops_[:, :, 64], at2, ALU.mult)
                    P.act(dd[:, 2:4], dd[:, 0:2], AF.Abs)
                    P.ts('vector', dd[:, 2:4], dd[:, 2:4], 1.0, None, ALU.max)
                    P.recip(dd[:, 4:6], dd[:, 2:4])
                    P.tt('vector', dd[:, 6:8], dd[:, 4:6], at2, ALU.mult)
                for h in range(2):
                    coef = dn[:, d_, 6 + h:7 + h] if is_ml else AA[:, c, d_, l0 + h:l0 + h + 1]
                    if first_dir[c] == d_:
                        P.act(Oacc[:, c, h, :], ops_[:, h, 0:64], AF.Identity, scale=coef)
                    else:
                        P.stt('vector', Oacc[:, c, h, :], ops_[:, h, 0:64], coef, Oacc[:, c, h, :], ALU.mult, ALU.add)
                kvp = pb(2 if d_ == 0 else 5, [130])
                if step == 0:
                    P.copy('vector', S, kvp)
                else:
                    P.stt('vector', S, S, decp[:, c, d_:d_ + 1], kvp, ALU.mult, ALU.add)

            for d_ in range(2):
                scan_front(0, d_)
                scan_kv(0, d_)
            for step in range(NCH):
                if step + 1 < NCH:
                    for d_ in range(2):
                        scan_front(step + 1, d_)
                for d_ in range(2):
                    scan_back(step, d_)
                if step + 1 < NCH:
                    for d_ in range(2):
                        scan_kv(step + 1, d_)
                if ps_ == 0 and step < 3:
                    ck('E2' + 'abc'[step], [Oacc.rearrange('p a b c -> p (a b c)')[:, 0:256], S32[0], S32[1]])
            if ps_ == 0:
                ck('E2', [Oacc.rearrange('p a b c -> p (a b c)')[:, 0:1024], Oacc.rearrange('p a b c -> p (a b c)')[:, 1024:2048]])
            if ps_ == 4:
                ck('F2', [Oacc.rearrange('p a b c -> p (a b c)')[:, 0:1024], Oacc.rearrange('p a b c -> p (a b c)')[:, 1024:2048]])
            if ps_ in (1, 5):
                mark(f'p{ps_}_scan')
            RM.release(mScan)
            sq = RM.alloc([NCH, 2, 64], F32)
            ms = RM.alloc([NCH, 2], F32)
            On = RM.alloc([NCH, 2, 64], BF16)
            P.act(sq, Oacc, AF.Square)
            P.reduce('vector', ms, sq, ALU.add)
            P.act(ms, ms, AF.Sqrt, bias=epsb, scale=1.0 / 64)
            P.recip(ms, ms)
            P.tt('vector', On, Oacc, bc(ms.unsqueeze(3), [128, NCH, 2, 64]), ALU.mult)
            nwi = (36 if is_ml else 32) + j
            for c4 in range(0, NCH, 4):
                tpo = pb(4, [4, 128], BF16, off=(c4 // 4 % 2) * 1024)
                for ci in range(4):
                    P.transpose(tpo[:, ci, :], On[:, c4 + ci].rearrange("p h d -> p (h d)"), identB)
                P.stt('vector', MT[:, ps_, c4 * 128:(c4 + 4) * 128], tpo.rearrange("p c t -> p (c t)"), colB[:, nwi:nwi + 1],
                      gT[:, c4 * 128:(c4 + 4) * 128], ALU.mult, ALU.mult)
        ck('E4', [MT[:, 0, 0:1024], MT[:, 7, 0:1024]])
        RM.release(mMF2)
        RX.release(0)
        mark('passes')
        X1 = RX.alloc([NCH, D], F32)
        xsp = RX.alloc([512], F32)
        Wo = RM.alloc([8, D], BF16)
        P.dma('gpsimd', Wo, ab_w_out[0].rearrange("(kc p) n -> p kc n", p=128))
        otmp = [RM.alloc([512], F32) for _ in range(2)]
        for c in range(NCH):
            P.dma('sync', X1[:, c, :], xw[c])
        for c in range(NCH):
            gi = 2 if c < 2 else 0
            for hf in range(2):
                po = pb(1 + hf, [512])
                for kc in range(8):
                    P.mm(po, MT[:, kc, c * 128:(c + 1) * 128], Wo[:, kc, hf * 512:(hf + 1) * 512], start=(kc == 0), stop=(kc == 7))
                P.tt('vector', otmp[hf], po, gB[:, gi, hf * 512:(hf + 1) * 512], ALU.mult)
                P.tt('vector', X1[:, c, hf * 512:(hf + 1) * 512], X1[:, c, hf * 512:(hf + 1) * 512], otmp[hf], ALU.add)
        RM.release(mMF)
        if dbg == 'l0m':
            for c in range(NCH):
                P.dma('sync', dbg_out[c], X1[:, c, :], store=True)

        mark('outproj0')
        def ffn(l, chunks, wm_sel, g_sel):
            m0 = RM.mark()
            w_in = ffn_w_in[l].rearrange("(kc p) n -> p kc n", p=128)
            w_out = ffn_w_out[l].rearrange("(f p) n -> p f n", p=128)
            Wout = RM.alloc([NFT, D], BF16)
            P.dma('gpsimd', Wout[:, 0:11, :], w_out[:, 0:11, :])
            P.dma('gpsimd', Wout[:, 11:22, :], w_out[:, 11:22, :])
            AT = RM.alloc([NFT, 512], BF16)
            wbuf = [RM.alloc([8, 2, 256], BF16) for _ in range(2)]
            h2 = RM.alloc([8, 512], BF16)
            o_ = xsp
            nq = len(chunks) // 4

            def prep(qi, i):
                c = chunks[4 * qi + i]
                v_ = wm_sel(c)
                make_hT(X1[:, c, :], h2[:, :, i * 128:(i + 1) * 128], WM[:, l, 1, v_, :], SHc[:, l, 1, v_, :])

            for i in range(4):
                prep(0, i)
            bi = 0
            for qi in range(nq):
                cq = chunks[4 * qi:4 * qi + 4]
                for fp in range(11):
                    wb = wbuf[bi % 2]
                    bi += 1
                    P.dma('gpsimd', wb[:, :, 0, :], w_in[:, :, fp * 256:(fp + 1) * 256])
                    P.dma('gpsimd', wb[:, :, 1, :], w_in[:, :, D_FF + fp * 256:D_FF + (fp + 1) * 256])
                    for f2 in range(2):
                        f = fp * 2 + f2
                        pgm = pb(1 + 2 * (f % 2), [512])
                        pum = pb(2 + 2 * (f % 2), [512])
                        for kc in range(8):
                            P.mm(pgm, wb[:, kc, 0, f2 * 128:(f2 + 1) * 128], h2[:, kc, :], start=(kc == 0), stop=(kc == 7))
                        for kc in range(8):
                            P.mm(pum, wb[:, kc, 1, f2 * 128:(f2 + 1) * 128], h2[:, kc, :], start=(kc == 0), stop=(kc == 7))
                        P.act(AT[:, f, :], pgm, AF.Silu)
                        P.tt('vector', AT[:, f, :], AT[:, f, :], pum, ALU.mult)
                k_ = 0
                for i, c in enumerate(cq):
                    for hf in range(2):
                        po = pb(5 + (k_ % 3), [512])
                        k_ += 1
                        for f in range(NFT):
                            P.mm(po, AT[:, f, i * 128:(i + 1) * 128], Wout[:, f, hf * 512:(hf + 1) * 512], start=(f == 0), stop=(f == NFT - 1))
                        P.tt('vector', o_, po, gB[:, g_sel(c), hf * 512:(hf + 1) * 512], ALU.mult)
                        P.tt('vector', X1[:, c, hf * 512:(hf + 1) * 512], X1[:, c, hf * 512:(hf + 1) * 512], o_, ALU.add)
                    if qi + 1 < nq:
                        prep(qi + 1, i)
            RM.release(m0)

        if dbg != 'l0m':
            ffn(0, list(range(NCH)), lambda c: 1 if c < 2 else 0, lambda c: 3 if c < 2 else 1)
        if dbg == 'l0':
            for c in range(NCH):
                P.dma('sync', dbg_out[c], X1[:, c, :], store=True)

        mark('ffn0')
        if dbg in (None, 'l1m'):
            modulation(1)
            make_wm(1)
            m1 = RM.mark()
            awin = at_w_in[0].rearrange("(kc p) n -> p kc n", p=128)
            kT1 = RM.alloc([2, NCH * 128], BF16)
            Vx1 = RM.alloc([NCH, 4, 65], BF16)
            P.memset('vector', Vx1[:, :, :, 64:65], 1.0)
            qnb = RM.alloc([64], F32)
            knb = RM.alloc([64], F32)
            P.dma('sync', qnb, bc(at_qn, [128, 64]))
            P.dma('sync', knb, bc(at_kn, [128, 64]))
            snk = RM.alloc([16], F32)
            P.dma('sync', snk, bc(at_sink, [128, 16]))
            rT = RM.alloc([2, NW, 32], F32)
            P.dma('sync', rT.rearrange("p a s f -> p a (s f)"), ropeT.rearrange("a p n -> p a n"))
            amb = RM.alloc([4, 4, 128], BF16)
            sm = RM.alloc([8], F32)
            sinkexp = RM.alloc([16], F32)
            hTc = [RM.alloc([8, 128], BF16)] * 2
            nms = RM.alloc([8], F32)
            qn = RM.alloc([8, 64], F32)
            nsq = qn
            qr = RM.alloc([2, 8, 32], F32)
            qtk = RM.alloc([16, 64], BF16)
            qtk2 = RM.alloc([16, 64], BF16)
            m2 = RM.mark()
            amf = RM.alloc([4, 128], F32)
            absq = RM.alloc([128], F32)
            P.dma('sync', amf, amask.rearrange("c p n -> p c n"))
            P.ts('vector', amf, amf, 1.0, 30000.0, ALU.subtract, ALU.mult)
            for v4 in range(4):
                P.copy('vector', amb[:, v4], bc(amf[:, v4, :].unsqueeze(1), [128, 4, 128]))
            P.act(absq[:, 0:64], qnb, AF.Abs)
            P.act(absq[:, 64:128], knb, AF.Abs)
            P.reduce('vector', sm[:, 0:1], absq[:, 0:64], ALU.max)
            P.reduce('vector', sm[:, 1:2], absq[:, 64:128], ALU.max)
            P.tt('vector', sm[:, 2:3], sm[:, 0:1], sm[:, 1:2], ALU.mult)
            P.ts('vector', sm[:, 3:4], sm[:, 2:3], -8.0, None, ALU.mult)
            negB = sm[:, 3:4]
            P.act(sinkexp, snk, AF.Exp, bias=negB)
            RM.release(m2)
            Wkv1 = RM.alloc([8, 512], BF16)
            P.dma('gpsimd', Wkv1, awin[:, :, 1024:1536])

            def qknorm_rope(ps3, nh, wb_, wch, dst):
                P.act(nsq[:, 0:nh, :], ps3, AF.Square)
                P.reduce('vector', nms[:, 0:nh], nsq[:, 0:nh, :], ALU.add)
                P.act(nms[:, 0:nh], nms[:, 0:nh], AF.Sqrt, bias=epsb, scale=1.0 / 64)
                P.recip(nms[:, 0:nh], nms[:, 0:nh])
                P.tt('vector', qn[:, 0:nh, :], ps3, bc(nms[:, 0:nh].unsqueeze(2), [128, nh, 64]), ALU.mult)
                if wch < 0:
                    P.tt('vector', dst, qn[:, 0:nh, :], bc(wb_.unsqueeze(1), [128, nh, 64]), ALU.mult)
                    return
                P.tt('vector', qn[:, 0:nh, :], qn[:, 0:nh, :], bc(wb_.unsqueeze(1), [128, nh, 64]), ALU.mult)
                x1, x2 = qn[:, 0:nh, 0:32], qn[:, 0:nh, 32:64]
                cs = bc(rT[:, 0, wch, :].unsqueeze(1), [128, nh, 32])
                sn = bc(rT[:, 1, wch, :].unsqueeze(1), [128, nh, 32])
                P.tt('vector', qr[:, 0, 0:nh], x1, cs, ALU.mult)
                P.tt('vector', qr[:, 1, 0:nh], x2, sn, ALU.mult)
                P.tt('vector', dst[:, :, 0:32], qr[:, 0, 0:nh], qr[:, 1, 0:nh], ALU.subtract)
                P.tt('vector', qr[:, 0, 0:nh], x2, cs, ALU.mult)
                P.tt('vector', qr[:, 1, 0:nh], x1, sn, ALU.mult)
                P.tt('vector', dst[:, :, 32:64], qr[:, 0, 0:nh], qr[:, 1, 0:nh], ALU.add)

            for c in range(NCH):
                v_ = 1 if c < 2 else 0
                hc_ = hTc[c % 2]
                make_hT(X1[:, c, :], hc_, WM[:, 1, 0, v_, :], SHc[:, 1, 0, v_, :], alt_bank=7)
                pkv = pb(1 + (c % 2), [512])
                for kc in range(8):
                    P.mm(pkv, hc_[:, kc, :], Wkv1[:, kc, :], start=(kc == 0), stop=(kc == 7))
                P.copy('scalar', Vx1[:, c, :, 0:64], pkv[:, 256:512].rearrange("p (h d) -> p h d", h=4))
                qknorm_rope(pkv[:, 0:256].rearrange("p (h d) -> p h d", h=4), 4, knb, (c - 2) if c >= 2 else -1, qtk[:, 0:4, :])
                tpk = pb(4, [2, 128], BF16)
                for t_ in range(2):
                    P.transpose(tpk[:, t_, :], qtk[:, 2 * t_:2 * t_ + 2, :].rearrange("p h d -> p (h d)"), identB)
                P.copy('scalar', kT1[:, :, c * 128:(c + 1) * 128], tpk)
            mark('l1kv')
            RM.release(m2)
            Wq1 = RM.alloc([8, 1024], BF16)
            P.dma('gpsimd', Wq1, awin[:, :, 0:1024])
            Wo1 = RM.alloc([8, D], BF16)
            P.dma('gpsimd', Wo1, at_w_out[0].rearrange("(kc p) n -> p kc n", p=128))
            qTc = RM.alloc([2, 4, 128], BF16)
            Eb = [RM.alloc([5, 4, 128], BF16) for _ in range(2)]
            Otk = RM.alloc([16, 64], BF16)
            OT = RM.alloc([8, 128], BF16)
            dn1 = RM.alloc([8], F32)
            ot1 = [xsp, RM.alloc([512], F32)]
            ei = 0
            for i in range(16):
                c = i + 3
                hc_ = hTc[i % 2]
                make_hT(X1[:, c, :], hc_, WM[:, 1, 0, 0, :], SHc[:, 1, 0, 0, :])
                for hf in range(2):
                    pq = pb(2 + hf, [512])
                    for kc in range(8):
                        P.mm(pq, hc_[:, kc, :], Wq1[:, kc, hf * 512:(hf + 1) * 512], start=(kc == 0), stop=(kc == 7))
                    qknorm_rope(pq.rearrange("p (h d) -> p h d", h=8), 8, qnb, c - 2, qtk[:, 8 * hf:8 * hf + 8, :])
                tpq = pb(4, [2, 4, 128], BF16)
                for tp_ in range(2):
                    P.copy('scalar', qtk2[:, 8 * tp_:8 * tp_ + 8, :].rearrange("p (j g) d -> p g j d", g=2),
                           qtk[:, 8 * tp_:8 * tp_ + 8, :].rearrange("p (g j) d -> p g j d", g=2))
                    for j in range(4):
                        src = qtk2[:, 8 * tp_ + 2 * j:8 * tp_ + 2 * j + 2, :]
                        P.transpose(tpq[:, tp_, j, :], src.rearrange("p h d -> p (h d)"), identB)
                P.copy('scalar', qTc, tpq)
                kblocks = [(c - 1, 0 if i == 0 else 1), (c, None), (c + 1, 3 if i == 15 else 2), (0, None), (1, None)]
                st_banks = [[5, 6], [0, 2]]

                def att_front(g):
                    tp_, hb = g // 2, slice(64 * (g % 2), 64 * (g % 2) + 64)
                    E = Eb[g % 2]
                    for bi_, (kc_, mk) in enumerate(kblocks):
                        pst = pb(st_banks[g % 2][bi_ % 2], [4, 128])
                        P.mm(pst.rearrange("p j t -> p (j t)"), kT1[hb, tp_, kc_ * 128:(kc_ + 1) * 128],
                             qTc[hb, tp_].rearrange("p j t -> p (j t)"), start=True, stop=(mk is None))
                        if mk is not None:
                            P.mm(pst.rearrange("p j t -> p (j t)"), identB, amb[:, mk].rearrange("p j t -> p (j t)"), start=False, stop=True)
                        P.act(E[:, bi_], pst, AF.Exp, bias=negB, scale=0.125)

                def att_back(g):
                    E = Eb[g % 2]
                    pso = pb(7 if g % 2 == 0 else 3, [4, 65])
                    for j in range(4):
                        for bi_, (kc_, mk) in enumerate(kblocks):
                            P.mm(pso[:, j, :], E[:, bi_, j, :], Vx1[:, kc_, g, :], start=(bi_ == 0), stop=(bi_ == 4))
                    P.tt('vector', dn1[:, 0:4], pso[:, :, 64], sinkexp[:, 4 * g:4 * g + 4], ALU.add)
                    P.recip(dn1[:, 4:8], dn1[:, 0:4])
                    P.tt('vector', Otk[:, 4 * g:4 * g + 4, :], pso[:, :, 0:64], bc(dn1[:, 4:8].unsqueeze(2), [128, 4, 64]), ALU.mult)

                att_front(0)
                for g in range(4):
                    if g + 1 < 4:
                        att_front(g + 1)
                    att_back(g)
                tpo = pb(4, [8, 128], BF16)
                for kc in range(8):
                    P.transpose(tpo[:, kc, :], Otk[:, 2 * kc:2 * kc + 2, :].rearrange("p h d -> p (h d)"), identB)
                P.copy('scalar', OT, tpo)
                for hf in range(2):
                    po = pb(1 + hf, [512])
                    for kc in range(8):
                        P.mm(po, OT[:, kc, :], Wo1[:, kc, hf * 512:(hf + 1) * 512], start=(kc == 0), stop=(kc == 7))
                    P.tt('vector', ot1[hf], po, gB[:, 0, hf * 512:(hf + 1) * 512], ALU.mult)
                    P.tt('vector', X1[:, c, hf * 512:(hf + 1) * 512], X1[:, c, hf * 512:(hf + 1) * 512], ot1[hf], ALU.add)
            mark('l1attn')
            RM.release(m1)
            if dbg == 'l1m':
                for c in range(NCH):
                    P.dma('sync', dbg_out[c], X1[:, c, :], store=True)
            else:
                ffn(1, list(range(3, 19)), lambda c: 0, lambda c: 1)
        for i in range(16):
            P.dma('sync', y[i * 128:(i + 1) * 128, :], X1[:, i + 3, :], store=True)
        mark('end')
        stats = P.emit()
        stats['marks'] = P.marks
        stats['peaks'] = (RP.peak, RX.peak, RM.peak)
    return nc, stats


def _rope_angles(pos):
    pos = np.asarray(pos)
    row = (pos // 64).astype(np.float32)
    col = (pos % 64).astype(np.float32)
    inv = (10000.0 ** (-np.arange(16, dtype=np.float32) / 16)).astype(np.float32)
    return np.concatenate([row[:, None] * inv, col[:, None] * inv], axis=-1).astype(np.float32)


def _core_inputs(inp, core):
    b, q = core // 4, core % 4
    x = np.asarray(inp['x'], np.float32)
    ctx = np.asarray(inp['ctx'], np.float32)
    xb = x[b].reshape(64, 128, D)
    fl = np.zeros((1, NF), np.float32)
    xw = np.zeros((NCH, 128, D), np.float32)
    xw[0:2] = ctx[b].reshape(2, 128, D)
    fl[0, FL_VF:FL_VF + 2] = 1.0
    wpos = np.full((NW, 128), -1, np.int64)
    for w in range(NW):
        ch = 16 * q - 1 + w
        if 0 <= ch < 64:
            xw[2 + w] = xb[ch]
            fl[0, FL_VF + 2 + w] = 1.0
            wpos[w] = ch * 128 + np.arange(128)
    xe = np.zeros((128, D), np.float32)
    tl = 128 * (16 * q - 1) - 1
    tr = 128 * (16 * q + 17)
    if 0 <= tl < SEQ:
        xe[0] = x[b, tl]
        fl[0, FL_EDGE] = 1.0
    if 0 <= tr < SEQ:
        xe[1] = x[b, tr]
        fl[0, FL_EDGE + 1] = 1.0
    slots = [(ch, 0) for ch in range(16 * q - 2, -1, -1)] + [(ch, 1) for ch in range(16 * q + 17, 64)]
    assert len(slots) <= NSLOT
    xs = np.zeros((NSLOT, 128, D), np.float32)
    xsh = np.zeros((128, D), np.float32)
    spos = np.zeros((NSLOT, 128), np.int64)
    for s, (ch, d_) in enumerate(slots):
        xs[s] = xb[ch]
        fl[0, (FL_FF if d_ == 0 else FL_FB) + s] = 1.0
        spos[s] = ch * 128 + np.arange(128)
        t0, t1 = ch * 128 - 1, ch * 128 + 128
        if t0 >= 0:
            xsh[2 * s] = x[b, t0]
            fl[0, FL_SH + 2 * s] = 1.0
        if t1 < SEQ:
            xsh[2 * s + 1] = x[b, t1]
            fl[0, FL_SH + 2 * s + 1] = 1.0
    wp = np.where(wpos < 0, 0, wpos).reshape(-1)
    ang = _rope_angles(wp)
    cosw, sinw = np.cos(ang), np.sin(ang)
    ropeT = np.stack([cosw.reshape(NW, 128, 32).transpose(1, 0, 2).reshape(128, NW * 32),
                      sinw.reshape(NW, 128, 32).transpose(1, 0, 2).reshape(128, NW * 32)]).astype(np.float32)
    p = np.arange(128)
    fi = p % 32
    cosF = cosw[:, fi].T
    sgn = np.where((p % 64) < 32, 1.0, -1.0)[:, None]
    sinF = sinw[:, fi].T * sgn
    ropeF = np.stack([cosF, sinF]).astype(np.float32)
    angs = _rope_angles(spos.reshape(-1))
    ropeS = np.stack([np.cos(angs).reshape(NSLOT, 128, 32).transpose(1, 0, 2).reshape(128, NSLOT * 32),
                      np.sin(angs).reshape(NSLOT, 128, 32).transpose(1, 0, 2).reshape(128, NSLOT * 32)]).astype(np.float32)
    u = np.arange(128)[:, None]
    s_ = np.arange(128)[None, :]
    consts = np.zeros((7, 128, 128), np.float32)
    consts[0] = np.eye(128)
    consts[1] = (u > s_)
    consts[2] = (u < s_)
    consts[3] = (u <= s_)
    consts[4] = (u >= s_)
    consts[5] = 1.0
    amask = np.zeros((4, 128, 128), np.float32)
    mL = (u >= s_).astype(np.float32)
    mR = (u <= s_).astype(np.float32)
    amask[0] = mL * (1.0 if q > 0 else 0.0)
    amask[1] = mL
    amask[2] = mR
    amask[3] = mR * (1.0 if q < 3 else 0.0)
    cvec = np.concatenate([np.asarray(inp['c'], np.float32)[b].reshape(8, 128),
                           np.asarray(inp['c_ctx'], np.float32).reshape(8, 128)], 0)
    m = dict(xw=xw, xe=xe, xs=xs, xsh=xsh, fl=fl, cvec=cvec, consts=consts, amask=amask,
             ropeF=ropeF, ropeT=ropeT, ropeS=ropeS)
    for k in ('ada_w', 'ada_b', 'norm_w', 'ffn_w_in', 'ffn_w_out', 'ab_w_in', 'ab_w_out', 'ret_log_gamma',
              'ret_norm_w', 'mlstm_conv_w', 'mlstm_conv_b', 'mlstm_gate_b', 'mlstm_norm_w', 'attn_w_in',
              'attn_w_out', 'attn_q_norm_w', 'attn_k_norm_w', 'attn_sink'):
        m[k] = np.ascontiguousarray(np.asarray(inp[k], np.float32))
    return m


_NC_CACHE = {}


def kernel(**inp):
    if 'nc' not in _NC_CACHE:
        _NC_CACHE['nc'] = build()[0]
    nc = _NC_CACHE['nc']
    in_maps = [_core_inputs(inp, c) for c in range(NCORE)]
    res = run_bass_kernel_spmd(nc, in_maps, core_ids=list(range(NCORE)))
    out = np.zeros((2, SEQ, D), np.float32)
    for c in range(NCORE):
        b, q = c // 4, c % 4
        out[b, 2048 * q:2048 * (q + 1)] = res.results[c]["y"]
    return out
```

```python
import contextlib
import math
import numpy as np
import concourse.bass as bass
import concourse.mybir as mybir
from concourse.bass_utils import run_bass_kernel_spmd

F32 = mybir.dt.float32
BF16 = mybir.dt.bfloat16
ALU = mybir.AluOpType
AF = mybir.ActivationFunctionType
AX = mybir.AxisListType

_DTSZ = {F32: 4, BF16: 2}
SEM_LIMIT = 12000
DMA_POOL = 12


def _region(ap):
    sp = str(ap.space).upper()
    if 'SB' not in sp and 'PSUM' not in sp:
        return None
    if 'PSUM' in sp:
        return (ap.name, 0, 128, 0, 2048)
    pat = ap.ap
    esz = _DTSZ[ap.dtype]
    pstep, pcount = pat[0]
    off = ap.offset
    if pstep == 0:
        p0 = 0
        f0 = off
    else:
        p0 = off // pstep
        f0 = off - p0 * pstep
    ext = 0
    for stp, cn in pat[1:]:
        ext += abs(stp) * (cn - 1)
    return (ap.name, p0, p0 + pcount, f0 * esz, (f0 + ext + 1) * esz)


class Prog:
    ENGS = ('tensor', 'vector', 'scalar', 'gpsimd', 'sync')

    def __init__(self, nc, same_engine_sync=True):
        self.nc = nc
        self.ops = []
        self.track = {}
        self.same_engine_sync = same_engine_sync
        self.dma_hist = {e: [] for e in self.ENGS}
        self.store_ops = []

    def _add(self, eng, fn, outs, ins, is_dma=False, extra_deps=(), force=False):
        if getattr(self, 'frozen', False) and not force:
            return -1
        idx = len(self.ops)
        deps = set(extra_deps)
        self.ops.append(dict(eng=eng, fn=fn, deps=deps, is_dma=is_dma, signaled=False))
        for ap in ins:
            r = _region(ap)
            if r is not None:
                self._access(idx, eng, r, False, deps)
        for ap in outs:
            r = _region(ap)
            if r is not None:
                self._access(idx, eng, r, True, deps)
        if is_dma:
            h = self.dma_hist[eng]
            if len(h) >= DMA_POOL:
                deps.add(h[-DMA_POOL])
            h.append(idx)
        deps.discard(idx)
        return idx

    def _access(self, idx, eng, r, is_write, deps):
        name, p0, p1, b0, b1 = r
        recs = self.track.get(name, [])
        keep = []
        for rec in recs:
            (q0, q1, c0, c1, oi, ow, oe) = rec
            if q1 <= p0 or p1 <= q0 or c1 <= b0 or b1 <= c0 or oi == idx:
                keep.append(rec)
                continue
            if is_write or ow or (name.startswith('pb') and oe != eng):
                pe_pe = (eng == 'tensor' and oe == 'tensor')
                if not pe_pe:
                    deps.add(oi)
            covered = (p0 <= q0 and q1 <= p1 and b0 <= c0 and c1 <= b1)
            if is_write and covered and not (eng == 'tensor' and oe == 'tensor' and not ow):
                continue
            if (not is_write) and (not ow) and oe == eng and covered and not self.ops[oi]['is_dma']:
                continue
            keep.append(rec)
        keep.append((p0, p1, b0, b1, idx, is_write, eng))
        self.track[name] = keep

    def op(self, eng, fn, outs, ins):
        return self._add(eng, fn, outs, ins)

    def dma(self, eng, out, in_, store=False, **kw):
        i = self._add(eng, lambda e: e.dma_start(out=out, in_=in_, **kw), [out], [in_], is_dma=True)
        if store and i >= 0:
            self.store_ops.append(i)
        return i

    def mm(self, out, lhsT, rhs, start=True, stop=True, after=None):
        i = self.op('tensor', lambda e: e.matmul(out, lhsT, rhs, start=start, stop=stop), [out], [lhsT, rhs])
        if after is not None and i >= 0 and after >= 0:
            self.ops[i]['deps'].add(after)
        return i

    def transpose(self, out, in_, ident):
        return self.op('tensor', lambda e: e.transpose(out, in_, ident), [out], [in_, ident])

    def act(self, out, in_, func, bias=None, scale=None, accum_out=None):
        kw = {}
        ins = [in_]
        outs = [out]
        if bias is not None:
            kw['bias'] = bias
            if not isinstance(bias, (int, float)):
                ins.append(bias)
        if scale is not None:
            kw['scale'] = scale
            if not isinstance(scale, (int, float)):
                ins.append(scale)
        if accum_out is not None:
            kw['accum_out'] = accum_out
            outs.append(accum_out)
        return self.op('scalar', lambda e: e.activation(out, in_, func, **kw), outs, ins)

    def tt(self, eng, out, in0, in1, op):
        return self.op(eng, lambda e: e.tensor_tensor(out, in0, in1, op), [out], [in0, in1])

    def ts(self, eng, out, in0, s1, s2, op0, op1=None):
        ins = [in0] + [s for s in (s1, s2) if s is not None and not isinstance(s, (int, float))]
        kw = {}
        if op1 is not None:
            kw['op1'] = op1
        return self.op(eng, lambda e: e.tensor_scalar(out, in0, s1, s2, op0, **kw), [out], ins)

    def stt(self, eng, out, in0, scalar, in1, op0, op1):
        ins = [in0, in1] + ([scalar] if not isinstance(scalar, (int, float)) else [])
        return self.op(eng, lambda e: e.scalar_tensor_tensor(out, in0, scalar, in1, op0, op1), [out], ins)

    def copy(self, eng, out, in_):
        if eng == 'scalar':
            return self.op(eng, lambda e: e.copy(out, in_), [out], [in_])
        return self.op(eng, lambda e: e.tensor_copy(out, in_), [out], [in_])

    def memset(self, eng, out, val):
        return self.op(eng, lambda e: e.memset(out, val), [out], [])

    def recip(self, out, in_):
        return self.op('vector', lambda e: e.reciprocal(out, in_), [out], [in_])

    def reduce(self, eng, out, in_, op, axis=AX.X):
        return self.op(eng, lambda e: e.tensor_reduce(out, in_, axis, op), [out], [in_])

    def emit(self):
        nc = self.nc
        ops = self.ops
        self._add('sync', None, [], [], extra_deps=self.store_ops, force=True)
        for o in ops:
            if o['is_dma']:
                o['signaled'] = True
            for d in o['deps']:
                ops[d]['signaled'] = True
        cnt = {e: 0 for e in self.ENGS}
        dcnt = {e: 0 for e in self.ENGS}
        nsem_eng = {e: 0 for e in self.ENGS}
        for o in ops:
            e = o['eng']
            if not o['signaled']:
                continue
            if o['is_dma']:
                k = dcnt[e]
                dcnt[e] += 1
                o['sem'] = ('d', e, k % DMA_POOL)
                o['val'] = 16 * (k // DMA_POOL + 1)
                o['sidx'] = None
            else:
                k = cnt[e]
                cnt[e] += 1
                o['sem'] = ('c', e, k // SEM_LIMIT)
                o['val'] = (k % SEM_LIMIT) + 1
                o['sidx'] = k
                nsem_eng[e] = k // SEM_LIMIT + 1
        sems = {}
        st = contextlib.ExitStack()
        for e in self.ENGS:
            for j in range(nsem_eng[e]):
                sems[('c', e, j)] = st.enter_context(nc.semaphore(f"c_{e}_{j}"))
            for j in range(min(DMA_POOL, dcnt[e])):
                sems[('d', e, j)] = st.enter_context(nc.semaphore(f"d_{e}_{j}"))
        seen = {e: {f: -1 for f in self.ENGS} for e in self.ENGS}
        seen_dma = {e: set() for e in self.ENGS}
        per_eng = {e: [] for e in self.ENGS}
        nwaits = 0
        for o in ops:
            e = o['eng']
            waits = {}
            for d in sorted(o['deps']):
                p = ops[d]
                if p['is_dma']:
                    if d in seen_dma[e]:
                        continue
                    seen_dma[e].add(d)
                    waits[p['sem']] = max(waits.get(p['sem'], 0), p['val'])
                else:
                    f = p['eng']
                    if f == e and not self.same_engine_sync:
                        continue
                    if p['sidx'] <= seen[e][f]:
                        continue
                    seen[e][f] = p['sidx']
                    waits[p['sem']] = max(waits.get(p['sem'], 0), p['val'])
            nwaits += len(waits)
            per_eng[e].append((o, list(waits.items())))
        self.stats = dict(n_ops=len(ops), n_waits=nwaits, per_eng={e: len(v) for e, v in per_eng.items()})
        with st, nc.Block() as block:
            def body(engname):
                def run(eng):
                    for o, waits in per_eng[engname]:
                        for key, val in waits:
                            eng.wait_ge(sems[key], val)
                        if o['fn'] is None:
                            continue
                        ins = o['fn'](eng)
                        if o['signaled']:
                            ins.then_inc(sems[o['sem']], 16 if o['is_dma'] else 1)
                return run
            block.tensor(body('tensor'))
            block.vector(body('vector'))
            block.scalar(body('scalar'))
            block.gpsimd(body('gpsimd'))
            block.sync(body('sync'))
        return self.stats


D = 1024
SEQ = 8192
NCORE = 8
NW = 18
NCH = 20
NSLOT = 48
NF = 256
EPS = 1e-6
D_FF = 2816
NFT = 22
LNK = math.log(0.125)
HT_N = 2564
CTX0 = 1
WIN0 = 259
FL_VF = 0
FL_EDGE = 20
FL_FF = 22
FL_FB = 70
FL_SH = 118


def bc(ap, shape):
    return ap.broadcast_to(shape)


class Region:
    def __init__(self, t, base, cap, name):
        self.t, self.base, self.cap, self.name = t, base, cap, name
        self.off = 0
        self.peak = 0

    def alloc(self, shape, dt):
        n = 1
        for s in shape:
            n *= s
        nb = (n * _DTSZ[dt] + 63) // 64 * 64
        if self.off + nb > self.cap:
            raise RuntimeError(f"region {self.name} overflow: {self.off + nb} > {self.cap}")
        a = self.base + self.off
        v = self.t[:, a // 4:(a + nb) // 4]
        if dt != F32:
            v = v.bitcast(dt)
        v = v[:, 0:n]
        self.off += nb
        self.peak = max(self.peak, self.off)
        if len(shape) == 1:
            return v
        names = [chr(ord('a') + i) for i in range(len(shape))]
        pat = "p (" + " ".join(names) + ") -> p " + " ".join(names)
        return v.rearrange(pat, **{nm: s for nm, s in zip(names, shape)})

    def mark(self):
        return self.off

    def release(self, m):
        self.off = m


class _Stop(Exception):
    pass


ARENA_BYTES = 212480
P_BYTES = 35328
X_BYTES = 83968


def build(dbg=None):
    nc = bass.Bass("TRN2", target_bir_lowering=False)

    def din(name, shape):
        return nc.dram_tensor(name, list(shape), F32, kind="ExternalInput").ap()

    xw = din("xw", [NCH, 128, D])
    xe = din("xe", [128, D])
    xs = din("xs", [NSLOT, 128, D])
    xsh = din("xsh", [128, D])
    fl = din("fl", [1, NF])
    cvec = din("cvec", [16, 128])
    ada_w = din("ada_w", [2, D, 6 * D])
    ada_b = din("ada_b", [2, 6 * D])
    norm_w = din("norm_w", [2, 2, D])
    ffn_w_in = din("ffn_w_in", [2, D, 2 * D_FF])
    ffn_w_out = din("ffn_w_out", [2, D_FF, D])
    ab_w_in = din("ab_w_in", [1, D, 4128])
    ab_w_out = din("ab_w_out", [1, D, D])
    ret_lg = din("ret_log_gamma", [1, 2, 8])
    ret_nw = din("ret_norm_w", [1, 512])
    conv_w = din("mlstm_conv_w", [1, 3, D])
    conv_b = din("mlstm_conv_b", [1, D])
    gate_b = din("mlstm_gate_b", [1, 4, 8])
    ml_nw = din("mlstm_norm_w", [1, 512])
    at_w_in = din("attn_w_in", [1, D, 1536])
    at_w_out = din("attn_w_out", [1, D, D])
    at_qn = din("attn_q_norm_w", [1, 64])
    at_kn = din("attn_k_norm_w", [1, 64])
    at_sink = din("attn_sink", [1, 16])
    consts = din("consts", [7, 128, 128])
    amask = din("amask", [4, 128, 128])
    ropeF = din("ropeF", [2, 128, 2304])
    ropeT = din("ropeT", [2, 128, NW * 32])
    ropeS = din("ropeS", [2, 128, NSLOT * 32])
    y = nc.dram_tensor("y", [2048, D], F32, kind="ExternalOutput").ap()
    dbg_out = None
    if dbg is not None:
        dbg_out = nc.dram_tensor("dbg", [NCH, 128, D], F32, kind="ExternalOutput").ap()

    st = contextlib.ExitStack()
    with st:
        arena_t = st.enter_context(nc.sbuf_tensor("arena", [128, ARENA_BYTES // 4], F32))
        RP = Region(arena_t, 0, P_BYTES, "P")
        RX = Region(arena_t, P_BYTES, X_BYTES, "X")
        RM = Region(arena_t, P_BYTES + X_BYTES, ARENA_BYTES - P_BYTES - X_BYTES, "MF")
        banks = [st.enter_context(nc.psum_tensor(f"pb{i}", [128, 512], F32)) for i in range(8)]
        P = Prog(nc)
        P.marks = []

        def mark(nm):
            P.marks.append((nm, sum(1 for o in P.ops if o['eng'] == 'tensor')))

        def ck(name, aps):
            if dbg != name:
                return
            k = 0
            for ap in aps:
                n = ap.shape[1] if len(ap.shape) == 2 else None
                flat = ap
                npart = ap.shape[0]
                P.dma('sync' if flat.dtype == F32 else 'gpsimd', dbg_out[k][0:npart, 0:flat.shape[1]], flat, store=True)
                k += 1
            P.frozen = True

        def pb(i, shape, dt=F32, off=0):
            n = 1
            for s in shape:
                n *= s
            nb = n * _DTSZ[dt]
            v = banks[i][:, off // 4:(off + nb + 3) // 4]
            if dt != F32:
                v = v.bitcast(dt)
            v = v[:, 0:n]
            if len(shape) == 1:
                return v
            names = [chr(ord('a') + k) for k in range(len(shape))]
            pat = "p (" + " ".join(names) + ") -> p " + " ".join(names)
            return v.rearrange(pat, **{nm: s for nm, s in zip(names, shape)})

        cst = RP.alloc([7, 128], F32)
        P.dma('sync', cst, consts.rearrange("c p n -> p c n"))
        identF = cst[:, 0, :]
        MfF, MbF = cst[:, 1, :], cst[:, 2, :]
        onesF = cst[:, 5, :]
        cstb = RP.alloc([7, 128], BF16)
        P.copy('vector', cstb, cst)
        identB = cstb[:, 0, :]
        maskF_b, maskB_b = cstb[:, 3, :], cstb[:, 4, :]
        FL = RP.alloc([NF], F32)
        P.dma('sync', FL, bc(fl, [128, NF]))
        epsb = RP.alloc([1], F32)
        P.memset('vector', epsb, EPS)
        lnk = RP.alloc([1], F32)
        P.memset('vector', lnk, LNK)
        colA = RP.alloc([112], F32)
        colB = RP.alloc([48], F32)
        svf = RP.alloc([16], F32)
        sv = RP.alloc([8, 2], BF16)
        modT = RP.alloc([2, 2, 48], F32)
        gB = RP.alloc([4, D], F32)
        WM = RP.alloc([2, 2, 2, 8], F32)
        SHc = RP.alloc([2, 2, 2, 8], F32)
        junk = RP.alloc([D], BF16)
        xnb = [RP.alloc([D], BF16) for _ in range(2)]
        t32 = RP.alloc([8, 128], F32)
        ssq = RP.alloc([4], F32)

        m0 = RM.mark()
        stg = RM.alloc([128], F32)
        stg2 = RM.alloc([128], F32)
        P.dma('sync', stg[0:16, :], cvec)
        P.dma('sync', stg[16:64, :], ada_b[0].rearrange("(j p) -> j p", p=128))
        P.dma('sync', stg[64:112, :], ada_b[1].rearrange("(j p) -> j p", p=128))
        tp = pb(0, [112])
        P.transpose(tp, stg[0:112, :], identF[0:112, 0:112])
        P.copy('vector', colA, tp)
        P.dma('sync', stg2[0:32, :], norm_w.rearrange("l i (j p) -> (l i j) p", p=128))
        P.dma('sync', stg2[32:36, :], ret_nw[0].rearrange("(j p) -> j p", p=128))
        P.dma('sync', stg2[36:40, :], ml_nw[0].rearrange("(j p) -> j p", p=128))
        P.dma('sync', stg2[40:48, :], conv_b[0].rearrange("(j p) -> j p", p=128))
        tp2 = pb(0, [48], off=1024)
        P.transpose(tp2, stg2[0:48, :], identF[0:48, 0:48])
        P.copy('vector', colB, tp2)
        RM.release(m0)
        ck('A0', [colA, colB, FL, cst[:, 1, :]])
        P.act(svf, colA[:, 0:16], AF.Silu)
        P.copy('vector', sv[:, :, 0], svf[:, 0:8])
        P.copy('vector', sv[:, :, 1], svf[:, 8:16])

        gslot = {(0, 0, 2): 0, (0, 0, 5): 1, (0, 1, 2): 2, (0, 1, 5): 3, (1, 0, 2): 0, (1, 0, 5): 1}

        def modulation(l):
            m0 = RM.mark()
            svrep = RM.alloc([2, 8, 128], BF16)
            for v_ in range(2):
                P.copy('vector', svrep[:, v_, :, :], bc(svf[:, 8 * v_:8 * v_ + 8].unsqueeze(2), [128, 8, 128]))
            wblk = [RM.alloc([8, 1024], BF16) for _ in range(2)]
            bbc = RM.alloc([1024], F32)
            for j in range(6):
                wb = wblk[j % 2]
                P.dma('gpsimd', wb, ada_w[l].rearrange("(kc p) n -> p kc n", p=128)[:, :, j * 1024:(j + 1) * 1024])
                ps = pb(1, [8, 2])
                for n in range(8):
                    for kc in range(8):
                        P.mm(ps[:, n, :], wb[:, kc, n * 128:(n + 1) * 128], sv[:, kc, :], start=(kc == 0), stop=(kc == 7))
                for v_ in range(2):
                    P.tt('vector', modT[:, l, v_, j * 8:(j + 1) * 8], ps[:, :, v_],
                         colA[:, 16 + 48 * l + j * 8:16 + 48 * l + j * 8 + 8], ALU.add)
                if j in (2, 5):
                    P.dma('sync', bbc, bc(ada_b[l:l + 1, j * 1024:(j + 1) * 1024], [128, 1024]))
                    for v_ in range(2):
                        if (l, v_, j) not in gslot:
                            continue
                        for hf in range(2):
                            pg = pb(2 + hf, [512])
                            for kc in range(8):
                                P.mm(pg, svrep[:, v_, kc, :], wb[:, kc, hf * 512:(hf + 1) * 512], start=(kc == 0), stop=(kc == 7))
                            P.tt('vector', gB[:, gslot[(l, v_, j)], hf * 512:(hf + 1) * 512], pg, bbc[:, hf * 512:(hf + 1) * 512], ALU.add)
            RM.release(m0)

        def make_wm(l):
            for i in range(2):
                for v_ in range(2):
                    sc = modT[:, l, v_, (3 * i + 1) * 8:(3 * i + 2) * 8]
                    nw = colB[:, (2 * l + i) * 8:(2 * l + i) * 8 + 8]
                    P.stt('vector', WM[:, l, i, v_, :], sc, 1.0, nw, ALU.add, ALU.mult)
                    P.copy('vector', SHc[:, l, i, v_, :], modT[:, l, v_, (3 * i) * 8:(3 * i) * 8 + 8])

        modulation(0)
        make_wm(0)
        mark('mod0')
        ck('A', [colA, colB, modT.rearrange('p a b c -> p (a b c)'), gB[:, 0, :], gB[:, 3, :], WM.rearrange('p a b c d -> p (a b c d)')])

        cnt_h = [0]

        def hT_part1(x_sb):
            k = cnt_h[0] % 2
            cnt_h[0] += 1
            ss = ssq[:, 2 * k:2 * k + 1]
            rs = ssq[:, 2 * k + 1:2 * k + 2]
            P.act(junk, x_sb, AF.Square, accum_out=ss)
            P.act(rs, ss, AF.Sqrt, bias=epsb, scale=1.0 / D)
            P.recip(rs, rs)
            xn = xnb[k]
            P.act(xn, x_sb, AF.Identity, scale=rs)
            return k

        def hT_part2(k, dest, wcol, shcol, flag=None, alt_bank=None):
            xn = xnb[k]
            tps = pb(0 if (alt_bank is None or k == 0) else alt_bank, [8, 128], BF16)
            for kc in range(8):
                P.transpose(tps[:, kc, :], xn[:, kc * 128:(kc + 1) * 128], identB)
            if flag is None:
                for kc in range(8):
                    P.act(dest[:, kc, :], tps[:, kc, :], AF.Identity, bias=shcol[:, kc:kc + 1], scale=wcol[:, kc:kc + 1])
            else:
                P.tt('vector', t32, tps, bc(wcol.unsqueeze(2), [128, 8, 128]), ALU.mult)
                P.tt('gpsimd', t32, t32, bc(shcol.unsqueeze(2), [128, 8, 128]), ALU.add)
                P.ts('vector', dest, t32, flag, None, ALU.mult)

        def make_hT(x_sb, dest, wcol, shcol, flag=None, alt_bank=None):
            k = hT_part1(x_sb)
            hT_part2(k, dest, wcol, shcol, flag, alt_bank)

        hT = RX.alloc([8, HT_N], BF16)
        AA = RX.alloc([NCH, 2, 16], F32)
        BB = RX.alloc([NCH, 2, 16], F32)
        DEC = RX.alloc([NCH, 2, 16], F32)
        Sacc = RX.alloc([16, 65], F32)
        RR = RX.alloc([2, 16], F32)
        expR = RX.alloc([2, 16], F32)
        mXf = RX.mark()
        mMF = RM.mark()
        win = ab_w_in[0].rearrange("(kc p) n -> p kc n", p=128)

        xbuf = [RX.alloc([D], F32) for _ in range(2)]
        P.memset('gpsimd', hT[:, :, 0:1], 0.0)
        P.memset('gpsimd', hT[:, :, 257:258], 0.0)

        def tokcols(c):
            return (CTX0 + c * 128) if c < 2 else (WIN0 + (c - 2) * 128)

        def hT_chunk_p1(c):
            xb = xbuf[c % 2]
            P.dma('sync', xb, xw[c])
            return hT_part1(xb)

        def hT_chunk_p2(c, k):
            v_ = 1 if c < 2 else 0
            col0 = tokcols(c)
            fg = None if c not in (2, NCH - 1) else FL[:, FL_VF + c:FL_VF + c + 1]
            hT_part2(k, hT[:, :, col0:col0 + 128], WM[:, 0, 0, v_, :], SHc[:, 0, 0, v_, :], fg, alt_bank=7)

        kprev = hT_chunk_p1(0)
        for c in range(NCH):
            knext = hT_chunk_p1(c + 1) if c + 1 < NCH else None
            hT_chunk_p2(c, kprev)
            kprev = knext
        hTh = RX.alloc([8, 128], BF16)
        xb = xbuf[0]
        P.dma('sync', xb, xe)
        make_hT(xb, hTh, WM[:, 0, 0, 0, :], SHc[:, 0, 0, 0, :])
        P.ts('vector', hT[:, :, 258:259], hTh[:, :, 0:1], FL[:, FL_EDGE:FL_EDGE + 1], None, ALU.mult)
        P.ts('vector', hT[:, :, 2563:2564], hTh[:, :, 1:2], FL[:, FL_EDGE + 1:FL_EDGE + 2], None, ALU.mult)

        ck('B', [hT[:, 0, 0:1024], hT[:, 7, 1540:2564]])
        mark('hT')
        Gpre = RX.alloc([NCH, 32], F32)
        LF = RX.alloc([NCH, 2, 16], F32)
        II = RX.alloc([NCH, 2, 16], F32)
        Wg = RM.alloc([8, 32], BF16)
        P.dma('gpsimd', Wg, win[:, :, 4096:4128])
        gbb = RM.alloc([32], F32)
        P.dma('sync', gbb, bc(gate_b[0].rearrange("a h -> (a h)").unsqueeze(0), [128, 32]))
        lgb = RM.alloc([2, 8], F32)
        P.dma('sync', lgb.rearrange("p a h -> p (a h)"), bc(ret_lg[0].rearrange("a h -> (a h)").unsqueeze(0), [128, 16]))
        ck('C0', [gbb, lgb.rearrange('p a h -> p (a h)'), Wg.rearrange('p a b -> p (a b)')])
        for c in range(NCH):
            c0 = tokcols(c)
            pg = pb(3, [128])[:, 32 * (c % 4):32 * (c % 4) + 32]
            for kc in range(8):
                P.mm(pg, hT[:, kc, c0:c0 + 128], Wg[:, kc, :], start=(kc == 0), stop=(kc == 7))
            P.tt('vector', Gpre[:, c, :], pg, gbb, ALU.add)
        ck('C1', [Gpre.rearrange('p a b -> p (a b)')])
        VFb = FL[:, FL_VF:FL_VF + NCH]
        tmpg = RX.alloc([NCH, 8], F32)
        for d_ in range(2):
            fcol = Gpre[:, :, 8 + 16 * d_:16 + 16 * d_]
            icol = Gpre[:, :, 16 * d_:8 + 16 * d_]
            P.act(tmpg, fcol, AF.Exp, scale=-1.0)
            P.act(tmpg, tmpg, AF.Ln, bias=1.0)
            P.stt('vector', LF[:, :, d_, 8:16], tmpg, -1.0, bc(VFb.unsqueeze(2), [128, NCH, 8]), ALU.mult, ALU.mult)
            P.tt('vector', LF[:, :, d_, 0:8], bc(lgb[:, d_, :].unsqueeze(1), [128, NCH, 8]), bc(VFb.unsqueeze(2), [128, NCH, 8]), ALU.mult)
            P.memset('gpsimd', II[:, :, d_, 0:8], 0.0)
            P.copy('gpsimd', II[:, :, d_, 8:16], icol)
        ck('C2', [LF.rearrange('p a b c -> p (a b c)'), II.rearrange('p a b c -> p (a b c)')])
        LFc = RX.alloc([2, NCH * 16], F32)
        for d_ in range(2):
            P.copy('gpsimd', LFc[:, d_, :].rearrange("p (c l) -> p c l", l=16), LF[:, :, d_, :])
        for d_ in range(2):
            pe = pb(4 + d_, [NCH, 16])
            pef = pe.rearrange("p c l -> p (c l)")
            for (a_, b_) in ((0, 128), (128, 256), (256, 320)):
                P.mm(pef[:, a_:b_], MfF if d_ == 0 else MbF, LFc[:, d_, a_:b_])
            if d_ == 0 and dbg == 'C2a':
                P.copy('vector', AA.rearrange('p a b c -> p (a b c)')[:, 0:320], pef)
                ck('C2a', [AA.rearrange('p a b c -> p (a b c)'), LFc.rearrange('p a b -> p (a b)')])
            P.act(AA[:, :, d_, :], pe, AF.Exp, scale=-1.0)
            if d_ == 0:
                ck('C2b', [AA.rearrange('p a b c -> p (a b c)')])
            P.tt('vector', BB[:, :, d_, :], pe, II[:, :, d_, :], ALU.add)
            if d_ == 0:
                ck('C2c', [BB.rearrange('p a b c -> p (a b c)')])
            P.act(BB[:, :, d_, :], BB[:, :, d_, :], AF.Exp, bias=lnk)
            if d_ == 0:
                ck('C2d', [BB.rearrange('p a b c -> p (a b c)')])
            P.tt('vector', BB[:, :, d_, :], BB[:, :, d_, :], bc(VFb.unsqueeze(2), [128, NCH, 16]), ALU.mult)
        ck('C3', [AA.rearrange('p a b c -> p (a b c)'), BB.rearrange('p a b c -> p (a b c)')])
        LFf = LF.rearrange("p a b c -> p (a b c)")
        for hf in range(2):
            pt_ = pb(6, [10, 2, 16])
            ptf = pt_.rearrange("p a b c -> p (a b c)")
            for (a_, b_) in ((0, 128), (128, 256), (256, 320)):
                P.mm(ptf[:, a_:b_], onesF, LFf[:, hf * 320 + a_:hf * 320 + b_])
            P.act(DEC[:, hf * 10:(hf + 1) * 10, :, :], pt_, AF.Exp)

        ck('C', [AA.rearrange('p a b c -> p (a b c)'), BB.rearrange('p a b c -> p (a b c)'), DEC.rearrange('p a b c -> p (a b c)'), Gpre.rearrange('p a b -> p (a b)')])
        mark('prepass')
        P.memset('vector', Sacc, 0.0)
        P.memset('vector', RR, 0.0)
        Wv = RM.alloc([8, 1024], BF16)
        P.dma('gpsimd', Wv[:, :, 0:512], win[:, :, 1024:1536])
        P.dma('gpsimd', Wv[:, :, 512:1024], win[:, :, 3072:3584])
        Wkr = RM.alloc([8, 512], BF16)
        P.dma('gpsimd', Wkr, win[:, :, 512:1024])
        Wtap = RM.alloc([3, 8, 512], BF16)
        m1 = RM.mark()
        cwk = RM.alloc([3, 512], F32)
        for j in range(3):
            P.dma('sync', cwk[:, j, :], bc(conv_w[0, j:j + 1, 512:1024], [128, 512]))
        Wkm = RM.alloc([8, 512], BF16)
        P.dma('gpsimd', Wkm, win[:, :, 2560:3072])
        for j in range(3):
            P.tt('vector', Wtap[:, j, :, :], Wkm, bc(cwk[:, j, :].unsqueeze(1), [128, 8, 512]), ALU.mult)
        RM.release(m1)
        cbb = RM.alloc([512], F32)
        P.dma('sync', cbb, bc(conv_b[0:1, 512:1024], [128, 512]))
        rS = RM.alloc([2, NSLOT, 32], F32)
        P.dma('sync', rS.rearrange("p a s f -> p a (s f)"), ropeS.rearrange("a p n -> p a n"))
        Kfb = RM.alloc([16, 128], BF16)
        Vext = RM.alloc([16, 65], BF16)
        ra = RM.alloc([2, 8, 32], F32)
        krf = [RM.alloc([512], F32) for _ in range(2)]
        Vsb = [RM.alloc([16, 64], BF16) for _ in range(2)]
        kmts = [RM.alloc([512], F32) for _ in range(2)]
        gps = [RM.alloc([32], F32) for _ in range(2)]
        xb = xbuf[1]
        P.dma('sync', xb, xsh)
        make_hT(xb, hTh, WM[:, 0, 0, 0, :], SHc[:, 0, 0, 0, :])
        P.tt('vector', hTh, hTh, bc(FL[:, FL_SH:FL_SH + 128].unsqueeze(1), [128, 8, 128]), ALU.mult)
        hTs = [RX.alloc([8, 130], BF16) for _ in range(2)]
        Ktok = RX.alloc([16, 64], BF16)
        sg = RX.alloc([160], F32)

        def slot_A1(s):
            k = s % 2
            xb = xbuf[k]
            P.dma('sync', xb, xs[s])
            ss = ssq[:, 2 * k:2 * k + 1]
            rs = ssq[:, 2 * k + 1:2 * k + 2]
            P.act(junk, xb, AF.Square, accum_out=ss)
            P.act(rs, ss, AF.Sqrt, bias=epsb, scale=1.0 / D)
            P.recip(rs, rs)
            P.act(xnb[k], xb, AF.Identity, scale=rs)

        def slot_A2(s):
            k = s % 2
            hs = hTs[k]
            xn = xnb[k]
            wcol, shcol = WM[:, 0, 0, 0, :], SHc[:, 0, 0, 0, :]
            P.copy('scalar', hs[:, :, 0:1], hTh[:, :, 2 * s:2 * s + 1])
            P.copy('scalar', hs[:, :, 129:130], hTh[:, :, 2 * s + 1:2 * s + 2])
            tps = pb(0, [8, 128], BF16)
            for kc in range(8):
                P.transpose(tps[:, kc, :], xn[:, kc * 128:(kc + 1) * 128], identB)
            for kc in range(8):
                P.act(hs[:, kc, 1:129], tps[:, kc, :], AF.Identity, bias=shcol[:, kc:kc + 1], scale=wcol[:, kc:kc + 1])
            pkr = pb(1, [512])
            for kc in range(8):
                P.mm(pkr, hs[:, kc, 1:129], Wkr[:, kc, :], start=(kc == 0), stop=(kc == 7))
            P.copy('scalar', krf[k], pkr)
            pkm = pb(2, [512])
            for j in range(3):
                for kc in range(8):
                    P.mm(pkm, hs[:, kc, j:j + 128], Wtap[:, j, kc, :], start=(j == 0 and kc == 0), stop=(j == 2 and kc == 7))
            P.tt('vector', kmts[k], pkm, cbb, ALU.add)
            for hf in range(2):
                pv = pb(3 + hf, [512])
                for kc in range(8):
                    P.mm(pv, hs[:, kc, 1:129], Wv[:, kc, hf * 512:(hf + 1) * 512], start=(kc == 0), stop=(kc == 7))
                P.copy('scalar', Vsb[k][:, hf * 8:(hf + 1) * 8, :], pv.rearrange("p (h d) -> p h d", h=8))
            pg = pb(5, [32])
            for kc in range(8):
                P.mm(pg, hs[:, kc, 1:129], Wg[:, kc, :], start=(kc == 0), stop=(kc == 7))
            P.tt('vector', gps[k], pg, gbb, ALU.add)

        fsel = sg[:, 32:40]
        isel = sg[:, 40:56]
        lf16 = sg[:, 56:72]
        lfm = sg[:, 72:104]
        rsel = sg[:, 104:120]
        bex = sg[:, 120:136]
        t8 = sg[:, 136:144]
        RRf = RR.rearrange("p a l -> p (a l)")

        def slot_B_early(s):
            ff = FL[:, FL_FF + s:FL_FF + s + 1]
            fb = FL[:, FL_FB + s:FL_FB + s + 1]
            gp = gps[s % 2]
            P.ts('vector', fsel, gp[:, 8:16], ff, None, ALU.mult)
            P.stt('vector', fsel, gp[:, 24:32], fb, fsel, ALU.mult, ALU.add)
            P.act(t8, fsel, AF.Exp, scale=-1.0)
            P.act(t8, t8, AF.Ln, bias=1.0)
            P.ts('vector', lf16[:, 0:8], lgb[:, 0, :], ff, None, ALU.mult)
            P.stt('vector', lf16[:, 0:8], lgb[:, 1, :], fb, lf16[:, 0:8], ALU.mult, ALU.add)
            P.ts('vector', lf16[:, 8:16], t8, -1.0, None, ALU.mult)
            P.ts('vector', lfm[:, 0:16], lf16, ff, None, ALU.mult)
            P.ts('vector', lfm[:, 16:32], lf16, fb, None, ALU.mult)
            pe = pb(5, [16], off=1024)
            P.mm(pe, MfF, lfm[:, 0:16], start=True, stop=False)
            P.mm(pe, MbF, lfm[:, 16:32], start=False, stop=True)
            pt_ = pb(5, [32], off=1536)
            P.mm(pt_, onesF, lfm)
            P.memset('vector', isel[:, 0:8], 0.0)
            P.ts('vector', isel[:, 8:16], gp[:, 0:8], ff, None, ALU.mult)
            P.stt('vector', isel[:, 8:16], gp[:, 16:24], fb, isel[:, 8:16], ALU.mult, ALU.add)
            P.ts('vector', rsel, RR[:, 0, :], ff, None, ALU.mult)
            P.stt('vector', rsel, RR[:, 1, :], fb, rsel, ALU.mult, ALU.add)
            P.tt('vector', rsel, rsel, isel, ALU.add)
            k3 = krf[s % 2].rearrange("p (h d) -> p h d", h=8)
            x1, x2 = k3[:, :, 0:32], k3[:, :, 32:64]
            cs = bc(rS[:, 0, s, :].unsqueeze(1), [128, 8, 32])
            sn = bc(rS[:, 1, s, :].unsqueeze(1), [128, 8, 32])
            P.tt('vector', ra[:, 0], x1, cs, ALU.mult)
            P.tt('vector', ra[:, 1], x2, sn, ALU.mult)
            P.tt('vector', Ktok[:, 0:8, 0:32], ra[:, 0], ra[:, 1], ALU.subtract)
            P.tt('vector', ra[:, 0], x2, cs, ALU.mult)
            P.tt('vector', ra[:, 1], x1, sn, ALU.mult)
            P.tt('vector', Ktok[:, 0:8, 32:64], ra[:, 0], ra[:, 1], ALU.add)
            P.act(Ktok[:, 8:16, :], kmts[s % 2].rearrange("p (h d) -> p h d", h=8), AF.Silu)
            P.act(Kfb[:, :, 0:64], Ktok, AF.Identity, scale=ff)
            P.act(Kfb[:, :, 64:128], Ktok, AF.Identity, scale=fb)
            P.tt('vector', rsel, rsel, pe, ALU.add)
            P.act(bex, rsel, AF.Exp, bias=lnk)
            P.tt('vector', RRf, RRf, pt_, ALU.add)
            P.tt('vector', Vext[:, :, 0:64], Vsb[s % 2], bc(bex.unsqueeze(2), [128, 16, 64]), ALU.mult)
            P.copy('scalar', Vext[:, :, 64], bex)

        def slot_B_late(s):
            kvb = [pb(6, [6, 65]), pb(7, [6, 65]), pb(5, [4, 65])]
            for ln in range(16):
                P.mm(kvb[ln // 6][:, ln % 6, :], Kfb[:, ln, :], Vext[:, ln, :])
            for g3 in range(3):
                nl = 6 if g3 < 2 else 4
                P.tt('vector', Sacc[:, g3 * 6:g3 * 6 + nl, :], Sacc[:, g3 * 6:g3 * 6 + nl, :], kvb[g3], ALU.add)

        slot_A1(0)
        slot_A1(1)
        slot_A2(0)
        for s in range(NSLOT):
            if s + 2 < NSLOT:
                slot_A1(s + 2)
            slot_B_early(s)
            if s + 1 < NSLOT:
                slot_A2(s + 1)
            slot_B_late(s)
        P.act(expR, RR, AF.Exp)
        ck('D', [Sacc.rearrange('p a b -> p (a b)')[:, 0:1024], RR.rearrange('p a b -> p (a b)'), expR.rearrange('p a b -> p (a b)')])
        RX.release(mXf)
        RM.release(mMF)

        mark('outside')
        MT = RM.alloc([8, NCH * 128], BF16)
        mMF2 = RM.mark()
        order = [list(range(NCH)), [1, 0] + list(range(NCH - 1, 1, -1))]
        first_dir = [0 if order[0].index(c_) <= order[1].index(c_) else 1 for c_ in range(NCH)]
        groups = [(CTX0, 256, 0, False)] + [(WIN0 + 512 * g_, 512, 256 + 512 * g_, True) for g_ in range(4)] + [(WIN0 + 2048, 256, 2304, True)]
        for ps_ in range(8):
            if ps_ in (1, 2, 5, 6):
                mark(f'p{ps_}_start')
            RX.release(mXf)
            RM.release(mMF2)
            is_ml = ps_ >= 4
            j = ps_ % 4
            l0 = (8 if is_ml else 0) + 2 * j
            qoff = (2048 if is_ml else 0) + 128 * j
            koff = (2560 if is_ml else 512) + 128 * j
            goff = (3584 if is_ml else 1536) + 128 * j
            voff = (3072 if is_ml else 1024) + 128 * j
            ntap = 3 if is_ml else 1
            qT = RX.alloc([NCH * 128], BF16)
            kT = RX.alloc([NCH * 128], BF16)
            gT = RX.alloc([NCH * 128], BF16)
            Kt = RX.alloc([NCH, 128], BF16)
            Vp = RX.alloc([NCH, 2, 64], BF16)
            Wq = RM.alloc([3, 8, 128], BF16)
            Wk = RM.alloc([3, 8, 128], BF16)
            Wgt = RM.alloc([8, 128], BF16)
            Wvp = RM.alloc([8, 128], BF16)
            Oacc = RM.alloc([NCH, 2, 64], F32)
            mScan = RM.mark()
            P.dma('gpsimd', Wq[:, 0], win[:, :, qoff:qoff + 128])
            P.dma('gpsimd', Wk[:, 0], win[:, :, koff:koff + 128])
            P.dma('gpsimd', Wgt, win[:, :, goff:goff + 128])
            P.dma('gpsimd', Wvp, win[:, :, voff:voff + 128])
            if is_ml:
                cwp = RM.alloc([2, 3, 128], F32)
                for wi, co in enumerate((128 * j, 512 + 128 * j)):
                    for tpi in range(3):
                        P.dma('sync', cwp[:, wi, tpi, :], bc(conv_w[0, tpi:tpi + 1, co:co + 128], [128, 128]))
                for wi, W_ in enumerate((Wq, Wk)):
                    for tpi in (2, 1, 0):
                        P.tt('vector', W_[:, tpi], W_[:, 0], bc(cwp[:, wi, tpi, :].unsqueeze(1), [128, 8, 128]), ALU.mult)
            rtmp = RM.alloc([2, 512], F32)
            rfb = [RM.alloc([2, 512], F32) for _ in range(2)]
            bi = 0
            for gi_, (hc0, n, lc0, isw) in enumerate(groups):
                rf = rfb[gi_ % 2]
                if isw and not is_ml:
                    w0_ = hc0 - WIN0
                    P.dma('sync', rf[:, :, 0:n], ropeF.rearrange("a p n -> p a n")[:, :, w0_:w0_ + n])
                for which, W_, dst in (('q', Wq, qT), ('k', Wk, kT), ('g', Wgt, gT)):
                    pp = pb(1 + (bi % 3), [512])[:, 0:n]
                    bi += 1
                    if which == 'g':
                        for kc in range(8):
                            P.mm(pp, W_[:, kc, :], hT[:, kc, hc0:hc0 + n], start=(kc == 0), stop=(kc == 7))
                        P.act(dst[:, lc0:lc0 + n], pp, AF.Sigmoid if is_ml else AF.Silu)
                        continue
                    for tpi in range(ntap):
                        sh_ = (tpi - 1) if is_ml else 0
                        for kc in range(8):
                            P.mm(pp, W_[:, tpi, kc, :], hT[:, kc, hc0 + sh_:hc0 + sh_ + n],
                                 start=(tpi == 0 and kc == 0), stop=(tpi == ntap - 1 and kc == 7))
                    if is_ml:
                        ci_ = 40 + (j if which == 'q' else 4 + j)
                        P.act(dst[:, lc0:lc0 + n], pp, AF.Silu, bias=colB[:, ci_:ci_ + 1])
                    elif not isw:
                        P.copy('scalar', dst[:, lc0:lc0 + n], pp)
                    else:
                        P.tt('vector', rtmp[:, 0, 0:n], pp, rf[:, 0, 0:n], ALU.mult)
                        for blk in range(4):
                            src = blk ^ 1
                            P.tt('vector', rtmp[blk * 32:(blk + 1) * 32, 1, 0:n], pp[src * 32:(src + 1) * 32, :],
                                 rf[src * 32:(src + 1) * 32, 1, 0:n], ALU.mult)
                        P.tt('vector', dst[:, lc0:lc0 + n], rtmp[:, 0, 0:n], rtmp[:, 1, 0:n], ALU.add)
            if ps_ in (1, 5):
                mark(f'p{ps_}_proj')
            for c8 in range(0, NCH, 8):
                ncc = min(8, NCH - c8)
                tpk = pb(4, [8, 128], BF16)
                for ci in range(ncc):
                    P.transpose(tpk[:, ci, :], kT[:, (c8 + ci) * 128:(c8 + ci + 1) * 128], identB)
                P.copy('scalar', Kt[:, c8:c8 + ncc, :], tpk[:, 0:ncc, :])
            for c4 in range(0, NCH, 4):
                pv = pb(1 + (c4 // 4) % 3, [4, 128])
                for ci in range(4):
                    c0 = tokcols(c4 + ci)
                    for kc in range(8):
                        P.mm(pv[:, ci, :], hT[:, kc, c0:c0 + 128], Wvp[:, kc, :], start=(kc == 0), stop=(kc == 7))
                P.copy('vector', Vp[:, c4:c4 + 4].rearrange("p c h d -> p c (h d)"), pv)
            if ps_ == 0:
                ck('E1', [qT[:, 0:1024], kT[:, 0:1024], gT[:, 0:1024], Kt.rearrange('p a b -> p (a b)')[:, 0:1024], Vp.rearrange('p a b c -> p (a b c)')[:, 0:1024]])
            if ps_ == 4:
                ck('F1', [qT[:, 0:1024], kT[:, 0:1024], gT[:, 0:1024], Kt.rearrange('p a b -> p (a b)')[:, 0:1024], Vp.rearrange('p a b c -> p (a b c)')[:, 0:1024]])
            if ps_ in (1, 5):
                mark(f'p{ps_}_kv')
            RM.release(mScan)
            decp = RM.alloc([NCH, 2], F32)
            P.copy('vector', decp[0:64], DEC[0:64, :, :, l0])
            P.copy('vector', decp[64:128], DEC[64:128, :, :, l0 + 1])
            S32 = [RM.alloc([130], F32) for _ in range(2)]
            Sbf = [RM.alloc([130], BF16) for _ in range(2)]
            stmp = RM.alloc([130], F32)
            ecol = RM.alloc([2], F32)
            Vts = [RM.alloc([2, 65], BF16) for _ in range(4)]
            dn = RM.alloc([2, 8], F32)
            P.memset('gpsimd', stmp, 0.0)
            for d_ in range(2):
                P.memset('gpsimd', S32[d_], 0.0)
                P.copy('vector', ecol[0:64, d_:d_ + 1], expR[0:64, d_, l0:l0 + 1])
                P.copy('vector', ecol[64:128, d_:d_ + 1], expR[64:128, d_, l0 + 1:l0 + 2])
            PTs = [RM.alloc([2, 128], BF16) for _ in range(4)]
            pt_banks = [[0, 6], [3, 7]]

            def scan_front(step, d_):
                c = order[d_][step]
                tk = slice(c * 128, (c + 1) * 128)
                Vt = Vts[2 * (step % 2) + d_]
                P.tt('vector', Vt[:, :, 0:64], Vp[:, c], bc(BB[:, c, d_, l0:l0 + 2].unsqueeze(2), [128, 2, 64]), ALU.mult)
                P.copy('scalar', Vt[:, :, 64], BB[:, c, d_, l0:l0 + 2])
                ptp = pb(pt_banks[d_][step % 2], [2, 128])
                prev_mm = None
                for h in range(2):
                    hb = slice(64 * h, 64 * h + 64)
                    prev_mm = P.mm(ptp[:, h, :], kT[hb, tk], qT[hb, tk], after=prev_mm)
                PT = PTs[2 * (step % 2) + d_]
                P.tt('vector', PT, ptp, bc((maskF_b if d_ == 0 else maskB_b).unsqueeze(1), [128, 2, 128]), ALU.mult)

            def scan_kv(step, d_):
                c = order[d_][step]
                Vt = Vts[2 * (step % 2) + d_]
                kvp = pb(2 if d_ == 0 else 5, [130])
                P.mm(kvp, Kt[:, c, :], Vt.rearrange("p h e -> p (h e)"))

            def scan_back(step, d_):
                c = order[d_][step]
                tk = slice(c * 128, (c + 1) * 128)
                S = S32[d_]
                Vt = Vts[2 * (step % 2) + d_]
                PT = PTs[2 * (step % 2) + d_]
                if step == 2:
                    r0 = 64 * d_
                    P.copy('scalar', stmp[0:64, 0:65], Sacc[r0:r0 + 64, l0, :])
                    P.copy('scalar', stmp[64:128, 65:130], Sacc[r0:r0 + 64, l0 + 1, :])
                    P.stt('vector', S, S, ecol[:, d_:d_ + 1], stmp, ALU.mult, ALU.add)
                if step > 0:
                    P.act(Sbf[d_], S, AF.Identity, scale=decp[:, c, d_:d_ + 1])
                ops_ = pb(1 if d_ == 0 else 4, [2, 65])
                for h in range(2):
                    hb = slice(64 * h, 64 * h + 64)
                    P.mm(ops_[:, h, :], PT[:, h, :], Vt[:, h, :], start=True, stop=(step == 0))
                    if step > 0:
                        P.mm(ops_[:, h, :], qT[hb, tk], Sbf[d_][hb, 65 * h:65 * h + 65], start=False, stop=True)
                at2 = AA[:, c, d_, l0:l0 + 2]
                if is_ml:
                    dd = dn[:, d_, :]
                    P.tt('vector', dd[:, 0:2], ops_[:, :, 64], at2, ALU.mult)
                    P.act(dd[:, 2:4], dd[:, 0:2], AF.Abs)
                    P.ts('vector', dd[:, 2:4], dd[:, 2:4], 1.0, None, ALU.max)
                    P.recip(dd[:, 4:6], dd[:, 2:4])
                    P.tt('vector', dd[:, 6:8], dd[:, 4:6], at2, ALU.mult)
                for h in range(2):
                    coef = dn[:, d_, 6 + h:7 + h] if is_ml else AA[:, c, d_, l0 + h:l0 + h + 1]
                    if first_dir[c] == d_:
                        P.act(Oacc[:, c, h, :], ops_[:, h, 0:64], AF.Identity, scale=coef)
                    else:
                        P.stt('vector', Oacc[:, c, h, :], ops_[:, h, 0:64], coef, Oacc[:, c, h, :], ALU.mult, ALU.add)
                kvp = pb(2 if d_ == 0 else 5, [130])
                if step == 0:
                    P.copy('vector', S, kvp)
                else:
                    P.stt('vector', S, S, decp[:, c, d_:d_ + 1], kvp, ALU.mult, ALU.add)

            for d_ in range(2):
                scan_front(0, d_)
                scan_kv(0, d_)
            for step in range(NCH):
                if step + 1 < NCH:
                    for d_ in range(2):
                        scan_front(step + 1, d_)
                for d_ in range(2):
                    scan_back(step, d_)
                if step + 1 < NCH:
                    for d_ in range(2):
                        scan_kv(step + 1, d_)
                if ps_ == 0 and step < 3:
                    ck('E2' + 'abc'[step], [Oacc.rearrange('p a b c -> p (a b c)')[:, 0:256], S32[0], S32[1]])
            if ps_ == 0:
                ck('E2', [Oacc.rearrange('p a b c -> p (a b c)')[:, 0:1024], Oacc.rearrange('p a b c -> p (a b c)')[:, 1024:2048]])
            if ps_ == 4:
                ck('F2', [Oacc.rearrange('p a b c -> p (a b c)')[:, 0:1024], Oacc.rearrange('p a b c -> p (a b c)')[:, 1024:2048]])
            if ps_ in (1, 5):
                mark(f'p{ps_}_scan')
            RM.release(mScan)
            sq = RM.alloc([NCH, 2, 64], F32)
            ms = RM.alloc([NCH, 2], F32)
            On = RM.alloc([NCH, 2, 64], BF16)
            P.act(sq, Oacc, AF.Square)
            P.reduce('vector', ms, sq, ALU.add)
            P.act(ms, ms, AF.Sqrt, bias=epsb, scale=1.0 / 64)
            P.recip(ms, ms)
            P.tt('vector', On, Oacc, bc(ms.unsqueeze(3), [128, NCH, 2, 64]), ALU.mult)
            nwi = (36 if is_ml else 32) + j
            for c4 in range(0, NCH, 4):
                tpo = pb(4, [4, 128], BF16, off=(c4 // 4 % 2) * 1024)
                for ci in range(4):
                    P.transpose(tpo[:, ci, :], On[:, c4 + ci].rearrange("p h d -> p (h d)"), identB)
                P.stt('vector', MT[:, ps_, c4 * 128:(c4 + 4) * 128], tpo.rearrange("p c t -> p (c t)"), colB[:, nwi:nwi + 1],
                      gT[:, c4 * 128:(c4 + 4) * 128], ALU.mult, ALU.mult)
        ck('E4', [MT[:, 0, 0:1024], MT[:, 7, 0:1024]])
        RM.release(mMF2)
        RX.release(0)
        mark('passes')
        X1 = RX.alloc([NCH, D], F32)
        xsp = RX.alloc([512], F32)
        Wo = RM.alloc([8, D], BF16)
        P.dma('gpsimd', Wo, ab_w_out[0].rearrange("(kc p) n -> p kc n", p=128))
        otmp = [RM.alloc([512], F32) for _ in range(2)]
        for c in range(NCH):
            P.dma('sync', X1[:, c, :], xw[c])
        for c in range(NCH):
            gi = 2 if c < 2 else 0
            for hf in range(2):
                po = pb(1 + hf, [512])
                for kc in range(8):
                    P.mm(po, MT[:, kc, c * 128:(c + 1) * 128], Wo[:, kc, hf * 512:(hf + 1) * 512], start=(kc == 0), stop=(kc == 7))
                P.tt('vector', otmp[hf], po, gB[:, gi, hf * 512:(hf + 1) * 512], ALU.mult)
                P.tt('vector', X1[:, c, hf * 512:(hf + 1) * 512], X1[:, c, hf * 512:(hf + 1) * 512], otmp[hf], ALU.add)
        RM.release(mMF)
        if dbg == 'l0m':
            for c in range(NCH):
                P.dma('sync', dbg_out[c], X1[:, c, :], store=True)

        mark('outproj0')
        def ffn(l, chunks, wm_sel, g_sel):
            m0 = RM.mark()
            w_in = ffn_w_in[l].rearrange("(kc p) n -> p kc n", p=128)
            w_out = ffn_w_out[l].rearrange("(f p) n -> p f n", p=128)
            Wout = RM.alloc([NFT, D], BF16)
            P.dma('gpsimd', Wout[:, 0:11, :], w_out[:, 0:11, :])
            P.dma('gpsimd', Wout[:, 11:22, :], w_out[:, 11:22, :])
            AT = RM.alloc([NFT, 512], BF16)
            wbuf = [RM.alloc([8, 2, 256], BF16) for _ in range(2)]
            h2 = RM.alloc([8, 512], BF16)
            o_ = xsp
            nq = len(chunks) // 4

            def prep(qi, i):
                c = chunks[4 * qi + i]
                v_ = wm_sel(c)
                make_hT(X1[:, c, :], h2[:, :, i * 128:(i + 1) * 128], WM[:, l, 1, v_, :], SHc[:, l, 1, v_, :])

            for i in range(4):
                prep(0, i)
            bi = 0
            for qi in range(nq):
                cq = chunks[4 * qi:4 * qi + 4]
                for fp in range(11):
                    wb = wbuf[bi % 2]
                    bi += 1
                    P.dma('gpsimd', wb[:, :, 0, :], w_in[:, :, fp * 256:(fp + 1) * 256])
                    P.dma('gpsimd', wb[:, :, 1, :], w_in[:, :, D_FF + fp * 256:D_FF + (fp + 1) * 256])
                    for f2 in range(2):
                        f = fp * 2 + f2
                        pgm = pb(1 + 2 * (f % 2), [512])
                        pum = pb(2 + 2 * (f % 2), [512])
                        for kc in range(8):
                            P.mm(pgm, wb[:, kc, 0, f2 * 128:(f2 + 1) * 128], h2[:, kc, :], start=(kc == 0), stop=(kc == 7))
                        for kc in range(8):
                            P.mm(pum, wb[:, kc, 1, f2 * 128:(f2 + 1) * 128], h2[:, kc, :], start=(kc == 0), stop=(kc == 7))
                        P.act(AT[:, f, :], pgm, AF.Silu)
                        P.tt('vector', AT[:, f, :], AT[:, f, :], pum, ALU.mult)
                k_ = 0
                for i, c in enumerate(cq):
                    for hf in range(2):
                        po = pb(5 + (k_ % 3), [512])
                        k_ += 1
                        for f in range(NFT):
                            P.mm(po, AT[:, f, i * 128:(i + 1) * 128], Wout[:, f, hf * 512:(hf + 1) * 512], start=(f == 0), stop=(f == NFT - 1))
                        P.tt('vector', o_, po, gB[:, g_sel(c), hf * 512:(hf + 1) * 512], ALU.mult)
                        P.tt('vector', X1[:, c, hf * 512:(hf + 1) * 512], X1[:, c, hf * 512:(hf + 1) * 512], o_, ALU.add)
                    if qi + 1 < nq:
                        prep(qi + 1, i)
            RM.release(m0)

        if dbg != 'l0m':
            ffn(0, list(range(NCH)), lambda c: 1 if c < 2 else 0, lambda c: 3 if c < 2 else 1)
        if dbg == 'l0':
            for c in range(NCH):
                P.dma('sync', dbg_out[c], X1[:, c, :], store=True)

        mark('ffn0')
        if dbg in (None, 'l1m'):
            modulation(1)
            make_wm(1)
            m1 = RM.mark()
            awin = at_w_in[0].rearrange("(kc p) n -> p kc n", p=128)
            kT1 = RM.alloc([2, NCH * 128], BF16)
            Vx1 = RM.alloc([NCH, 4, 65], BF16)
            P.memset('vector', Vx1[:, :, :, 64:65], 1.0)
            qnb = RM.alloc([64], F32)
            knb = RM.alloc([64], F32)
            P.dma('sync', qnb, bc(at_qn, [128, 64]))
            P.dma('sync', knb, bc(at_kn, [128, 64]))
            snk = RM.alloc([16], F32)
            P.dma('sync', snk, bc(at_sink, [128, 16]))
            rT = RM.alloc([2, NW, 32], F32)
            P.dma('sync', rT.rearrange("p a s f -> p a (s f)"), ropeT.rearrange("a p n -> p a n"))
            amb = RM.alloc([4, 4, 128], BF16)
            sm = RM.alloc([8], F32)
            sinkexp = RM.alloc([16], F32)
            hTc = [RM.alloc([8, 128], BF16)] * 2
            nms = RM.alloc([8], F32)
            qn = RM.alloc([8, 64], F32)
            nsq = qn
            qr = RM.alloc([2, 8, 32], F32)
            qtk = RM.alloc([16, 64], BF16)
            qtk2 = RM.alloc([16, 64], BF16)
            m2 = RM.mark()
            amf = RM.alloc([4, 128], F32)
            absq = RM.alloc([128], F32)
            P.dma('sync', amf, amask.rearrange("c p n -> p c n"))
            P.ts('vector', amf, amf, 1.0, 30000.0, ALU.subtract, ALU.mult)
            for v4 in range(4):
                P.copy('vector', amb[:, v4], bc(amf[:, v4, :].unsqueeze(1), [128, 4, 128]))
            P.act(absq[:, 0:64], qnb, AF.Abs)
            P.act(absq[:, 64:128], knb, AF.Abs)
            P.reduce('vector', sm[:, 0:1], absq[:, 0:64], ALU.max)
            P.reduce('vector', sm[:, 1:2], absq[:, 64:128], ALU.max)
            P.tt('vector', sm[:, 2:3], sm[:, 0:1], sm[:, 1:2], ALU.mult)
            P.ts('vector', sm[:, 3:4], sm[:, 2:3], -8.0, None, ALU.mult)
            negB = sm[:, 3:4]
            P.act(sinkexp, snk, AF.Exp, bias=negB)
            RM.release(m2)
            Wkv1 = RM.alloc([8, 512], BF16)
            P.dma('gpsimd', Wkv1, awin[:, :, 1024:1536])

            def qknorm_rope(ps3, nh, wb_, wch, dst):
                P.act(nsq[:, 0:nh, :], ps3, AF.Square)
                P.reduce('vector', nms[:, 0:nh], nsq[:, 0:nh, :], ALU.add)
                P.act(nms[:, 0:nh], nms[:, 0:nh], AF.Sqrt, bias=epsb, scale=1.0 / 64)
                P.recip(nms[:, 0:nh], nms[:, 0:nh])
                P.tt('vector', qn[:, 0:nh, :], ps3, bc(nms[:, 0:nh].unsqueeze(2), [128, nh, 64]), ALU.mult)
                if wch < 0:
                    P.tt('vector', dst, qn[:, 0:nh, :], bc(wb_.unsqueeze(1), [128, nh, 64]), ALU.mult)
                    return
                P.tt('vector', qn[:, 0:nh, :], qn[:, 0:nh, :], bc(wb_.unsqueeze(1), [128, nh, 64]), ALU.mult)
                x1, x2 = qn[:, 0:nh, 0:32], qn[:, 0:nh, 32:64]
                cs = bc(rT[:, 0, wch, :].unsqueeze(1), [128, nh, 32])
                sn = bc(rT[:, 1, wch, :].unsqueeze(1), [128, nh, 32])
                P.tt('vector', qr[:, 0, 0:nh], x1, cs, ALU.mult)
                P.tt('vector', qr[:, 1, 0:nh], x2, sn, ALU.mult)
                P.tt('vector', dst[:, :, 0:32], qr[:, 0, 0:nh], qr[:, 1, 0:nh], ALU.subtract)
                P.tt('vector', qr[:, 0, 0:nh], x2, cs, ALU.mult)
                P.tt('vector', qr[:, 1, 0:nh], x1, sn, ALU.mult)
                P.tt('vector', dst[:, :, 32:64], qr[:, 0, 0:nh], qr[:, 1, 0:nh], ALU.add)

            for c in range(NCH):
                v_ = 1 if c < 2 else 0
                hc_ = hTc[c % 2]
                make_hT(X1[:, c, :], hc_, WM[:, 1, 0, v_, :], SHc[:, 1, 0, v_, :], alt_bank=7)
                pkv = pb(1 + (c % 2), [512])
                for kc in range(8):
                    P.mm(pkv, hc_[:, kc, :], Wkv1[:, kc, :], start=(kc == 0), stop=(kc == 7))
                P.copy('scalar', Vx1[:, c, :, 0:64], pkv[:, 256:512].rearrange("p (h d) -> p h d", h=4))
                qknorm_rope(pkv[:, 0:256].rearrange("p (h d) -> p h d", h=4), 4, knb, (c - 2) if c >= 2 else -1, qtk[:, 0:4, :])
                tpk = pb(4, [2, 128], BF16)
                for t_ in range(2):
                    P.transpose(tpk[:, t_, :], qtk[:, 2 * t_:2 * t_ + 2, :].rearrange("p h d -> p (h d)"), identB)
                P.copy('scalar', kT1[:, :, c * 128:(c + 1) * 128], tpk)
            mark('l1kv')
            RM.release(m2)
            Wq1 = RM.alloc([8, 1024], BF16)
            P.dma('gpsimd', Wq1, awin[:, :, 0:1024])
            Wo1 = RM.alloc([8, D], BF16)
            P.dma('gpsimd', Wo1, at_w_out[0].rearrange("(kc p) n -> p kc n", p=128))
            qTc = RM.alloc([2, 4, 128], BF16)
            Eb = [RM.alloc([5, 4, 128], BF16) for _ in range(2)]
            Otk = RM.alloc([16, 64], BF16)
            OT = RM.alloc([8, 128], BF16)
            dn1 = RM.alloc([8], F32)
            ot1 = [xsp, xsp]
            qTcs = [qTc, RM.alloc([2, 4, 128], BF16)]

            def att_prologue(i):
                c = i + 3
                hc_ = hTc[0]
                make_hT(X1[:, c, :], hc_, WM[:, 1, 0, 0, :], SHc[:, 1, 0, 0, :])
                for hf in range(2):
                    pq = pb(2 + hf, [512])
                    for kc in range(8):
                        P.mm(pq, hc_[:, kc, :], Wq1[:, kc, hf * 512:(hf + 1) * 512], start=(kc == 0), stop=(kc == 7))
                    qknorm_rope(pq.rearrange("p (h d) -> p h d", h=8), 8, qnb, c - 2, qtk[:, 8 * hf:8 * hf + 8, :])
                tpq = pb(4, [2, 4, 128], BF16)
                for tp_ in range(2):
                    P.copy('scalar', qtk2[:, 8 * tp_:8 * tp_ + 8, :].rearrange("p (j g) d -> p g j d", g=2),
                           qtk[:, 8 * tp_:8 * tp_ + 8, :].rearrange("p (g j) d -> p g j d", g=2))
                    for j in range(4):
                        src = qtk2[:, 8 * tp_ + 2 * j:8 * tp_ + 2 * j + 2, :]
                        P.transpose(tpq[:, tp_, j, :], src.rearrange("p h d -> p (h d)"), identB)
                P.copy('scalar', qTcs[i % 2], tpq)

            att_prologue(0)
            for i in range(16):
                c = i + 3
                qTc = qTcs[i % 2]
                kblocks = [(c - 1, 0 if i == 0 else 1), (c, None), (c + 1, 3 if i == 15 else 2), (0, None), (1, None)]
                st_banks = [[5, 6], [0, 2]]

                def att_front(g):
                    tp_, hb = g // 2, slice(64 * (g % 2), 64 * (g % 2) + 64)
                    E = Eb[g % 2]
                    for bi_, (kc_, mk) in enumerate(kblocks):
                        pst = pb(st_banks[g % 2][bi_ % 2], [4, 128])
                        P.mm(pst.rearrange("p j t -> p (j t)"), kT1[hb, tp_, kc_ * 128:(kc_ + 1) * 128],
                             qTc[hb, tp_].rearrange("p j t -> p (j t)"), start=True, stop=(mk is None))
                        if mk is not None:
                            P.mm(pst.rearrange("p j t -> p (j t)"), identB, amb[:, mk].rearrange("p j t -> p (j t)"), start=False, stop=True)
                        P.act(E[:, bi_], pst, AF.Exp, bias=negB, scale=0.125)

                def att_back(g):
                    E = Eb[g % 2]
                    pso = pb(7 if g % 2 == 0 else 3, [4, 65])
                    for j in range(4):
                        for bi_, (kc_, mk) in enumerate(kblocks):
                            P.mm(pso[:, j, :], E[:, bi_, j, :], Vx1[:, kc_, g, :], start=(bi_ == 0), stop=(bi_ == 4))
                    P.tt('vector', dn1[:, 0:4], pso[:, :, 64], sinkexp[:, 4 * g:4 * g + 4], ALU.add)
                    P.recip(dn1[:, 4:8], dn1[:, 0:4])
                    P.tt('vector', Otk[:, 4 * g:4 * g + 4, :], pso[:, :, 0:64], bc(dn1[:, 4:8].unsqueeze(2), [128, 4, 64]), ALU.mult)

                att_front(0)
                for g in range(4):
                    if g + 1 < 4:
                        att_front(g + 1)
                    att_back(g)
                    if g == 2 and i + 1 < 16:
                        att_prologue(i + 1)
                tpo = pb(4, [8, 128], BF16)
                for kc in range(8):
                    P.transpose(tpo[:, kc, :], Otk[:, 2 * kc:2 * kc + 2, :].rearrange("p h d -> p (h d)"), identB)
                P.copy('scalar', OT, tpo)
                for hf in range(2):
                    po = pb(1 + hf, [512])
                    for kc in range(8):
                        P.mm(po, OT[:, kc, :], Wo1[:, kc, hf * 512:(hf + 1) * 512], start=(kc == 0), stop=(kc == 7))
                    P.tt('vector', ot1[hf], po, gB[:, 0, hf * 512:(hf + 1) * 512], ALU.mult)
                    P.tt('vector', X1[:, c, hf * 512:(hf + 1) * 512], X1[:, c, hf * 512:(hf + 1) * 512], ot1[hf], ALU.add)
            mark('l1attn')
            RM.release(m1)
            if dbg == 'l1m':
                for c in range(NCH):
                    P.dma('sync', dbg_out[c], X1[:, c, :], store=True)
            else:
                ffn(1, list(range(3, 19)), lambda c: 0, lambda c: 1)
        for i in range(16):
            P.dma('sync', y[i * 128:(i + 1) * 128, :], X1[:, i + 3, :], store=True)
        mark('end')
        stats = P.emit()
        stats['marks'] = P.marks
        stats['peaks'] = (RP.peak, RX.peak, RM.peak)
    return nc, stats


def _rope_angles(pos):
    pos = np.asarray(pos)
    row = (pos // 64).astype(np.float32)
    col = (pos % 64).astype(np.float32)
    inv = (10000.0 ** (-np.arange(16, dtype=np.float32) / 16)).astype(np.float32)
    return np.concatenate([row[:, None] * inv, col[:, None] * inv], axis=-1).astype(np.float32)


def _core_inputs(inp, core):
    b, q = core // 4, core % 4
    x = np.asarray(inp['x'], np.float32)
    ctx = np.asarray(inp['ctx'], np.float32)
    xb = x[b].reshape(64, 128, D)
    fl = np.zeros((1, NF), np.float32)
    xw = np.zeros((NCH, 128, D), np.float32)
    xw[0:2] = ctx[b].reshape(2, 128, D)
    fl[0, FL_VF:FL_VF + 2] = 1.0
    wpos = np.full((NW, 128), -1, np.int64)
    for w in range(NW):
        ch = 16 * q - 1 + w
        if 0 <= ch < 64:
            xw[2 + w] = xb[ch]
            fl[0, FL_VF + 2 + w] = 1.0
            wpos[w] = ch * 128 + np.arange(128)
    xe = np.zeros((128, D), np.float32)
    tl = 128 * (16 * q - 1) - 1
    tr = 128 * (16 * q + 17)
    if 0 <= tl < SEQ:
        xe[0] = x[b, tl]
        fl[0, FL_EDGE] = 1.0
    if 0 <= tr < SEQ:
        xe[1] = x[b, tr]
        fl[0, FL_EDGE + 1] = 1.0
    slots = [(ch, 0) for ch in range(16 * q - 2, -1, -1)] + [(ch, 1) for ch in range(16 * q + 17, 64)]
    assert len(slots) <= NSLOT
    xs = np.zeros((NSLOT, 128, D), np.float32)
    xsh = np.zeros((128, D), np.float32)
    spos = np.zeros((NSLOT, 128), np.int64)
    for s, (ch, d_) in enumerate(slots):
        xs[s] = xb[ch]
        fl[0, (FL_FF if d_ == 0 else FL_FB) + s] = 1.0
        spos[s] = ch * 128 + np.arange(128)
        t0, t1 = ch * 128 - 1, ch * 128 + 128
        if t0 >= 0:
            xsh[2 * s] = x[b, t0]
            fl[0, FL_SH + 2 * s] = 1.0
        if t1 < SEQ:
            xsh[2 * s + 1] = x[b, t1]
            fl[0, FL_SH + 2 * s + 1] = 1.0
    wp = np.where(wpos < 0, 0, wpos).reshape(-1)
    ang = _rope_angles(wp)
    cosw, sinw = np.cos(ang), np.sin(ang)
    ropeT = np.stack([cosw.reshape(NW, 128, 32).transpose(1, 0, 2).reshape(128, NW * 32),
                      sinw.reshape(NW, 128, 32).transpose(1, 0, 2).reshape(128, NW * 32)]).astype(np.float32)
    p = np.arange(128)
    fi = p % 32
    cosF = cosw[:, fi].T
    sgn = np.where((p % 64) < 32, 1.0, -1.0)[:, None]
    sinF = sinw[:, fi].T * sgn
    ropeF = np.stack([cosF, sinF]).astype(np.float32)
    angs = _rope_angles(spos.reshape(-1))
    ropeS = np.stack([np.cos(angs).reshape(NSLOT, 128, 32).transpose(1, 0, 2).reshape(128, NSLOT * 32),
                      np.sin(angs).reshape(NSLOT, 128, 32).transpose(1, 0, 2).reshape(128, NSLOT * 32)]).astype(np.float32)
    u = np.arange(128)[:, None]
    s_ = np.arange(128)[None, :]
    consts = np.zeros((7, 128, 128), np.float32)
    consts[0] = np.eye(128)
    consts[1] = (u > s_)
    consts[2] = (u < s_)
    consts[3] = (u <= s_)
    consts[4] = (u >= s_)
    consts[5] = 1.0
    amask = np.zeros((4, 128, 128), np.float32)
    mL = (u >= s_).astype(np.float32)
    mR = (u <= s_).astype(np.float32)
    amask[0] = mL * (1.0 if q > 0 else 0.0)
    amask[1] = mL
    amask[2] = mR
    amask[3] = mR * (1.0 if q < 3 else 0.0)
    cvec = np.concatenate([np.asarray(inp['c'], np.float32)[b].reshape(8, 128),
                           np.asarray(inp['c_ctx'], np.float32).reshape(8, 128)], 0)
    m = dict(xw=xw, xe=xe, xs=xs, xsh=xsh, fl=fl, cvec=cvec, consts=consts, amask=amask,
             ropeF=ropeF, ropeT=ropeT, ropeS=ropeS)
    for k in ('ada_w', 'ada_b', 'norm_w', 'ffn_w_in', 'ffn_w_out', 'ab_w_in', 'ab_w_out', 'ret_log_gamma',
              'ret_norm_w', 'mlstm_conv_w', 'mlstm_conv_b', 'mlstm_gate_b', 'mlstm_norm_w', 'attn_w_in',
              'attn_w_out', 'attn_q_norm_w', 'attn_k_norm_w', 'attn_sink'):
        m[k] = np.ascontiguousarray(np.asarray(inp[k], np.float32))
    return m


_NC_CACHE = {}


def kernel(**inp):
    if 'nc' not in _NC_CACHE:
        _NC_CACHE['nc'] = build()[0]
    nc = _NC_CACHE['nc']
    in_maps = [_core_inputs(inp, c) for c in range(NCORE)]
    res = run_bass_kernel_spmd(nc, in_maps, core_ids=list(range(NCORE)))
    out = np.zeros((2, SEQ, D), np.float32)
    for c in range(NCORE):
        b, q = c // 4, c % 4
        out[b, 2048 * q:2048 * (q + 1)] = res.results[c]["y"]
    return out
```
